# Optimizing a Trainium2 kernel written in Bass

```python
import math
import jax, jax.numpy as jnp
from jax import lax
import numpy as np

D_MODEL = 1024
BATCH = 4
SEQ = 4096
DEPTH = 4

N_MIXERS = 3
GRID_W = 64
EPS = 1e-6
CONV_WIDTH = 31
NAT_HEADS = 16
NAT_HEAD_DIM = D_MODEL // NAT_HEADS
NAT_MAX_KH = 8
NAT_KW = 16
RET_HEADS = 4
RET_KEY_DIM = D_MODEL // RET_HEADS
RET_VAL_DIM = 2 * RET_KEY_DIM
RET_CHUNK = 128
RET_ROPE_BASE = 10000.0
FFN_DIM = ((8 * D_MODEL // 3 + 127) // 128) * 128
FFN_CONV_WIDTH = 3
N_CONV_LAYERS = (DEPTH + 2) // 3
N_NAT_LAYERS = (DEPTH + 1) // 3
N_RET_LAYERS = DEPTH // 3

kernel_name = 'hybrid_conv_nat_retention_encoder'


def rmsnorm(x, g):
    xf = x.astype(jnp.float32)
    y = xf * lax.rsqrt(jnp.mean(xf * xf, axis=-1, keepdims=True) + EPS)
    return y.astype(x.dtype) * g


def layernorm(x, g, b):
    xf = x.astype(jnp.float32)
    mu = jnp.mean(xf, axis=-1, keepdims=True)
    var = jnp.mean(jnp.square(xf - mu), axis=-1, keepdims=True)
    y = (xf - mu) * lax.rsqrt(var + EPS)
    return y.astype(x.dtype) * g + b


def dwconv(x, w, b):
    k, c = w.shape
    y = lax.conv_general_dilated(
        x, w[:, None, :].astype(x.dtype), window_strides=(1,),
        padding=[(k // 2, k // 2)], dimension_numbers=('NWC', 'WIO', 'NWC'),
        feature_group_count=c)
    return y + b


def conv_module(h, w_in, b_in, dw_w, dw_b, ln_g, ln_b, w_out):
    a, gt = jnp.split(h @ w_in + b_in, 2, axis=-1)
    u = a * jax.nn.sigmoid(gt)
    u = dwconv(u, dw_w, dw_b)
    u = layernorm(u, ln_g, ln_b)
    return jax.nn.silu(u) @ w_out


def nat_mixer(h, w_qkv, q_g, k_g, rpb, w_out):
    b, s, _ = h.shape
    rows = s // GRID_W
    kh = min(NAT_MAX_KH, rows)
    qkv = (h @ w_qkv).reshape(b, s, 3, NAT_HEADS, NAT_HEAD_DIM)
    q = rmsnorm(qkv[:, :, 0], q_g) * (NAT_HEAD_DIM ** -0.5)
    k = rmsnorm(qkv[:, :, 1], k_g)
    v = qkv[:, :, 2]

    def to_grid(t):
        return t.reshape(b, rows, GRID_W, NAT_HEADS, NAT_HEAD_DIM).transpose(0, 3, 1, 2, 4)

    qg, kg, vg = to_grid(q), to_grid(k), to_grid(v)
    cols = jnp.arange(GRID_W)
    col_start = jnp.clip(cols - NAT_KW // 2, 0, GRID_W - NAT_KW)
    col_idx = col_start[:, None] + jnp.arange(NAT_KW)[None, :]
    dc_idx = col_idx - cols[:, None] + (NAT_KW - 1)

    def row_fn(r):
        rs = jnp.clip(r - kh // 2, 0, rows - kh)
        q_r = lax.dynamic_index_in_dim(qg, r, axis=2, keepdims=False)
        k_win = lax.dynamic_slice_in_dim(kg, rs, kh, axis=2)[:, :, :, col_idx]
        v_win = lax.dynamic_slice_in_dim(vg, rs, kh, axis=2)[:, :, :, col_idx]
        dr_idx = rs + jnp.arange(kh) - r + (NAT_MAX_KH - 1)
        bias = rpb[:, dr_idx[:, None, None], dc_idx[None, :, :]]
        sc = jnp.einsum('bhqd,bhrqwd->bhqrw', q_r, k_win).astype(jnp.float32)
        sc = sc + bias.transpose(0, 2, 1, 3)[None].astype(jnp.float32)
        p = jax.nn.softmax(sc.reshape(b, NAT_HEADS, GRID_W, kh * NAT_KW), axis=-1)
        p = p.reshape(b, NAT_HEADS, GRID_W, kh, NAT_KW).astype(v_win.dtype)
        return jnp.einsum('bhqrw,bhrqwd->bhqd', p, v_win)

    out = lax.map(row_fn, jnp.arange(rows))
    out = out.transpose(1, 0, 3, 2, 4).reshape(b, s, D_MODEL)
    return out @ w_out


def rotary(x):
    s, d = x.shape[1], x.shape[-1]
    half = d // 2
    theta = 1.0 / (RET_ROPE_BASE ** jnp.linspace(0.0, 1.0, half, dtype=jnp.float32))
    ang = jnp.arange(s, dtype=jnp.float32)[:, None] * theta[None, :]
    cos, sin = jnp.cos(ang)[:, None, :], jnp.sin(ang)[:, None, :]
    x1, x2 = x[..., :half].astype(jnp.float32), x[..., half:].astype(jnp.float32)
    return jnp.concatenate([x1 * cos - x2 * sin, x1 * sin + x2 * cos], axis=-1)


def retention_scan(q, k, v, log_gamma, include_diag):
    b, h, s, dk = q.shape
    dv = v.shape[-1]
    nc = s // RET_CHUNK

    def chunks(t):
        return jnp.moveaxis(t.reshape(b, h, nc, RET_CHUNK, t.shape[-1]), 2, 0)

    pos = jnp.arange(RET_CHUNK, dtype=jnp.float32)
    diff = pos[:, None] - pos[None, :]
    mask = (diff >= 0) if include_diag else (diff > 0)
    expo = jnp.where(mask, diff, 0.0)[None] * log_gamma[:, None, None]
    decay_intra = jnp.where(mask[None], jnp.exp(expo), 0.0)
    q_decay = jnp.exp((pos + 1.0)[None, :] * log_gamma[:, None])
    k_decay = jnp.exp((RET_CHUNK - 1.0 - pos)[None, :] * log_gamma[:, None])
    chunk_decay = jnp.exp(RET_CHUNK * log_gamma)

    def step(state, inp):
        qc, kc, vc = inp
        inter = jnp.einsum('bhcd,bhde->bhce', qc, state) * q_decay[None, :, :, None]
        sc = jnp.einsum('bhid,bhjd->bhij', qc, kc) * decay_intra[None]
        out = inter + jnp.einsum('bhij,bhje->bhie', sc, vc)
        state = state * chunk_decay[None, :, None, None] + jnp.einsum(
            'bhjd,bhje->bhde', kc * k_decay[None, :, :, None], vc)
        return state, out

    state0 = jnp.zeros((b, h, dk, dv), jnp.float32)
    _, outs = lax.scan(step, state0, (chunks(q), chunks(k), chunks(v)))
    return jnp.moveaxis(outs, 0, 2).reshape(b, h, s, dv)


def retention_mixer(h, w_in, log2_inv_decay, gn_g, w_out):
    b, s, _ = h.shape
    hk = RET_HEADS * RET_KEY_DIM
    hv = RET_HEADS * RET_VAL_DIM
    q, k, v, g = jnp.split(h @ w_in, [hk, 2 * hk, 2 * hk + hv], axis=-1)
    q = rotary(q.reshape(b, s, RET_HEADS, RET_KEY_DIM)).transpose(0, 2, 1, 3)
    k = (rotary(k.reshape(b, s, RET_HEADS, RET_KEY_DIM)) * (RET_KEY_DIM ** -0.5)).transpose(0, 2, 1, 3)
    v = v.reshape(b, s, RET_HEADS, RET_VAL_DIM).transpose(0, 2, 1, 3).astype(jnp.float32)
    log_gamma = jnp.log1p(-jnp.exp2(-log2_inv_decay.astype(jnp.float32)))
    o_fwd = retention_scan(q, k, v, log_gamma[0], True)
    o_bwd = jnp.flip(retention_scan(jnp.flip(q, 2), jnp.flip(k, 2), jnp.flip(v, 2),
                                    log_gamma[1], False), 2)
    o = (o_fwd + o_bwd).transpose(0, 2, 1, 3)
    mu = jnp.mean(o, axis=-1, keepdims=True)
    var = jnp.mean(jnp.square(o - mu), axis=-1, keepdims=True)
    o = ((o - mu) * lax.rsqrt(var + EPS)).reshape(b, s, hv).astype(h.dtype) * gn_g
    return (jax.nn.silu(g) * o) @ w_out


def conv_ffn(h, w_up, dw_w, dw_b, w_down):
    u = dwconv(h @ w_up, dw_w, dw_b)
    val, gate = jnp.split(u, 2, axis=-1)
    return (jax.nn.gelu(gate) * val) @ w_down


def setup_inputs(seed: int = 0) -> dict:
    key = jax.random.key(seed)
    ks = iter(jax.random.split(key, 32))
    f32 = jnp.float32
    D = D_MODEL

    def nrm(shape, scale):
        return jax.random.normal(next(ks), shape, f32) * scale

    def gain(shape):
        return 1.0 + nrm(shape, 0.02)

    ret_in = 2 * RET_HEADS * RET_KEY_DIM + 2 * RET_HEADS * RET_VAL_DIM
    hv = RET_HEADS * RET_VAL_DIM
    decay_base = 5.0 + jnp.arange(RET_HEADS, dtype=f32)
    return {
        'x': nrm((BATCH, SEQ, D), 1.0),
        'norm1_g': gain((DEPTH, D)),
        'norm2_g': gain((DEPTH, D)),
        'conv_w_in': nrm((N_CONV_LAYERS, D, 2 * D), D ** -0.5),
        'conv_b_in': nrm((N_CONV_LAYERS, 2 * D), 0.02),
        'conv_dw_w': nrm((N_CONV_LAYERS, CONV_WIDTH, D), CONV_WIDTH ** -0.5),
        'conv_dw_b': nrm((N_CONV_LAYERS, D), 0.02),
        'conv_ln_g': gain((N_CONV_LAYERS, D)),
        'conv_ln_b': nrm((N_CONV_LAYERS, D), 0.02),
        'conv_w_out': nrm((N_CONV_LAYERS, D, D), D ** -0.5),
        'nat_w_qkv': nrm((N_NAT_LAYERS, D, 3 * D), D ** -0.5),
        'nat_q_norm_g': gain((N_NAT_LAYERS, NAT_HEAD_DIM)),
        'nat_k_norm_g': gain((N_NAT_LAYERS, NAT_HEAD_DIM)),
        'nat_rpb': nrm((N_NAT_LAYERS, NAT_HEADS, 2 * NAT_MAX_KH - 1, 2 * NAT_KW - 1), 0.02),
        'nat_w_out': nrm((N_NAT_LAYERS, D, D), D ** -0.5),
        'ret_w_in': nrm((N_RET_LAYERS, D, ret_in), D ** -0.5),
        'ret_log2_inv_decay': decay_base[None, None, :] + nrm((N_RET_LAYERS, 2, RET_HEADS), 0.1),
        'ret_gn_g': gain((N_RET_LAYERS, hv)),
        'ret_w_out': nrm((N_RET_LAYERS, hv, D), hv ** -0.5),
        'ffn_w_up': nrm((DEPTH, D, 2 * FFN_DIM), D ** -0.5),
        'ffn_dw_w': nrm((DEPTH, FFN_CONV_WIDTH, 2 * FFN_DIM), FFN_CONV_WIDTH ** -0.5),
        'ffn_dw_b': nrm((DEPTH, 2 * FFN_DIM), 0.02),
        'ffn_w_down': nrm((DEPTH, FFN_DIM, D), FFN_DIM ** -0.5),
    }


def reference(x, norm1_g, norm2_g, conv_w_in, conv_b_in, conv_dw_w, conv_dw_b,
              conv_ln_g, conv_ln_b, conv_w_out, nat_w_qkv, nat_q_norm_g,
              nat_k_norm_g, nat_rpb, nat_w_out, ret_w_in, ret_log2_inv_decay,
              ret_gn_g, ret_w_out, ffn_w_up, ffn_dw_w, ffn_dw_b, ffn_w_down):
    for i in range(DEPTH):
        mixer, j = i % N_MIXERS, i // N_MIXERS
        h = rmsnorm(x, norm1_g[i])
        if mixer == 0:
            y = conv_module(h, conv_w_in[j], conv_b_in[j], conv_dw_w[j], conv_dw_b[j],
                            conv_ln_g[j], conv_ln_b[j], conv_w_out[j])
        elif mixer == 1:
            y = nat_mixer(h, nat_w_qkv[j], nat_q_norm_g[j], nat_k_norm_g[j],
                          nat_rpb[j], nat_w_out[j])
        else:
            y = retention_mixer(h, ret_w_in[j], ret_log2_inv_decay[j], ret_gn_g[j],
                                ret_w_out[j])
        x = x + y.astype(x.dtype)
        h = rmsnorm(x, norm2_g[i])
        x = x + conv_ffn(h, ffn_w_up[i], ffn_dw_w[i], ffn_dw_b[i], ffn_w_down[i]).astype(x.dtype)
    return x
```

```python
import contextlib
import numpy as np
import concourse.bass as bass
import concourse.mybir as mybir
from concourse.bass_utils import run_bass_kernel_spmd

F32 = mybir.dt.float32
BF16 = mybir.dt.bfloat16
AF = mybir.ActivationFunctionType
ALU = mybir.AluOpType
AX = mybir.AxisListType

D = 1024
SEQ = 4096
BATCH = 4
T = 2048
NCORES = 8
FFN = 2816
NH = FFN // 128
EPS = 1e-6


class Tok:
    __slots__ = ("lw", "rd", "name", "sem", "dcount", "last_dma")

    def __init__(self, name=""):
        self.lw = None
        self.rd = []
        self.name = name
        self.sem = None
        self.dcount = 0
        self.last_dma = None


class Op:
    __slots__ = ("eng", "fn", "deps", "is_dma", "dtok", "has_dep", "sem", "val",
                 "waits", "know", "is_out", "inc", "is_barrier")

    def __init__(self, eng, fn, is_dma=False, dtok=None):
        self.eng = eng
        self.fn = fn
        self.deps = set()
        self.is_dma = is_dma
        self.dtok = dtok
        self.has_dep = False
        self.sem = None
        self.val = 0
        self.waits = ()
        self.know = None
        self.is_out = False
        self.inc = 16 if is_dma else 1
        self.is_barrier = False


class Prog:
    ENGS = ("pe", "act", "dve", "pool", "sp")

    def __init__(self, nc, stack):
        self.nc = nc
        self.stack = stack
        self.ops = []
        self.nsb = 0
        self.nsem = 0
        self.out_ops = []
        self.pfx = ""
        self.bar_start = 0
        self.prev_bar = []

    def sb(self, shape, dtype, name=None):
        self.nsb += 1
        return self.stack.enter_context(
            self.nc.sbuf_tensor(self.pfx + (name or f"sb{self.nsb}"), list(shape), dtype))

    def ps(self, shape, dtype, name=None):
        self.nsb += 1
        return self.stack.enter_context(
            self.nc.psum_tensor(self.pfx + (name or f"ps{self.nsb}"), list(shape), dtype))

    def barrier(self):
        last = {}
        dmas = {}
        for op in self.ops[self.bar_start:]:
            if op.is_dma:
                dmas[id(op.dtok)] = op
            else:
                last[op.eng] = op
        deps = set(last.values()) | set(dmas.values()) | set(self.prev_bar)
        bars = []
        for eng in self.ENGS:
            op = Op(eng, lambda e: e.nop())
            op.deps = set(deps)
            op.is_barrier = (eng == self.ENGS[0])
            self.ops.append(op)
            bars.append(op)
        self.prev_bar = bars
        self.bar_start = len(self.ops)

    def new_sem(self, name=None):
        self.nsem += 1
        return self.stack.enter_context(self.nc.semaphore(name or f"sem{self.nsem}"))

    def add(self, eng, fn, reads=(), writes=(), is_dma=False, dtok=None, is_out=False):
        op = Op(eng, fn, is_dma, dtok)
        op.is_out = is_out
        for t in reads:
            if t.lw is not None:
                op.deps.add(t.lw)
        for t in writes:
            for r in t.rd:
                op.deps.add(r)
            if t.lw is not None:
                op.deps.add(t.lw)
        if is_dma:
            if dtok.last_dma is not None:
                op.deps.add(dtok.last_dma)
            dtok.last_dma = op
        for t in reads:
            t.rd.append(op)
        for t in writes:
            t.rd = []
            t.lw = op
        op.deps.discard(op)
        if eng == "pe" and not is_dma:
            op.deps = {d for d in op.deps if not (d.eng == "pe" and not d.is_dma)}
        self.ops.append(op)
        if is_out:
            self.out_ops.append(op)
        return op

    def coll(self, fn, reads, writes, dtok):
        op = self.add("pool", fn, reads, writes, is_dma=True, dtok=dtok)
        op.inc = 1
        return op

    def dma(self, queue, out, in_, reads, writes, dtok, is_out=False, **kw):
        return self.add(queue, lambda e: e.dma_start(out=out, in_=in_, **kw),
                        reads, writes, is_dma=True, dtok=dtok, is_out=is_out)

    def emit(self):
        ops = self.ops
        for op in ops:
            for d in op.deps:
                d.has_dep = True
        esem = {e: self.new_sem("eng_" + e) for e in self.ENGS}
        cnt = {e: 0 for e in self.ENGS}
        free_sems = []
        live_toks = []
        for op in ops:
            if op.is_barrier:
                for t in live_toks:
                    free_sems.append((t.sem, t.dcount))
                live_toks = []
            if op.is_dma:
                t = op.dtok
                if t.sem is None:
                    if free_sems:
                        t.sem, t.dcount = free_sems.pop()
                    else:
                        t.sem = self.new_sem()
                    live_toks.append(t)
                t.dcount += op.inc
                op.sem = t.sem
                op.val = t.dcount
            elif op.has_dep:
                cnt[op.eng] += 1
                op.sem = esem[op.eng]
                op.val = cnt[op.eng]
        seen = {e: {} for e in self.ENGS}
        nwaits = 0
        for op in ops:
            s = seen[op.eng]
            waits = {}
            for d in op.deps:
                k = id(d.sem)
                if s.get(k, (None, 0))[1] >= d.val:
                    continue
                if waits.get(k, (None, 0))[1] < d.val:
                    waits[k] = (d.sem, d.val)
            for d in op.deps:
                if d.know is not None:
                    for k, v in d.know.items():
                        if s.get(k, (None, 0))[1] < v[1]:
                            s[k] = v
            for k, v in waits.items():
                if s.get(k, (None, 0))[1] < v[1]:
                    s[k] = v
            op.waits = list(waits.values())
            nwaits += len(op.waits)
            if op.sem is not None:
                kn = dict(s)
                kn[id(op.sem)] = (op.sem, op.val)
                op.know = kn
                if not op.is_dma:
                    s[id(op.sem)] = (op.sem, op.val)
        by = {e: [o for o in ops if o.eng == e] for e in self.ENGS}
        finals = [(o.sem, o.val) for o in self.out_ops]
        self.stats = dict(nops=len(ops), nwaits=nwaits,
                          per_eng={e: len(by[e]) for e in self.ENGS}, nsem=self.nsem)

        def run(name, e):
            for op in by[name]:
                for sem, val in op.waits:
                    e.wait_ge(sem, val)
                inst = op.fn(e)
                if op.sem is not None:
                    inst.then_inc(op.sem, op.inc)
            if name == "sp":
                for sem, val in finals:
                    e.wait_ge(sem, val)

        with self.nc.Block() as block:
            @block.tensor
            def _(e):
                run("pe", e)

            @block.scalar
            def _(e):
                run("act", e)

            @block.vector
            def _(e):
                run("dve", e)

            @block.gpsimd
            def _(e):
                run("pool", e)

            @block.sync
            def _(e):
                run("sp", e)


class Ring:
    def __init__(self, P, n, shape, dtype, name, psum=False):
        self.bufs = []
        for i in range(n):
            t = (P.ps if psum else P.sb)(shape, dtype, f"{name}{i}")
            self.bufs.append((t, Tok(f"{name}{i}")))
        self.i = 0

    def next(self):
        b = self.bufs[self.i % len(self.bufs)]
        self.i += 1
        return b


def mm(P, out, lhsT, rhs, start, stop, reads, writes, skip=False):
    return P.add("pe", lambda e: e.matmul(out, lhsT, rhs, start=start, stop=stop, skip_group_check=skip),
                 reads, writes)


def emit_rmsnorm(P, C, x_dram, g_col, gtok, h_all, h_tok, ps_ring, tiles, two_pass=False, hcol=None):
    fr = C["fr"]
    sqr = C["sqr"]
    ones = C["ones_bf"]
    assert len(fr.bufs) >= (4 if two_pass else 9)
    for ti, (t0, n) in enumerate(tiles):
        d0 = t0 if hcol is None else hcol[ti]
        xs = []
        pst, pstok = ps_ring.next()
        for c in range(8):
            xt, xtok = fr.next()
            P.dma("sp", xt[:, 0:n], x_dram[c * 128:(c + 1) * 128, t0:t0 + n], [], [xtok], xtok)
            xs.append((xt, xtok))
            sq, sqtok = sqr.next()
            P.add("act", lambda e, o=sq[:, 0:n], i=xt[:, 0:n]: e.activation(out=o, in_=i, func=AF.Square),
                  [xtok], [sqtok])
            mm(P, pst[:, 0:n], ones[:], sq[:, 0:n], c == 0, c == 7, [sqtok, C["ctok"]], [pstok])
        rs, rstok = C["rsr"].next()
        P.add("act", lambda e, o=rs[:, 0:n], i=pst[:, 0:n]: e.activation(
            out=o, in_=i, func=AF.Sqrt, bias=C["eps_col"][:, 0:1], scale=1.0 / D), [pstok, C["ctok"]], [rstok])
        P.add("dve", lambda e, o=rs[:, 0:n]: e.reciprocal(out=o, in_=o), [rstok], [rstok])
        for c in range(8):
            if two_pass:
                xt, xtok = fr.next()
                P.dma("sp", xt[:, 0:n], x_dram[c * 128:(c + 1) * 128, t0:t0 + n], [], [xtok], xtok)
            else:
                xt, xtok = xs[c]
            P.add("dve", lambda e, o=h_all[:, c, d0:d0 + n], i=xt[:, 0:n], g=g_col[:, c:c + 1], r=rs[:, 0:n]:
                  e.scalar_tensor_tensor(out=o, in0=i, scalar=g, in1=r, op0=ALU.mult, op1=ALU.mult),
                  [xtok, rstok, gtok], [h_tok[ti][c]])


def make_common(P, nfr=14):
    C = {}
    C["fr"] = Ring(P, nfr, [128, 512], F32, "fr")
    C["sqr"] = Ring(P, 3, [128, 512], BF16, "sqr")
    C["rsr"] = Ring(P, 2, [128, 512], F32, "rsr")
    C["ones_bf"] = P.sb([128, 128], BF16, "ones_bf")
    C["eps_col"] = P.sb([128, 1], F32, "eps_col")
    C["ctok"] = Tok("consts")
    P.add("pool", lambda e: e.memset(C["ones_bf"][:], 1.0), [], [C["ctok"]])
    P.add("pool", lambda e: e.memset(C["eps_col"][:], EPS), [], [C["ctok"]])
    return C


def load_cols(P, C, dst, src_dram, scale=None):
    tok = Tok("par")
    P.dma("sp", dst[:], src_dram, [], [tok], tok)
    if scale is not None:
        P.add("pool", lambda e: e.tensor_scalar(out=dst[:], in0=dst[:], scalar1=float(scale), scalar2=None,
                                                 op0=ALU.mult), [tok], [tok])
    return tok


def htoks_for(h_tok, tiles, c, lo, hi):
    return [h_tok[i][c] for i, (t0, n) in enumerate(tiles) if t0 < hi and t0 + n > lo]


def emit_ffn(P, C, x_in, x_out, g2c, w_up, dww, dwb, w_down, is_out=False):
    NT = T + 2
    g_col = P.sb([128, 8], F32, "ffn_g")
    gtok = load_cols(P, C, g_col, g2c)
    dww_sb = P.sb([128, 3 * 2 * NH], F32, "ffn_dww")
    dwb_sb = P.sb([128, 2 * NH], F32, "ffn_dwb")
    dwtok = load_cols(P, C, dww_sb, dww)
    dbtok = load_cols(P, C, dwb_sb, dwb)

    h_all = P.sb([128, 8, NT], BF16, "ffn_h")
    tiles = [(0, 410), (410, 410), (820, 410), (1230, 410), (1640, NT - 1640)]
    h_tok = [[Tok(f"h{i}_{c}") for c in range(8)] for i in range(len(tiles))]
    ps_stat = Ring(P, 1, [128, 512], F32, "ps_stat", psum=True)
    emit_rmsnorm(P, C, x_in, g_col, gtok, h_all, h_tok, ps_stat, tiles)

    act_all = P.sb([128, NH, T], BF16, "ffn_act")
    act_tok = [[Tok(f"act{j}_{i}") for i in range(5)] for j in range(NH)]

    wst = Ring(P, 2, [128, NH * 128], F32, "wst")
    wbf = Ring(P, 2, [128, NH * 128], BF16, "wbf")
    ps_up = Ring(P, 4, [128, 512], F32, "ps_up", psum=True)
    fr = C["fr"]
    w_up_v = w_up.rearrange("(kc p) n -> p kc n", p=128)
    ctiles = [(0, 410), (410, 410), (820, 410), (1230, 410), (1640, 408)]
    for j in range(NH):
        st, sttok = wst.next()
        stv = st[:, 0:2048].rearrange("p (k n) -> p k n", k=8)
        P.dma("sp", stv[:, :, 0:128], w_up_v[:, :, j * 128:(j + 1) * 128], [], [sttok], sttok)
        P.dma("sp", stv[:, :, 128:256], w_up_v[:, :, FFN + j * 128:FFN + (j + 1) * 128], [], [sttok], sttok)
        wb, wbtok = wbf.next()
        P.add("pool", lambda e, o=wb[:, 0:2048], i=st[:, 0:2048]: e.tensor_copy(out=o, in_=i), [sttok], [wbtok])
        wbv = wb[:, 0:2048].rearrange("p (k n) -> p k n", k=8)
        for ci, (o0, n) in enumerate(ctiles):
            ncol = n + 2
            pv, pvtok = ps_up.next()
            pg, pgtok = ps_up.next()
            for half, (pt, pttok) in enumerate(((pv, pvtok), (pg, pgtok))):
                for kc in range(8):
                    mm(P, pt[:, 0:ncol], wbv[:, kc, half * 128:(half + 1) * 128], h_all[:, kc, o0:o0 + ncol],
                       kc == 0, kc == 7, [wbtok] + htoks_for(h_tok, tiles, kc, o0, o0 + ncol), [pttok])
            av, avtok = fr.next()
            ag, agtok = fr.next()
            for half, (pt, pttok, acc, acctok) in enumerate(((pv, pvtok, av, avtok), (pg, pgtok, ag, agtok))):
                ch = half * NH + j
                w0 = dww_sb[:, 0 * 2 * NH + ch:0 * 2 * NH + ch + 1]
                w1 = dww_sb[:, 1 * 2 * NH + ch:1 * 2 * NH + ch + 1]
                w2 = dww_sb[:, 2 * 2 * NH + ch:2 * 2 * NH + ch + 1]
                bb = dwb_sb[:, ch:ch + 1]
                P.add("act", lambda e, o=acc[:, 0:n], i=pt[:, 1:n + 1], s=w1, b=bb:
                      e.activation(out=o, in_=i, func=AF.Identity, bias=b, scale=s),
                      [pttok, dwtok, dbtok], [acctok])
                P.add("dve", lambda e, o=acc[:, 0:n], i=pt[:, 0:n], s=w0:
                      e.scalar_tensor_tensor(out=o, in0=i, scalar=s, in1=o, op0=ALU.mult, op1=ALU.add),
                      [pttok, acctok, dwtok], [acctok])
                P.add("dve", lambda e, o=acc[:, 0:n], i=pt[:, 2:n + 2], s=w2:
                      e.scalar_tensor_tensor(out=o, in0=i, scalar=s, in1=o, op0=ALU.mult, op1=ALU.add),
                      [pttok, acctok, dwtok], [acctok])
            ge, getok = fr.next()
            P.add("act", lambda e, o=ge[:, 0:n], i=ag[:, 0:n]: e.activation(out=o, in_=i, func=AF.Gelu_apprx_tanh),
                  [agtok], [getok])
            P.add("pool", lambda e, o=act_all[:, j, o0:o0 + n], a=ge[:, 0:n], b=av[:, 0:n]:
                  e.tensor_tensor(out=o, in0=a, in1=b, op=ALU.mult), [getok, avtok], [act_tok[j][ci]])

    ps_dn = Ring(P, 2, [128, 512], F32, "ps_dn", psum=True)
    w_dn_v = w_down.rearrange("(j p) n -> p j n", p=128)
    for o in range(8):
        st, sttok = wst.next()
        stv = st[:].rearrange("p (j n) -> p j n", j=NH)
        P.dma("sp", stv, w_dn_v[:, :, o * 128:(o + 1) * 128], [], [sttok], sttok)
        wb, wbtok = wbf.next()
        P.add("pool", lambda e, oo=wb[:], i=st[:]: e.tensor_copy(out=oo, in_=i), [sttok], [wbtok])
        wbv = wb[:].rearrange("p (j n) -> p j n", j=NH)
        for tt in range(T // 512):
            t0 = tt * 512
            xt, xtok = fr.next()
            P.dma("sp", xt[:], x_in[o * 128:(o + 1) * 128, 1 + t0:1 + t0 + 512], [], [xtok], xtok)
            pt, pttok = ps_dn.next()
            for j in range(NH):
                rd = [wbtok] + [act_tok[j][ci] for ci, (o0, n) in enumerate(ctiles) if o0 < t0 + 512 and o0 + n > t0]
                mm(P, pt[:], wbv[:, j, :], act_all[:, j, t0:t0 + 512], j == 0, j == NH - 1, rd, [pttok])
            P.add("dve", lambda e, oo=xt[:], a=pt[:]: e.tensor_tensor(out=oo, in0=a, in1=oo, op=ALU.add),
                  [pttok, xtok], [xtok])
            P.dma("sp", x_out[o * 128:(o + 1) * 128, t0:t0 + 512], xt[:], [xtok], [], xtok, is_out=is_out)


CW = 31
HC = 15


def emit_conv(P, C, x_in, x_out, mask, g1c, w_in, b_in, dw_w, dw_b, ln_g, ln_b, w_out, is_out=False):
    NT = T + 2 * HC
    fr = C["fr"]
    g_col = P.sb([128, 8], F32, "cv_g")
    gtok = load_cols(P, C, g_col, g1c)
    bin_sb = P.sb([128, 16], F32, "cv_bin")
    bintok = load_cols(P, C, bin_sb, b_in)
    dww_sb = P.sb([128, CW * 8], F32, "cv_dww")
    dwwtok = load_cols(P, C, dww_sb, dw_w)
    dwb_sb = P.sb([128, 8], F32, "cv_dwb")
    dwbtok = load_cols(P, C, dwb_sb, dw_b)
    lng_sb = P.sb([128, 8], F32, "cv_lng")
    lngtok = load_cols(P, C, lng_sb, ln_g)
    lnb_sb = P.sb([128, 8], F32, "cv_lnb")
    lnbtok = load_cols(P, C, lnb_sb, ln_b)
    mask_sb = P.sb([128, 2 * HC], F32, "cv_mask")
    masktok = load_cols(P, C, mask_sb, mask)
    ones_f = P.sb([128, 128], F32, "ones_f")
    P.add("pool", lambda e: e.memset(ones_f[:], 1.0), [], [C["ctok"]])

    h_all = P.sb([128, 8, NT], BF16, "cv_h")
    tiles = [(0, 416), (416, 416), (832, 416), (1248, 416), (1664, NT - 1664)]
    h_tok = [[Tok(f"cvh{i}_{c}") for c in range(8)] for i in range(len(tiles))]
    ps_stat = Ring(P, 2, [128, 512], F32, "ps_stat", psum=True)
    emit_rmsnorm(P, C, x_in, g_col, gtok, h_all, h_tok, ps_stat, tiles)

    v_all = P.sb([128, 8, T], F32, "cv_v")
    v_tok = [[Tok(f"cvv{c}_{i}") for i in range(4)] for c in range(8)]
    ur = Ring(P, 2, [128, NT], F32, "cv_u")
    wst = Ring(P, 2, [128, 2048], F32, "cv_wst")
    wbf = Ring(P, 2, [128, 2048], BF16, "cv_wbf")
    ps_up = Ring(P, 4, [128, 512], F32, "ps_up", psum=True)
    w_in_v = w_in.rearrange("(kc p) n -> p kc n", p=128)
    KD = 20
    for c in range(8):
        st, sttok = wst.next()
        stv = st[:].rearrange("p (k n) -> p k n", k=8)
        P.dma("sp", stv[:, :, 0:128], w_in_v[:, :, c * 128:(c + 1) * 128], [], [sttok], sttok)
        P.dma("sp", stv[:, :, 128:256], w_in_v[:, :, D + c * 128:D + (c + 1) * 128], [], [sttok], sttok)
        wb, wbtok = wbf.next()
        P.add("pool", lambda e, o=wb[:], i=st[:]: e.tensor_copy(out=o, in_=i), [sttok], [wbtok])
        wbv = wb[:].rearrange("p (k n) -> p k n", k=8)
        u, utok = ur.next()
        for ti, (t0, n) in enumerate(tiles):
            pa, patok = ps_up.next()
            pg, pgtok = ps_up.next()
            for half, (pt, pttok) in enumerate(((pa, patok), (pg, pgtok))):
                for kc in range(8):
                    mm(P, pt[:, 0:n], wbv[:, kc, half * 128:(half + 1) * 128], h_all[:, kc, t0:t0 + n],
                       kc == 0, kc == 7, [wbtok, h_tok[ti][kc]], [pttok])
            sg, sgtok = fr.next()
            P.add("act", lambda e, o=sg[:, 0:n], i=pg[:, 0:n], b=bin_sb[:, 8 + c:9 + c]:
                  e.activation(out=o, in_=i, func=AF.Sigmoid, bias=b, scale=1.0), [pgtok, bintok], [sgtok])
            P.add("dve", lambda e, o=u[:, t0:t0 + n], i=pa[:, 0:n], b=bin_sb[:, c:c + 1], g=sg[:, 0:n]:
                  e.scalar_tensor_tensor(out=o, in0=i, scalar=b, in1=g, op0=ALU.add, op1=ALU.mult),
                  [patok, sgtok, bintok], [utok])
        P.add("pool", lambda e, o=u[:, 0:HC], m=mask_sb[:, 0:HC]: e.tensor_tensor(out=o, in0=o, in1=m, op=ALU.mult),
              [utok, masktok], [utok])
        P.add("pool", lambda e, o=u[:, T + HC:NT], m=mask_sb[:, HC:2 * HC]: e.tensor_tensor(out=o, in0=o, in1=m, op=ALU.mult),
              [utok, masktok], [utok])
        for tt in range(4):
            t0 = tt * 512
            va = v_all[:, c, t0:t0 + 512]
            vb, vbtok = fr.next()
            wk = lambda k: dww_sb[:, k * 8 + c:k * 8 + c + 1]
            P.add("act", lambda e, o=va, i=u[:, t0:t0 + 512], s=wk(0), b=dwb_sb[:, c:c + 1]:
                  e.activation(out=o, in_=i, func=AF.Identity, bias=b, scale=s), [utok, dwwtok, dwbtok], [v_tok[c][tt]])
            for k in range(1, KD + 1):
                P.add("dve", lambda e, o=va, i=u[:, t0 + k:t0 + k + 512], s=wk(k):
                      e.scalar_tensor_tensor(out=o, in0=i, scalar=s, in1=o, op0=ALU.mult, op1=ALU.add),
                      [utok, dwwtok, v_tok[c][tt]], [v_tok[c][tt]])
            P.add("act", lambda e, o=vb[:], i=u[:, t0 + KD + 1:t0 + KD + 1 + 512], s=wk(KD + 1):
                  e.activation(out=o, in_=i, func=AF.Identity, scale=s), [utok, dwwtok], [vbtok])
            for k in range(KD + 2, CW):
                tp, tptok = fr.next()
                P.add("act", lambda e, o=tp[:], i=u[:, t0 + k:t0 + k + 512], s=wk(k):
                      e.activation(out=o, in_=i, func=AF.Identity, scale=s), [utok, dwwtok], [tptok])
                P.add("pool", lambda e, o=vb[:], a=tp[:]: e.tensor_tensor(out=o, in0=o, in1=a, op=ALU.add),
                      [tptok, vbtok], [vbtok])
            P.add("dve", lambda e, o=va, b=vb[:]: e.tensor_tensor(out=o, in0=o, in1=b, op=ALU.add),
                  [vbtok, v_tok[c][tt]], [v_tok[c][tt]])

    wo_bf = P.sb([128, 8, D], BF16, "cv_wo")
    wotok = [Tok(f"wo{i}") for i in range(4)]
    w_out_v = w_out.rearrange("(kc p) n -> p kc n", p=128)
    for i in range(4):
        st, sttok = wst.next()
        stv = st[:].rearrange("p (k n) -> p k n", k=8)
        P.dma("sp", stv, w_out_v[:, :, i * 256:(i + 1) * 256], [], [sttok], sttok)
        P.add("pool", lambda e, o=wo_bf[:, :, i * 256:(i + 1) * 256], s_=stv: e.tensor_copy(out=o, in_=s_),
              [sttok], [wotok[i]])

    ps_o = Ring(P, 2, [128, 512], F32, "ps_o", psum=True)
    for tt in range(4):
        t0 = tt * 512
        p1, p1tok = ps_stat.next()
        p2, p2tok = ps_stat.next()
        for c in range(8):
            mm(P, p1[:], ones_f[:], v_all[:, c, t0:t0 + 512], c == 0, c == 7, [v_tok[c][tt], C["ctok"]], [p1tok])
        for c in range(8):
            sq, sqtok = fr.next()
            P.add("act", lambda e, o=sq[:], i=v_all[:, c, t0:t0 + 512]: e.activation(out=o, in_=i, func=AF.Square),
                  [v_tok[c][tt]], [sqtok])
            mm(P, p2[:], ones_f[:], sq[:], c == 0, c == 7, [sqtok, C["ctok"]], [p2tok])
        mu, mutok = fr.next()
        P.add("act", lambda e, o=mu[:], i=p1[:]: e.activation(out=o, in_=i, func=AF.Identity, scale=1.0 / D),
              [p1tok], [mutok])
        rs, rstok = fr.next()
        P.add("dve", lambda e, o=rs[:], a=mu[:]: e.tensor_tensor(out=o, in0=a, in1=a, op=ALU.mult), [mutok], [rstok])
        P.add("dve", lambda e, o=rs[:], i=p2[:]: e.scalar_tensor_tensor(out=o, in0=i, scalar=1.0 / D, in1=o,
                                                                        op0=ALU.mult, op1=ALU.subtract),
              [p2tok, rstok], [rstok])
        P.add("act", lambda e, o=rs[:]: e.activation(out=o, in_=o, func=AF.Sqrt, bias=C["eps_col"][:, 0:1], scale=1.0),
              [rstok, C["ctok"]], [rstok])
        P.add("dve", lambda e, o=rs[:]: e.reciprocal(out=o, in_=o), [rstok], [rstok])
        for c in range(8):
            dd, ddtok = fr.next()
            P.add("pool", lambda e, o=dd[:], a=v_all[:, c, t0:t0 + 512], m=mu[:]:
                  e.tensor_tensor(out=o, in0=a, in1=m, op=ALU.subtract), [v_tok[c][tt], mutok], [ddtok])
            P.add("dve", lambda e, o=dd[:], r=rs[:]: e.tensor_tensor(out=o, in0=o, in1=r, op=ALU.mult),
                  [ddtok, rstok], [ddtok])
            P.add("act", lambda e, o=h_all[:, c, t0:t0 + 512], i=dd[:], g=lng_sb[:, c:c + 1], b=lnb_sb[:, c:c + 1]:
                  e.activation(out=o, in_=i, func=AF.Silu, bias=b, scale=g),
                  [ddtok, lngtok, lnbtok], [h_tok[i][c] for i in range(len(tiles))])
        for o in range(8):
            pt, pttok = ps_o.next()
            for kc in range(8):
                mm(P, pt[:], wo_bf[:, kc, o * 128:(o + 1) * 128], h_all[:, kc, t0:t0 + 512], kc == 0, kc == 7,
                   [wotok[o // 2], h_tok[tt][kc]], [pttok])
            xt, xtok = fr.next()
            P.dma("sp", xt[:], x_in[o * 128:(o + 1) * 128, HC + t0:HC + t0 + 512], [], [xtok], xtok)
            P.add("dve", lambda e, oo=xt[:], a=pt[:]: e.tensor_tensor(out=oo, in0=a, in1=oo, op=ALU.add),
                  [pttok, xtok], [xtok])
            P.dma("sp", x_out[o * 128:(o + 1) * 128, t0:t0 + 512], xt[:], [xtok], [], xtok, is_out=is_out)


HN = 256
NEG = -30000.0
NE = 7


def nat_es(qp):
    if qp == 0:
        return list(range(0, 6))
    if qp == 15:
        return list(range(-1, 5))
    return list(range(0, 5))


def nat_pidx(qp):
    return {0: 0, 1: 1, 14: 3, 15: 4}.get(qp, 2)


def emit_nat(P, C, x_in, x_out, g1c, w_qkv, qg, kg, bias, pen, ohk, bd, w_out, is_out=False):
    NT = T + 2 * HN
    fr = C["fr"]
    g_col = P.sb([128, 8], F32, "nt_g")
    gtok = load_cols(P, C, g_col, g1c)
    qg_sb = P.sb([128, 2], F32, "nt_qg")
    qgtok = load_cols(P, C, qg_sb, qg, scale=0.125)
    kg_sb = P.sb([128, 1], F32, "nt_kg")
    kgtok = load_cols(P, C, kg_sb, kg)
    ctok = C["ctok"]
    bd_f = P.sb([128, 128], F32, "nt_bd")
    bdtok = load_cols(P, C, bd_f, bd)
    ident_f = P.sb([128, 128], F32, "nt_idf")
    ident = P.sb([128, 128], BF16, "nt_id")
    P.add("pool", lambda e: e.memset(ident_f[:], 1.0), [], [ctok])
    P.add("pool", lambda e: e.affine_select(out=ident_f[:], in_=ident_f[:], pattern=[[-1, 128]], compare_op=ALU.is_equal,
                                            fill=0.0, base=0, channel_multiplier=1), [ctok], [ctok])
    P.add("pool", lambda e: e.tensor_copy(out=ident[:], in_=ident_f[:]), [ctok], [ctok])
    pen_f = P.sb([2, 5 * NE * 128], F32, "nt_penf")
    pen_bf = P.sb([2, 5 * NE * 128], BF16, "nt_pen")
    pentok = load_cols(P, C, pen_f, pen)
    P.add("pool", lambda e: e.tensor_copy(out=pen_bf[:], in_=pen_f[:]), [pentok], [pentok])
    ohk_f = P.sb([2, 128], F32, "nt_ohkf")
    ohk_bf = P.sb([2, 128], BF16, "nt_ohk")
    ohktok = load_cols(P, C, ohk_f, ohk)
    P.add("pool", lambda e: e.tensor_copy(out=ohk_bf[:], in_=ohk_f[:]), [ohktok], [ohktok])

    h_all = P.sb([128, 8, NT], BF16, "nt_h")
    tiles = [(i * 512, 512) for i in range(NT // 512)]
    h_tok = [[Tok(f"nth{i}_{c}") for c in range(8)] for i in range(len(tiles))]
    ps_pr = Ring(P, 2, [128, 512], F32, "ps_pr", psum=True)
    emit_rmsnorm(P, C, x_in, g_col, gtok, h_all, h_tok, ps_pr, tiles, two_pass=True)

    attn_all = P.sb([128, 8, T], BF16, "nt_attn")
    attn_tok = [Tok(f"attn{hp}") for hp in range(8)]
    wst = Ring(P, 2, [128, 8, 128], F32, "nt_wst")
    wq_r = Ring(P, 2, [128, 8, 128], BF16, "nt_wq")
    wk_r = Ring(P, 2, [128, 8, 128], BF16, "nt_wk")
    wv_r = Ring(P, 2, [128, 8, 128], BF16, "nt_wv")
    bst = Ring(P, 1, [128, 2 * NE * 128], F32, "nt_bst")
    bbf = Ring(P, 2, [128, 2 * NE * 128], BF16, "nt_bbf")
    q_r = Ring(P, 1, [128, 2, T], BF16, "nt_q")
    k_r = Ring(P, 1, [128, NT], BF16, "nt_k")
    v_r = Ring(P, 1, [128, 2, NT // 128, 128], BF16, "nt_v")
    for (vb_, vbtok_) in v_r.bufs:
        P.add("pool", lambda e, o=vb_[:]: e.memset(o, 0.0), [], [vbtok_])
    onesz = P.sb([128, 2, 128], BF16, "nt_onesz")
    P.add("pool", lambda e: e.memset(onesz[:], 0.0), [], [ctok])
    P.add("pool", lambda e: e.memset(onesz[:, 0, 0:64], 1.0), [], [ctok])
    P.add("pool", lambda e: e.memset(onesz[:, 1, 64:128], 1.0), [], [ctok])
    p_r = Ring(P, 3, [128, 6 * 128], BF16, "nt_p")
    ps_sc = Ring(P, 2, [128, 1024], F32, "ps_sc", psum=True)
    ps_pv = Ring(P, 2, [128, 512], F32, "ps_pv", psum=True)
    w_v = w_qkv.rearrange("(kc p) n -> p kc n", p=128)
    allh = lambda ti: [h_tok[ti][c] for c in range(8)]

    for hp in range(8):
        wts = []
        for which, ring in enumerate((wq_r, wk_r, wv_r)):
            st, sttok = wst.next()
            P.dma("sp", st[:], w_v[:, :, which * D + hp * 128:which * D + (hp + 1) * 128], [], [sttok], sttok)
            wb, wbtok = ring.next()
            P.add("pool", lambda e, o=wb[:], i=st[:]: e.tensor_copy(out=o, in_=i), [sttok], [wbtok])
            wts.append((wb, wbtok))
        (wq, wqtok), (wk, wktok), (wv, wvtok) = wts
        bs, bstok = bst.next()
        P.dma("sp", bs[:], bias[hp], [], [bstok], bstok)
        bb, bbtok = bbf.next()
        P.add("pool", lambda e, o=bb[:], i=bs[:]: e.tensor_copy(out=o, in_=i), [bstok], [bbtok])

        q_sb, qtok = q_r.next()
        k_sb, ktok = k_r.next()
        v_sb, vtok = v_r.next()
        for (dst, dtok, wmat, wtok, gsb, gt, tl) in (
                (q_sb, qtok, wq, wqtok, qg_sb, qgtok, [(HN + i * 512, i * 512) for i in range(4)]),
                (k_sb, ktok, wk, wktok, kg_sb, kgtok, [(i * 512, i * 512) for i in range(5)])):
            for (hs, ds) in tl:
                ti = hs // 512
                pr, prtok = ps_pr.next()
                for kc in range(8):
                    mm(P, pr[:], wmat[:, kc, :], h_all[:, kc, hs:hs + 512], kc == 0, kc == 7,
                       [wtok] + [h_tok[i][kc] for i in range(len(tiles)) if i * 512 < hs + 512 and (i + 1) * 512 > hs], [prtok])
                sq, sqtok = fr.next()
                P.add("act", lambda e, o=sq[:], i=pr[:]: e.activation(out=o, in_=i, func=AF.Square), [prtok], [sqtok])
                pq, pqtok = ps_pr.next()
                mm(P, pq[:], bd_f[:], sq[:], True, True, [sqtok, bdtok], [pqtok])
                rs, rstok = fr.next()
                P.add("act", lambda e, o=rs[:], i=pq[:]: e.activation(out=o, in_=i, func=AF.Sqrt,
                                                                      bias=C["eps_col"][:, 0:1], scale=1.0 / 64),
                      [pqtok, ctok], [rstok])
                P.add("dve", lambda e, o=rs[:]: e.reciprocal(out=o, in_=o), [rstok], [rstok])
                if dst is q_sb:
                    for hh_ in range(2):
                        P.add("dve", lambda e, o=dst[:, hh_, ds:ds + 512], i=pr[:], g=gsb[:, hh_:hh_ + 1], r=rs[:]:
                              e.scalar_tensor_tensor(out=o, in0=i, scalar=g, in1=r, op0=ALU.mult, op1=ALU.mult),
                              [prtok, rstok, gt], [dtok])
                else:
                    P.add("dve", lambda e, o=dst[:, ds:ds + 512], i=pr[:], g=gsb[:, 0:1], r=rs[:]:
                          e.scalar_tensor_tensor(out=o, in0=i, scalar=g, in1=r, op0=ALU.mult, op1=ALU.mult),
                          [prtok, rstok, gt], [dtok])
        for blk in range(NT // 128):
            pr, prtok = ps_pr.next()
            for kc in range(8):
                mm(P, pr[:, 0:128], h_all[:, kc, blk * 128:(blk + 1) * 128], wv[:, kc, :], kc == 0, kc == 7,
                   [wvtok, h_tok[blk // 4][kc]], [prtok])
            for hh_ in range(2):
                P.add("act", lambda e, o=v_sb[:, hh_, blk, 64 * hh_:64 * hh_ + 64], i=pr[:, 64 * hh_:64 * hh_ + 64]:
                      e.activation(out=o, in_=i, func=AF.Identity), [prtok], [vtok])
        for qp in range(16):
            es = nat_es(qp)
            ne = len(es)
            pix = nat_pidx(qp)
            pts = []
            for hh in range(2):
                sc, sctok = ps_sc.next()
                for idx, e_ in enumerate(es):
                    kb = qp + e_
                    mm(P, sc[:, idx * 128:(idx + 1) * 128], k_sb[:, kb * 128:(kb + 1) * 128],
                       q_sb[:, hh, qp * 128:(qp + 1) * 128], idx % 4 == 0, False, [ktok, qtok], [sctok], skip=True)
                boff = (hh * NE + es[0] + 1) * 128
                poff = (pix * NE + es[0] + 1) * 128
                for (c0, c1) in ((0, 512), (512, ne * 128)):
                    mm(P, sc[:, c0:c1], ident[:], bb[:, boff + c0:boff + c1], False, False, [bbtok, ctok], [sctok], skip=True)
                    mm(P, sc[:, c0:c1], ohk_bf[:], pen_bf[:, poff + c0:poff + c1], False, True, [ohktok, pentok], [sctok], skip=True)
                pt, pttok = p_r.next()
                for (c0, c1) in ((0, 512), (512, ne * 128)):
                    P.add("act", lambda e, o=pt[:, c0:c1], i=sc[:, c0:c1]: e.activation(out=o, in_=i, func=AF.Exp),
                          [sctok], [pttok])
                pts.append((pt, pttok))
            pv, pvtok = ps_pv.next()
            n_mm = 2 * ne
            cnt = 0
            for hh in range(2):
                pt, pttok = pts[hh]
                for idx, e_ in enumerate(es):
                    kb = qp + e_
                    mm(P, pv[:, 0:128], v_sb[:, hh, kb, :], pt[:, idx * 128:(idx + 1) * 128], cnt == 0, cnt == n_mm - 1,
                       [vtok, pttok], [pvtok])
                    cnt += 1
            cnt = 0
            for hh in range(2):
                pt, pttok = pts[hh]
                for idx, e_ in enumerate(es):
                    mm(P, pv[:, 128:256], onesz[:, hh, :], pt[:, idx * 128:(idx + 1) * 128], cnt == 0, cnt == n_mm - 1,
                       [pttok, ctok], [pvtok])
                    cnt += 1
            if C.get("dbg") is not None and hp == 0 and qp == 2:
                dbg_dump(P, C, 0, pts[0][0][:, 0:128], [pts[0][1]])
                dbg_dump(P, C, 1, pts[0][0][:, 128:256], [pts[0][1]])
                dbg_dump(P, C, 2, q_sb[:, 0, 256:384], [qtok])
                dbg_dump(P, C, 3, q_sb[:, 1, 256:384], [qtok])
                dbg_dump(P, C, 4, k_sb[:, 256:384], [ktok])
                dbg_dump(P, C, 5, k_sb[:, 384:512], [ktok])
                dbg_dump(P, C, 6, v_sb[:, 0, 2, :], [vtok])
                dbg_dump(P, C, 7, v_sb[:, 1, 2, :], [vtok])
            rd, rdtok = fr.next()
            P.add("dve", lambda e, o=rd[:, 0:128], i=pv[:, 128:256]: e.reciprocal(out=o, in_=i), [pvtok], [rdtok])
            P.add("dve", lambda e, o=attn_all[:, hp, qp * 128:(qp + 1) * 128], a=pv[:, 0:128], b=rd[:, 0:128]:
                  e.tensor_tensor(out=o, in0=a, in1=b, op=ALU.mult), [pvtok, rdtok], [attn_tok[hp]])
            if C.get("dbg") is not None and hp == 0 and qp == 2:
                dbg_dump(P, C, 8, attn_all[:, 0, 256:384], [attn_tok[0]])
                dbg_dump(P, C, 9, rd[:, 0:128], [rdtok])

    wo_st = Ring(P, 2, [128, 8, 128], F32, "nt_wost")
    wo_r = Ring(P, 2, [128, 8, 128], BF16, "nt_wo")
    w_out_v = w_out.rearrange("(kc p) n -> p kc n", p=128)
    for o in range(8):
        st, sttok = wo_st.next()
        P.dma("sp", st[:], w_out_v[:, :, o * 128:(o + 1) * 128], [], [sttok], sttok)
        wb, wbtok = wo_r.next()
        P.add("pool", lambda e, oo=wb[:], i=st[:]: e.tensor_copy(out=oo, in_=i), [sttok], [wbtok])
        for tt in range(4):
            t0 = tt * 512
            pt, pttok = ps_pr.next()
            for kc in range(8):
                mm(P, pt[:], wb[:, kc, :], attn_all[:, kc, t0:t0 + 512], kc == 0, kc == 7, [wbtok, attn_tok[kc]], [pttok])
            xt, xtok = fr.next()
            P.dma("sp", xt[:], x_in[o * 128:(o + 1) * 128, HN + t0:HN + t0 + 512], [], [xtok], xtok)
            P.add("dve", lambda e, oo=xt[:], a=pt[:]: e.tensor_tensor(out=oo, in0=a, in1=oo, op=ALU.add),
                  [pttok, xtok], [xtok])
            P.dma("sp", x_out[o * 128:(o + 1) * 128, t0:t0 + 512], xt[:], [xtok], [], xtok, is_out=is_out)


NB = 48
RB = NB * 128
RH = 4
LN16 = -2.772588722239781


def emit_ret(P, C, nc, x_in, x_out, g1c, w_in, cosT, sinT, l2d, gng, w_out, is_out=False):
    fr = C["fr"]
    ctok = C["ctok"]
    g_col = P.sb([128, 8], F32, "rt_g")
    gtok = load_cols(P, C, g_col, g1c)
    gng_sb = P.sb([128, 16], F32, "rt_gng")
    gngtok = load_cols(P, C, gng_sb, gng)
    ones_f = P.sb([128, 128], F32, "rt_ones_f")
    P.add("pool", lambda e: e.memset(ones_f[:], 1.0), [], [ctok])
    lg = P.sb([128, 8], F32, "rt_lg")
    nlg = P.sb([128, 8], F32, "rt_nlg")
    one_col = P.sb([128, 1], F32, "rt_one")
    ln16_col = P.sb([128, 1], F32, "rt_ln16")
    P.add("pool", lambda e: e.memset(one_col[:], 1.0), [], [ctok])
    P.add("pool", lambda e: e.memset(ln16_col[:], LN16), [], [ctok])
    lgtok = load_cols(P, C, lg, l2d)
    P.add("act", lambda e: e.activation(out=lg[:], in_=lg[:], func=AF.Exp, scale=-0.6931471805599453), [lgtok], [lgtok])
    P.add("act", lambda e: e.activation(out=lg[:], in_=lg[:], func=AF.Ln, bias=one_col[:, 0:1], scale=-1.0),
          [lgtok, ctok], [lgtok])
    P.add("dve", lambda e: e.tensor_scalar(out=nlg[:], in0=lg[:], scalar1=-1.0, scalar2=None, op0=ALU.mult),
          [lgtok], [lgtok])
    d1i = P.sb([128, 128], mybir.dt.int32, "rt_d1i")
    d1 = P.sb([128, 128], F32, "rt_d1")
    dbi = P.sb([128, NB], mybir.dt.int32, "rt_dbi")
    dbf = P.sb([128, NB], F32, "rt_dbf")
    itok = Tok("iota")
    P.add("pool", lambda e: e.iota(d1i[:], pattern=[[1, 128]], base=0, channel_multiplier=-1), [], [itok])
    P.add("pool", lambda e: e.iota(dbi[:], pattern=[[128, NB]], base=0, channel_multiplier=0), [], [itok])
    P.add("dve", lambda e: e.tensor_copy(out=d1[:], in_=d1i[:]), [itok], [itok])
    P.add("dve", lambda e: e.tensor_copy(out=dbf[:], in_=dbi[:]), [itok], [itok])

    h_dram = nc.dram_tensor("rt_h_dram", [D, RB], BF16, kind="Internal").ap()
    gT_dram = nc.dram_tensor("rt_gT_dram", [2 * D, T], BF16, kind="Internal").ap()
    h_dv = h_dram.rearrange("(c p) t -> p c t", p=128)
    gT_dv = gT_dram.rearrange("(c p) t -> p c t", p=128)
    bigr = Ring(P, 2, [128, 16, 512], BF16, "rt_big")
    ps_a = Ring(P, 2, [128, 512], F32, "ps_a", psum=True)
    ps_s = Ring(P, 2, [128, 512], F32, "ps_s", psum=True)
    ps_o = Ring(P, 2, [128, 512], F32, "ps_o", psum=True)
    ntile = RB // 512
    hd_tok = [Tok(f"hd{i}") for i in range(ntile)]
    for ti in range(ntile):
        hb, hbtok = bigr.next()
        emit_rmsnorm(P, C, x_in, g_col, gtok, hb, [[hbtok] * 8], ps_a, [(ti * 512, 512)], two_pass=True, hcol=[0])
        P.dma("sp", h_dv[:, :, ti * 512:(ti + 1) * 512], hb[:, 0:8, :], [hbtok], [hd_tok[ti]], hbtok)

    k_fm = P.sb([128, 2, RB], BF16, "rt_k")
    v_tok = P.sb([128, NB, 512], BF16, "rt_v")
    q_fm = P.sb([128, 2, T], BF16, "rt_q")
    o_fm = P.sb([128, 4, T], F32, "rt_o")
    ktok, vtok, qtok = Tok("k"), Tok("v"), Tok("q")
    otok = [Tok(f"o{i}") for i in range(16)]
    wq = P.sb([128, 8, 256], BF16, "rt_wq")
    wk = P.sb([128, 8, 256], BF16, "rt_wk")
    wv = P.sb([128, 8, 512], BF16, "rt_wv")
    wg = P.sb([128, 8, 512], BF16, "rt_wg")
    wtok = Tok("w")
    wst = Ring(P, 2, [128, 8, 128], F32, "rt_wst")
    w_v = w_in.rearrange("(kc p) n -> p kc n", p=128)
    gf = P.sb([128, 128], F32, "rt_gf")
    gb = P.sb([128, 128], F32, "rt_gb")
    gd = P.sb([128, 128], F32, "rt_gd")
    gd2 = P.sb([128, 128], F32, "rt_gd2")
    sf = P.sb([128, NB], F32, "rt_sf")
    sbk = P.sb([128, NB], F32, "rt_sb")
    gtk = Tok("G")
    p_r = Ring(P, 6, [128, 128], BF16, "rt_p")
    gT_tok = [[Tok(f"gT{h}_{t}") for t in range(4)] for h in range(RH)]

    for h in range(RH):
        segs = ([(wq, i_ * 128, h * 256 + i_ * 128, 128) for i_ in range(2)] +
                [(wk, i_ * 128, D + h * 256 + i_ * 128, 128) for i_ in range(2)] +
                [(wv, i_ * 128, 2 * D + h * 512 + i_ * 128, 128) for i_ in range(4)] +
                [(wg, i_ * 128, 4 * D + h * 512 + i_ * 128, 128) for i_ in range(4)])
        for (dst, dcol, scol, n) in segs:
            st, sttok = wst.next()
            P.dma("sp", st[:], w_v[:, :, scol:scol + n], [], [sttok], sttok)
            P.add("pool", lambda e, o=dst[:, :, dcol:dcol + n], i=st[:]: e.tensor_copy(out=o, in_=i), [sttok], [wtok])
        P.add("act", lambda e, sc_=lg[:, h:h + 1]: e.activation(out=gf[:], in_=d1[:], func=AF.Exp, bias=ln16_col[:, 0:1], scale=sc_),
              [itok, lgtok, ctok], [gtk])
        P.add("act", lambda e, sc_=nlg[:, 4 + h:5 + h]: e.activation(out=gb[:], in_=d1[:], func=AF.Exp, bias=ln16_col[:, 0:1], scale=sc_),
              [itok, lgtok, ctok], [gtk])
        P.add("pool", lambda e: e.affine_select(out=gd[:], in_=gf[:], pattern=[[1, 128]], compare_op=ALU.is_ge, fill=0.0,
                                                base=0, channel_multiplier=-1), [gtk], [gtk])
        P.add("pool", lambda e: e.affine_select(out=gd2[:], in_=gb[:], pattern=[[-1, 128]], compare_op=ALU.is_gt, fill=0.0,
                                                base=0, channel_multiplier=1), [gtk], [gtk])
        P.add("pool", lambda e: e.tensor_tensor(out=gd[:], in0=gd[:], in1=gd2[:], op=ALU.add), [gtk], [gtk])
        P.add("act", lambda e, sc_=lg[:, h:h + 1]: e.activation(out=sf[:], in_=dbf[:], func=AF.Exp, scale=sc_), [itok, lgtok], [gtk])
        P.add("act", lambda e, sc_=lg[:, 4 + h:5 + h]: e.activation(out=sbk[:], in_=dbf[:], func=AF.Exp, scale=sc_), [itok, lgtok], [gtk])

        for ti in range(ntile):
            hb, hbtok = bigr.next()
            P.dma("sp", hb[:, 0:8, :], h_dv[:, :, ti * 512:(ti + 1) * 512], [hd_tok[ti]], [hbtok], hbtok)
            cs, cstok = fr.next()
            sn, sntok = fr.next()
            P.dma("sp", cs[:], cosT[:, ti * 512:(ti + 1) * 512], [], [cstok], cstok)
            P.dma("sp", sn[:], sinT[:, ti * 512:(ti + 1) * 512], [], [sntok], sntok)
            todo = [(wk, k_fm, ktok, ti * 512)]
            if 4 <= ti < 8:
                todo.append((wq, q_fm, qtok, (ti - 4) * 512))
            for (wmat, dst, dtok, dcol) in todo:
                p1, p1tok = ps_a.next()
                p2, p2tok = ps_a.next()
                for dc, (pt, pttok) in enumerate(((p1, p1tok), (p2, p2tok))):
                    for kc in range(8):
                        mm(P, pt[:], wmat[:, kc, dc * 128:(dc + 1) * 128], hb[:, kc, :], kc == 0, kc == 7, [wtok, hbtok], [pttok])
                t1, t1tok = fr.next()
                t2, t2tok = fr.next()
                P.add("dve", lambda e, o=t1[:], a=p1[:], b=cs[:]: e.tensor_tensor(out=o, in0=a, in1=b, op=ALU.mult), [p1tok, cstok], [t1tok])
                P.add("dve", lambda e, o=t2[:], a=p2[:], b=sn[:]: e.tensor_tensor(out=o, in0=a, in1=b, op=ALU.mult), [p2tok, sntok], [t2tok])
                P.add("pool", lambda e, o=dst[:, 0, dcol:dcol + 512], a=t1[:], b=t2[:]: e.tensor_tensor(out=o, in0=a, in1=b, op=ALU.subtract),
                      [t1tok, t2tok], [dtok])
                P.add("dve", lambda e, o=t1[:], a=p1[:], b=sn[:]: e.tensor_tensor(out=o, in0=a, in1=b, op=ALU.mult), [p1tok, sntok], [t1tok])
                P.add("dve", lambda e, o=t2[:], a=p2[:], b=cs[:]: e.tensor_tensor(out=o, in0=a, in1=b, op=ALU.mult), [p2tok, cstok], [t2tok])
                P.add("pool", lambda e, o=dst[:, 1, dcol:dcol + 512], a=t1[:], b=t2[:]: e.tensor_tensor(out=o, in0=a, in1=b, op=ALU.add),
                      [t1tok, t2tok], [dtok])
            for bl in range(4):
                blk = ti * 4 + bl
                pv, pvtok = ps_a.next()
                for kc in range(8):
                    mm(P, pv[:], hb[:, kc, bl * 128:(bl + 1) * 128], wv[:, kc, :], kc == 0, kc == 7, [wtok, hbtok], [pvtok])
                P.add("act", lambda e, o=v_tok[:, blk, :], i=pv[:]: e.activation(out=o, in_=i, func=AF.Identity), [pvtok], [vtok])

        for i in range(16):
            po, potok = ps_o.next()
            first = True
            for cg in range(NB // 4):
                sc, sctok = ps_s.next()
                for sub in range(4):
                    c = cg * 4 + sub
                    for dc in range(2):
                        mm(P, sc[:, sub * 128:(sub + 1) * 128], k_fm[:, dc, c * 128:(c + 1) * 128], q_fm[:, dc, i * 128:(i + 1) * 128],
                           sub == 0 and dc == 0, dc == 1, [ktok, qtok], [sctok], skip=True)
                for sub in range(4):
                    c = cg * 4 + sub
                    dl = 16 + i - c
                    pt, pttok = p_r.next()
                    if dl > 0:
                        P.add("dve", lambda e, o=pt[:], a=sc[:, sub * 128:(sub + 1) * 128], s_=sf[:, dl:dl + 1]:
                              e.scalar_tensor_tensor(out=o, in0=a, scalar=s_, in1=gf[:], op0=ALU.mult, op1=ALU.mult), [sctok, gtk], [pttok])
                    elif dl < 0:
                        P.add("dve", lambda e, o=pt[:], a=sc[:, sub * 128:(sub + 1) * 128], s_=sbk[:, -dl:-dl + 1]:
                              e.scalar_tensor_tensor(out=o, in0=a, scalar=s_, in1=gb[:], op0=ALU.mult, op1=ALU.mult), [sctok, gtk], [pttok])
                    else:
                        P.add("dve", lambda e, o=pt[:], a=sc[:, sub * 128:(sub + 1) * 128]:
                              e.tensor_tensor(out=o, in0=a, in1=gd[:], op=ALU.mult), [sctok, gtk], [pttok])
                    for ec in range(4):
                        mm(P, po[:, ec * 128:(ec + 1) * 128], v_tok[:, c, ec * 128:(ec + 1) * 128], pt[:],
                           first, c == NB - 1, [vtok, pttok], [potok], skip=True)
                        first = False
            P.add("act", lambda e, o=o_fm[:, :, i * 128:(i + 1) * 128], a=po[:].rearrange("p (a b) -> p a b", a=4):
                  e.activation(out=o, in_=a, func=AF.Identity), [potok], [otok[i]])

        for tt in range(4):
            t0 = tt * 512
            ots = [otok[tt * 4 + b_] for b_ in range(4)]
            p1, p1tok = ps_a.next()
            p2, p2tok = ps_a.next()
            for ec in range(4):
                mm(P, p1[:], ones_f[:], o_fm[:, ec, t0:t0 + 512], ec == 0, ec == 3, ots + [ctok], [p1tok])
            for ec in range(4):
                sq, sqtok = fr.next()
                P.add("act", lambda e, o=sq[:], a=o_fm[:, ec, t0:t0 + 512]: e.activation(out=o, in_=a, func=AF.Square), ots, [sqtok])
                mm(P, p2[:], ones_f[:], sq[:], ec == 0, ec == 3, [sqtok, ctok], [p2tok])
            mu, mutok = C["rsr"].next()
            P.add("act", lambda e, o=mu[:], a=p1[:]: e.activation(out=o, in_=a, func=AF.Identity, scale=1.0 / 512), [p1tok], [mutok])
            rs, rstok = C["rsr"].next()
            P.add("dve", lambda e, o=rs[:], a=mu[:]: e.tensor_tensor(out=o, in0=a, in1=a, op=ALU.mult), [mutok], [rstok])
            P.add("dve", lambda e, o=rs[:], a=p2[:]: e.scalar_tensor_tensor(out=o, in0=a, scalar=1.0 / 512, in1=o, op0=ALU.mult, op1=ALU.subtract),
                  [p2tok, rstok], [rstok])
            P.add("act", lambda e, o=rs[:]: e.activation(out=o, in_=o, func=AF.Sqrt, bias=C["eps_col"][:, 0:1], scale=1.0), [rstok, ctok], [rstok])
            P.add("dve", lambda e, o=rs[:]: e.reciprocal(out=o, in_=o), [rstok], [rstok])
            hb, hbtok = bigr.next()
            P.dma("sp", hb[:, 0:8, :], h_dv[:, :, 2048 + t0:2048 + t0 + 512], [hd_tok[4 + tt]], [hbtok], hbtok)
            gt_sb, gttok = bigr.next()
            for ec in range(4):
                pg, pgtok = ps_a.next()
                for kc in range(8):
                    mm(P, pg[:], wg[:, kc, ec * 128:(ec + 1) * 128], hb[:, kc, :], kc == 0, kc == 7, [wtok, hbtok], [pgtok])
                sg, sgtok = fr.next()
                P.add("act", lambda e, o=sg[:], a=pg[:]: e.activation(out=o, in_=a, func=AF.Silu), [pgtok], [sgtok])
                dd, ddtok = fr.next()
                P.add("pool", lambda e, o=dd[:], a=o_fm[:, ec, t0:t0 + 512], m=mu[:]: e.tensor_tensor(out=o, in0=a, in1=m, op=ALU.subtract),
                      ots + [mutok], [ddtok])
                P.add("dve", lambda e, o=dd[:], r=rs[:]: e.tensor_tensor(out=o, in0=o, in1=r, op=ALU.mult), [ddtok, rstok], [ddtok])
                P.add("dve", lambda e, o=gt_sb[:, ec, :], a=dd[:], g_=gng_sb[:, h * 4 + ec:h * 4 + ec + 1], s_=sg[:]:
                      e.scalar_tensor_tensor(out=o, in0=a, scalar=g_, in1=s_, op0=ALU.mult, op1=ALU.mult), [ddtok, sgtok, gngtok], [gttok])
            P.dma("sp", gT_dv[:, h * 4:(h + 1) * 4, t0:t0 + 512], gt_sb[:, 0:4, :], [gttok], [gT_tok[h][tt]], gttok)

    wo_bufs = [wq[:].rearrange("p a b -> p (a b)").rearrange("p (j n) -> p j n", j=16),
               wk[:].rearrange("p a b -> p (a b)").rearrange("p (j n) -> p j n", j=16)]
    w_out_v = w_out.rearrange("(j p) n -> p j n", p=128)
    for o in range(8):
        wb, wbtok = wo_bufs[o % 2], wtok
        for half in range(2):
            st, sttok = wst.next()
            P.dma("sp", st[:], w_out_v[:, half * 8:(half + 1) * 8, o * 128:(o + 1) * 128], [], [sttok], sttok)
            P.add("pool", lambda e, oo=wb[:, half * 8:(half + 1) * 8, :], i=st[:]: e.tensor_copy(out=oo, in_=i), [sttok], [wbtok])
        for tt in range(4):
            t0 = tt * 512
            gb_, gbtok = bigr.next()
            P.dma("sp", gb_[:], gT_dv[:, :, t0:t0 + 512], [gT_tok[h_][tt] for h_ in range(RH)], [gbtok], gbtok)
            pt, pttok = ps_a.next()
            for j in range(16):
                mm(P, pt[:], wb[:, j, :], gb_[:, j, :], j == 0, j == 15, [wbtok, gbtok], [pttok])
            xt, xtok = fr.next()
            P.dma("sp", xt[:], x_in[o * 128:(o + 1) * 128, 2048 + t0:2048 + t0 + 512], [], [xtok], xtok)
            P.add("dve", lambda e, oo=xt[:], a=pt[:]: e.tensor_tensor(out=oo, in0=a, in1=oo, op=ALU.add), [pttok, xtok], [xtok])
            P.dma("sp", x_out[o * 128:(o + 1) * 128, t0:t0 + 512], xt[:], [xtok], [], xtok, is_out=is_out)


def build_ffn_prog():
    nc = bass.Bass("TRN2", target_bir_lowering=False)
    x_in = nc.dram_tensor("x_in", [D, T + 2], F32, kind="ExternalInput").ap()
    g2c = nc.dram_tensor("g2c", [128, 8], F32, kind="ExternalInput").ap()
    w_up = nc.dram_tensor("w_up", [D, 2 * FFN], F32, kind="ExternalInput").ap()
    dww = nc.dram_tensor("dww", [128, 3 * 2 * NH], F32, kind="ExternalInput").ap()
    dwb = nc.dram_tensor("dwb", [128, 2 * NH], F32, kind="ExternalInput").ap()
    w_down = nc.dram_tensor("w_down", [FFN, D], F32, kind="ExternalInput").ap()
    x_out = nc.dram_tensor("x_out", [D, T], F32, kind="ExternalOutput").ap()
    with contextlib.ExitStack() as stack:
        P = Prog(nc, stack)
        C = make_common(P)
        emit_ffn(P, C, x_in, x_out, g2c, w_up, dww, dwb, w_down, is_out=True)
        P.emit()
        print("ffn prog stats", P.stats)
    return nc


def cols(v, n):
    return np.ascontiguousarray(v.reshape(n, 128).T)


def shard_tokens_fm(xfull, halo):
    out = []
    for c in range(NCORES):
        b, hf = c // 2, c % 2
        lo, hi = hf * T - halo, (hf + 1) * T + halo
        buf = np.zeros((T + 2 * halo, D), np.float32)
        slo, shi = max(lo, 0), min(hi, SEQ)
        buf[slo - lo:shi - lo] = xfull[b, slo:shi]
        out.append(np.ascontiguousarray(buf.T))
    return out


def unshard_tokens_fm(outs):
    x = np.empty((BATCH, SEQ, D), np.float32)
    for c in range(NCORES):
        b, hf = c // 2, c % 2
        x[b, hf * T:(hf + 1) * T] = outs[c].T
    return x


def ffn_inmaps(x, i, norm2_g, ffn_w_up, ffn_dw_w, ffn_dw_b, ffn_w_down):
    xs = shard_tokens_fm(x, 1) if x is not None else None
    dww = np.concatenate([cols(ffn_dw_w[i, k], 2 * NH) for k in range(3)], axis=1)
    common = {
        "g2c": cols(norm2_g[i], 8),
        "w_up": np.ascontiguousarray(ffn_w_up[i]),
        "dww": np.ascontiguousarray(dww),
        "dwb": cols(ffn_dw_b[i], 2 * NH),
        "w_down": np.ascontiguousarray(ffn_w_down[i]),
    }
    return [dict(common, x_in=xs[c]) if xs is not None else dict(common) for c in range(NCORES)]


def build_conv_prog():
    nc = bass.Bass("TRN2", target_bir_lowering=False)
    dt = lambda name, shape, kind="ExternalInput": nc.dram_tensor(name, shape, F32, kind=kind).ap()
    x_in = dt("x_in", [D, T + 2 * HC])
    mask = dt("mask", [128, 2 * HC])
    g1c = dt("g1c", [128, 8])
    w_in = dt("w_in", [D, 2 * D])
    b_in = dt("b_in", [128, 16])
    dw_w = dt("dw_w", [128, CW * 8])
    dw_b = dt("dw_b", [128, 8])
    ln_g = dt("ln_g", [128, 8])
    ln_b = dt("ln_b", [128, 8])
    w_out = dt("w_out", [D, D])
    x_out = dt("x_out", [D, T], "ExternalOutput")
    with contextlib.ExitStack() as stack:
        P = Prog(nc, stack)
        C = make_common(P, nfr=12)
        emit_conv(P, C, x_in, x_out, mask, g1c, w_in, b_in, dw_w, dw_b, ln_g, ln_b, w_out, is_out=True)
        P.emit()
        print("conv prog stats", P.stats)
    return nc


def conv_inmaps(x, j, g1, conv_w_in, conv_b_in, conv_dw_w, conv_dw_b, conv_ln_g, conv_ln_b, conv_w_out):
    xs = shard_tokens_fm(x, HC) if x is not None else None
    dww = np.concatenate([cols(conv_dw_w[j, k], 8) for k in range(CW)], axis=1)
    common = {
        "g1c": cols(g1, 8),
        "w_in": np.ascontiguousarray(conv_w_in[j]),
        "b_in": cols(conv_b_in[j], 16),
        "dw_w": np.ascontiguousarray(dww),
        "dw_b": cols(conv_dw_b[j], 8),
        "ln_g": cols(conv_ln_g[j], 8),
        "ln_b": cols(conv_ln_b[j], 8),
        "w_out": np.ascontiguousarray(conv_w_out[j]),
    }
    maps = []
    for c in range(NCORES):
        hf = c % 2
        m = np.ones((128, 2 * HC), np.float32)
        if hf == 0:
            m[:, :HC] = 0.0
        else:
            m[:, HC:] = 0.0
        maps.append(dict(common, x_in=xs[c], mask=m) if xs is not None else dict(common, mask=m))
    return maps


def dbg_dump(P, C, slot, src, toks):
    t, ttok = C["dbgr"].next()
    P.add("act", lambda e: e.activation(out=t[:], in_=src, func=AF.Identity), toks, [ttok])
    P.dma("sp", C["dbg"][:, slot * 128:(slot + 1) * 128], t[:], [ttok], [], ttok, is_out=True)


def build_nat_prog(debug=False):
    nc = bass.Bass("TRN2", target_bir_lowering=False)
    dt = lambda name, shape, kind="ExternalInput": nc.dram_tensor(name, shape, F32, kind=kind).ap()
    x_in = dt("x_in", [D, T + 2 * HN])
    g1c = dt("g1c", [128, 8])
    w_qkv = dt("w_qkv", [D, 3 * D])
    qg = dt("qg", [128, 2])
    kg = dt("kg", [128, 1])
    bias = dt("bias", [8, 128, 2 * NE * 128])
    pen = dt("pen", [2, 5 * NE * 128])
    ohk = dt("ohk", [2, 128])
    bd = dt("bd", [128, 128])
    w_out = dt("w_out", [D, D])
    x_out = dt("x_out", [D, T], "ExternalOutput")
    with contextlib.ExitStack() as stack:
        P = Prog(nc, stack)
        C = make_common(P, nfr=6)
        if debug:
            C["dbg"] = dt("dbg", [128, 16 * 128], "ExternalOutput")
            C["dbgr"] = Ring(P, 2, [128, 128], F32, "dbgr")
        emit_nat(P, C, x_in, x_out, g1c, w_qkv, qg, kg, bias, pen, ohk, bd, w_out, is_out=True)
        P.emit()
        print("nat prog stats", P.stats)
    return nc


def nat_bias_table(rpb):
    kc = np.arange(64)[:, None]
    qc = np.arange(64)[None, :]
    cs = np.clip(qc - 8, 0, 48)
    win = (kc >= cs) & (kc < cs + 16)
    dc = np.clip(kc - qc + 15, 0, 30)
    out = np.full((8, 128, 2, NE, 128), NEG, np.float32)
    for hp in range(8):
        for hh in range(2):
            h = 2 * hp + hh
            for ei in range(NE):
                e_ = ei - 1
                for kp in range(2):
                    for qp_ in range(2):
                        dr = 2 * e_ + 3 + kp - qp_
                        if dr < 0 or dr > 14:
                            continue
                        blk = np.where(win, rpb[h, dr][dc], np.float32(NEG))
                        out[hp, kp * 64:(kp + 1) * 64, hh, ei, qp_ * 64:(qp_ + 1) * 64] = blk
    return out.reshape(8, 128, 2 * NE * 128)


def nat_pen_table(hf):
    out = np.full((2, 5, NE, 128), NEG, np.float32)
    for pix, qp in enumerate((0, 1, 7, 14, 15)):
        for ei in range(NE):
            e_ = ei - 1
            for kp in range(2):
                for qp_ in range(2):
                    r = 32 * hf + 2 * qp + qp_
                    kr = 32 * hf + 2 * qp + 2 * e_ - 4 + kp
                    rs = min(max(r - 4, 0), 56)
                    if 0 <= kr < 64 and rs <= kr < rs + 8:
                        out[kp, pix, ei, qp_ * 64:(qp_ + 1) * 64] = 0.0
    return out.reshape(2, 5 * NE * 128)


def nat_qg2(g):
    out = np.zeros((128, 2), np.float32)
    out[0:64, 0] = g
    out[64:128, 1] = g
    return out


def nat_inmaps(x, g1, nat_w_qkv, nat_q_norm_g, nat_k_norm_g, nat_rpb, nat_w_out):
    xs = shard_tokens_fm(x, HN) if x is not None else None
    ohk = np.zeros((2, 128), np.float32)
    ohk[0, :64] = 1.0
    ohk[1, 64:] = 1.0
    common = {
        "g1c": cols(g1, 8),
        "w_qkv": np.ascontiguousarray(nat_w_qkv[0]),
        "qg": nat_qg2(nat_q_norm_g[0]),
        "kg": np.ascontiguousarray(np.tile(nat_k_norm_g[0], 2)[:, None]),
        "bias": nat_bias_table(nat_rpb[0]),
        "ohk": ohk,
        "bd": np.kron(np.eye(2, dtype=np.float32), np.ones((64, 64), np.float32)),
        "w_out": np.ascontiguousarray(nat_w_out[0]),
    }
    pens = [nat_pen_table(0), nat_pen_table(1)]
    return [dict(common, x_in=xs[c], pen=pens[c % 2]) if xs is not None else dict(common, pen=pens[c % 2]) for c in range(NCORES)]


def build_ret_prog():
    nc = bass.Bass("TRN2", target_bir_lowering=False)
    dt = lambda name, shape, kind="ExternalInput": nc.dram_tensor(name, shape, F32, kind=kind).ap()
    x_in = dt("x_in", [D, RB])
    g1c = dt("g1c", [128, 8])
    w_in = dt("w_in", [D, 6 * D])
    cosT = dt("cosT", [128, RB])
    sinT = dt("sinT", [128, RB])
    l2d = dt("l2d", [128, 8])
    gng = dt("gng", [128, 16])
    w_out = dt("w_out", [2 * D, D])
    x_out = dt("x_out", [D, T], "ExternalOutput")
    with contextlib.ExitStack() as stack:
        P = Prog(nc, stack)
        C = make_common(P, nfr=8)
        emit_ret(P, C, nc, x_in, x_out, g1c, w_in, cosT, sinT, l2d, gng, w_out, is_out=True)
        P.emit()
        print("ret prog stats", P.stats)
    return nc


def ret_rope_tables(hf):
    theta = (1.0 / (np.float32(10000.0) ** np.linspace(0.0, 1.0, 128, dtype=np.float32))).astype(np.float32)
    pos = (np.arange(RB, dtype=np.float32) - np.float32(2048.0) + np.float32(2048.0 * hf)).astype(np.float32)
    ang = (theta[:, None] * pos[None, :]).astype(np.float32)
    return np.cos(ang).astype(np.float32), np.sin(ang).astype(np.float32)


def ret_inmaps(x, g1, ret_w_in, ret_log2_inv_decay, ret_gn_g, ret_w_out):
    common = {
        "g1c": cols(g1, 8),
        "w_in": np.ascontiguousarray(ret_w_in[0]),
        "l2d": np.ascontiguousarray(np.tile(ret_log2_inv_decay[0].reshape(1, 8), (128, 1))),
        "gng": cols(ret_gn_g[0], 16),
        "w_out": np.ascontiguousarray(ret_w_out[0]),
    }
    tabs = [ret_rope_tables(0), ret_rope_tables(1)]
    maps = []
    for c in range(NCORES):
        b, hf = c // 2, c % 2
        if x is None:
            maps.append(dict(common, cosT=tabs[hf][0], sinT=tabs[hf][1]))
            continue
        buf = np.zeros((RB, D), np.float32)
        off = 2048 - 2048 * hf
        buf[off:off + SEQ] = x[b]
        maps.append(dict(common, x_in=np.ascontiguousarray(buf.T), cosT=tabs[hf][0], sinT=tabs[hf][1]))
    return maps


STAGES = [("c0", "conv", HC), ("f0", "ffn", 1), ("n1", "nat", HN), ("f1", "ffn", 1),
          ("r2", "ret", 2048), ("f2", "ffn", 1), ("c3", "conv", HC), ("f3", "ffn", 1)]
STAGE_IN = {
    "conv": [("mask", [128, 2 * HC]), ("g1c", [128, 8]), ("w_in", [D, 2 * D]), ("b_in", [128, 16]), ("dw_w", [128, CW * 8]),
             ("dw_b", [128, 8]), ("ln_g", [128, 8]), ("ln_b", [128, 8]), ("w_out", [D, D])],
    "ffn": [("g2c", [128, 8]), ("w_up", [D, 2 * FFN]), ("dww", [128, 3 * 2 * NH]), ("dwb", [128, 2 * NH]), ("w_down", [FFN, D])],
    "nat": [("g1c", [128, 8]), ("w_qkv", [D, 3 * D]), ("qg", [128, 2]), ("kg", [128, 1]), ("bias", [8, 128, 2 * NE * 128]),
            ("pen", [2, 5 * NE * 128]), ("ohk", [2, 128]), ("bd", [128, 128]), ("w_out", [D, D])],
    "ret": [("g1c", [128, 8]), ("w_in", [D, 6 * D]), ("cosT", [128, RB]), ("sinT", [128, RB]), ("l2d", [128, 8]),
            ("gng", [128, 16]), ("w_out", [2 * D, D])],
}
STAGE_NFR = {"conv": 12, "ffn": 14, "nat": 6, "ret": 8}


def emit_exchange(P, C, nc, name, x_next, H, hmask_sb, hmtok):
    fr = C["fr"]
    kw = dict(allow_slow_non_contiguous=True) if H < 8 else {}
    groups = [[0, 1], [2, 3], [4, 5], [6, 7]]
    if H == 2048:
        for q in range(4):
            snd = nc.dram_tensor(f"{name}_snd{q}", [D, 512], F32, kind="Internal").ap()
            gath = nc.dram_tensor(f"{name}_gath{q}", [2 * D, 512], F32, kind="Internal").ap()
            stok, gtok, cctok = Tok("snd"), Tok("gath"), Tok("cc")
            P.dma("sp", snd, x_next[:, H + q * 512:H + (q + 1) * 512], [], [stok], stok)
            P.coll(lambda e, s_=snd, g_=gath: e.collective_compute("AllGather", ALU.bypass, replica_groups=groups,
                                                                   ins=[s_], outs=[g_]), [stok], [gtok], cctok)
            for (src, dcol, mi) in ((gath[0:D, :], q * 512, 0), (gath[D:2 * D, :], H + T + q * 512, 1)):
                for c in range(8):
                    xt, xtok = fr.next()
                    P.dma("sp", xt[:], src[c * 128:(c + 1) * 128, :], [gtok], [xtok], xtok)
                    P.add("dve", lambda e, o=xt[:], m=hmask_sb[:, mi:mi + 1]:
                          e.tensor_scalar(out=o, in0=o, scalar1=m, scalar2=None, op0=ALU.mult), [xtok, hmtok], [xtok])
                    P.dma("sp", x_next[c * 128:(c + 1) * 128, dcol:dcol + 512], xt[:], [xtok], [], xtok)
        return
    snd = nc.dram_tensor(name + "_snd", [2 * D, H], F32, kind="Internal").ap()
    gath = nc.dram_tensor(name + "_gath", [4 * D, H], F32, kind="Internal").ap()
    stok = Tok("snd")
    P.dma("sp", snd[0:D, :], x_next[:, H:2 * H], [], [stok], stok, **kw)
    P.dma("sp", snd[D:2 * D, :], x_next[:, T:T + H], [], [stok], stok, **kw)
    srcs = [(gath[D:2 * D, :], 0, 0), (gath[2 * D:3 * D, :], H + T, 1)]
    gtok, cctok = Tok("gath"), Tok("cc")
    P.coll(lambda e: e.collective_compute("AllGather", ALU.bypass, replica_groups=groups,
                                          ins=[snd], outs=[gath]), [stok], [gtok], cctok)
    for (src, dcol, mi) in srcs:
        for c in range(8):
            for t0 in range(0, H, 512):
                n = min(512, H - t0)
                xt, xtok = fr.next()
                P.dma("sp", xt[:, 0:n], src[c * 128:(c + 1) * 128, t0:t0 + n], [gtok], [xtok], xtok, **kw)
                P.add("dve", lambda e, o=xt[:, 0:n], m=hmask_sb[:, mi:mi + 1]:
                      e.tensor_scalar(out=o, in0=o, scalar1=m, scalar2=None, op0=ALU.mult), [xtok, hmtok], [xtok])
                P.dma("sp", x_next[c * 128:(c + 1) * 128, dcol + t0:dcol + t0 + n], xt[:, 0:n], [xtok], [], xtok, **kw)


def build_fused_prog(nst=8):
    stages = STAGES[:nst]
    nc = bass.Bass("TRN2", target_bir_lowering=False)
    dt = lambda name, shape, kind="ExternalInput": nc.dram_tensor(name, shape, F32, kind=kind).ap()
    aps = {}
    for (sn, kind, H) in stages:
        aps[sn] = {k: dt(f"{sn}_{k}", shp) for (k, shp) in STAGE_IN[kind]}
    hmask = dt("hmask", [128, 2])
    bufs = {}
    for si, (sn, kind, H) in enumerate(stages):
        width = RB if kind == "ret" else T + 2 * H
        bufs[sn] = dt(f"{sn}_x_in", [D, width], "ExternalInput" if si == 0 else "Internal")
    y = dt("x_out", [D, T], "ExternalOutput")
    with contextlib.ExitStack() as stack:
        P = Prog(nc, stack)
        for si, (sn, kind, H) in enumerate(stages):
            last = si == len(stages) - 1
            x_in = bufs[sn]
            if last:
                x_out = y
            else:
                nsn, nkind, nH = stages[si + 1]
                x_out = bufs[nsn][:, nH:nH + T]
            a = aps[sn]
            with contextlib.ExitStack() as st:
                P.stack = st
                P.pfx = sn + "_"
                C = make_common(P, nfr=STAGE_NFR[kind])
                if kind == "conv":
                    emit_conv(P, C, x_in, x_out, a["mask"], a["g1c"], a["w_in"], a["b_in"], a["dw_w"], a["dw_b"], a["ln_g"],
                              a["ln_b"], a["w_out"], is_out=last)
                elif kind == "ffn":
                    emit_ffn(P, C, x_in, x_out, a["g2c"], a["w_up"], a["dww"], a["dwb"], a["w_down"], is_out=last)
                elif kind == "nat":
                    emit_nat(P, C, x_in, x_out, a["g1c"], a["w_qkv"], a["qg"], a["kg"], a["bias"], a["pen"], a["ohk"], a["bd"],
                             a["w_out"], is_out=last)
                else:
                    emit_ret(P, C, nc, x_in, x_out, a["g1c"], a["w_in"], a["cosT"], a["sinT"], a["l2d"], a["gng"], a["w_out"],
                             is_out=last)
            P.barrier()
            if not last:
                with contextlib.ExitStack() as st:
                    P.stack = st
                    P.pfx = sn + "x_"
                    C = make_common(P, nfr=8)
                    hm_sb = P.sb([128, 2], F32, "hmask")
                    hmtok = load_cols(P, C, hm_sb, hmask)
                    emit_exchange(P, C, nc, sn + "x", bufs[nsn], nH, hm_sb, hmtok)
                P.barrier()
        P.stack = stack
        P.pfx = ""
        P.emit()
        print("fused prog stats", P.stats)
    return nc


def fused_inmaps(a):
    per_stage = {}
    per_stage["c0"] = conv_inmaps(a["x"], 0, a["norm1_g"][0], a["conv_w_in"], a["conv_b_in"], a["conv_dw_w"], a["conv_dw_b"],
                                  a["conv_ln_g"], a["conv_ln_b"], a["conv_w_out"])
    per_stage["c3"] = conv_inmaps(None, 1, a["norm1_g"][3], a["conv_w_in"], a["conv_b_in"], a["conv_dw_w"], a["conv_dw_b"],
                                  a["conv_ln_g"], a["conv_ln_b"], a["conv_w_out"])
    per_stage["n1"] = nat_inmaps(None, a["norm1_g"][1], a["nat_w_qkv"], a["nat_q_norm_g"], a["nat_k_norm_g"], a["nat_rpb"],
                                 a["nat_w_out"])
    per_stage["r2"] = ret_inmaps(None, a["norm1_g"][2], a["ret_w_in"], a["ret_log2_inv_decay"], a["ret_gn_g"], a["ret_w_out"])
    for i in range(4):
        per_stage[f"f{i}"] = ffn_inmaps(None, i, a["norm2_g"], a["ffn_w_up"], a["ffn_dw_w"], a["ffn_dw_b"], a["ffn_w_down"])
    maps = []
    for c in range(NCORES):
        m = {}
        for sn, lst in per_stage.items():
            for k, v in lst[c].items():
                m[f"{sn}_{k}"] = v
        hm = np.zeros((128, 2), np.float32)
        hm[:, 0] = float(c % 2)
        hm[:, 1] = float(1 - c % 2)
        m["hmask"] = hm
        maps.append(m)
    return maps


_PROGS = {}


def _prog(name, builder):
    if name not in _PROGS:
        _PROGS[name] = builder()
    return _PROGS[name]


def _launch(nc, maps):
    res = run_bass_kernel_spmd(nc, maps, core_ids=list(range(NCORES)))
    return unshard_tokens_fm([r["x_out"] for r in res.results])


def kernel_unfused(x, norm1_g, norm2_g, conv_w_in, conv_b_in, conv_dw_w, conv_dw_b, conv_ln_g, conv_ln_b, conv_w_out,
           nat_w_qkv, nat_q_norm_g, nat_k_norm_g, nat_rpb, nat_w_out, ret_w_in, ret_log2_inv_decay, ret_gn_g,
           ret_w_out, ffn_w_up, ffn_dw_w, ffn_dw_b, ffn_w_down):
    a = {k: np.asarray(v, np.float32) for k, v in locals().items()}
    xc = a["x"]
    for i in range(4):
        mixer, j = i % 3, i // 3
        if mixer == 0:
            maps = conv_inmaps(xc, j, a["norm1_g"][i], a["conv_w_in"], a["conv_b_in"], a["conv_dw_w"], a["conv_dw_b"],
                               a["conv_ln_g"], a["conv_ln_b"], a["conv_w_out"])
            xc = _launch(_prog("conv", build_conv_prog), maps)
        elif mixer == 1:
            maps = nat_inmaps(xc, a["norm1_g"][i], a["nat_w_qkv"], a["nat_q_norm_g"], a["nat_k_norm_g"], a["nat_rpb"],
                              a["nat_w_out"])
            xc = _launch(_prog("nat", build_nat_prog), maps)
        else:
            maps = ret_inmaps(xc, a["norm1_g"][i], a["ret_w_in"], a["ret_log2_inv_decay"], a["ret_gn_g"], a["ret_w_out"])
            xc = _launch(_prog("ret", build_ret_prog), maps)
        maps = ffn_inmaps(xc, i, a["norm2_g"], a["ffn_w_up"], a["ffn_dw_w"], a["ffn_dw_b"], a["ffn_w_down"])
        xc = _launch(_prog("ffn", build_ffn_prog), maps)
    return xc


def kernel(x, norm1_g, norm2_g, conv_w_in, conv_b_in, conv_dw_w, conv_dw_b, conv_ln_g, conv_ln_b, conv_w_out,
           nat_w_qkv, nat_q_norm_g, nat_k_norm_g, nat_rpb, nat_w_out, ret_w_in, ret_log2_inv_decay, ret_gn_g,
           ret_w_out, ffn_w_up, ffn_dw_w, ffn_dw_b, ffn_w_down):
    a = {k: np.asarray(v, np.float32) for k, v in locals().items()}
    maps = fused_inmaps(a)
    nc = _prog("fused", build_fused_prog)
    res = run_bass_kernel_spmd(nc, maps, core_ids=list(range(NCORES)))
    return unshard_tokens_fm([r["x_out"] for r in res.results])
```

```python
import contextlib
import numpy as np
import concourse.bass as bass
import concourse.mybir as mybir
from concourse.bass_utils import run_bass_kernel_spmd

F32 = mybir.dt.float32
BF16 = mybir.dt.bfloat16
AF = mybir.ActivationFunctionType
ALU = mybir.AluOpType
AX = mybir.AxisListType

D = 1024
SEQ = 4096
BATCH = 4
T = 2048
NCORES = 8
FFN = 2816
NH = FFN // 128
EPS = 1e-6


class Tok:
    __slots__ = ("lw", "rd", "name", "sem", "dcount", "last_dma")

    def __init__(self, name=""):
        self.lw = None
        self.rd = []
        self.name = name
        self.sem = None
        self.dcount = 0
        self.last_dma = None


class Op:
    __slots__ = ("eng", "fn", "deps", "is_dma", "dtok", "has_dep", "sem", "val",
                 "waits", "know", "is_out", "inc", "is_barrier")

    def __init__(self, eng, fn, is_dma=False, dtok=None):
        self.eng = eng
        self.fn = fn
        self.deps = set()
        self.is_dma = is_dma
        self.dtok = dtok
        self.has_dep = False
        self.sem = None
        self.val = 0
        self.waits = ()
        self.know = None
        self.is_out = False
        self.inc = 16 if is_dma else 1
        self.is_barrier = False


class Prog:
    ENGS = ("pe", "act", "dve", "pool", "sp")

    def __init__(self, nc, stack):
        self.nc = nc
        self.stack = stack
        self.ops = []
        self.nsb = 0
        self.nsem = 0
        self.out_ops = []
        self.pfx = ""
        self.bar_start = 0
        self.prev_bar = []

    def sb(self, shape, dtype, name=None):
        self.nsb += 1
        return self.stack.enter_context(
            self.nc.sbuf_tensor(self.pfx + (name or f"sb{self.nsb}"), list(shape), dtype))

    def ps(self, shape, dtype, name=None):
        self.nsb += 1
        return self.stack.enter_context(
            self.nc.psum_tensor(self.pfx + (name or f"ps{self.nsb}"), list(shape), dtype))

    def barrier(self):
        last = {}
        dmas = {}
        for op in self.ops[self.bar_start:]:
            if op.is_dma:
                dmas[id(op.dtok)] = op
            else:
                last[op.eng] = op
        deps = set(last.values()) | set(dmas.values()) | set(self.prev_bar)
        bars = []
        for eng in self.ENGS:
            op = Op(eng, lambda e: e.nop())
            op.deps = set(deps)
            op.is_barrier = (eng == self.ENGS[0])
            self.ops.append(op)
            bars.append(op)
        self.prev_bar = bars
        self.bar_start = len(self.ops)

    def new_sem(self, name=None):
        self.nsem += 1
        return self.stack.enter_context(self.nc.semaphore(name or f"sem{self.nsem}"))

    def add(self, eng, fn, reads=(), writes=(), is_dma=False, dtok=None, is_out=False):
        op = Op(eng, fn, is_dma, dtok)
        op.is_out = is_out
        for t in reads:
            if t.lw is not None:
                op.deps.add(t.lw)
        for t in writes:
            for r in t.rd:
                op.deps.add(r)
            if t.lw is not None:
                op.deps.add(t.lw)
        if is_dma:
            if dtok.last_dma is not None:
                op.deps.add(dtok.last_dma)
            dtok.last_dma = op
        for t in reads:
            t.rd.append(op)
        for t in writes:
            t.rd = []
            t.lw = op
        op.deps.discard(op)
        if eng == "pe" and not is_dma:
            op.deps = {d for d in op.deps if not (d.eng == "pe" and not d.is_dma)}
        self.ops.append(op)
        if is_out:
            self.out_ops.append(op)
        return op

    def coll(self, fn, reads, writes, dtok):
        op = self.add("pool", fn, reads, writes, is_dma=True, dtok=dtok)
        op.inc = 1
        return op

    def dma(self, queue, out, in_, reads, writes, dtok, is_out=False, **kw):
        return self.add(queue, lambda e: e.dma_start(out=out, in_=in_, **kw),
                        reads, writes, is_dma=True, dtok=dtok, is_out=is_out)

    def emit(self):
        ops = self.ops
        for op in ops:
            for d in op.deps:
                d.has_dep = True
        esem = {e: self.new_sem("eng_" + e) for e in self.ENGS}
        cnt = {e: 0 for e in self.ENGS}
        free_sems = []
        live_toks = []
        for op in ops:
            if op.is_barrier:
                for t in live_toks:
                    free_sems.append((t.sem, t.dcount))
                live_toks = []
            if op.is_dma:
                t = op.dtok
                if t.sem is None:
                    if free_sems:
                        t.sem, t.dcount = free_sems.pop()
                    else:
                        t.sem = self.new_sem()
                    live_toks.append(t)
                t.dcount += op.inc
                op.sem = t.sem
                op.val = t.dcount
            elif op.has_dep:
                cnt[op.eng] += 1
                op.sem = esem[op.eng]
                op.val = cnt[op.eng]
        seen = {e: {} for e in self.ENGS}
        nwaits = 0
        for op in ops:
            s = seen[op.eng]
            waits = {}
            for d in op.deps:
                k = id(d.sem)
                if s.get(k, (None, 0))[1] >= d.val:
                    continue
                if waits.get(k, (None, 0))[1] < d.val:
                    waits[k] = (d.sem, d.val)
            for d in op.deps:
                if d.know is not None:
                    for k, v in d.know.items():
                        if s.get(k, (None, 0))[1] < v[1]:
                            s[k] = v
            for k, v in waits.items():
                if s.get(k, (None, 0))[1] < v[1]:
                    s[k] = v
            op.waits = list(waits.values())
            nwaits += len(op.waits)
            if op.sem is not None:
                kn = dict(s)
                kn[id(op.sem)] = (op.sem, op.val)
                op.know = kn
                if not op.is_dma:
                    s[id(op.sem)] = (op.sem, op.val)
        by = {e: [o for o in ops if o.eng == e] for e in self.ENGS}
        finals = [(o.sem, o.val) for o in self.out_ops]
        self.stats = dict(nops=len(ops), nwaits=nwaits,
                          per_eng={e: len(by[e]) for e in self.ENGS}, nsem=self.nsem)

        def run(name, e):
            for op in by[name]:
                for sem, val in op.waits:
                    e.wait_ge(sem, val)
                inst = op.fn(e)
                if op.sem is not None:
                    inst.then_inc(op.sem, op.inc)
            if name == "sp":
                for sem, val in finals:
                    e.wait_ge(sem, val)

        with self.nc.Block() as block:
            @block.tensor
            def _(e):
                run("pe", e)

            @block.scalar
            def _(e):
                run("act", e)

            @block.vector
            def _(e):
                run("dve", e)

            @block.gpsimd
            def _(e):
                run("pool", e)

            @block.sync
            def _(e):
                run("sp", e)


class Ring:
    def __init__(self, P, n, shape, dtype, name, psum=False):
        self.bufs = []
        for i in range(n):
            t = (P.ps if psum else P.sb)(shape, dtype, f"{name}{i}")
            self.bufs.append((t, Tok(f"{name}{i}")))
        self.i = 0

    def next(self):
        b = self.bufs[self.i % len(self.bufs)]
        self.i += 1
        return b


def mm(P, out, lhsT, rhs, start, stop, reads, writes, skip=False):
    return P.add("pe", lambda e: e.matmul(out, lhsT, rhs, start=start, stop=stop, skip_group_check=skip),
                 reads, writes)


def emit_rmsnorm(P, C, x_dram, g_col, gtok, h_all, h_tok, ps_ring, tiles, two_pass=False, hcol=None):
    fr = C["fr"]
    sqr = C["sqr"]
    ones = C["ones_bf"]
    assert len(fr.bufs) >= (4 if two_pass else 9)
    for ti, (t0, n) in enumerate(tiles):
        d0 = t0 if hcol is None else hcol[ti]
        xs = []
        pst, pstok = ps_ring.next()
        for c in range(8):
            xt, xtok = fr.next()
            P.dma("sp", xt[:, 0:n], x_dram[c * 128:(c + 1) * 128, t0:t0 + n], [], [xtok], xtok)
            xs.append((xt, xtok))
            sq, sqtok = sqr.next()
            P.add("act", lambda e, o=sq[:, 0:n], i=xt[:, 0:n]: e.activation(out=o, in_=i, func=AF.Square),
                  [xtok], [sqtok])
            mm(P, pst[:, 0:n], ones[:], sq[:, 0:n], c == 0, c == 7, [sqtok, C["ctok"]], [pstok])
        rs, rstok = C["rsr"].next()
        P.add("act", lambda e, o=rs[:, 0:n], i=pst[:, 0:n]: e.activation(
            out=o, in_=i, func=AF.Sqrt, bias=C["eps_col"][:, 0:1], scale=1.0 / D), [pstok, C["ctok"]], [rstok])
        P.add("dve", lambda e, o=rs[:, 0:n]: e.reciprocal(out=o, in_=o), [rstok], [rstok])
        for c in range(8):
            if two_pass:
                xt, xtok = fr.next()
                P.dma("sp", xt[:, 0:n], x_dram[c * 128:(c + 1) * 128, t0:t0 + n], [], [xtok], xtok)
            else:
                xt, xtok = xs[c]
            P.add("dve", lambda e, o=h_all[:, c, d0:d0 + n], i=xt[:, 0:n], g=g_col[:, c:c + 1], r=rs[:, 0:n]:
                  e.scalar_tensor_tensor(out=o, in0=i, scalar=g, in1=r, op0=ALU.mult, op1=ALU.mult),
                  [xtok, rstok, gtok], [h_tok[ti][c]])


def make_common(P, nfr=14):
    C = {}
    C["fr"] = Ring(P, nfr, [128, 512], F32, "fr")
    C["sqr"] = Ring(P, 3, [128, 512], BF16, "sqr")
    C["rsr"] = Ring(P, 2, [128, 512], F32, "rsr")
    C["ones_bf"] = P.sb([128, 128], BF16, "ones_bf")
    C["eps_col"] = P.sb([128, 1], F32, "eps_col")
    C["ctok"] = Tok("consts")
    P.add("pool", lambda e: e.memset(C["ones_bf"][:], 1.0), [], [C["ctok"]])
    P.add("pool", lambda e: e.memset(C["eps_col"][:], EPS), [], [C["ctok"]])
    return C


def load_cols(P, C, dst, src_dram, scale=None):
    tok = Tok("par")
    P.dma("sp", dst[:], src_dram, [], [tok], tok)
    if scale is not None:
        P.add("pool", lambda e: e.tensor_scalar(out=dst[:], in0=dst[:], scalar1=float(scale), scalar2=None,
                                                 op0=ALU.mult), [tok], [tok])
    return tok


def htoks_for(h_tok, tiles, c, lo, hi):
    return [h_tok[i][c] for i, (t0, n) in enumerate(tiles) if t0 < hi and t0 + n > lo]


def emit_ffn(P, C, x_in, x_out, g2c, w_up, dww, dwb, w_down, is_out=False):
    NT = T + 2
    g_col = P.sb([128, 8], F32, "ffn_g")
    gtok = load_cols(P, C, g_col, g2c)
    dww_sb = P.sb([128, 3 * 2 * NH], F32, "ffn_dww")
    dwb_sb = P.sb([128, 2 * NH], F32, "ffn_dwb")
    dwtok = load_cols(P, C, dww_sb, dww)
    dbtok = load_cols(P, C, dwb_sb, dwb)

    h_all = P.sb([128, 8, NT], BF16, "ffn_h")
    tiles = [(0, 410), (410, 410), (820, 410), (1230, 410), (1640, NT - 1640)]
    h_tok = [[Tok(f"h{i}_{c}") for c in range(8)] for i in range(len(tiles))]
    ps_stat = Ring(P, 1, [128, 512], F32, "ps_stat", psum=True)
    emit_rmsnorm(P, C, x_in, g_col, gtok, h_all, h_tok, ps_stat, tiles)

    act_all = P.sb([128, NH, T], BF16, "ffn_act")
    act_tok = [[Tok(f"act{j}_{i}") for i in range(5)] for j in range(NH)]

    wst = Ring(P, 2, [128, NH * 128], F32, "wst")
    wbf = Ring(P, 2, [128, NH * 128], BF16, "wbf")
    ps_up = Ring(P, 7, [128, 512], F32, "ps_up", psum=True)
    fr = C["fr"]
    w_up_v = w_up.rearrange("(kc p) n -> p kc n", p=128)
    ctiles = [(0, 410), (410, 410), (820, 410), (1230, 410), (1640, 408)]
    def load_up(j):
        st, sttok = wst.next()
        stv = st[:, 0:2048].rearrange("p (k n) -> p k n", k=8)
        P.dma("sp", stv[:, :, 0:128], w_up_v[:, :, j * 128:(j + 1) * 128], [], [sttok], sttok)
        P.dma("sp", stv[:, :, 128:256], w_up_v[:, :, FFN + j * 128:FFN + (j + 1) * 128], [], [sttok], sttok)
        wb, wbtok = wbf.next()
        P.add("pool", lambda e, o=wb[:, 0:2048], i=st[:, 0:2048]: e.tensor_copy(out=o, in_=i), [sttok], [wbtok])
        return wb[:, 0:2048].rearrange("p (k n) -> p k n", k=8), wbtok

    nxt = load_up(0)
    for j in range(NH):
        wbv, wbtok = nxt
        if j + 1 < NH:
            nxt = load_up(j + 1)
        for ci, (o0, n) in enumerate(ctiles):
            ncol = n + 2
            pv, pvtok = ps_up.next()
            pg, pgtok = ps_up.next()
            for half, (pt, pttok) in enumerate(((pv, pvtok), (pg, pgtok))):
                for kc in range(8):
                    mm(P, pt[:, 0:ncol], wbv[:, kc, half * 128:(half + 1) * 128], h_all[:, kc, o0:o0 + ncol],
                       kc == 0, kc == 7, [wbtok] + htoks_for(h_tok, tiles, kc, o0, o0 + ncol), [pttok])
            av, avtok = fr.next()
            ag, agtok = fr.next()
            for half, (pt, pttok, acc, acctok) in enumerate(((pv, pvtok, av, avtok), (pg, pgtok, ag, agtok))):
                ch = half * NH + j
                w0 = dww_sb[:, 0 * 2 * NH + ch:0 * 2 * NH + ch + 1]
                w1 = dww_sb[:, 1 * 2 * NH + ch:1 * 2 * NH + ch + 1]
                w2 = dww_sb[:, 2 * 2 * NH + ch:2 * 2 * NH + ch + 1]
                bb = dwb_sb[:, ch:ch + 1]
                P.add("act", lambda e, o=acc[:, 0:n], i=pt[:, 1:n + 1], s=w1, b=bb:
                      e.activation(out=o, in_=i, func=AF.Identity, bias=b, scale=s),
                      [pttok, dwtok, dbtok], [acctok])
                P.add("dve", lambda e, o=acc[:, 0:n], i=pt[:, 0:n], s=w0:
                      e.scalar_tensor_tensor(out=o, in0=i, scalar=s, in1=o, op0=ALU.mult, op1=ALU.add),
                      [pttok, acctok, dwtok], [acctok])
                P.add("dve", lambda e, o=acc[:, 0:n], i=pt[:, 2:n + 2], s=w2:
                      e.scalar_tensor_tensor(out=o, in0=i, scalar=s, in1=o, op0=ALU.mult, op1=ALU.add),
                      [pttok, acctok, dwtok], [acctok])
            ge, getok = fr.next()
            P.add("act", lambda e, o=ge[:, 0:n], i=ag[:, 0:n]: e.activation(out=o, in_=i, func=AF.Gelu_apprx_tanh),
                  [agtok], [getok])
            P.add("dve", lambda e, o=act_all[:, j, o0:o0 + n], a=ge[:, 0:n], b=av[:, 0:n]:
                  e.tensor_tensor(out=o, in0=a, in1=b, op=ALU.mult), [getok, avtok], [act_tok[j][ci]])

    ps_dn = ps_up
    w_dn_v = w_down.rearrange("(j p) n -> p j n", p=128)
    def load_dn(o):
        st, sttok = wst.next()
        stv = st[:].rearrange("p (j n) -> p j n", j=NH)
        P.dma("sp", stv, w_dn_v[:, :, o * 128:(o + 1) * 128], [], [sttok], sttok)
        wb, wbtok = wbf.next()
        P.add("pool", lambda e, oo=wb[:], i=st[:]: e.tensor_copy(out=oo, in_=i), [sttok], [wbtok])
        return wb[:].rearrange("p (j n) -> p j n", j=NH), wbtok

    nxt = load_dn(0)
    for o in range(8):
        wbv, wbtok = nxt
        if o + 1 < 8:
            nxt = load_dn(o + 1)
        for tt in range(T // 512):
            t0 = tt * 512
            xt, xtok = fr.next()
            P.dma("sp", xt[:], x_in[o * 128:(o + 1) * 128, 1 + t0:1 + t0 + 512], [], [xtok], xtok)
            pt, pttok = ps_dn.next()
            for j in range(NH):
                rd = [wbtok] + [act_tok[j][ci] for ci, (o0, n) in enumerate(ctiles) if o0 < t0 + 512 and o0 + n > t0]
                mm(P, pt[:], wbv[:, j, :], act_all[:, j, t0:t0 + 512], j == 0, j == NH - 1, rd, [pttok])
            P.add("dve", lambda e, oo=xt[:], a=pt[:]: e.tensor_tensor(out=oo, in0=a, in1=oo, op=ALU.add),
                  [pttok, xtok], [xtok])
            P.dma("sp", x_out[o * 128:(o + 1) * 128, t0:t0 + 512], xt[:], [xtok], [], xtok, is_out=is_out)


CW = 31
HC = 15


def emit_conv(P, C, x_in, x_out, mask, g1c, w_in, b_in, dw_w, dw_b, ln_g, ln_b, w_out, is_out=False):
    NT = T + 2 * HC
    fr = C["fr"]
    g_col = P.sb([128, 8], F32, "cv_g")
    gtok = load_cols(P, C, g_col, g1c)
    bin_sb = P.sb([128, 16], F32, "cv_bin")
    bintok = load_cols(P, C, bin_sb, b_in)
    dww_sb = P.sb([128, CW * 8], F32, "cv_dww")
    dwwtok = load_cols(P, C, dww_sb, dw_w)
    dwb_sb = P.sb([128, 8], F32, "cv_dwb")
    dwbtok = load_cols(P, C, dwb_sb, dw_b)
    lng_sb = P.sb([128, 8], F32, "cv_lng")
    lngtok = load_cols(P, C, lng_sb, ln_g)
    lnb_sb = P.sb([128, 8], F32, "cv_lnb")
    lnbtok = load_cols(P, C, lnb_sb, ln_b)
    mask_sb = P.sb([128, 2 * HC], F32, "cv_mask")
    masktok = load_cols(P, C, mask_sb, mask)
    ones_f = P.sb([128, 128], F32, "ones_f")
    P.add("pool", lambda e: e.memset(ones_f[:], 1.0), [], [C["ctok"]])

    h_all = P.sb([128, 8, NT], BF16, "cv_h")
    tiles = [(0, 416), (416, 416), (832, 416), (1248, 416), (1664, NT - 1664)]
    h_tok = [[Tok(f"cvh{i}_{c}") for c in range(8)] for i in range(len(tiles))]
    ps_stat = Ring(P, 2, [128, 512], F32, "ps_stat", psum=True)
    emit_rmsnorm(P, C, x_in, g_col, gtok, h_all, h_tok, ps_stat, tiles)

    v_all = P.sb([128, 8, T], F32, "cv_v")
    v_tok = [[Tok(f"cvv{c}_{i}") for i in range(4)] for c in range(8)]
    ur = Ring(P, 2, [128, NT], F32, "cv_u")
    wst = Ring(P, 2, [128, 2048], F32, "cv_wst")
    wbf = Ring(P, 2, [128, 2048], BF16, "cv_wbf")
    ps_up = Ring(P, 4, [128, 512], F32, "ps_up", psum=True)
    w_in_v = w_in.rearrange("(kc p) n -> p kc n", p=128)
    KD = 20
    for c in range(8):
        st, sttok = wst.next()
        stv = st[:].rearrange("p (k n) -> p k n", k=8)
        P.dma("sp", stv[:, :, 0:128], w_in_v[:, :, c * 128:(c + 1) * 128], [], [sttok], sttok)
        P.dma("sp", stv[:, :, 128:256], w_in_v[:, :, D + c * 128:D + (c + 1) * 128], [], [sttok], sttok)
        wb, wbtok = wbf.next()
        P.add("pool", lambda e, o=wb[:], i=st[:]: e.tensor_copy(out=o, in_=i), [sttok], [wbtok])
        wbv = wb[:].rearrange("p (k n) -> p k n", k=8)
        u, utok = ur.next()
        for ti, (t0, n) in enumerate(tiles):
            pa, patok = ps_up.next()
            pg, pgtok = ps_up.next()
            for half, (pt, pttok) in enumerate(((pa, patok), (pg, pgtok))):
                for kc in range(8):
                    mm(P, pt[:, 0:n], wbv[:, kc, half * 128:(half + 1) * 128], h_all[:, kc, t0:t0 + n],
                       kc == 0, kc == 7, [wbtok, h_tok[ti][kc]], [pttok])
            sg, sgtok = fr.next()
            P.add("act", lambda e, o=sg[:, 0:n], i=pg[:, 0:n], b=bin_sb[:, 8 + c:9 + c]:
                  e.activation(out=o, in_=i, func=AF.Sigmoid, bias=b, scale=1.0), [pgtok, bintok], [sgtok])
            P.add("dve", lambda e, o=u[:, t0:t0 + n], i=pa[:, 0:n], b=bin_sb[:, c:c + 1], g=sg[:, 0:n]:
                  e.scalar_tensor_tensor(out=o, in0=i, scalar=b, in1=g, op0=ALU.add, op1=ALU.mult),
                  [patok, sgtok, bintok], [utok])
        P.add("pool", lambda e, o=u[:, 0:HC], m=mask_sb[:, 0:HC]: e.tensor_tensor(out=o, in0=o, in1=m, op=ALU.mult),
              [utok, masktok], [utok])
        P.add("pool", lambda e, o=u[:, T + HC:NT], m=mask_sb[:, HC:2 * HC]: e.tensor_tensor(out=o, in0=o, in1=m, op=ALU.mult),
              [utok, masktok], [utok])
        for tt in range(4):
            t0 = tt * 512
            va = v_all[:, c, t0:t0 + 512]
            vb, vbtok = fr.next()
            wk = lambda k: dww_sb[:, k * 8 + c:k * 8 + c + 1]
            P.add("act", lambda e, o=va, i=u[:, t0:t0 + 512], s=wk(0), b=dwb_sb[:, c:c + 1]:
                  e.activation(out=o, in_=i, func=AF.Identity, bias=b, scale=s), [utok, dwwtok, dwbtok], [v_tok[c][tt]])
            for k in range(1, KD + 1):
                P.add("dve", lambda e, o=va, i=u[:, t0 + k:t0 + k + 512], s=wk(k):
                      e.scalar_tensor_tensor(out=o, in0=i, scalar=s, in1=o, op0=ALU.mult, op1=ALU.add),
                      [utok, dwwtok, v_tok[c][tt]], [v_tok[c][tt]])
            P.add("act", lambda e, o=vb[:], i=u[:, t0 + KD + 1:t0 + KD + 1 + 512], s=wk(KD + 1):
                  e.activation(out=o, in_=i, func=AF.Identity, scale=s), [utok, dwwtok], [vbtok])
            for k in range(KD + 2, CW):
                tp, tptok = fr.next()
                P.add("act", lambda e, o=tp[:], i=u[:, t0 + k:t0 + k + 512], s=wk(k):
                      e.activation(out=o, in_=i, func=AF.Identity, scale=s), [utok, dwwtok], [tptok])
                P.add("pool", lambda e, o=vb[:], a=tp[:]: e.tensor_tensor(out=o, in0=o, in1=a, op=ALU.add),
                      [tptok, vbtok], [vbtok])
            P.add("dve", lambda e, o=va, b=vb[:]: e.tensor_tensor(out=o, in0=o, in1=b, op=ALU.add),
                  [vbtok, v_tok[c][tt]], [v_tok[c][tt]])

    wo_bf = P.sb([128, 8, D], BF16, "cv_wo")
    wotok = [Tok(f"wo{i}") for i in range(4)]
    w_out_v = w_out.rearrange("(kc p) n -> p kc n", p=128)
    for i in range(4):
        st, sttok = wst.next()
        stv = st[:].rearrange("p (k n) -> p k n", k=8)
        P.dma("sp", stv, w_out_v[:, :, i * 256:(i + 1) * 256], [], [sttok], sttok)
        P.add("pool", lambda e, o=wo_bf[:, :, i * 256:(i + 1) * 256], s_=stv: e.tensor_copy(out=o, in_=s_),
              [sttok], [wotok[i]])

    ps_o = Ring(P, 2, [128, 512], F32, "ps_o", psum=True)
    for tt in range(4):
        t0 = tt * 512
        p1, p1tok = ps_stat.next()
        p2, p2tok = ps_stat.next()
        for c in range(8):
            mm(P, p1[:], ones_f[:], v_all[:, c, t0:t0 + 512], c == 0, c == 7, [v_tok[c][tt], C["ctok"]], [p1tok])
        for c in range(8):
            sq, sqtok = fr.next()
            P.add("act", lambda e, o=sq[:], i=v_all[:, c, t0:t0 + 512]: e.activation(out=o, in_=i, func=AF.Square),
                  [v_tok[c][tt]], [sqtok])
            mm(P, p2[:], ones_f[:], sq[:], c == 0, c == 7, [sqtok, C["ctok"]], [p2tok])
        mu, mutok = fr.next()
        P.add("act", lambda e, o=mu[:], i=p1[:]: e.activation(out=o, in_=i, func=AF.Identity, scale=1.0 / D),
              [p1tok], [mutok])
        rs, rstok = fr.next()
        P.add("dve", lambda e, o=rs[:], a=mu[:]: e.tensor_tensor(out=o, in0=a, in1=a, op=ALU.mult), [mutok], [rstok])
        P.add("dve", lambda e, o=rs[:], i=p2[:]: e.scalar_tensor_tensor(out=o, in0=i, scalar=1.0 / D, in1=o,
                                                                        op0=ALU.mult, op1=ALU.subtract),
              [p2tok, rstok], [rstok])
        P.add("act", lambda e, o=rs[:]: e.activation(out=o, in_=o, func=AF.Sqrt, bias=C["eps_col"][:, 0:1], scale=1.0),
              [rstok, C["ctok"]], [rstok])
        P.add("dve", lambda e, o=rs[:]: e.reciprocal(out=o, in_=o), [rstok], [rstok])
        for c in range(8):
            dd, ddtok = fr.next()
            P.add("pool", lambda e, o=dd[:], a=v_all[:, c, t0:t0 + 512], m=mu[:]:
                  e.tensor_tensor(out=o, in0=a, in1=m, op=ALU.subtract), [v_tok[c][tt], mutok], [ddtok])
            P.add("dve", lambda e, o=dd[:], r=rs[:]: e.tensor_tensor(out=o, in0=o, in1=r, op=ALU.mult),
                  [ddtok, rstok], [ddtok])
            P.add("act", lambda e, o=h_all[:, c, t0:t0 + 512], i=dd[:], g=lng_sb[:, c:c + 1], b=lnb_sb[:, c:c + 1]:
                  e.activation(out=o, in_=i, func=AF.Silu, bias=b, scale=g),
                  [ddtok, lngtok, lnbtok], [h_tok[i][c] for i in range(len(tiles))])
        for o in range(8):
            pt, pttok = ps_o.next()
            for kc in range(8):
                mm(P, pt[:], wo_bf[:, kc, o * 128:(o + 1) * 128], h_all[:, kc, t0:t0 + 512], kc == 0, kc == 7,
                   [wotok[o // 2], h_tok[tt][kc]], [pttok])
            xt, xtok = fr.next()
            P.dma("sp", xt[:], x_in[o * 128:(o + 1) * 128, HC + t0:HC + t0 + 512], [], [xtok], xtok)
            P.add("dve", lambda e, oo=xt[:], a=pt[:]: e.tensor_tensor(out=oo, in0=a, in1=oo, op=ALU.add),
                  [pttok, xtok], [xtok])
            P.dma("sp", x_out[o * 128:(o + 1) * 128, t0:t0 + 512], xt[:], [xtok], [], xtok, is_out=is_out)


HN = 256
NEG = -30000.0
NE = 7


def nat_es(qp):
    if qp == 0:
        return list(range(0, 6))
    if qp == 15:
        return list(range(-1, 5))
    return list(range(0, 5))


def nat_pidx(qp):
    return {0: 0, 1: 1, 14: 3, 15: 4}.get(qp, 2)


def emit_nat(P, C, x_in, x_out, g1c, w_qkv, qg, kg, bias, pen, ohk, bd, w_out, is_out=False):
    NT = T + 2 * HN
    fr = C["fr"]
    g_col = P.sb([128, 8], F32, "nt_g")
    gtok = load_cols(P, C, g_col, g1c)
    qg_sb = P.sb([128, 2], F32, "nt_qg")
    qgtok = load_cols(P, C, qg_sb, qg, scale=0.125)
    kg_sb = P.sb([128, 1], F32, "nt_kg")
    kgtok = load_cols(P, C, kg_sb, kg)
    ctok = C["ctok"]
    bd_f = P.sb([128, 128], F32, "nt_bd")
    bdtok = load_cols(P, C, bd_f, bd)
    ident_f = P.sb([128, 128], F32, "nt_idf")
    ident = P.sb([128, 128], BF16, "nt_id")
    P.add("pool", lambda e: e.memset(ident_f[:], 1.0), [], [ctok])
    P.add("pool", lambda e: e.affine_select(out=ident_f[:], in_=ident_f[:], pattern=[[-1, 128]], compare_op=ALU.is_equal,
                                            fill=0.0, base=0, channel_multiplier=1), [ctok], [ctok])
    P.add("pool", lambda e: e.tensor_copy(out=ident[:], in_=ident_f[:]), [ctok], [ctok])
    pen_f = P.sb([2, 5 * NE * 128], F32, "nt_penf")
    pen_bf = P.sb([2, 5 * NE * 128], BF16, "nt_pen")
    pentok = load_cols(P, C, pen_f, pen)
    P.add("pool", lambda e: e.tensor_copy(out=pen_bf[:], in_=pen_f[:]), [pentok], [pentok])
    ohk_f = P.sb([2, 128], F32, "nt_ohkf")
    ohk_bf = P.sb([2, 128], BF16, "nt_ohk")
    ohktok = load_cols(P, C, ohk_f, ohk)
    P.add("pool", lambda e: e.tensor_copy(out=ohk_bf[:], in_=ohk_f[:]), [ohktok], [ohktok])

    h_all = P.sb([128, 8, NT], BF16, "nt_h")
    tiles = [(i * 512, 512) for i in range(NT // 512)]
    h_tok = [[Tok(f"nth{i}_{c}") for c in range(8)] for i in range(len(tiles))]
    ps_pr = Ring(P, 2, [128, 512], F32, "ps_pr", psum=True)
    emit_rmsnorm(P, C, x_in, g_col, gtok, h_all, h_tok, ps_pr, tiles, two_pass=True)

    attn_all = P.sb([128, 8, T], BF16, "nt_attn")
    attn_tok = [Tok(f"attn{hp}") for hp in range(8)]
    wst = Ring(P, 2, [128, 8, 128], F32, "nt_wst")
    wq_r = Ring(P, 2, [128, 8, 128], BF16, "nt_wq")
    wk_r = Ring(P, 2, [128, 8, 128], BF16, "nt_wk")
    wv_r = Ring(P, 2, [128, 8, 128], BF16, "nt_wv")
    bst = Ring(P, 1, [128, 2 * NE * 128], F32, "nt_bst")
    bbf = Ring(P, 2, [128, 2 * NE * 128], BF16, "nt_bbf")
    q_r = Ring(P, 1, [128, 2, T], BF16, "nt_q")
    k_r = Ring(P, 1, [128, NT], BF16, "nt_k")
    v_r = Ring(P, 1, [128, 2, NT // 128, 128], BF16, "nt_v")
    for (vb_, vbtok_) in v_r.bufs:
        P.add("pool", lambda e, o=vb_[:]: e.memset(o, 0.0), [], [vbtok_])
    onesz = P.sb([128, 2, 128], BF16, "nt_onesz")
    P.add("pool", lambda e: e.memset(onesz[:], 0.0), [], [ctok])
    P.add("pool", lambda e: e.memset(onesz[:, 0, 0:64], 1.0), [], [ctok])
    P.add("pool", lambda e: e.memset(onesz[:, 1, 64:128], 1.0), [], [ctok])
    p_r = Ring(P, 3, [128, 6 * 128], BF16, "nt_p")
    ps_sc = Ring(P, 2, [128, 1024], F32, "ps_sc", psum=True)
    ps_pv = Ring(P, 2, [128, 512], F32, "ps_pv", psum=True)
    w_v = w_qkv.rearrange("(kc p) n -> p kc n", p=128)
    allh = lambda ti: [h_tok[ti][c] for c in range(8)]

    for hp in range(8):
        wts = []
        for which, ring in enumerate((wq_r, wk_r, wv_r)):
            st, sttok = wst.next()
            P.dma("sp", st[:], w_v[:, :, which * D + hp * 128:which * D + (hp + 1) * 128], [], [sttok], sttok)
            wb, wbtok = ring.next()
            P.add("pool", lambda e, o=wb[:], i=st[:]: e.tensor_copy(out=o, in_=i), [sttok], [wbtok])
            wts.append((wb, wbtok))
        (wq, wqtok), (wk, wktok), (wv, wvtok) = wts
        bs, bstok = bst.next()
        P.dma("sp", bs[:], bias[hp], [], [bstok], bstok)
        bb, bbtok = bbf.next()
        P.add("pool", lambda e, o=bb[:], i=bs[:]: e.tensor_copy(out=o, in_=i), [bstok], [bbtok])

        q_sb, qtok = q_r.next()
        k_sb, ktok = k_r.next()
        v_sb, vtok = v_r.next()
        for (dst, dtok, wmat, wtok, gsb, gt, tl) in (
                (q_sb, qtok, wq, wqtok, qg_sb, qgtok, [(HN + i * 512, i * 512) for i in range(4)]),
                (k_sb, ktok, wk, wktok, kg_sb, kgtok, [(i * 512, i * 512) for i in range(5)])):
            for (hs, ds) in tl:
                ti = hs // 512
                pr, prtok = ps_pr.next()
                for kc in range(8):
                    mm(P, pr[:], wmat[:, kc, :], h_all[:, kc, hs:hs + 512], kc == 0, kc == 7,
                       [wtok] + [h_tok[i][kc] for i in range(len(tiles)) if i * 512 < hs + 512 and (i + 1) * 512 > hs], [prtok])
                sq, sqtok = fr.next()
                P.add("act", lambda e, o=sq[:], i=pr[:]: e.activation(out=o, in_=i, func=AF.Square), [prtok], [sqtok])
                pq, pqtok = ps_pr.next()
                mm(P, pq[:], bd_f[:], sq[:], True, True, [sqtok, bdtok], [pqtok])
                rs, rstok = fr.next()
                P.add("act", lambda e, o=rs[:], i=pq[:]: e.activation(out=o, in_=i, func=AF.Sqrt,
                                                                      bias=C["eps_col"][:, 0:1], scale=1.0 / 64),
                      [pqtok, ctok], [rstok])
                P.add("dve", lambda e, o=rs[:]: e.reciprocal(out=o, in_=o), [rstok], [rstok])
                if dst is q_sb:
                    for hh_ in range(2):
                        P.add("dve", lambda e, o=dst[:, hh_, ds:ds + 512], i=pr[:], g=gsb[:, hh_:hh_ + 1], r=rs[:]:
                              e.scalar_tensor_tensor(out=o, in0=i, scalar=g, in1=r, op0=ALU.mult, op1=ALU.mult),
                              [prtok, rstok, gt], [dtok])
                else:
                    P.add("dve", lambda e, o=dst[:, ds:ds + 512], i=pr[:], g=gsb[:, 0:1], r=rs[:]:
                          e.scalar_tensor_tensor(out=o, in0=i, scalar=g, in1=r, op0=ALU.mult, op1=ALU.mult),
                          [prtok, rstok, gt], [dtok])
        for blk in range(NT // 128):
            pr, prtok = ps_pr.next()
            for kc in range(8):
                mm(P, pr[:, 0:128], h_all[:, kc, blk * 128:(blk + 1) * 128], wv[:, kc, :], kc == 0, kc == 7,
                   [wvtok, h_tok[blk // 4][kc]], [prtok])
            for hh_ in range(2):
                P.add("act", lambda e, o=v_sb[:, hh_, blk, 64 * hh_:64 * hh_ + 64], i=pr[:, 64 * hh_:64 * hh_ + 64]:
                      e.activation(out=o, in_=i, func=AF.Identity), [prtok], [vtok])
        for qp in range(16):
            es = nat_es(qp)
            ne = len(es)
            pix = nat_pidx(qp)
            pts = []
            for hh in range(2):
                sc, sctok = ps_sc.next()
                for idx, e_ in enumerate(es):
                    kb = qp + e_
                    mm(P, sc[:, idx * 128:(idx + 1) * 128], k_sb[:, kb * 128:(kb + 1) * 128],
                       q_sb[:, hh, qp * 128:(qp + 1) * 128], idx % 4 == 0, False, [ktok, qtok], [sctok], skip=True)
                boff = (hh * NE + es[0] + 1) * 128
                poff = (pix * NE + es[0] + 1) * 128
                for (c0, c1) in ((0, 512), (512, ne * 128)):
                    mm(P, sc[:, c0:c1], ident[:], bb[:, boff + c0:boff + c1], False, False, [bbtok, ctok], [sctok], skip=True)
                    mm(P, sc[:, c0:c1], ohk_bf[:], pen_bf[:, poff + c0:poff + c1], False, True, [ohktok, pentok], [sctok], skip=True)
                pt, pttok = p_r.next()
                for (c0, c1) in ((0, 512), (512, ne * 128)):
                    P.add("act", lambda e, o=pt[:, c0:c1], i=sc[:, c0:c1]: e.activation(out=o, in_=i, func=AF.Exp),
                          [sctok], [pttok])
                pts.append((pt, pttok))
            pv, pvtok = ps_pv.next()
            n_mm = 2 * ne
            cnt = 0
            for hh in range(2):
                pt, pttok = pts[hh]
                for idx, e_ in enumerate(es):
                    kb = qp + e_
                    mm(P, pv[:, 0:128], v_sb[:, hh, kb, :], pt[:, idx * 128:(idx + 1) * 128], cnt == 0, cnt == n_mm - 1,
                       [vtok, pttok], [pvtok])
                    cnt += 1
            cnt = 0
            for hh in range(2):
                pt, pttok = pts[hh]
                for idx, e_ in enumerate(es):
                    mm(P, pv[:, 128:256], onesz[:, hh, :], pt[:, idx * 128:(idx + 1) * 128], cnt == 0, cnt == n_mm - 1,
                       [pttok, ctok], [pvtok])
                    cnt += 1
            if C.get("dbg") is not None and hp == 0 and qp == 2:
                dbg_dump(P, C, 0, pts[0][0][:, 0:128], [pts[0][1]])
                dbg_dump(P, C, 1, pts[0][0][:, 128:256], [pts[0][1]])
                dbg_dump(P, C, 2, q_sb[:, 0, 256:384], [qtok])
                dbg_dump(P, C, 3, q_sb[:, 1, 256:384], [qtok])
                dbg_dump(P, C, 4, k_sb[:, 256:384], [ktok])
                dbg_dump(P, C, 5, k_sb[:, 384:512], [ktok])
                dbg_dump(P, C, 6, v_sb[:, 0, 2, :], [vtok])
                dbg_dump(P, C, 7, v_sb[:, 1, 2, :], [vtok])
            rd, rdtok = fr.next()
            P.add("dve", lambda e, o=rd[:, 0:128], i=pv[:, 128:256]: e.reciprocal(out=o, in_=i), [pvtok], [rdtok])
            P.add("dve", lambda e, o=attn_all[:, hp, qp * 128:(qp + 1) * 128], a=pv[:, 0:128], b=rd[:, 0:128]:
                  e.tensor_tensor(out=o, in0=a, in1=b, op=ALU.mult), [pvtok, rdtok], [attn_tok[hp]])
            if C.get("dbg") is not None and hp == 0 and qp == 2:
                dbg_dump(P, C, 8, attn_all[:, 0, 256:384], [attn_tok[0]])
                dbg_dump(P, C, 9, rd[:, 0:128], [rdtok])

    wo_st = Ring(P, 2, [128, 8, 128], F32, "nt_wost")
    wo_r = Ring(P, 2, [128, 8, 128], BF16, "nt_wo")
    w_out_v = w_out.rearrange("(kc p) n -> p kc n", p=128)
    for o in range(8):
        st, sttok = wo_st.next()
        P.dma("sp", st[:], w_out_v[:, :, o * 128:(o + 1) * 128], [], [sttok], sttok)
        wb, wbtok = wo_r.next()
        P.add("pool", lambda e, oo=wb[:], i=st[:]: e.tensor_copy(out=oo, in_=i), [sttok], [wbtok])
        for tt in range(4):
            t0 = tt * 512
            pt, pttok = ps_pr.next()
            for kc in range(8):
                mm(P, pt[:], wb[:, kc, :], attn_all[:, kc, t0:t0 + 512], kc == 0, kc == 7, [wbtok, attn_tok[kc]], [pttok])
            xt, xtok = fr.next()
            P.dma("sp", xt[:], x_in[o * 128:(o + 1) * 128, HN + t0:HN + t0 + 512], [], [xtok], xtok)
            P.add("dve", lambda e, oo=xt[:], a=pt[:]: e.tensor_tensor(out=oo, in0=a, in1=oo, op=ALU.add),
                  [pttok, xtok], [xtok])
            P.dma("sp", x_out[o * 128:(o + 1) * 128, t0:t0 + 512], xt[:], [xtok], [], xtok, is_out=is_out)


NB = 48
RB = NB * 128
RH = 4
LN16 = -2.772588722239781


def emit_ret(P, C, nc, x_in, x_out, g1c, w_in, cosT, sinT, l2d, gng, w_out, is_out=False):
    fr = C["fr"]
    ctok = C["ctok"]
    g_col = P.sb([128, 8], F32, "rt_g")
    gtok = load_cols(P, C, g_col, g1c)
    gng_sb = P.sb([128, 16], F32, "rt_gng")
    gngtok = load_cols(P, C, gng_sb, gng)
    ones_f = P.sb([128, 128], F32, "rt_ones_f")
    P.add("pool", lambda e: e.memset(ones_f[:], 1.0), [], [ctok])
    lg = P.sb([128, 8], F32, "rt_lg")
    nlg = P.sb([128, 8], F32, "rt_nlg")
    one_col = P.sb([128, 1], F32, "rt_one")
    ln16_col = P.sb([128, 1], F32, "rt_ln16")
    P.add("pool", lambda e: e.memset(one_col[:], 1.0), [], [ctok])
    P.add("pool", lambda e: e.memset(ln16_col[:], LN16), [], [ctok])
    lgtok = load_cols(P, C, lg, l2d)
    P.add("act", lambda e: e.activation(out=lg[:], in_=lg[:], func=AF.Exp, scale=-0.6931471805599453), [lgtok], [lgtok])
    P.add("act", lambda e: e.activation(out=lg[:], in_=lg[:], func=AF.Ln, bias=one_col[:, 0:1], scale=-1.0),
          [lgtok, ctok], [lgtok])
    P.add("dve", lambda e: e.tensor_scalar(out=nlg[:], in0=lg[:], scalar1=-1.0, scalar2=None, op0=ALU.mult),
          [lgtok], [lgtok])
    d1i = P.sb([128, 128], mybir.dt.int32, "rt_d1i")
    d1 = P.sb([128, 128], F32, "rt_d1")
    dbi = P.sb([128, NB], mybir.dt.int32, "rt_dbi")
    dbf = P.sb([128, NB], F32, "rt_dbf")
    itok = Tok("iota")
    P.add("pool", lambda e: e.iota(d1i[:], pattern=[[1, 128]], base=0, channel_multiplier=-1), [], [itok])
    P.add("pool", lambda e: e.iota(dbi[:], pattern=[[128, NB]], base=0, channel_multiplier=0), [], [itok])
    P.add("dve", lambda e: e.tensor_copy(out=d1[:], in_=d1i[:]), [itok], [itok])
    P.add("dve", lambda e: e.tensor_copy(out=dbf[:], in_=dbi[:]), [itok], [itok])

    h_dram = nc.dram_tensor("rt_h_dram", [D, RB], BF16, kind="Internal").ap()
    gT_dram = nc.dram_tensor("rt_gT_dram", [2 * D, T], BF16, kind="Internal").ap()
    h_dv = h_dram.rearrange("(c p) t -> p c t", p=128)
    gT_dv = gT_dram.rearrange("(c p) t -> p c t", p=128)
    bigr = Ring(P, 2, [128, 16, 512], BF16, "rt_big")
    ps_a = Ring(P, 2, [128, 512], F32, "ps_a", psum=True)
    ps_s = Ring(P, 2, [128, 512], F32, "ps_s", psum=True)
    ps_o = Ring(P, 2, [128, 512], F32, "ps_o", psum=True)
    ntile = RB // 512
    hd_tok = [Tok(f"hd{i}") for i in range(ntile)]
    for ti in range(ntile):
        hb, hbtok = bigr.next()
        emit_rmsnorm(P, C, x_in, g_col, gtok, hb, [[hbtok] * 8], ps_a, [(ti * 512, 512)], two_pass=True, hcol=[0])
        P.dma("sp", h_dv[:, :, ti * 512:(ti + 1) * 512], hb[:, 0:8, :], [hbtok], [hd_tok[ti]], hbtok)

    k_fm = P.sb([128, 2, RB], BF16, "rt_k")
    v_tok = P.sb([128, NB, 512], BF16, "rt_v")
    q_fm = P.sb([128, 2, T], BF16, "rt_q")
    o_fm = P.sb([128, 4, T], F32, "rt_o")
    ktok, vtok, qtok = Tok("k"), Tok("v"), Tok("q")
    otok = [Tok(f"o{i}") for i in range(16)]
    wq = P.sb([128, 8, 256], BF16, "rt_wq")
    wk = P.sb([128, 8, 256], BF16, "rt_wk")
    wv = P.sb([128, 8, 512], BF16, "rt_wv")
    wg = P.sb([128, 8, 512], BF16, "rt_wg")
    wtok = Tok("w")
    wqtok, wgtok = Tok("wqkv"), Tok("wg")
    wst = Ring(P, 2, [128, 8, 128], F32, "rt_wst")
    w_v = w_in.rearrange("(kc p) n -> p kc n", p=128)
    gf = P.sb([128, 128], F32, "rt_gf")
    gb = P.sb([128, 128], F32, "rt_gb")
    gd = P.sb([128, 128], F32, "rt_gd")
    gd2 = P.sb([128, 128], F32, "rt_gd2")
    sf = P.sb([128, NB], F32, "rt_sf")
    sbk = P.sb([128, NB], F32, "rt_sb")
    gtk = Tok("G")
    p_r = Ring(P, 10, [128, 128], BF16, "rt_p")
    gT_tok = [[Tok(f"gT{h}_{t}") for t in range(4)] for h in range(RH)]

    for h in range(RH):
        def load_w(hh_, which):
            segs = []
            if which == "qkv":
                segs += [(wq, i_ * 128, hh_ * 256 + i_ * 128, wqtok) for i_ in range(2)]
                segs += [(wk, i_ * 128, D + hh_ * 256 + i_ * 128, wqtok) for i_ in range(2)]
                segs += [(wv, i_ * 128, 2 * D + hh_ * 512 + i_ * 128, wqtok) for i_ in range(4)]
            else:
                segs += [(wg, i_ * 128, 4 * D + hh_ * 512 + i_ * 128, wgtok) for i_ in range(4)]
            for (dst, dcol, scol, tk) in segs:
                st, sttok = wst.next()
                P.dma("sp", st[:], w_v[:, :, scol:scol + 128], [], [sttok], sttok)
                P.add("pool", lambda e, o=dst[:, :, dcol:dcol + 128], i=st[:]: e.tensor_copy(out=o, in_=i), [sttok], [tk])

        if h == 0:
            load_w(0, "qkv")
        load_w(h, "g")
        P.add("act", lambda e, sc_=lg[:, h:h + 1]: e.activation(out=gf[:], in_=d1[:], func=AF.Exp, bias=ln16_col[:, 0:1], scale=sc_),
              [itok, lgtok, ctok], [gtk])
        P.add("act", lambda e, sc_=nlg[:, 4 + h:5 + h]: e.activation(out=gb[:], in_=d1[:], func=AF.Exp, bias=ln16_col[:, 0:1], scale=sc_),
              [itok, lgtok, ctok], [gtk])
        P.add("pool", lambda e: e.affine_select(out=gd[:], in_=gf[:], pattern=[[1, 128]], compare_op=ALU.is_ge, fill=0.0,
                                                base=0, channel_multiplier=-1), [gtk], [gtk])
        P.add("pool", lambda e: e.affine_select(out=gd2[:], in_=gb[:], pattern=[[-1, 128]], compare_op=ALU.is_gt, fill=0.0,
                                                base=0, channel_multiplier=1), [gtk], [gtk])
        P.add("pool", lambda e: e.tensor_tensor(out=gd[:], in0=gd[:], in1=gd2[:], op=ALU.add), [gtk], [gtk])
        P.add("act", lambda e, sc_=lg[:, h:h + 1]: e.activation(out=sf[:], in_=dbf[:], func=AF.Exp, scale=sc_), [itok, lgtok], [gtk])
        P.add("act", lambda e, sc_=lg[:, 4 + h:5 + h]: e.activation(out=sbk[:], in_=dbf[:], func=AF.Exp, scale=sc_), [itok, lgtok], [gtk])

        for ti in range(ntile):
            hb, hbtok = bigr.next()
            P.dma("sp", hb[:, 0:8, :], h_dv[:, :, ti * 512:(ti + 1) * 512], [hd_tok[ti]], [hbtok], hbtok)
            cs, cstok = fr.next()
            sn, sntok = fr.next()
            P.dma("sp", cs[:], cosT[:, ti * 512:(ti + 1) * 512], [], [cstok], cstok)
            P.dma("sp", sn[:], sinT[:, ti * 512:(ti + 1) * 512], [], [sntok], sntok)
            todo = [(wk, k_fm, ktok, ti * 512)]
            if 4 <= ti < 8:
                todo.append((wq, q_fm, qtok, (ti - 4) * 512))
            for (wmat, dst, dtok, dcol) in todo:
                p1, p1tok = ps_a.next()
                p2, p2tok = ps_a.next()
                for dc, (pt, pttok) in enumerate(((p1, p1tok), (p2, p2tok))):
                    for kc in range(8):
                        mm(P, pt[:], wmat[:, kc, dc * 128:(dc + 1) * 128], hb[:, kc, :], kc == 0, kc == 7, [wqtok, hbtok], [pttok])
                t1, t1tok = fr.next()
                t2, t2tok = fr.next()
                P.add("dve", lambda e, o=t1[:], a=p1[:], b=cs[:]: e.tensor_tensor(out=o, in0=a, in1=b, op=ALU.mult), [p1tok, cstok], [t1tok])
                P.add("dve", lambda e, o=t2[:], a=p2[:], b=sn[:]: e.tensor_tensor(out=o, in0=a, in1=b, op=ALU.mult), [p2tok, sntok], [t2tok])
                P.add("pool", lambda e, o=dst[:, 0, dcol:dcol + 512], a=t1[:], b=t2[:]: e.tensor_tensor(out=o, in0=a, in1=b, op=ALU.subtract),
                      [t1tok, t2tok], [dtok])
                P.add("dve", lambda e, o=t1[:], a=p1[:], b=sn[:]: e.tensor_tensor(out=o, in0=a, in1=b, op=ALU.mult), [p1tok, sntok], [t1tok])
                P.add("dve", lambda e, o=t2[:], a=p2[:], b=cs[:]: e.tensor_tensor(out=o, in0=a, in1=b, op=ALU.mult), [p2tok, cstok], [t2tok])
                P.add("pool", lambda e, o=dst[:, 1, dcol:dcol + 512], a=t1[:], b=t2[:]: e.tensor_tensor(out=o, in0=a, in1=b, op=ALU.add),
                      [t1tok, t2tok], [dtok])
            for bl in range(4):
                blk = ti * 4 + bl
                pv, pvtok = ps_a.next()
                for kc in range(8):
                    mm(P, pv[:], hb[:, kc, bl * 128:(bl + 1) * 128], wv[:, kc, :], kc == 0, kc == 7, [wqtok, hbtok], [pvtok])
                P.add("act", lambda e, o=v_tok[:, blk, :], i=pv[:]: e.activation(out=o, in_=i, func=AF.Identity), [pvtok], [vtok])

        if h + 1 < RH:
            load_w(h + 1, "qkv")
        NG = NB // 4

        def scores(i, cg):
            sc, sctok = ps_s.next()
            for sub in range(4):
                c = cg * 4 + sub
                for dc in range(2):
                    mm(P, sc[:, sub * 128:(sub + 1) * 128], k_fm[:, dc, c * 128:(c + 1) * 128], q_fm[:, dc, i * 128:(i + 1) * 128],
                       sub == 0 and dc == 0, dc == 1, [ktok, qtok], [sctok], skip=True)
            return sc, sctok

        seq = [(i, cg) for i in range(16) for cg in range(NG)]
        cur = scores(*seq[0])
        po, potok = None, None
        for si_, (i, cg) in enumerate(seq):
            sc, sctok = cur
            if cg == 0:
                po, potok = ps_o.next()
            pts = []
            for sub in range(4):
                c = cg * 4 + sub
                dl = 16 + i - c
                pt, pttok = p_r.next()
                if dl > 0:
                    P.add("dve", lambda e, o=pt[:], a=sc[:, sub * 128:(sub + 1) * 128], s_=sf[:, dl:dl + 1]:
                          e.scalar_tensor_tensor(out=o, in0=a, scalar=s_, in1=gf[:], op0=ALU.mult, op1=ALU.mult), [sctok, gtk], [pttok])
                elif dl < 0:
                    P.add("dve", lambda e, o=pt[:], a=sc[:, sub * 128:(sub + 1) * 128], s_=sbk[:, -dl:-dl + 1]:
                          e.scalar_tensor_tensor(out=o, in0=a, scalar=s_, in1=gb[:], op0=ALU.mult, op1=ALU.mult), [sctok, gtk], [pttok])
                else:
                    P.add("dve", lambda e, o=pt[:], a=sc[:, sub * 128:(sub + 1) * 128]:
                          e.tensor_tensor(out=o, in0=a, in1=gd[:], op=ALU.mult), [sctok, gtk], [pttok])
                pts.append((pt, pttok, c))
            if si_ + 1 < len(seq):
                cur = scores(*seq[si_ + 1])
            for (pt, pttok, c) in pts:
                for ec in range(4):
                    mm(P, po[:, ec * 128:(ec + 1) * 128], v_tok[:, c, ec * 128:(ec + 1) * 128], pt[:],
                       c == 0 and ec == 0, c == NB - 1, [vtok, pttok], [potok], skip=True)
            if cg == NG - 1:
                P.add("act", lambda e, o=o_fm[:, :, i * 128:(i + 1) * 128], a=po[:].rearrange("p (a b) -> p a b", a=4):
                      e.activation(out=o, in_=a, func=AF.Identity), [potok], [otok[i]])

        for tt in range(4):
            t0 = tt * 512
            ots = [otok[tt * 4 + b_] for b_ in range(4)]
            p1, p1tok = ps_a.next()
            p2, p2tok = ps_a.next()
            for ec in range(4):
                mm(P, p1[:], ones_f[:], o_fm[:, ec, t0:t0 + 512], ec == 0, ec == 3, ots + [ctok], [p1tok])
            for ec in range(4):
                sq, sqtok = fr.next()
                P.add("act", lambda e, o=sq[:], a=o_fm[:, ec, t0:t0 + 512]: e.activation(out=o, in_=a, func=AF.Square), ots, [sqtok])
                mm(P, p2[:], ones_f[:], sq[:], ec == 0, ec == 3, [sqtok, ctok], [p2tok])
            mu, mutok = C["rsr"].next()
            P.add("act", lambda e, o=mu[:], a=p1[:]: e.activation(out=o, in_=a, func=AF.Identity, scale=1.0 / 512), [p1tok], [mutok])
            rs, rstok = C["rsr"].next()
            P.add("dve", lambda e, o=rs[:], a=mu[:]: e.tensor_tensor(out=o, in0=a, in1=a, op=ALU.mult), [mutok], [rstok])
            P.add("dve", lambda e, o=rs[:], a=p2[:]: e.scalar_tensor_tensor(out=o, in0=a, scalar=1.0 / 512, in1=o, op0=ALU.mult, op1=ALU.subtract),
                  [p2tok, rstok], [rstok])
            P.add("act", lambda e, o=rs[:]: e.activation(out=o, in_=o, func=AF.Sqrt, bias=C["eps_col"][:, 0:1], scale=1.0), [rstok, ctok], [rstok])
            P.add("dve", lambda e, o=rs[:]: e.reciprocal(out=o, in_=o), [rstok], [rstok])
            hb, hbtok = bigr.next()
            P.dma("sp", hb[:, 0:8, :], h_dv[:, :, 2048 + t0:2048 + t0 + 512], [hd_tok[4 + tt]], [hbtok], hbtok)
            gt_sb, gttok = bigr.next()
            for ec in range(4):
                pg, pgtok = ps_a.next()
                for kc in range(8):
                    mm(P, pg[:], wg[:, kc, ec * 128:(ec + 1) * 128], hb[:, kc, :], kc == 0, kc == 7, [wgtok, hbtok], [pgtok])
                sg, sgtok = fr.next()
                P.add("act", lambda e, o=sg[:], a=pg[:]: e.activation(out=o, in_=a, func=AF.Silu), [pgtok], [sgtok])
                dd, ddtok = fr.next()
                P.add("pool", lambda e, o=dd[:], a=o_fm[:, ec, t0:t0 + 512], m=mu[:]: e.tensor_tensor(out=o, in0=a, in1=m, op=ALU.subtract),
                      ots + [mutok], [ddtok])
                P.add("dve", lambda e, o=dd[:], r=rs[:]: e.tensor_tensor(out=o, in0=o, in1=r, op=ALU.mult), [ddtok, rstok], [ddtok])
                P.add("dve", lambda e, o=gt_sb[:, ec, :], a=dd[:], g_=gng_sb[:, h * 4 + ec:h * 4 + ec + 1], s_=sg[:]:
                      e.scalar_tensor_tensor(out=o, in0=a, scalar=g_, in1=s_, op0=ALU.mult, op1=ALU.mult), [ddtok, sgtok, gngtok], [gttok])
            P.dma("sp", gT_dv[:, h * 4:(h + 1) * 4, t0:t0 + 512], gt_sb[:, 0:4, :], [gttok], [gT_tok[h][tt]], gttok)

    wo_bufs = [wq[:].rearrange("p a b -> p (a b)").rearrange("p (j n) -> p j n", j=16),
               wk[:].rearrange("p a b -> p (a b)").rearrange("p (j n) -> p j n", j=16)]
    w_out_v = w_out.rearrange("(j p) n -> p j n", p=128)
    for o in range(8):
        wb, wbtok = wo_bufs[o % 2], wqtok
        for half in range(2):
            st, sttok = wst.next()
            P.dma("sp", st[:], w_out_v[:, half * 8:(half + 1) * 8, o * 128:(o + 1) * 128], [], [sttok], sttok)
            P.add("pool", lambda e, oo=wb[:, half * 8:(half + 1) * 8, :], i=st[:]: e.tensor_copy(out=oo, in_=i), [sttok], [wbtok])
        for tt in range(4):
            t0 = tt * 512
            gb_, gbtok = bigr.next()
            P.dma("sp", gb_[:], gT_dv[:, :, t0:t0 + 512], [gT_tok[h_][tt] for h_ in range(RH)], [gbtok], gbtok)
            pt, pttok = ps_a.next()
            for j in range(16):
                mm(P, pt[:], wb[:, j, :], gb_[:, j, :], j == 0, j == 15, [wbtok, gbtok], [pttok])
            xt, xtok = fr.next()
            P.dma("sp", xt[:], x_in[o * 128:(o + 1) * 128, 2048 + t0:2048 + t0 + 512], [], [xtok], xtok)
            P.add("dve", lambda e, oo=xt[:], a=pt[:]: e.tensor_tensor(out=oo, in0=a, in1=oo, op=ALU.add), [pttok, xtok], [xtok])
            P.dma("sp", x_out[o * 128:(o + 1) * 128, t0:t0 + 512], xt[:], [xtok], [], xtok, is_out=is_out)


def build_ffn_prog():
    nc = bass.Bass("TRN2", target_bir_lowering=False)
    x_in = nc.dram_tensor("x_in", [D, T + 2], F32, kind="ExternalInput").ap()
    g2c = nc.dram_tensor("g2c", [128, 8], F32, kind="ExternalInput").ap()
    w_up = nc.dram_tensor("w_up", [D, 2 * FFN], F32, kind="ExternalInput").ap()
    dww = nc.dram_tensor("dww", [128, 3 * 2 * NH], F32, kind="ExternalInput").ap()
    dwb = nc.dram_tensor("dwb", [128, 2 * NH], F32, kind="ExternalInput").ap()
    w_down = nc.dram_tensor("w_down", [FFN, D], F32, kind="ExternalInput").ap()
    x_out = nc.dram_tensor("x_out", [D, T], F32, kind="ExternalOutput").ap()
    with contextlib.ExitStack() as stack:
        P = Prog(nc, stack)
        C = make_common(P)
        emit_ffn(P, C, x_in, x_out, g2c, w_up, dww, dwb, w_down, is_out=True)
        P.emit()
        print("ffn prog stats", P.stats)
    return nc


def cols(v, n):
    return np.ascontiguousarray(v.reshape(n, 128).T)


def shard_tokens_fm(xfull, halo):
    out = []
    for c in range(NCORES):
        b, hf = c // 2, c % 2
        lo, hi = hf * T - halo, (hf + 1) * T + halo
        buf = np.zeros((T + 2 * halo, D), np.float32)
        slo, shi = max(lo, 0), min(hi, SEQ)
        buf[slo - lo:shi - lo] = xfull[b, slo:shi]
        out.append(np.ascontiguousarray(buf.T))
    return out


def unshard_tokens_fm(outs):
    x = np.empty((BATCH, SEQ, D), np.float32)
    for c in range(NCORES):
        b, hf = c // 2, c % 2
        x[b, hf * T:(hf + 1) * T] = outs[c].T
    return x


def ffn_inmaps(x, i, norm2_g, ffn_w_up, ffn_dw_w, ffn_dw_b, ffn_w_down):
    xs = shard_tokens_fm(x, 1) if x is not None else None
    dww = np.concatenate([cols(ffn_dw_w[i, k], 2 * NH) for k in range(3)], axis=1)
    common = {
        "g2c": cols(norm2_g[i], 8),
        "w_up": np.ascontiguousarray(ffn_w_up[i]),
        "dww": np.ascontiguousarray(dww),
        "dwb": cols(ffn_dw_b[i], 2 * NH),
        "w_down": np.ascontiguousarray(ffn_w_down[i]),
    }
    return [dict(common, x_in=xs[c]) if xs is not None else dict(common) for c in range(NCORES)]


def build_conv_prog():
    nc = bass.Bass("TRN2", target_bir_lowering=False)
    dt = lambda name, shape, kind="ExternalInput": nc.dram_tensor(name, shape, F32, kind=kind).ap()
    x_in = dt("x_in", [D, T + 2 * HC])
    mask = dt("mask", [128, 2 * HC])
    g1c = dt("g1c", [128, 8])
    w_in = dt("w_in", [D, 2 * D])
    b_in = dt("b_in", [128, 16])
    dw_w = dt("dw_w", [128, CW * 8])
    dw_b = dt("dw_b", [128, 8])
    ln_g = dt("ln_g", [128, 8])
    ln_b = dt("ln_b", [128, 8])
    w_out = dt("w_out", [D, D])
    x_out = dt("x_out", [D, T], "ExternalOutput")
    with contextlib.ExitStack() as stack:
        P = Prog(nc, stack)
        C = make_common(P, nfr=12)
        emit_conv(P, C, x_in, x_out, mask, g1c, w_in, b_in, dw_w, dw_b, ln_g, ln_b, w_out, is_out=True)
        P.emit()
        print("conv prog stats", P.stats)
    return nc


def conv_inmaps(x, j, g1, conv_w_in, conv_b_in, conv_dw_w, conv_dw_b, conv_ln_g, conv_ln_b, conv_w_out):
    xs = shard_tokens_fm(x, HC) if x is not None else None
    dww = np.concatenate([cols(conv_dw_w[j, k], 8) for k in range(CW)], axis=1)
    common = {
        "g1c": cols(g1, 8),
        "w_in": np.ascontiguousarray(conv_w_in[j]),
        "b_in": cols(conv_b_in[j], 16),
        "dw_w": np.ascontiguousarray(dww),
        "dw_b": cols(conv_dw_b[j], 8),
        "ln_g": cols(conv_ln_g[j], 8),
        "ln_b": cols(conv_ln_b[j], 8),
        "w_out": np.ascontiguousarray(conv_w_out[j]),
    }
    maps = []
    for c in range(NCORES):
        hf = c % 2
        m = np.ones((128, 2 * HC), np.float32)
        if hf == 0:
            m[:, :HC] = 0.0
        else:
            m[:, HC:] = 0.0
        maps.append(dict(common, x_in=xs[c], mask=m) if xs is not None else dict(common, mask=m))
    return maps


def dbg_dump(P, C, slot, src, toks):
    t, ttok = C["dbgr"].next()
    P.add("act", lambda e: e.activation(out=t[:], in_=src, func=AF.Identity), toks, [ttok])
    P.dma("sp", C["dbg"][:, slot * 128:(slot + 1) * 128], t[:], [ttok], [], ttok, is_out=True)


def build_nat_prog(debug=False):
    nc = bass.Bass("TRN2", target_bir_lowering=False)
    dt = lambda name, shape, kind="ExternalInput": nc.dram_tensor(name, shape, F32, kind=kind).ap()
    x_in = dt("x_in", [D, T + 2 * HN])
    g1c = dt("g1c", [128, 8])
    w_qkv = dt("w_qkv", [D, 3 * D])
    qg = dt("qg", [128, 2])
    kg = dt("kg", [128, 1])
    bias = dt("bias", [8, 128, 2 * NE * 128])
    pen = dt("pen", [2, 5 * NE * 128])
    ohk = dt("ohk", [2, 128])
    bd = dt("bd", [128, 128])
    w_out = dt("w_out", [D, D])
    x_out = dt("x_out", [D, T], "ExternalOutput")
    with contextlib.ExitStack() as stack:
        P = Prog(nc, stack)
        C = make_common(P, nfr=6)
        if debug:
            C["dbg"] = dt("dbg", [128, 16 * 128], "ExternalOutput")
            C["dbgr"] = Ring(P, 2, [128, 128], F32, "dbgr")
        emit_nat(P, C, x_in, x_out, g1c, w_qkv, qg, kg, bias, pen, ohk, bd, w_out, is_out=True)
        P.emit()
        print("nat prog stats", P.stats)
    return nc


def nat_bias_table(rpb):
    kc = np.arange(64)[:, None]
    qc = np.arange(64)[None, :]
    cs = np.clip(qc - 8, 0, 48)
    win = (kc >= cs) & (kc < cs + 16)
    dc = np.clip(kc - qc + 15, 0, 30)
    out = np.full((8, 128, 2, NE, 128), NEG, np.float32)
    for hp in range(8):
        for hh in range(2):
            h = 2 * hp + hh
            for ei in range(NE):
                e_ = ei - 1
                for kp in range(2):
                    for qp_ in range(2):
                        dr = 2 * e_ + 3 + kp - qp_
                        if dr < 0 or dr > 14:
                            continue
                        blk = np.where(win, rpb[h, dr][dc], np.float32(NEG))
                        out[hp, kp * 64:(kp + 1) * 64, hh, ei, qp_ * 64:(qp_ + 1) * 64] = blk
    return out.reshape(8, 128, 2 * NE * 128)


def nat_pen_table(hf):
    out = np.full((2, 5, NE, 128), NEG, np.float32)
    for pix, qp in enumerate((0, 1, 7, 14, 15)):
        for ei in range(NE):
            e_ = ei - 1
            for kp in range(2):
                for qp_ in range(2):
                    r = 32 * hf + 2 * qp + qp_
                    kr = 32 * hf + 2 * qp + 2 * e_ - 4 + kp
                    rs = min(max(r - 4, 0), 56)
                    if 0 <= kr < 64 and rs <= kr < rs + 8:
                        out[kp, pix, ei, qp_ * 64:(qp_ + 1) * 64] = 0.0
    return out.reshape(2, 5 * NE * 128)


def nat_qg2(g):
    out = np.zeros((128, 2), np.float32)
    out[0:64, 0] = g
    out[64:128, 1] = g
    return out


def nat_inmaps(x, g1, nat_w_qkv, nat_q_norm_g, nat_k_norm_g, nat_rpb, nat_w_out):
    xs = shard_tokens_fm(x, HN) if x is not None else None
    ohk = np.zeros((2, 128), np.float32)
    ohk[0, :64] = 1.0
    ohk[1, 64:] = 1.0
    common = {
        "g1c": cols(g1, 8),
        "w_qkv": np.ascontiguousarray(nat_w_qkv[0]),
        "qg": nat_qg2(nat_q_norm_g[0]),
        "kg": np.ascontiguousarray(np.tile(nat_k_norm_g[0], 2)[:, None]),
        "bias": nat_bias_table(nat_rpb[0]),
        "ohk": ohk,
        "bd": np.kron(np.eye(2, dtype=np.float32), np.ones((64, 64), np.float32)),
        "w_out": np.ascontiguousarray(nat_w_out[0]),
    }
    pens = [nat_pen_table(0), nat_pen_table(1)]
    return [dict(common, x_in=xs[c], pen=pens[c % 2]) if xs is not None else dict(common, pen=pens[c % 2]) for c in range(NCORES)]


def build_ret_prog():
    nc = bass.Bass("TRN2", target_bir_lowering=False)
    dt = lambda name, shape, kind="ExternalInput": nc.dram_tensor(name, shape, F32, kind=kind).ap()
    x_in = dt("x_in", [D, RB])
    g1c = dt("g1c", [128, 8])
    w_in = dt("w_in", [D, 6 * D])
    cosT = dt("cosT", [128, RB])
    sinT = dt("sinT", [128, RB])
    l2d = dt("l2d", [128, 8])
    gng = dt("gng", [128, 16])
    w_out = dt("w_out", [2 * D, D])
    x_out = dt("x_out", [D, T], "ExternalOutput")
    with contextlib.ExitStack() as stack:
        P = Prog(nc, stack)
        C = make_common(P, nfr=8)
        emit_ret(P, C, nc, x_in, x_out, g1c, w_in, cosT, sinT, l2d, gng, w_out, is_out=True)
        P.emit()
        print("ret prog stats", P.stats)
    return nc


def ret_rope_tables(hf):
    theta = (1.0 / (np.float32(10000.0) ** np.linspace(0.0, 1.0, 128, dtype=np.float32))).astype(np.float32)
    pos = (np.arange(RB, dtype=np.float32) - np.float32(2048.0) + np.float32(2048.0 * hf)).astype(np.float32)
    ang = (theta[:, None] * pos[None, :]).astype(np.float32)
    return np.cos(ang).astype(np.float32), np.sin(ang).astype(np.float32)


def ret_inmaps(x, g1, ret_w_in, ret_log2_inv_decay, ret_gn_g, ret_w_out):
    common = {
        "g1c": cols(g1, 8),
        "w_in": np.ascontiguousarray(ret_w_in[0]),
        "l2d": np.ascontiguousarray(np.tile(ret_log2_inv_decay[0].reshape(1, 8), (128, 1))),
        "gng": cols(ret_gn_g[0], 16),
        "w_out": np.ascontiguousarray(ret_w_out[0]),
    }
    tabs = [ret_rope_tables(0), ret_rope_tables(1)]
    maps = []
    for c in range(NCORES):
        b, hf = c // 2, c % 2
        if x is None:
            maps.append(dict(common, cosT=tabs[hf][0], sinT=tabs[hf][1]))
            continue
        buf = np.zeros((RB, D), np.float32)
        off = 2048 - 2048 * hf
        buf[off:off + SEQ] = x[b]
        maps.append(dict(common, x_in=np.ascontiguousarray(buf.T), cosT=tabs[hf][0], sinT=tabs[hf][1]))
    return maps


STAGES = [("c0", "conv", HC), ("f0", "ffn", 1), ("n1", "nat", HN), ("f1", "ffn", 1),
          ("r2", "ret", 2048), ("f2", "ffn", 1), ("c3", "conv", HC), ("f3", "ffn", 1)]
STAGE_IN = {
    "conv": [("mask", [128, 2 * HC]), ("g1c", [128, 8]), ("w_in", [D, 2 * D]), ("b_in", [128, 16]), ("dw_w", [128, CW * 8]),
             ("dw_b", [128, 8]), ("ln_g", [128, 8]), ("ln_b", [128, 8]), ("w_out", [D, D])],
    "ffn": [("g2c", [128, 8]), ("w_up", [D, 2 * FFN]), ("dww", [128, 3 * 2 * NH]), ("dwb", [128, 2 * NH]), ("w_down", [FFN, D])],
    "nat": [("g1c", [128, 8]), ("w_qkv", [D, 3 * D]), ("qg", [128, 2]), ("kg", [128, 1]), ("bias", [8, 128, 2 * NE * 128]),
            ("pen", [2, 5 * NE * 128]), ("ohk", [2, 128]), ("bd", [128, 128]), ("w_out", [D, D])],
    "ret": [("g1c", [128, 8]), ("w_in", [D, 6 * D]), ("cosT", [128, RB]), ("sinT", [128, RB]), ("l2d", [128, 8]),
            ("gng", [128, 16]), ("w_out", [2 * D, D])],
}
STAGE_NFR = {"conv": 12, "ffn": 14, "nat": 6, "ret": 8}


def emit_exchange(P, C, nc, name, x_next, H, hmask_sb, hmtok):
    fr = C["fr"]
    kw = dict(allow_slow_non_contiguous=True) if H < 8 else {}
    groups = [[0, 1], [2, 3], [4, 5], [6, 7]]
    if H == 2048:
        for q in range(4):
            snd = nc.dram_tensor(f"{name}_snd{q}", [D, 512], F32, kind="Internal").ap()
            gath = nc.dram_tensor(f"{name}_gath{q}", [2 * D, 512], F32, kind="Internal").ap()
            stok, gtok, cctok = Tok("snd"), Tok("gath"), Tok("cc")
            P.dma("sp", snd, x_next[:, H + q * 512:H + (q + 1) * 512], [], [stok], stok)
            P.coll(lambda e, s_=snd, g_=gath: e.collective_compute("AllGather", ALU.bypass, replica_groups=groups,
                                                                   ins=[s_], outs=[g_]), [stok], [gtok], cctok)
            for (src, dcol, mi) in ((gath[0:D, :], q * 512, 0), (gath[D:2 * D, :], H + T + q * 512, 1)):
                for c in range(8):
                    xt, xtok = fr.next()
                    P.dma("sp", xt[:], src[c * 128:(c + 1) * 128, :], [gtok], [xtok], xtok)
                    P.add("dve", lambda e, o=xt[:], m=hmask_sb[:, mi:mi + 1]:
                          e.tensor_scalar(out=o, in0=o, scalar1=m, scalar2=None, op0=ALU.mult), [xtok, hmtok], [xtok])
                    P.dma("sp", x_next[c * 128:(c + 1) * 128, dcol:dcol + 512], xt[:], [xtok], [], xtok)
        return
    snd = nc.dram_tensor(name + "_snd", [2 * D, H], F32, kind="Internal").ap()
    gath = nc.dram_tensor(name + "_gath", [4 * D, H], F32, kind="Internal").ap()
    stok = Tok("snd")
    P.dma("sp", snd[0:D, :], x_next[:, H:2 * H], [], [stok], stok, **kw)
    P.dma("sp", snd[D:2 * D, :], x_next[:, T:T + H], [], [stok], stok, **kw)
    srcs = [(gath[D:2 * D, :], 0, 0), (gath[2 * D:3 * D, :], H + T, 1)]
    gtok, cctok = Tok("gath"), Tok("cc")
    P.coll(lambda e: e.collective_compute("AllGather", ALU.bypass, replica_groups=groups,
                                          ins=[snd], outs=[gath]), [stok], [gtok], cctok)
    for (src, dcol, mi) in srcs:
        for c in range(8):
            for t0 in range(0, H, 512):
                n = min(512, H - t0)
                xt, xtok = fr.next()
                P.dma("sp", xt[:, 0:n], src[c * 128:(c + 1) * 128, t0:t0 + n], [gtok], [xtok], xtok, **kw)
                P.add("dve", lambda e, o=xt[:, 0:n], m=hmask_sb[:, mi:mi + 1]:
                      e.tensor_scalar(out=o, in0=o, scalar1=m, scalar2=None, op0=ALU.mult), [xtok, hmtok], [xtok])
                P.dma("sp", x_next[c * 128:(c + 1) * 128, dcol + t0:dcol + t0 + n], xt[:, 0:n], [xtok], [], xtok, **kw)


def build_fused_prog(nst=8):
    stages = STAGES[:nst]
    nc = bass.Bass("TRN2", target_bir_lowering=False)
    dt = lambda name, shape, kind="ExternalInput": nc.dram_tensor(name, shape, F32, kind=kind).ap()
    aps = {}
    for (sn, kind, H) in stages:
        aps[sn] = {k: dt(f"{sn}_{k}", shp) for (k, shp) in STAGE_IN[kind]}
    hmask = dt("hmask", [128, 2])
    bufs = {}
    for si, (sn, kind, H) in enumerate(stages):
        width = RB if kind == "ret" else T + 2 * H
        bufs[sn] = dt(f"{sn}_x_in", [D, width], "ExternalInput" if si == 0 else "Internal")
    y = dt("x_out", [D, T], "ExternalOutput")
    with contextlib.ExitStack() as stack:
        P = Prog(nc, stack)
        for si, (sn, kind, H) in enumerate(stages):
            last = si == len(stages) - 1
            x_in = bufs[sn]
            if last:
                x_out = y
            else:
                nsn, nkind, nH = stages[si + 1]
                x_out = bufs[nsn][:, nH:nH + T]
            a = aps[sn]
            with contextlib.ExitStack() as st:
                P.stack = st
                P.pfx = sn + "_"
                C = make_common(P, nfr=STAGE_NFR[kind])
                if kind == "conv":
                    emit_conv(P, C, x_in, x_out, a["mask"], a["g1c"], a["w_in"], a["b_in"], a["dw_w"], a["dw_b"], a["ln_g"],
                              a["ln_b"], a["w_out"], is_out=last)
                elif kind == "ffn":
                    emit_ffn(P, C, x_in, x_out, a["g2c"], a["w_up"], a["dww"], a["dwb"], a["w_down"], is_out=last)
                elif kind == "nat":
                    emit_nat(P, C, x_in, x_out, a["g1c"], a["w_qkv"], a["qg"], a["kg"], a["bias"], a["pen"], a["ohk"], a["bd"],
                             a["w_out"], is_out=last)
                else:
                    emit_ret(P, C, nc, x_in, x_out, a["g1c"], a["w_in"], a["cosT"], a["sinT"], a["l2d"], a["gng"], a["w_out"],
                             is_out=last)
            P.barrier()
            if not last:
                with contextlib.ExitStack() as st:
                    P.stack = st
                    P.pfx = sn + "x_"
                    C = make_common(P, nfr=8)
                    hm_sb = P.sb([128, 2], F32, "hmask")
                    hmtok = load_cols(P, C, hm_sb, hmask)
                    emit_exchange(P, C, nc, sn + "x", bufs[nsn], nH, hm_sb, hmtok)
                P.barrier()
        P.stack = stack
        P.pfx = ""
        P.emit()
        print("fused prog stats", P.stats)
    return nc


def fused_inmaps(a):
    per_stage = {}
    per_stage["c0"] = conv_inmaps(a["x"], 0, a["norm1_g"][0], a["conv_w_in"], a["conv_b_in"], a["conv_dw_w"], a["conv_dw_b"],
                                  a["conv_ln_g"], a["conv_ln_b"], a["conv_w_out"])
    per_stage["c3"] = conv_inmaps(None, 1, a["norm1_g"][3], a["conv_w_in"], a["conv_b_in"], a["conv_dw_w"], a["conv_dw_b"],
                                  a["conv_ln_g"], a["conv_ln_b"], a["conv_w_out"])
    per_stage["n1"] = nat_inmaps(None, a["norm1_g"][1], a["nat_w_qkv"], a["nat_q_norm_g"], a["nat_k_norm_g"], a["nat_rpb"],
                                 a["nat_w_out"])
    per_stage["r2"] = ret_inmaps(None, a["norm1_g"][2], a["ret_w_in"], a["ret_log2_inv_decay"], a["ret_gn_g"], a["ret_w_out"])
    for i in range(4):
        per_stage[f"f{i}"] = ffn_inmaps(None, i, a["norm2_g"], a["ffn_w_up"], a["ffn_dw_w"], a["ffn_dw_b"], a["ffn_w_down"])
    maps = []
    for c in range(NCORES):
        m = {}
        for sn, lst in per_stage.items():
            for k, v in lst[c].items():
                m[f"{sn}_{k}"] = v
        hm = np.zeros((128, 2), np.float32)
        hm[:, 0] = float(c % 2)
        hm[:, 1] = float(1 - c % 2)
        m["hmask"] = hm
        maps.append(m)
    return maps


_PROGS = {}


def _prog(name, builder):
    if name not in _PROGS:
        _PROGS[name] = builder()
    return _PROGS[name]


def _launch(nc, maps):
    res = run_bass_kernel_spmd(nc, maps, core_ids=list(range(NCORES)))
    return unshard_tokens_fm([r["x_out"] for r in res.results])


def kernel_unfused(x, norm1_g, norm2_g, conv_w_in, conv_b_in, conv_dw_w, conv_dw_b, conv_ln_g, conv_ln_b, conv_w_out,
           nat_w_qkv, nat_q_norm_g, nat_k_norm_g, nat_rpb, nat_w_out, ret_w_in, ret_log2_inv_decay, ret_gn_g,
           ret_w_out, ffn_w_up, ffn_dw_w, ffn_dw_b, ffn_w_down):
    a = {k: np.asarray(v, np.float32) for k, v in locals().items()}
    xc = a["x"]
    for i in range(4):
        mixer, j = i % 3, i // 3
        if mixer == 0:
            maps = conv_inmaps(xc, j, a["norm1_g"][i], a["conv_w_in"], a["conv_b_in"], a["conv_dw_w"], a["conv_dw_b"],
                               a["conv_ln_g"], a["conv_ln_b"], a["conv_w_out"])
            xc = _launch(_prog("conv", build_conv_prog), maps)
        elif mixer == 1:
            maps = nat_inmaps(xc, a["norm1_g"][i], a["nat_w_qkv"], a["nat_q_norm_g"], a["nat_k_norm_g"], a["nat_rpb"],
                              a["nat_w_out"])
            xc = _launch(_prog("nat", build_nat_prog), maps)
        else:
            maps = ret_inmaps(xc, a["norm1_g"][i], a["ret_w_in"], a["ret_log2_inv_decay"], a["ret_gn_g"], a["ret_w_out"])
            xc = _launch(_prog("ret", build_ret_prog), maps)
        maps = ffn_inmaps(xc, i, a["norm2_g"], a["ffn_w_up"], a["ffn_dw_w"], a["ffn_dw_b"], a["ffn_w_down"])
        xc = _launch(_prog("ffn", build_ffn_prog), maps)
    return xc


def kernel(x, norm1_g, norm2_g, conv_w_in, conv_b_in, conv_dw_w, conv_dw_b, conv_ln_g, conv_ln_b, conv_w_out,
           nat_w_qkv, nat_q_norm_g, nat_k_norm_g, nat_rpb, nat_w_out, ret_w_in, ret_log2_inv_decay, ret_gn_g,
           ret_w_out, ffn_w_up, ffn_dw_w, ffn_dw_b, ffn_w_down):
    a = {k: np.asarray(v, np.float32) for k, v in locals().items()}
    maps = fused_inmaps(a)
    nc = _prog("fused", build_fused_prog)
    res = run_bass_kernel_spmd(nc, maps, core_ids=list(range(NCORES)))
    return unshard_tokens_fm([r["x_out"] for r in res.results])
```

```python
import contextlib
import numpy as np
import concourse.bass as bass
import concourse.mybir as mybir
from concourse.bass_utils import run_bass_kernel_spmd

F32 = mybir.dt.float32
BF16 = mybir.dt.bfloat16
AF = mybir.ActivationFunctionType
ALU = mybir.AluOpType
AX = mybir.AxisListType

D = 1024
SEQ = 4096
BATCH = 4
T = 2048
NCORES = 8
FFN = 2816
NH = FFN // 128
EPS = 1e-6


class Tok:
    __slots__ = ("lw", "rd", "name", "sem", "dcount", "last_dma")

    def __init__(self, name=""):
        self.lw = None
        self.rd = []
        self.name = name
        self.sem = None
        self.dcount = 0
        self.last_dma = None


class Op:
    __slots__ = ("eng", "fn", "deps", "is_dma", "dtok", "has_dep", "sem", "val",
                 "waits", "know", "is_out", "inc", "is_barrier")

    def __init__(self, eng, fn, is_dma=False, dtok=None):
        self.eng = eng
        self.fn = fn
        self.deps = set()
        self.is_dma = is_dma
        self.dtok = dtok
        self.has_dep = False
        self.sem = None
        self.val = 0
        self.waits = ()
        self.know = None
        self.is_out = False
        self.inc = 16 if is_dma else 1
        self.is_barrier = False


class Prog:
    ENGS = ("pe", "act", "dve", "pool", "sp")

    def __init__(self, nc, stack):
        self.nc = nc
        self.stack = stack
        self.ops = []
        self.nsb = 0
        self.nsem = 0
        self.out_ops = []
        self.pfx = ""
        self.bar_start = 0
        self.prev_bar = []

    def sb(self, shape, dtype, name=None):
        self.nsb += 1
        return self.stack.enter_context(
            self.nc.sbuf_tensor(self.pfx + (name or f"sb{self.nsb}"), list(shape), dtype))

    def ps(self, shape, dtype, name=None):
        self.nsb += 1
        return self.stack.enter_context(
            self.nc.psum_tensor(self.pfx + (name or f"ps{self.nsb}"), list(shape), dtype))

    def barrier(self):
        last = {}
        dmas = {}
        for op in self.ops[self.bar_start:]:
            if op.is_dma:
                dmas[id(op.dtok)] = op
            else:
                last[op.eng] = op
        deps = set(last.values()) | set(dmas.values()) | set(self.prev_bar)
        bars = []
        for eng in self.ENGS:
            op = Op(eng, lambda e: e.nop())
            op.deps = set(deps)
            op.is_barrier = (eng == self.ENGS[0])
            self.ops.append(op)
            bars.append(op)
        self.prev_bar = bars
        self.bar_start = len(self.ops)

    def new_sem(self, name=None):
        self.nsem += 1
        return self.stack.enter_context(self.nc.semaphore(name or f"sem{self.nsem}"))

    def add(self, eng, fn, reads=(), writes=(), is_dma=False, dtok=None, is_out=False):
        op = Op(eng, fn, is_dma, dtok)
        op.is_out = is_out
        for t in reads:
            if t.lw is not None:
                op.deps.add(t.lw)
        for t in writes:
            for r in t.rd:
                op.deps.add(r)
            if t.lw is not None:
                op.deps.add(t.lw)
        if is_dma:
            if dtok.last_dma is not None:
                op.deps.add(dtok.last_dma)
            dtok.last_dma = op
        for t in reads:
            t.rd.append(op)
        for t in writes:
            t.rd = []
            t.lw = op
        op.deps.discard(op)
        if eng == "pe" and not is_dma:
            op.deps = {d for d in op.deps if not (d.eng == "pe" and not d.is_dma)}
        self.ops.append(op)
        if is_out:
            self.out_ops.append(op)
        return op

    def coll(self, fn, reads, writes, dtok):
        op = self.add("pool", fn, reads, writes, is_dma=True, dtok=dtok)
        op.inc = 1
        return op

    def dma(self, queue, out, in_, reads, writes, dtok, is_out=False, **kw):
        return self.add(queue, lambda e: e.dma_start(out=out, in_=in_, **kw),
                        reads, writes, is_dma=True, dtok=dtok, is_out=is_out)

    def emit(self):
        ops = self.ops
        for op in ops:
            for d in op.deps:
                d.has_dep = True
        esem = {e: self.new_sem("eng_" + e) for e in self.ENGS}
        cnt = {e: 0 for e in self.ENGS}
        free_sems = []
        live_toks = []
        for op in ops:
            if op.is_barrier:
                for t in live_toks:
                    free_sems.append((t.sem, t.dcount))
                live_toks = []
            if op.is_dma:
                t = op.dtok
                if t.sem is None:
                    if free_sems:
                        t.sem, t.dcount = free_sems.pop()
                    else:
                        t.sem = self.new_sem()
                    live_toks.append(t)
                t.dcount += op.inc
                op.sem = t.sem
                op.val = t.dcount
            elif op.has_dep:
                cnt[op.eng] += 1
                op.sem = esem[op.eng]
                op.val = cnt[op.eng]
        seen = {e: {} for e in self.ENGS}
        nwaits = 0
        for op in ops:
            s = seen[op.eng]
            waits = {}
            for d in op.deps:
                k = id(d.sem)
                if s.get(k, (None, 0))[1] >= d.val:
                    continue
                if waits.get(k, (None, 0))[1] < d.val:
                    waits[k] = (d.sem, d.val)
            for d in op.deps:
                if d.know is not None:
                    for k, v in d.know.items():
                        if s.get(k, (None, 0))[1] < v[1]:
                            s[k] = v
            for k, v in waits.items():
                if s.get(k, (None, 0))[1] < v[1]:
                    s[k] = v
            op.waits = list(waits.values())
            nwaits += len(op.waits)
            if op.sem is not None:
                kn = dict(s)
                kn[id(op.sem)] = (op.sem, op.val)
                op.know = kn
                if not op.is_dma:
                    s[id(op.sem)] = (op.sem, op.val)
        by = {e: [o for o in ops if o.eng == e] for e in self.ENGS}
        finals = [(o.sem, o.val) for o in self.out_ops]
        self.stats = dict(nops=len(ops), nwaits=nwaits,
                          per_eng={e: len(by[e]) for e in self.ENGS}, nsem=self.nsem)

        def run(name, e):
            for op in by[name]:
                for sem, val in op.waits:
                    e.wait_ge(sem, val)
                inst = op.fn(e)
                if op.sem is not None:
                    inst.then_inc(op.sem, op.inc)
            if name == "sp":
                for sem, val in finals:
                    e.wait_ge(sem, val)

        with self.nc.Block() as block:
            @block.tensor
            def _(e):
                run("pe", e)

            @block.scalar
            def _(e):
                run("act", e)

            @block.vector
            def _(e):
                run("dve", e)

            @block.gpsimd
            def _(e):
                run("pool", e)

            @block.sync
            def _(e):
                run("sp", e)


class Ring:
    def __init__(self, P, n, shape, dtype, name, psum=False):
        self.bufs = []
        for i in range(n):
            t = (P.ps if psum else P.sb)(shape, dtype, f"{name}{i}")
            self.bufs.append((t, Tok(f"{name}{i}")))
        self.i = 0

    def next(self):
        b = self.bufs[self.i % len(self.bufs)]
        self.i += 1
        return b


def mm(P, out, lhsT, rhs, start, stop, reads, writes, skip=False):
    return P.add("pe", lambda e: e.matmul(out, lhsT, rhs, start=start, stop=stop, skip_group_check=skip),
                 reads, writes)


def emit_rmsnorm(P, C, x_dram, g_col, gtok, h_all, h_tok, ps_ring, tiles, two_pass=False, hcol=None):
    fr = C["fr"]
    sqr = C["sqr"]
    ones = C["ones_bf"]
    assert len(fr.bufs) >= (4 if two_pass else 9)
    for ti, (t0, n) in enumerate(tiles):
        d0 = t0 if hcol is None else hcol[ti]
        xs = []
        pst, pstok = ps_ring.next()
        for c in range(8):
            xt, xtok = fr.next()
            P.dma("sp", xt[:, 0:n], x_dram[c * 128:(c + 1) * 128, t0:t0 + n], [], [xtok], xtok)
            xs.append((xt, xtok))
            sq, sqtok = sqr.next()
            P.add("act", lambda e, o=sq[:, 0:n], i=xt[:, 0:n]: e.activation(out=o, in_=i, func=AF.Square),
                  [xtok], [sqtok])
            mm(P, pst[:, 0:n], ones[:], sq[:, 0:n], c == 0, c == 7, [sqtok, C["ctok"]], [pstok])
        rs, rstok = C["rsr"].next()
        P.add("act", lambda e, o=rs[:, 0:n], i=pst[:, 0:n]: e.activation(
            out=o, in_=i, func=AF.Sqrt, bias=C["eps_col"][:, 0:1], scale=1.0 / D), [pstok, C["ctok"]], [rstok])
        P.add("dve", lambda e, o=rs[:, 0:n]: e.reciprocal(out=o, in_=o), [rstok], [rstok])
        for c in range(8):
            if two_pass:
                xt, xtok = fr.next()
                P.dma("sp", xt[:, 0:n], x_dram[c * 128:(c + 1) * 128, t0:t0 + n], [], [xtok], xtok)
            else:
                xt, xtok = xs[c]
            P.add("dve", lambda e, o=h_all[:, c, d0:d0 + n], i=xt[:, 0:n], g=g_col[:, c:c + 1], r=rs[:, 0:n]:
                  e.scalar_tensor_tensor(out=o, in0=i, scalar=g, in1=r, op0=ALU.mult, op1=ALU.mult),
                  [xtok, rstok, gtok], [h_tok[ti][c]])


def make_common(P, nfr=14):
    C = {}
    C["fr"] = Ring(P, nfr, [128, 512], F32, "fr")
    C["sqr"] = Ring(P, 3, [128, 512], BF16, "sqr")
    C["rsr"] = Ring(P, 2, [128, 512], F32, "rsr")
    C["ones_bf"] = P.sb([128, 128], BF16, "ones_bf")
    C["eps_col"] = P.sb([128, 1], F32, "eps_col")
    C["ctok"] = Tok("consts")
    P.add("pool", lambda e: e.memset(C["ones_bf"][:], 1.0), [], [C["ctok"]])
    P.add("pool", lambda e: e.memset(C["eps_col"][:], EPS), [], [C["ctok"]])
    return C


def load_cols(P, C, dst, src_dram, scale=None):
    tok = Tok("par")
    P.dma("sp", dst[:], src_dram, [], [tok], tok)
    if scale is not None:
        P.add("pool", lambda e: e.tensor_scalar(out=dst[:], in0=dst[:], scalar1=float(scale), scalar2=None,
                                                 op0=ALU.mult), [tok], [tok])
    return tok


def htoks_for(h_tok, tiles, c, lo, hi):
    return [h_tok[i][c] for i, (t0, n) in enumerate(tiles) if t0 < hi and t0 + n > lo]


def emit_ffn(P, C, x_in, x_out, g2c, w_up, dww, dwb, w_down, is_out=False):
    NT = T + 2
    g_col = P.sb([128, 8], F32, "ffn_g")
    gtok = load_cols(P, C, g_col, g2c)
    dww_sb = P.sb([128, 3 * 2 * NH], F32, "ffn_dww")
    dwb_sb = P.sb([128, 2 * NH], F32, "ffn_dwb")
    dwtok = load_cols(P, C, dww_sb, dww)
    dbtok = load_cols(P, C, dwb_sb, dwb)

    h_all = P.sb([128, 8, NT], BF16, "ffn_h")
    tiles = [(0, 410), (410, 410), (820, 410), (1230, 410), (1640, NT - 1640)]
    h_tok = [[Tok(f"h{i}_{c}") for c in range(8)] for i in range(len(tiles))]
    ps_stat = Ring(P, 1, [128, 512], F32, "ps_stat", psum=True)
    emit_rmsnorm(P, C, x_in, g_col, gtok, h_all, h_tok, ps_stat, tiles)

    act_all = P.sb([128, NH, T], BF16, "ffn_act")
    act_tok = [[Tok(f"act{j}_{i}") for i in range(5)] for j in range(NH)]

    wst = Ring(P, 2, [128, NH * 128], F32, "wst")
    wbf = Ring(P, 2, [128, NH * 128], BF16, "wbf")
    ps_up = Ring(P, 7, [128, 512], F32, "ps_up", psum=True)
    fr = C["fr"]
    w_up_v = w_up.rearrange("(kc p) n -> p kc n", p=128)
    ctiles = [(0, 410), (410, 410), (820, 410), (1230, 410), (1640, 408)]
    def load_up(j):
        st, sttok = wst.next()
        stv = st[:, 0:2048].rearrange("p (k n) -> p k n", k=8)
        P.dma("sp", stv[:, :, 0:128], w_up_v[:, :, j * 128:(j + 1) * 128], [], [sttok], sttok)
        P.dma("sp", stv[:, :, 128:256], w_up_v[:, :, FFN + j * 128:FFN + (j + 1) * 128], [], [sttok], sttok)
        wb, wbtok = wbf.next()
        P.add("pool", lambda e, o=wb[:, 0:2048], i=st[:, 0:2048]: e.tensor_copy(out=o, in_=i), [sttok], [wbtok])
        return wb[:, 0:2048].rearrange("p (k n) -> p k n", k=8), wbtok

    nxt = load_up(0)
    for j in range(NH):
        wbv, wbtok = nxt
        if j + 1 < NH:
            nxt = load_up(j + 1)
        for ci, (o0, n) in enumerate(ctiles):
            ncol = n + 2
            pv, pvtok = ps_up.next()
            pg, pgtok = ps_up.next()
            for half, (pt, pttok) in enumerate(((pv, pvtok), (pg, pgtok))):
                for kc in range(8):
                    mm(P, pt[:, 0:ncol], wbv[:, kc, half * 128:(half + 1) * 128], h_all[:, kc, o0:o0 + ncol],
                       kc == 0, kc == 7, [wbtok] + htoks_for(h_tok, tiles, kc, o0, o0 + ncol), [pttok])
            av, avtok = fr.next()
            ag, agtok = fr.next()
            for half, (pt, pttok, acc, acctok) in enumerate(((pv, pvtok, av, avtok), (pg, pgtok, ag, agtok))):
                ch = half * NH + j
                w0 = dww_sb[:, 0 * 2 * NH + ch:0 * 2 * NH + ch + 1]
                w1 = dww_sb[:, 1 * 2 * NH + ch:1 * 2 * NH + ch + 1]
                w2 = dww_sb[:, 2 * 2 * NH + ch:2 * 2 * NH + ch + 1]
                bb = dwb_sb[:, ch:ch + 1]
                P.add("act", lambda e, o=acc[:, 0:n], i=pt[:, 1:n + 1], s=w1, b=bb:
                      e.activation(out=o, in_=i, func=AF.Identity, bias=b, scale=s),
                      [pttok, dwtok, dbtok], [acctok])
                P.add("dve", lambda e, o=acc[:, 0:n], i=pt[:, 0:n], s=w0:
                      e.scalar_tensor_tensor(out=o, in0=i, scalar=s, in1=o, op0=ALU.mult, op1=ALU.add),
                      [pttok, acctok, dwtok], [acctok])
                P.add("dve", lambda e, o=acc[:, 0:n], i=pt[:, 2:n + 2], s=w2:
                      e.scalar_tensor_tensor(out=o, in0=i, scalar=s, in1=o, op0=ALU.mult, op1=ALU.add),
                      [pttok, acctok, dwtok], [acctok])
            ge, getok = fr.next()
            P.add("act", lambda e, o=ge[:, 0:n], i=ag[:, 0:n]: e.activation(out=o, in_=i, func=AF.Gelu_apprx_tanh),
                  [agtok], [getok])
            P.add("dve", lambda e, o=act_all[:, j, o0:o0 + n], a=ge[:, 0:n], b=av[:, 0:n]:
                  e.tensor_tensor(out=o, in0=a, in1=b, op=ALU.mult), [getok, avtok], [act_tok[j][ci]])

    ps_dn = ps_up
    w_dn_v = w_down.rearrange("(j p) n -> p j n", p=128)
    def load_dn(o):
        st, sttok = wst.next()
        stv = st[:].rearrange("p (j n) -> p j n", j=NH)
        P.dma("sp", stv, w_dn_v[:, :, o * 128:(o + 1) * 128], [], [sttok], sttok)
        wb, wbtok = wbf.next()
        P.add("pool", lambda e, oo=wb[:], i=st[:]: e.tensor_copy(out=oo, in_=i), [sttok], [wbtok])
        return wb[:].rearrange("p (j n) -> p j n", j=NH), wbtok

    nxt = load_dn(0)
    for o in range(8):
        wbv, wbtok = nxt
        if o + 1 < 8:
            nxt = load_dn(o + 1)
        for tt in range(T // 512):
            t0 = tt * 512
            xt, xtok = fr.next()
            P.dma("sp", xt[:], x_in[o * 128:(o + 1) * 128, 1 + t0:1 + t0 + 512], [], [xtok], xtok)
            pt, pttok = ps_dn.next()
            for j in range(NH):
                rd = [wbtok] + [act_tok[j][ci] for ci, (o0, n) in enumerate(ctiles) if o0 < t0 + 512 and o0 + n > t0]
                mm(P, pt[:], wbv[:, j, :], act_all[:, j, t0:t0 + 512], j == 0, j == NH - 1, rd, [pttok])
            P.add("dve", lambda e, oo=xt[:], a=pt[:]: e.tensor_tensor(out=oo, in0=a, in1=oo, op=ALU.add),
                  [pttok, xtok], [xtok])
            P.dma("sp", x_out[o * 128:(o + 1) * 128, t0:t0 + 512], xt[:], [xtok], [], xtok, is_out=is_out)


CW = 31
HC = 15


def emit_conv(P, C, x_in, x_out, mask, g1c, w_in, b_in, dw_w, dw_b, ln_g, ln_b, w_out, is_out=False):
    NT = T + 2 * HC
    fr = C["fr"]
    g_col = P.sb([128, 8], F32, "cv_g")
    gtok = load_cols(P, C, g_col, g1c)
    bin_sb = P.sb([128, 16], F32, "cv_bin")
    bintok = load_cols(P, C, bin_sb, b_in)
    dww_sb = P.sb([128, CW * 8], F32, "cv_dww")
    dwwtok = load_cols(P, C, dww_sb, dw_w)
    dwb_sb = P.sb([128, 8], F32, "cv_dwb")
    dwbtok = load_cols(P, C, dwb_sb, dw_b)
    lng_sb = P.sb([128, 8], F32, "cv_lng")
    lngtok = load_cols(P, C, lng_sb, ln_g)
    lnb_sb = P.sb([128, 8], F32, "cv_lnb")
    lnbtok = load_cols(P, C, lnb_sb, ln_b)
    mask_sb = P.sb([128, 2 * HC], F32, "cv_mask")
    masktok = load_cols(P, C, mask_sb, mask)
    ones_f = P.sb([128, 128], F32, "ones_f")
    P.add("pool", lambda e: e.memset(ones_f[:], 1.0), [], [C["ctok"]])

    h_all = P.sb([128, 8, NT], BF16, "cv_h")
    tiles = [(0, 416), (416, 416), (832, 416), (1248, 416), (1664, NT - 1664)]
    h_tok = [[Tok(f"cvh{i}_{c}") for c in range(8)] for i in range(len(tiles))]
    ps_stat = Ring(P, 2, [128, 512], F32, "ps_stat", psum=True)
    emit_rmsnorm(P, C, x_in, g_col, gtok, h_all, h_tok, ps_stat, tiles)

    v_all = P.sb([128, 8, T], F32, "cv_v")
    v_tok = [[Tok(f"cvv{c}_{i}") for i in range(4)] for c in range(8)]
    ur = Ring(P, 2, [128, NT], F32, "cv_u")
    wst = Ring(P, 2, [128, 2048], F32, "cv_wst")
    wbf = Ring(P, 2, [128, 2048], BF16, "cv_wbf")
    ps_up = Ring(P, 4, [128, 512], F32, "ps_up", psum=True)
    w_in_v = w_in.rearrange("(kc p) n -> p kc n", p=128)
    KD = 20
    for c in range(8):
        st, sttok = wst.next()
        stv = st[:].rearrange("p (k n) -> p k n", k=8)
        P.dma("sp", stv[:, :, 0:128], w_in_v[:, :, c * 128:(c + 1) * 128], [], [sttok], sttok)
        P.dma("sp", stv[:, :, 128:256], w_in_v[:, :, D + c * 128:D + (c + 1) * 128], [], [sttok], sttok)
        wb, wbtok = wbf.next()
        P.add("pool", lambda e, o=wb[:], i=st[:]: e.tensor_copy(out=o, in_=i), [sttok], [wbtok])
        wbv = wb[:].rearrange("p (k n) -> p k n", k=8)
        u, utok = ur.next()
        for ti, (t0, n) in enumerate(tiles):
            pa, patok = ps_up.next()
            pg, pgtok = ps_up.next()
            for half, (pt, pttok) in enumerate(((pa, patok), (pg, pgtok))):
                for kc in range(8):
                    mm(P, pt[:, 0:n], wbv[:, kc, half * 128:(half + 1) * 128], h_all[:, kc, t0:t0 + n],
                       kc == 0, kc == 7, [wbtok, h_tok[ti][kc]], [pttok])
            sg, sgtok = fr.next()
            P.add("act", lambda e, o=sg[:, 0:n], i=pg[:, 0:n], b=bin_sb[:, 8 + c:9 + c]:
                  e.activation(out=o, in_=i, func=AF.Sigmoid, bias=b, scale=1.0), [pgtok, bintok], [sgtok])
            P.add("dve", lambda e, o=u[:, t0:t0 + n], i=pa[:, 0:n], b=bin_sb[:, c:c + 1], g=sg[:, 0:n]:
                  e.scalar_tensor_tensor(out=o, in0=i, scalar=b, in1=g, op0=ALU.add, op1=ALU.mult),
                  [patok, sgtok, bintok], [utok])
        P.add("pool", lambda e, o=u[:, 0:HC], m=mask_sb[:, 0:HC]: e.tensor_tensor(out=o, in0=o, in1=m, op=ALU.mult),
              [utok, masktok], [utok])
        P.add("pool", lambda e, o=u[:, T + HC:NT], m=mask_sb[:, HC:2 * HC]: e.tensor_tensor(out=o, in0=o, in1=m, op=ALU.mult),
              [utok, masktok], [utok])
        for tt in range(4):
            t0 = tt * 512
            va = v_all[:, c, t0:t0 + 512]
            vb, vbtok = fr.next()
            wk = lambda k: dww_sb[:, k * 8 + c:k * 8 + c + 1]
            P.add("act", lambda e, o=va, i=u[:, t0:t0 + 512], s=wk(0), b=dwb_sb[:, c:c + 1]:
                  e.activation(out=o, in_=i, func=AF.Identity, bias=b, scale=s), [utok, dwwtok, dwbtok], [v_tok[c][tt]])
            for k in range(1, KD + 1):
                P.add("dve", lambda e, o=va, i=u[:, t0 + k:t0 + k + 512], s=wk(k):
                      e.scalar_tensor_tensor(out=o, in0=i, scalar=s, in1=o, op0=ALU.mult, op1=ALU.add),
                      [utok, dwwtok, v_tok[c][tt]], [v_tok[c][tt]])
            P.add("act", lambda e, o=vb[:], i=u[:, t0 + KD + 1:t0 + KD + 1 + 512], s=wk(KD + 1):
                  e.activation(out=o, in_=i, func=AF.Identity, scale=s), [utok, dwwtok], [vbtok])
            for k in range(KD + 2, CW):
                tp, tptok = fr.next()
                P.add("act", lambda e, o=tp[:], i=u[:, t0 + k:t0 + k + 512], s=wk(k):
                      e.activation(out=o, in_=i, func=AF.Identity, scale=s), [utok, dwwtok], [tptok])
                P.add("pool", lambda e, o=vb[:], a=tp[:]: e.tensor_tensor(out=o, in0=o, in1=a, op=ALU.add),
                      [tptok, vbtok], [vbtok])
            P.add("dve", lambda e, o=va, b=vb[:]: e.tensor_tensor(out=o, in0=o, in1=b, op=ALU.add),
                  [vbtok, v_tok[c][tt]], [v_tok[c][tt]])

    wo_bf = P.sb([128, 8, D], BF16, "cv_wo")
    wotok = [Tok(f"wo{i}") for i in range(4)]
    w_out_v = w_out.rearrange("(kc p) n -> p kc n", p=128)
    for i in range(4):
        st, sttok = wst.next()
        stv = st[:].rearrange("p (k n) -> p k n", k=8)
        P.dma("sp", stv, w_out_v[:, :, i * 256:(i + 1) * 256], [], [sttok], sttok)
        P.add("pool", lambda e, o=wo_bf[:, :, i * 256:(i + 1) * 256], s_=stv: e.tensor_copy(out=o, in_=s_),
              [sttok], [wotok[i]])

    ps_o = Ring(P, 2, [128, 512], F32, "ps_o", psum=True)
    for tt in range(4):
        t0 = tt * 512
        p1, p1tok = ps_stat.next()
        p2, p2tok = ps_stat.next()
        for c in range(8):
            mm(P, p1[:], ones_f[:], v_all[:, c, t0:t0 + 512], c == 0, c == 7, [v_tok[c][tt], C["ctok"]], [p1tok])
        for c in range(8):
            sq, sqtok = fr.next()
            P.add("act", lambda e, o=sq[:], i=v_all[:, c, t0:t0 + 512]: e.activation(out=o, in_=i, func=AF.Square),
                  [v_tok[c][tt]], [sqtok])
            mm(P, p2[:], ones_f[:], sq[:], c == 0, c == 7, [sqtok, C["ctok"]], [p2tok])
        mu, mutok = fr.next()
        P.add("act", lambda e, o=mu[:], i=p1[:]: e.activation(out=o, in_=i, func=AF.Identity, scale=1.0 / D),
              [p1tok], [mutok])
        rs, rstok = fr.next()
        P.add("dve", lambda e, o=rs[:], a=mu[:]: e.tensor_tensor(out=o, in0=a, in1=a, op=ALU.mult), [mutok], [rstok])
        P.add("dve", lambda e, o=rs[:], i=p2[:]: e.scalar_tensor_tensor(out=o, in0=i, scalar=1.0 / D, in1=o,
                                                                        op0=ALU.mult, op1=ALU.subtract),
              [p2tok, rstok], [rstok])
        P.add("act", lambda e, o=rs[:]: e.activation(out=o, in_=o, func=AF.Sqrt, bias=C["eps_col"][:, 0:1], scale=1.0),
              [rstok, C["ctok"]], [rstok])
        P.add("dve", lambda e, o=rs[:]: e.reciprocal(out=o, in_=o), [rstok], [rstok])
        for c in range(8):
            dd, ddtok = fr.next()
            P.add("pool", lambda e, o=dd[:], a=v_all[:, c, t0:t0 + 512], m=mu[:]:
                  e.tensor_tensor(out=o, in0=a, in1=m, op=ALU.subtract), [v_tok[c][tt], mutok], [ddtok])
            P.add("dve", lambda e, o=dd[:], r=rs[:]: e.tensor_tensor(out=o, in0=o, in1=r, op=ALU.mult),
                  [ddtok, rstok], [ddtok])
            P.add("act", lambda e, o=h_all[:, c, t0:t0 + 512], i=dd[:], g=lng_sb[:, c:c + 1], b=lnb_sb[:, c:c + 1]:
                  e.activation(out=o, in_=i, func=AF.Silu, bias=b, scale=g),
                  [ddtok, lngtok, lnbtok], [h_tok[i][c] for i in range(len(tiles))])
        for o in range(8):
            pt, pttok = ps_o.next()
            for kc in range(8):
                mm(P, pt[:], wo_bf[:, kc, o * 128:(o + 1) * 128], h_all[:, kc, t0:t0 + 512], kc == 0, kc == 7,
                   [wotok[o // 2], h_tok[tt][kc]], [pttok])
            xt, xtok = fr.next()
            P.dma("sp", xt[:], x_in[o * 128:(o + 1) * 128, HC + t0:HC + t0 + 512], [], [xtok], xtok)
            P.add("dve", lambda e, oo=xt[:], a=pt[:]: e.tensor_tensor(out=oo, in0=a, in1=oo, op=ALU.add),
                  [pttok, xtok], [xtok])
            P.dma("sp", x_out[o * 128:(o + 1) * 128, t0:t0 + 512], xt[:], [xtok], [], xtok, is_out=is_out)


HN = 256
NEG = -30000.0
NE = 7


def nat_es(qp):
    if qp == 0:
        return list(range(0, 6))
    if qp == 15:
        return list(range(-1, 5))
    return list(range(0, 5))


def nat_pidx(qp):
    return {0: 0, 1: 1, 14: 3, 15: 4}.get(qp, 2)


def emit_nat(P, C, x_in, x_out, g1c, w_qkv, qg, kg, bias, pen, ohk, bd, w_out, is_out=False):
    NT = T + 2 * HN
    fr = C["fr"]
    g_col = P.sb([128, 8], F32, "nt_g")
    gtok = load_cols(P, C, g_col, g1c)
    qg_sb = P.sb([128, 2], F32, "nt_qg")
    qgtok = load_cols(P, C, qg_sb, qg, scale=0.125)
    kg_sb = P.sb([128, 1], F32, "nt_kg")
    kgtok = load_cols(P, C, kg_sb, kg)
    ctok = C["ctok"]
    bd_f = P.sb([128, 128], F32, "nt_bd")
    bdtok = load_cols(P, C, bd_f, bd)
    ident_f = P.sb([128, 128], F32, "nt_idf")
    ident = P.sb([128, 128], BF16, "nt_id")
    P.add("pool", lambda e: e.memset(ident_f[:], 1.0), [], [ctok])
    P.add("pool", lambda e: e.affine_select(out=ident_f[:], in_=ident_f[:], pattern=[[-1, 128]], compare_op=ALU.is_equal,
                                            fill=0.0, base=0, channel_multiplier=1), [ctok], [ctok])
    P.add("pool", lambda e: e.tensor_copy(out=ident[:], in_=ident_f[:]), [ctok], [ctok])
    pen_f = P.sb([2, 5 * NE * 128], F32, "nt_penf")
    pen_bf = P.sb([2, 5 * NE * 128], BF16, "nt_pen")
    pentok = load_cols(P, C, pen_f, pen)
    P.add("pool", lambda e: e.tensor_copy(out=pen_bf[:], in_=pen_f[:]), [pentok], [pentok])
    ohk_f = P.sb([2, 128], F32, "nt_ohkf")
    ohk_bf = P.sb([2, 128], BF16, "nt_ohk")
    ohktok = load_cols(P, C, ohk_f, ohk)
    P.add("pool", lambda e: e.tensor_copy(out=ohk_bf[:], in_=ohk_f[:]), [ohktok], [ohktok])

    h_all = P.sb([128, 8, NT], BF16, "nt_h")
    tiles = [(i * 512, 512) for i in range(NT // 512)]
    h_tok = [[Tok(f"nth{i}_{c}") for c in range(8)] for i in range(len(tiles))]
    ps_pr = Ring(P, 2, [128, 512], F32, "ps_pr", psum=True)
    emit_rmsnorm(P, C, x_in, g_col, gtok, h_all, h_tok, ps_pr, tiles, two_pass=True)

    attn_all = P.sb([128, 8, T], BF16, "nt_attn")
    attn_tok = [Tok(f"attn{hp}") for hp in range(8)]
    wst = Ring(P, 2, [128, 8, 128], F32, "nt_wst")
    wq_r = Ring(P, 2, [128, 8, 128], BF16, "nt_wq")
    wk_r = Ring(P, 2, [128, 8, 128], BF16, "nt_wk")
    wv_r = Ring(P, 2, [128, 8, 128], BF16, "nt_wv")
    bst = Ring(P, 1, [128, 2 * NE * 128], F32, "nt_bst")
    bbf = Ring(P, 2, [128, 2 * NE * 128], BF16, "nt_bbf")
    q_r = Ring(P, 1, [128, 2, T], BF16, "nt_q")
    k_r = Ring(P, 1, [128, NT], BF16, "nt_k")
    v_r = Ring(P, 1, [128, 2, NT // 128, 128], BF16, "nt_v")
    for (vb_, vbtok_) in v_r.bufs:
        P.add("pool", lambda e, o=vb_[:]: e.memset(o, 0.0), [], [vbtok_])
    onesz = P.sb([128, 2, 128], BF16, "nt_onesz")
    P.add("pool", lambda e: e.memset(onesz[:], 0.0), [], [ctok])
    P.add("pool", lambda e: e.memset(onesz[:, 0, 0:64], 1.0), [], [ctok])
    P.add("pool", lambda e: e.memset(onesz[:, 1, 64:128], 1.0), [], [ctok])
    p_r = Ring(P, 3, [128, 6 * 128], BF16, "nt_p")
    ps_sc = Ring(P, 2, [128, 1024], F32, "ps_sc", psum=True)
    ps_pv = Ring(P, 2, [128, 512], F32, "ps_pv", psum=True)
    w_v = w_qkv.rearrange("(kc p) n -> p kc n", p=128)
    allh = lambda ti: [h_tok[ti][c] for c in range(8)]

    for hp in range(8):
        wts = []
        for which, ring in enumerate((wq_r, wk_r, wv_r)):
            st, sttok = wst.next()
            P.dma("sp", st[:], w_v[:, :, which * D + hp * 128:which * D + (hp + 1) * 128], [], [sttok], sttok)
            wb, wbtok = ring.next()
            P.add("pool", lambda e, o=wb[:], i=st[:]: e.tensor_copy(out=o, in_=i), [sttok], [wbtok])
            wts.append((wb, wbtok))
        (wq, wqtok), (wk, wktok), (wv, wvtok) = wts
        bs, bstok = bst.next()
        P.dma("sp", bs[:], bias[hp], [], [bstok], bstok)
        bb, bbtok = bbf.next()
        P.add("pool", lambda e, o=bb[:], i=bs[:]: e.tensor_copy(out=o, in_=i), [bstok], [bbtok])

        q_sb, qtok = q_r.next()
        k_sb, ktok = k_r.next()
        v_sb, vtok = v_r.next()
        for (dst, dtok, wmat, wtok, gsb, gt, tl) in (
                (q_sb, qtok, wq, wqtok, qg_sb, qgtok, [(HN + i * 512, i * 512) for i in range(4)]),
                (k_sb, ktok, wk, wktok, kg_sb, kgtok, [(i * 512, i * 512) for i in range(5)])):
            for (hs, ds) in tl:
                ti = hs // 512
                pr, prtok = ps_pr.next()
                for kc in range(8):
                    mm(P, pr[:], wmat[:, kc, :], h_all[:, kc, hs:hs + 512], kc == 0, kc == 7,
                       [wtok] + [h_tok[i][kc] for i in range(len(tiles)) if i * 512 < hs + 512 and (i + 1) * 512 > hs], [prtok])
                sq, sqtok = fr.next()
                P.add("act", lambda e, o=sq[:], i=pr[:]: e.activation(out=o, in_=i, func=AF.Square), [prtok], [sqtok])
                pq, pqtok = ps_pr.next()
                mm(P, pq[:], bd_f[:], sq[:], True, True, [sqtok, bdtok], [pqtok])
                rs, rstok = fr.next()
                P.add("act", lambda e, o=rs[:], i=pq[:]: e.activation(out=o, in_=i, func=AF.Sqrt,
                                                                      bias=C["eps_col"][:, 0:1], scale=1.0 / 64),
                      [pqtok, ctok], [rstok])
                P.add("dve", lambda e, o=rs[:]: e.reciprocal(out=o, in_=o), [rstok], [rstok])
                if dst is q_sb:
                    for hh_ in range(2):
                        P.add("dve", lambda e, o=dst[:, hh_, ds:ds + 512], i=pr[:], g=gsb[:, hh_:hh_ + 1], r=rs[:]:
                              e.scalar_tensor_tensor(out=o, in0=i, scalar=g, in1=r, op0=ALU.mult, op1=ALU.mult),
                              [prtok, rstok, gt], [dtok])
                else:
                    P.add("dve", lambda e, o=dst[:, ds:ds + 512], i=pr[:], g=gsb[:, 0:1], r=rs[:]:
                          e.scalar_tensor_tensor(out=o, in0=i, scalar=g, in1=r, op0=ALU.mult, op1=ALU.mult),
                          [prtok, rstok, gt], [dtok])
        for blk in range(NT // 128):
            pr, prtok = ps_pr.next()
            for kc in range(8):
                mm(P, pr[:, 0:128], h_all[:, kc, blk * 128:(blk + 1) * 128], wv[:, kc, :], kc == 0, kc == 7,
                   [wvtok, h_tok[blk // 4][kc]], [prtok])
            for hh_ in range(2):
                P.add("act", lambda e, o=v_sb[:, hh_, blk, 64 * hh_:64 * hh_ + 64], i=pr[:, 64 * hh_:64 * hh_ + 64]:
                      e.activation(out=o, in_=i, func=AF.Identity), [prtok], [vtok])
        for qp in range(16):
            es = nat_es(qp)
            ne = len(es)
            pix = nat_pidx(qp)
            pts = []
            for hh in range(2):
                sc, sctok = ps_sc.next()
                for idx, e_ in enumerate(es):
                    kb = qp + e_
                    mm(P, sc[:, idx * 128:(idx + 1) * 128], k_sb[:, kb * 128:(kb + 1) * 128],
                       q_sb[:, hh, qp * 128:(qp + 1) * 128], idx % 4 == 0, False, [ktok, qtok], [sctok], skip=True)
                boff = (hh * NE + es[0] + 1) * 128
                poff = (pix * NE + es[0] + 1) * 128
                for (c0, c1) in ((0, 512), (512, ne * 128)):
                    mm(P, sc[:, c0:c1], ident[:], bb[:, boff + c0:boff + c1], False, False, [bbtok, ctok], [sctok], skip=True)
                    mm(P, sc[:, c0:c1], ohk_bf[:], pen_bf[:, poff + c0:poff + c1], False, True, [ohktok, pentok], [sctok], skip=True)
                pt, pttok = p_r.next()
                for (c0, c1) in ((0, 512), (512, ne * 128)):
                    P.add("act", lambda e, o=pt[:, c0:c1], i=sc[:, c0:c1]: e.activation(out=o, in_=i, func=AF.Exp),
                          [sctok], [pttok])
                pts.append((pt, pttok))
            pv, pvtok = ps_pv.next()
            n_mm = 2 * ne
            cnt = 0
            for hh in range(2):
                pt, pttok = pts[hh]
                for idx, e_ in enumerate(es):
                    kb = qp + e_
                    mm(P, pv[:, 0:128], v_sb[:, hh, kb, :], pt[:, idx * 128:(idx + 1) * 128], cnt == 0, cnt == n_mm - 1,
                       [vtok, pttok], [pvtok])
                    cnt += 1
            cnt = 0
            for hh in range(2):
                pt, pttok = pts[hh]
                for idx, e_ in enumerate(es):
                    mm(P, pv[:, 128:256], onesz[:, hh, :], pt[:, idx * 128:(idx + 1) * 128], cnt == 0, cnt == n_mm - 1,
                       [pttok, ctok], [pvtok])
                    cnt += 1
            if C.get("dbg") is not None and hp == 0 and qp == 2:
                dbg_dump(P, C, 0, pts[0][0][:, 0:128], [pts[0][1]])
                dbg_dump(P, C, 1, pts[0][0][:, 128:256], [pts[0][1]])
                dbg_dump(P, C, 2, q_sb[:, 0, 256:384], [qtok])
                dbg_dump(P, C, 3, q_sb[:, 1, 256:384], [qtok])
                dbg_dump(P, C, 4, k_sb[:, 256:384], [ktok])
                dbg_dump(P, C, 5, k_sb[:, 384:512], [ktok])
                dbg_dump(P, C, 6, v_sb[:, 0, 2, :], [vtok])
                dbg_dump(P, C, 7, v_sb[:, 1, 2, :], [vtok])
            rd, rdtok = fr.next()
            P.add("dve", lambda e, o=rd[:, 0:128], i=pv[:, 128:256]: e.reciprocal(out=o, in_=i), [pvtok], [rdtok])
            P.add("dve", lambda e, o=attn_all[:, hp, qp * 128:(qp + 1) * 128], a=pv[:, 0:128], b=rd[:, 0:128]:
                  e.tensor_tensor(out=o, in0=a, in1=b, op=ALU.mult), [pvtok, rdtok], [attn_tok[hp]])
            if C.get("dbg") is not None and hp == 0 and qp == 2:
                dbg_dump(P, C, 8, attn_all[:, 0, 256:384], [attn_tok[0]])
                dbg_dump(P, C, 9, rd[:, 0:128], [rdtok])

    wo_st = Ring(P, 2, [128, 8, 128], F32, "nt_wost")
    wo_r = Ring(P, 2, [128, 8, 128], BF16, "nt_wo")
    w_out_v = w_out.rearrange("(kc p) n -> p kc n", p=128)
    for o in range(8):
        st, sttok = wo_st.next()
        P.dma("sp", st[:], w_out_v[:, :, o * 128:(o + 1) * 128], [], [sttok], sttok)
        wb, wbtok = wo_r.next()
        P.add("pool", lambda e, oo=wb[:], i=st[:]: e.tensor_copy(out=oo, in_=i), [sttok], [wbtok])
        for tt in range(4):
            t0 = tt * 512
            pt, pttok = ps_pr.next()
            for kc in range(8):
                mm(P, pt[:], wb[:, kc, :], attn_all[:, kc, t0:t0 + 512], kc == 0, kc == 7, [wbtok, attn_tok[kc]], [pttok])
            xt, xtok = fr.next()
            P.dma("sp", xt[:], x_in[o * 128:(o + 1) * 128, HN + t0:HN + t0 + 512], [], [xtok], xtok)
            P.add("dve", lambda e, oo=xt[:], a=pt[:]: e.tensor_tensor(out=oo, in0=a, in1=oo, op=ALU.add),
                  [pttok, xtok], [xtok])
            P.dma("sp", x_out[o * 128:(o + 1) * 128, t0:t0 + 512], xt[:], [xtok], [], xtok, is_out=is_out)


NB = 48
RB = NB * 128
RH = 4
LN16 = -2.772588722239781


def emit_ret(P, C, nc, x_in, x_out, g1c, w_in, cosT, sinT, l2d, gng, w_out, is_out=False):
    fr = C["fr"]
    ctok = C["ctok"]
    g_col = P.sb([128, 8], F32, "rt_g")
    gtok = load_cols(P, C, g_col, g1c)
    gng_sb = P.sb([128, 16], F32, "rt_gng")
    gngtok = load_cols(P, C, gng_sb, gng)
    ones_f = P.sb([128, 128], F32, "rt_ones_f")
    P.add("pool", lambda e: e.memset(ones_f[:], 1.0), [], [ctok])
    lg = P.sb([128, 8], F32, "rt_lg")
    nlg = P.sb([128, 8], F32, "rt_nlg")
    one_col = P.sb([128, 1], F32, "rt_one")
    ln16_col = P.sb([128, 1], F32, "rt_ln16")
    P.add("pool", lambda e: e.memset(one_col[:], 1.0), [], [ctok])
    P.add("pool", lambda e: e.memset(ln16_col[:], LN16), [], [ctok])
    lgtok = load_cols(P, C, lg, l2d)
    P.add("act", lambda e: e.activation(out=lg[:], in_=lg[:], func=AF.Exp, scale=-0.6931471805599453), [lgtok], [lgtok])
    P.add("act", lambda e: e.activation(out=lg[:], in_=lg[:], func=AF.Ln, bias=one_col[:, 0:1], scale=-1.0),
          [lgtok, ctok], [lgtok])
    P.add("dve", lambda e: e.tensor_scalar(out=nlg[:], in0=lg[:], scalar1=-1.0, scalar2=None, op0=ALU.mult),
          [lgtok], [lgtok])
    d1i = P.sb([128, 128], mybir.dt.int32, "rt_d1i")
    d1 = P.sb([128, 128], F32, "rt_d1")
    dbi = P.sb([128, NB], mybir.dt.int32, "rt_dbi")
    dbf = P.sb([128, NB], F32, "rt_dbf")
    itok = Tok("iota")
    P.add("pool", lambda e: e.iota(d1i[:], pattern=[[1, 128]], base=0, channel_multiplier=-1), [], [itok])
    P.add("pool", lambda e: e.iota(dbi[:], pattern=[[128, NB]], base=0, channel_multiplier=0), [], [itok])
    P.add("dve", lambda e: e.tensor_copy(out=d1[:], in_=d1i[:]), [itok], [itok])
    P.add("dve", lambda e: e.tensor_copy(out=dbf[:], in_=dbi[:]), [itok], [itok])

    h_dram = nc.dram_tensor("rt_h_dram", [D, RB], BF16, kind="Internal").ap()
    gT_dram = nc.dram_tensor("rt_gT_dram", [2 * D, T], BF16, kind="Internal").ap()
    h_dv = h_dram.rearrange("(c p) t -> p c t", p=128)
    gT_dv = gT_dram.rearrange("(c p) t -> p c t", p=128)
    bigr = Ring(P, 2, [128, 16, 512], BF16, "rt_big")
    ps_a = Ring(P, 2, [128, 512], F32, "ps_a", psum=True)
    ps_s = Ring(P, 2, [128, 512], F32, "ps_s", psum=True)
    ps_o = Ring(P, 2, [128, 512], F32, "ps_o", psum=True)
    ntile = RB // 512
    hd_tok = [Tok(f"hd{i}") for i in range(ntile)]
    for ti in range(ntile):
        hb, hbtok = bigr.next()
        emit_rmsnorm(P, C, x_in, g_col, gtok, hb, [[hbtok] * 8], ps_a, [(ti * 512, 512)], two_pass=True, hcol=[0])
        P.dma("sp", h_dv[:, :, ti * 512:(ti + 1) * 512], hb[:, 0:8, :], [hbtok], [hd_tok[ti]], hbtok)

    k_fm = P.sb([128, 2, RB], BF16, "rt_k")
    v_tok = P.sb([128, NB, 512], BF16, "rt_v")
    q_fm = P.sb([128, 2, T], BF16, "rt_q")
    o_fm = P.sb([128, 4, T], F32, "rt_o")
    ktok, vtok, qtok = Tok("k"), Tok("v"), Tok("q")
    otok = [Tok(f"o{i}") for i in range(16)]
    wq = P.sb([128, 8, 256], BF16, "rt_wq")
    wk = P.sb([128, 8, 256], BF16, "rt_wk")
    wv = P.sb([128, 8, 512], BF16, "rt_wv")
    wg = P.sb([128, 8, 512], BF16, "rt_wg")
    wtok = Tok("w")
    wqtok, wgtok = Tok("wqkv"), Tok("wg")
    wst = Ring(P, 2, [128, 8, 128], F32, "rt_wst")
    w_v = w_in.rearrange("(kc p) n -> p kc n", p=128)
    gf = P.sb([128, 128], F32, "rt_gf")
    gb = P.sb([128, 128], F32, "rt_gb")
    gd = P.sb([128, 128], F32, "rt_gd")
    gd2 = P.sb([128, 128], F32, "rt_gd2")
    sf = P.sb([128, NB], F32, "rt_sf")
    sbk = P.sb([128, NB], F32, "rt_sb")
    gtk = Tok("G")
    p_r = Ring(P, 10, [128, 128], BF16, "rt_p")
    gT_tok = [[Tok(f"gT{h}_{t}") for t in range(4)] for h in range(RH)]

    for h in range(RH):
        def load_w(hh_, which):
            segs = []
            if which == "qkv":
                segs += [(wq, i_ * 128, hh_ * 256 + i_ * 128, wqtok) for i_ in range(2)]
                segs += [(wk, i_ * 128, D + hh_ * 256 + i_ * 128, wqtok) for i_ in range(2)]
                segs += [(wv, i_ * 128, 2 * D + hh_ * 512 + i_ * 128, wqtok) for i_ in range(4)]
            else:
                segs += [(wg, i_ * 128, 4 * D + hh_ * 512 + i_ * 128, wgtok) for i_ in range(4)]
            for (dst, dcol, scol, tk) in segs:
                st, sttok = wst.next()
                P.dma("sp", st[:], w_v[:, :, scol:scol + 128], [], [sttok], sttok)
                P.add("pool", lambda e, o=dst[:, :, dcol:dcol + 128], i=st[:]: e.tensor_copy(out=o, in_=i), [sttok], [tk])

        if h == 0:
            load_w(0, "qkv")
        load_w(h, "g")
        P.add("act", lambda e, sc_=lg[:, h:h + 1]: e.activation(out=gf[:], in_=d1[:], func=AF.Exp, bias=ln16_col[:, 0:1], scale=sc_),
              [itok, lgtok, ctok], [gtk])
        P.add("act", lambda e, sc_=nlg[:, 4 + h:5 + h]: e.activation(out=gb[:], in_=d1[:], func=AF.Exp, bias=ln16_col[:, 0:1], scale=sc_),
              [itok, lgtok, ctok], [gtk])
        P.add("pool", lambda e: e.affine_select(out=gd[:], in_=gf[:], pattern=[[1, 128]], compare_op=ALU.is_ge, fill=0.0,
                                                base=0, channel_multiplier=-1), [gtk], [gtk])
        P.add("pool", lambda e: e.affine_select(out=gd2[:], in_=gb[:], pattern=[[-1, 128]], compare_op=ALU.is_gt, fill=0.0,
                                                base=0, channel_multiplier=1), [gtk], [gtk])
        P.add("pool", lambda e: e.tensor_tensor(out=gd[:], in0=gd[:], in1=gd2[:], op=ALU.add), [gtk], [gtk])
        P.add("act", lambda e, sc_=lg[:, h:h + 1]: e.activation(out=sf[:], in_=dbf[:], func=AF.Exp, scale=sc_), [itok, lgtok], [gtk])
        P.add("act", lambda e, sc_=lg[:, 4 + h:5 + h]: e.activation(out=sbk[:], in_=dbf[:], func=AF.Exp, scale=sc_), [itok, lgtok], [gtk])

        for ti in range(ntile):
            hb, hbtok = bigr.next()
            P.dma("sp", hb[:, 0:8, :], h_dv[:, :, ti * 512:(ti + 1) * 512], [hd_tok[ti]], [hbtok], hbtok)
            cs, cstok = fr.next()
            sn, sntok = fr.next()
            P.dma("sp", cs[:], cosT[:, ti * 512:(ti + 1) * 512], [], [cstok], cstok)
            P.dma("sp", sn[:], sinT[:, ti * 512:(ti + 1) * 512], [], [sntok], sntok)
            todo = [(wk, k_fm, ktok, ti * 512)]
            if 4 <= ti < 8:
                todo.append((wq, q_fm, qtok, (ti - 4) * 512))
            for (wmat, dst, dtok, dcol) in todo:
                p1, p1tok = ps_a.next()
                p2, p2tok = ps_a.next()
                for dc, (pt, pttok) in enumerate(((p1, p1tok), (p2, p2tok))):
                    for kc in range(8):
                        mm(P, pt[:], wmat[:, kc, dc * 128:(dc + 1) * 128], hb[:, kc, :], kc == 0, kc == 7, [wqtok, hbtok], [pttok])
                t1, t1tok = fr.next()
                t2, t2tok = fr.next()
                P.add("dve", lambda e, o=t1[:], a=p1[:], b=cs[:]: e.tensor_tensor(out=o, in0=a, in1=b, op=ALU.mult), [p1tok, cstok], [t1tok])
                P.add("dve", lambda e, o=t2[:], a=p2[:], b=sn[:]: e.tensor_tensor(out=o, in0=a, in1=b, op=ALU.mult), [p2tok, sntok], [t2tok])
                P.add("pool", lambda e, o=dst[:, 0, dcol:dcol + 512], a=t1[:], b=t2[:]: e.tensor_tensor(out=o, in0=a, in1=b, op=ALU.subtract),
                      [t1tok, t2tok], [dtok])
                P.add("dve", lambda e, o=t1[:], a=p1[:], b=sn[:]: e.tensor_tensor(out=o, in0=a, in1=b, op=ALU.mult), [p1tok, sntok], [t1tok])
                P.add("dve", lambda e, o=t2[:], a=p2[:], b=cs[:]: e.tensor_tensor(out=o, in0=a, in1=b, op=ALU.mult), [p2tok, cstok], [t2tok])
                P.add("pool", lambda e, o=dst[:, 1, dcol:dcol + 512], a=t1[:], b=t2[:]: e.tensor_tensor(out=o, in0=a, in1=b, op=ALU.add),
                      [t1tok, t2tok], [dtok])
            for bl in range(4):
                blk = ti * 4 + bl
                pv, pvtok = ps_a.next()
                for kc in range(8):
                    mm(P, pv[:], hb[:, kc, bl * 128:(bl + 1) * 128], wv[:, kc, :], kc == 0, kc == 7, [wqtok, hbtok], [pvtok])
                P.add("act", lambda e, o=v_tok[:, blk, :], i=pv[:]: e.activation(out=o, in_=i, func=AF.Identity), [pvtok], [vtok])

        if h + 1 < RH:
            load_w(h + 1, "qkv")
        NG = NB // 4

        def scores(i, cg):
            sc, sctok = ps_s.next()
            for sub in range(4):
                c = cg * 4 + sub
                for dc in range(2):
                    mm(P, sc[:, sub * 128:(sub + 1) * 128], k_fm[:, dc, c * 128:(c + 1) * 128], q_fm[:, dc, i * 128:(i + 1) * 128],
                       sub == 0 and dc == 0, dc == 1, [ktok, qtok], [sctok], skip=True)
            return sc, sctok

        seq = [(i, cg) for i in range(16) for cg in range(NG)]
        cur = scores(*seq[0])
        po, potok = None, None
        for si_, (i, cg) in enumerate(seq):
            sc, sctok = cur
            if cg == 0:
                po, potok = ps_o.next()
            pts = []
            for sub in range(4):
                c = cg * 4 + sub
                dl = 16 + i - c
                pt, pttok = p_r.next()
                if dl > 0:
                    P.add("dve", lambda e, o=pt[:], a=sc[:, sub * 128:(sub + 1) * 128], s_=sf[:, dl:dl + 1]:
                          e.scalar_tensor_tensor(out=o, in0=a, scalar=s_, in1=gf[:], op0=ALU.mult, op1=ALU.mult), [sctok, gtk], [pttok])
                elif dl < 0:
                    P.add("dve", lambda e, o=pt[:], a=sc[:, sub * 128:(sub + 1) * 128], s_=sbk[:, -dl:-dl + 1]:
                          e.scalar_tensor_tensor(out=o, in0=a, scalar=s_, in1=gb[:], op0=ALU.mult, op1=ALU.mult), [sctok, gtk], [pttok])
                else:
                    P.add("dve", lambda e, o=pt[:], a=sc[:, sub * 128:(sub + 1) * 128]:
                          e.tensor_tensor(out=o, in0=a, in1=gd[:], op=ALU.mult), [sctok, gtk], [pttok])
                pts.append((pt, pttok, c))
            if si_ + 1 < len(seq):
                cur = scores(*seq[si_ + 1])
            for (pt, pttok, c) in pts:
                for ec in range(4):
                    mm(P, po[:, ec * 128:(ec + 1) * 128], v_tok[:, c, ec * 128:(ec + 1) * 128], pt[:],
                       c == 0 and ec == 0, c == NB - 1, [vtok, pttok], [potok], skip=True)
            if cg == NG - 1:
                P.add("act", lambda e, o=o_fm[:, :, i * 128:(i + 1) * 128], a=po[:].rearrange("p (a b) -> p a b", a=4):
                      e.activation(out=o, in_=a, func=AF.Identity), [potok], [otok[i]])

        for tt in range(4):
            t0 = tt * 512
            ots = [otok[tt * 4 + b_] for b_ in range(4)]
            p1, p1tok = ps_a.next()
            p2, p2tok = ps_a.next()
            for ec in range(4):
                mm(P, p1[:], ones_f[:], o_fm[:, ec, t0:t0 + 512], ec == 0, ec == 3, ots + [ctok], [p1tok])
            for ec in range(4):
                sq, sqtok = fr.next()
                P.add("act", lambda e, o=sq[:], a=o_fm[:, ec, t0:t0 + 512]: e.activation(out=o, in_=a, func=AF.Square), ots, [sqtok])
                mm(P, p2[:], ones_f[:], sq[:], ec == 0, ec == 3, [sqtok, ctok], [p2tok])
            mu, mutok = C["rsr"].next()
            P.add("act", lambda e, o=mu[:], a=p1[:]: e.activation(out=o, in_=a, func=AF.Identity, scale=1.0 / 512), [p1tok], [mutok])
            rs, rstok = C["rsr"].next()
            P.add("dve", lambda e, o=rs[:], a=mu[:]: e.tensor_tensor(out=o, in0=a, in1=a, op=ALU.mult), [mutok], [rstok])
            P.add("dve", lambda e, o=rs[:], a=p2[:]: e.scalar_tensor_tensor(out=o, in0=a, scalar=1.0 / 512, in1=o, op0=ALU.mult, op1=ALU.subtract),
                  [p2tok, rstok], [rstok])
            P.add("act", lambda e, o=rs[:]: e.activation(out=o, in_=o, func=AF.Sqrt, bias=C["eps_col"][:, 0:1], scale=1.0), [rstok, ctok], [rstok])
            P.add("dve", lambda e, o=rs[:]: e.reciprocal(out=o, in_=o), [rstok], [rstok])
            hb, hbtok = bigr.next()
            P.dma("sp", hb[:, 0:8, :], h_dv[:, :, 2048 + t0:2048 + t0 + 512], [hd_tok[4 + tt]], [hbtok], hbtok)
            gt_sb, gttok = bigr.next()
            for ec in range(4):
                pg, pgtok = ps_a.next()
                for kc in range(8):
                    mm(P, pg[:], wg[:, kc, ec * 128:(ec + 1) * 128], hb[:, kc, :], kc == 0, kc == 7, [wgtok, hbtok], [pgtok])
                sg, sgtok = fr.next()
                P.add("act", lambda e, o=sg[:], a=pg[:]: e.activation(out=o, in_=a, func=AF.Silu), [pgtok], [sgtok])
                dd, ddtok = fr.next()
                P.add("pool", lambda e, o=dd[:], a=o_fm[:, ec, t0:t0 + 512], m=mu[:]: e.tensor_tensor(out=o, in0=a, in1=m, op=ALU.subtract),
                      ots + [mutok], [ddtok])
                P.add("dve", lambda e, o=dd[:], r=rs[:]: e.tensor_tensor(out=o, in0=o, in1=r, op=ALU.mult), [ddtok, rstok], [ddtok])
                P.add("dve", lambda e, o=gt_sb[:, ec, :], a=dd[:], g_=gng_sb[:, h * 4 + ec:h * 4 + ec + 1], s_=sg[:]:
                      e.scalar_tensor_tensor(out=o, in0=a, scalar=g_, in1=s_, op0=ALU.mult, op1=ALU.mult), [ddtok, sgtok, gngtok], [gttok])
            P.dma("sp", gT_dv[:, h * 4:(h + 1) * 4, t0:t0 + 512], gt_sb[:, 0:4, :], [gttok], [gT_tok[h][tt]], gttok)

    wo_bufs = [wq[:].rearrange("p a b -> p (a b)").rearrange("p (j n) -> p j n", j=16),
               wk[:].rearrange("p a b -> p (a b)").rearrange("p (j n) -> p j n", j=16)]
    w_out_v = w_out.rearrange("(j p) n -> p j n", p=128)
    for o in range(8):
        wb, wbtok = wo_bufs[o % 2], wqtok
        for half in range(2):
            st, sttok = wst.next()
            P.dma("sp", st[:], w_out_v[:, half * 8:(half + 1) * 8, o * 128:(o + 1) * 128], [], [sttok], sttok)
            P.add("pool", lambda e, oo=wb[:, half * 8:(half + 1) * 8, :], i=st[:]: e.tensor_copy(out=oo, in_=i), [sttok], [wbtok])
        for tt in range(4):
            t0 = tt * 512
            gb_, gbtok = bigr.next()
            P.dma("sp", gb_[:], gT_dv[:, :, t0:t0 + 512], [gT_tok[h_][tt] for h_ in range(RH)], [gbtok], gbtok)
            pt, pttok = ps_a.next()
            for j in range(16):
                mm(P, pt[:], wb[:, j, :], gb_[:, j, :], j == 0, j == 15, [wbtok, gbtok], [pttok])
            xt, xtok = fr.next()
            P.dma("sp", xt[:], x_in[o * 128:(o + 1) * 128, 2048 + t0:2048 + t0 + 512], [], [xtok], xtok)
            P.add("dve", lambda e, oo=xt[:], a=pt[:]: e.tensor_tensor(out=oo, in0=a, in1=oo, op=ALU.add), [pttok, xtok], [xtok])
            P.dma("sp", x_out[o * 128:(o + 1) * 128, t0:t0 + 512], xt[:], [xtok], [], xtok, is_out=is_out)


def build_ffn_prog():
    nc = bass.Bass("TRN2", target_bir_lowering=False)
    x_in = nc.dram_tensor("x_in", [D, T + 2], F32, kind="ExternalInput").ap()
    g2c = nc.dram_tensor("g2c", [128, 8], F32, kind="ExternalInput").ap()
    w_up = nc.dram_tensor("w_up", [D, 2 * FFN], F32, kind="ExternalInput").ap()
    dww = nc.dram_tensor("dww", [128, 3 * 2 * NH], F32, kind="ExternalInput").ap()
    dwb = nc.dram_tensor("dwb", [128, 2 * NH], F32, kind="ExternalInput").ap()
    w_down = nc.dram_tensor("w_down", [FFN, D], F32, kind="ExternalInput").ap()
    x_out = nc.dram_tensor("x_out", [D, T], F32, kind="ExternalOutput").ap()
    with contextlib.ExitStack() as stack:
        P = Prog(nc, stack)
        C = make_common(P)
        emit_ffn(P, C, x_in, x_out, g2c, w_up, dww, dwb, w_down, is_out=True)
        P.emit()
        print("ffn prog stats", P.stats)
    return nc


def cols(v, n):
    return np.ascontiguousarray(v.reshape(n, 128).T)


def shard_tokens_fm(xfull, halo):
    out = []
    for c in range(NCORES):
        b, hf = c // 2, c % 2
        lo, hi = hf * T - halo, (hf + 1) * T + halo
        buf = np.zeros((T + 2 * halo, D), np.float32)
        slo, shi = max(lo, 0), min(hi, SEQ)
        buf[slo - lo:shi - lo] = xfull[b, slo:shi]
        out.append(np.ascontiguousarray(buf.T))
    return out


def unshard_tokens_fm(outs):
    x = np.empty((BATCH, SEQ, D), np.float32)
    for c in range(NCORES):
        b, hf = c // 2, c % 2
        x[b, hf * T:(hf + 1) * T] = outs[c].T
    return x


def ffn_inmaps(x, i, norm2_g, ffn_w_up, ffn_dw_w, ffn_dw_b, ffn_w_down):
    xs = shard_tokens_fm(x, 1) if x is not None else None
    dww = np.concatenate([cols(ffn_dw_w[i, k], 2 * NH) for k in range(3)], axis=1)
    common = {
        "g2c": cols(norm2_g[i], 8),
        "w_up": np.ascontiguousarray(ffn_w_up[i]),
        "dww": np.ascontiguousarray(dww),
        "dwb": cols(ffn_dw_b[i], 2 * NH),
        "w_down": np.ascontiguousarray(ffn_w_down[i]),
    }
    return [dict(common, x_in=xs[c]) if xs is not None else dict(common) for c in range(NCORES)]


def build_conv_prog():
    nc = bass.Bass("TRN2", target_bir_lowering=False)
    dt = lambda name, shape, kind="ExternalInput": nc.dram_tensor(name, shape, F32, kind=kind).ap()
    x_in = dt("x_in", [D, T + 2 * HC])
    mask = dt("mask", [128, 2 * HC])
    g1c = dt("g1c", [128, 8])
    w_in = dt("w_in", [D, 2 * D])
    b_in = dt("b_in", [128, 16])
    dw_w = dt("dw_w", [128, CW * 8])
    dw_b = dt("dw_b", [128, 8])
    ln_g = dt("ln_g", [128, 8])
    ln_b = dt("ln_b", [128, 8])
    w_out = dt("w_out", [D, D])
    x_out = dt("x_out", [D, T], "ExternalOutput")
    with contextlib.ExitStack() as stack:
        P = Prog(nc, stack)
        C = make_common(P, nfr=12)
        emit_conv(P, C, x_in, x_out, mask, g1c, w_in, b_in, dw_w, dw_b, ln_g, ln_b, w_out, is_out=True)
        P.emit()
        print("conv prog stats", P.stats)
    return nc


def conv_inmaps(x, j, g1, conv_w_in, conv_b_in, conv_dw_w, conv_dw_b, conv_ln_g, conv_ln_b, conv_w_out):
    xs = shard_tokens_fm(x, HC) if x is not None else None
    dww = np.concatenate([cols(conv_dw_w[j, k], 8) for k in range(CW)], axis=1)
    common = {
        "g1c": cols(g1, 8),
        "w_in": np.ascontiguousarray(conv_w_in[j]),
        "b_in": cols(conv_b_in[j], 16),
        "dw_w": np.ascontiguousarray(dww),
        "dw_b": cols(conv_dw_b[j], 8),
        "ln_g": cols(conv_ln_g[j], 8),
        "ln_b": cols(conv_ln_b[j], 8),
        "w_out": np.ascontiguousarray(conv_w_out[j]),
    }
    maps = []
    for c in range(NCORES):
        hf = c % 2
        m = np.ones((128, 2 * HC), np.float32)
        if hf == 0:
            m[:, :HC] = 0.0
        else:
            m[:, HC:] = 0.0
        maps.append(dict(common, x_in=xs[c], mask=m) if xs is not None else dict(common, mask=m))
    return maps


def dbg_dump(P, C, slot, src, toks):
    t, ttok = C["dbgr"].next()
    P.add("act", lambda e: e.activation(out=t[:], in_=src, func=AF.Identity), toks, [ttok])
    P.dma("sp", C["dbg"][:, slot * 128:(slot + 1) * 128], t[:], [ttok], [], ttok, is_out=True)


def build_nat_prog(debug=False):
    nc = bass.Bass("TRN2", target_bir_lowering=False)
    dt = lambda name, shape, kind="ExternalInput": nc.dram_tensor(name, shape, F32, kind=kind).ap()
    x_in = dt("x_in", [D, T + 2 * HN])
    g1c = dt("g1c", [128, 8])
    w_qkv = dt("w_qkv", [D, 3 * D])
    qg = dt("qg", [128, 2])
    kg = dt("kg", [128, 1])
    bias = dt("bias", [8, 128, 2 * NE * 128])
    pen = dt("pen", [2, 5 * NE * 128])
    ohk = dt("ohk", [2, 128])
    bd = dt("bd", [128, 128])
    w_out = dt("w_out", [D, D])
    x_out = dt("x_out", [D, T], "ExternalOutput")
    with contextlib.ExitStack() as stack:
        P = Prog(nc, stack)
        C = make_common(P, nfr=6)
        if debug:
            C["dbg"] = dt("dbg", [128, 16 * 128], "ExternalOutput")
            C["dbgr"] = Ring(P, 2, [128, 128], F32, "dbgr")
        emit_nat(P, C, x_in, x_out, g1c, w_qkv, qg, kg, bias, pen, ohk, bd, w_out, is_out=True)
        P.emit()
        print("nat prog stats", P.stats)
    return nc


def nat_bias_table(rpb):
    kc = np.arange(64)[:, None]
    qc = np.arange(64)[None, :]
    cs = np.clip(qc - 8, 0, 48)
    win = (kc >= cs) & (kc < cs + 16)
    dc = np.clip(kc - qc + 15, 0, 30)
    out = np.full((8, 128, 2, NE, 128), NEG, np.float32)
    for hp in range(8):
        for hh in range(2):
            h = 2 * hp + hh
            for ei in range(NE):
                e_ = ei - 1
                for kp in range(2):
                    for qp_ in range(2):
                        dr = 2 * e_ + 3 + kp - qp_
                        if dr < 0 or dr > 14:
                            continue
                        blk = np.where(win, rpb[h, dr][dc], np.float32(NEG))
                        out[hp, kp * 64:(kp + 1) * 64, hh, ei, qp_ * 64:(qp_ + 1) * 64] = blk
    return out.reshape(8, 128, 2 * NE * 128)


def nat_pen_table(hf):
    out = np.full((2, 5, NE, 128), NEG, np.float32)
    for pix, qp in enumerate((0, 1, 7, 14, 15)):
        for ei in range(NE):
            e_ = ei - 1
            for kp in range(2):
                for qp_ in range(2):
                    r = 32 * hf + 2 * qp + qp_
                    kr = 32 * hf + 2 * qp + 2 * e_ - 4 + kp
                    rs = min(max(r - 4, 0), 56)
                    if 0 <= kr < 64 and rs <= kr < rs + 8:
                        out[kp, pix, ei, qp_ * 64:(qp_ + 1) * 64] = 0.0
    return out.reshape(2, 5 * NE * 128)


def nat_qg2(g):
    out = np.zeros((128, 2), np.float32)
    out[0:64, 0] = g
    out[64:128, 1] = g
    return out


def nat_inmaps(x, g1, nat_w_qkv, nat_q_norm_g, nat_k_norm_g, nat_rpb, nat_w_out):
    xs = shard_tokens_fm(x, HN) if x is not None else None
    ohk = np.zeros((2, 128), np.float32)
    ohk[0, :64] = 1.0
    ohk[1, 64:] = 1.0
    common = {
        "g1c": cols(g1, 8),
        "w_qkv": np.ascontiguousarray(nat_w_qkv[0]),
        "qg": nat_qg2(nat_q_norm_g[0]),
        "kg": np.ascontiguousarray(np.tile(nat_k_norm_g[0], 2)[:, None]),
        "bias": nat_bias_table(nat_rpb[0]),
        "ohk": ohk,
        "bd": np.kron(np.eye(2, dtype=np.float32), np.ones((64, 64), np.float32)),
        "w_out": np.ascontiguousarray(nat_w_out[0]),
    }
    pens = [nat_pen_table(0), nat_pen_table(1)]
    return [dict(common, x_in=xs[c], pen=pens[c % 2]) if xs is not None else dict(common, pen=pens[c % 2]) for c in range(NCORES)]


def build_ret_prog():
    nc = bass.Bass("TRN2", target_bir_lowering=False)
    dt = lambda name, shape, kind="ExternalInput": nc.dram_tensor(name, shape, F32, kind=kind).ap()
    x_in = dt("x_in", [D, RB])
    g1c = dt("g1c", [128, 8])
    w_in = dt("w_in", [D, 6 * D])
    cosT = dt("cosT", [128, RB])
    sinT = dt("sinT", [128, RB])
    l2d = dt("l2d", [128, 8])
    gng = dt("gng", [128, 16])
    w_out = dt("w_out", [2 * D, D])
    x_out = dt("x_out", [D, T], "ExternalOutput")
    with contextlib.ExitStack() as stack:
        P = Prog(nc, stack)
        C = make_common(P, nfr=8)
        emit_ret(P, C, nc, x_in, x_out, g1c, w_in, cosT, sinT, l2d, gng, w_out, is_out=True)
        P.emit()
        print("ret prog stats", P.stats)
    return nc


def ret_rope_tables(hf):
    theta = (1.0 / (np.float32(10000.0) ** np.linspace(0.0, 1.0, 128, dtype=np.float32))).astype(np.float32)
    pos = (np.arange(RB, dtype=np.float32) - np.float32(2048.0) + np.float32(2048.0 * hf)).astype(np.float32)
    ang = (theta[:, None] * pos[None, :]).astype(np.float32)
    return np.cos(ang).astype(np.float32), np.sin(ang).astype(np.float32)


def ret_inmaps(x, g1, ret_w_in, ret_log2_inv_decay, ret_gn_g, ret_w_out):
    common = {
        "g1c": cols(g1, 8),
        "w_in": np.ascontiguousarray(ret_w_in[0]),
        "l2d": np.ascontiguousarray(np.tile(ret_log2_inv_decay[0].reshape(1, 8), (128, 1))),
        "gng": cols(ret_gn_g[0], 16),
        "w_out": np.ascontiguousarray(ret_w_out[0]),
    }
    tabs = [ret_rope_tables(0), ret_rope_tables(1)]
    maps = []
    for c in range(NCORES):
        b, hf = c // 2, c % 2
        if x is None:
            maps.append(dict(common, cosT=tabs[hf][0], sinT=tabs[hf][1]))
            continue
        buf = np.zeros((RB, D), np.float32)
        off = 2048 - 2048 * hf
        buf[off:off + SEQ] = x[b]
        maps.append(dict(common, x_in=np.ascontiguousarray(buf.T), cosT=tabs[hf][0], sinT=tabs[hf][1]))
    return maps


STAGES = [("c0", "conv", HC), ("f0", "ffn", 1), ("n1", "nat", HN), ("f1", "ffn", 1),
          ("r2", "ret", 2048), ("f2", "ffn", 1), ("c3", "conv", HC), ("f3", "ffn", 1)]
STAGE_IN = {
    "conv": [("mask", [128, 2 * HC]), ("g1c", [128, 8]), ("w_in", [D, 2 * D]), ("b_in", [128, 16]), ("dw_w", [128, CW * 8]),
             ("dw_b", [128, 8]), ("ln_g", [128, 8]), ("ln_b", [128, 8]), ("w_out", [D, D])],
    "ffn": [("g2c", [128, 8]), ("w_up", [D, 2 * FFN]), ("dww", [128, 3 * 2 * NH]), ("dwb", [128, 2 * NH]), ("w_down", [FFN, D])],
    "nat": [("g1c", [128, 8]), ("w_qkv", [D, 3 * D]), ("qg", [128, 2]), ("kg", [128, 1]), ("bias", [8, 128, 2 * NE * 128]),
            ("pen", [2, 5 * NE * 128]), ("ohk", [2, 128]), ("bd", [128, 128]), ("w_out", [D, D])],
    "ret": [("g1c", [128, 8]), ("w_in", [D, 6 * D]), ("cosT", [128, RB]), ("sinT", [128, RB]), ("l2d", [128, 8]),
            ("gng", [128, 16]), ("w_out", [2 * D, D])],
}
STAGE_NFR = {"conv": 12, "ffn": 14, "nat": 6, "ret": 8}


def emit_exchange(P, C, nc, name, x_next, H, hmask_sb, hmtok):
    kw = dict(allow_slow_non_contiguous=True) if H < 8 else {}
    groups = [[0, 1], [2, 3], [4, 5], [6, 7]]
    xv = x_next.rearrange("(c p) t -> p c t", p=128)
    W = min(H, 512)
    hr = Ring(P, 2, [128, 8, W], F32, "xh")

    def fill(src2d, dcol, mi, gtok):
        srcv = src2d.rearrange("(c p) t -> p c t", p=128)
        xt, xtok = hr.next()
        P.dma("sp", xt[:], srcv, [gtok], [xtok], xtok, **kw)
        P.add("dve", lambda e, o=xt[:], m=hmask_sb[:, mi:mi + 1]:
              e.tensor_scalar(out=o, in0=o, scalar1=m, scalar2=None, op0=ALU.mult), [xtok, hmtok], [xtok])
        P.dma("sp", xv[:, :, dcol:dcol + W], xt[:], [xtok], [], xtok, **kw)

    if H == 2048:
        for q in range(4):
            snd = nc.dram_tensor(f"{name}_snd{q}", [D, 512], F32, kind="Internal").ap()
            gath = nc.dram_tensor(f"{name}_gath{q}", [2 * D, 512], F32, kind="Internal").ap()
            stok, gtok, cctok = Tok("snd"), Tok("gath"), Tok("cc")
            P.dma("sp", snd, x_next[:, H + q * 512:H + (q + 1) * 512], [], [stok], stok)
            P.coll(lambda e, s_=snd, g_=gath: e.collective_compute("AllGather", ALU.bypass, replica_groups=groups,
                                                                   ins=[s_], outs=[g_]), [stok], [gtok], cctok)
            fill(gath[0:D, :], q * 512, 0, gtok)
            fill(gath[D:2 * D, :], H + T + q * 512, 1, gtok)
        return
    snd = nc.dram_tensor(name + "_snd", [2 * D, H], F32, kind="Internal").ap()
    gath = nc.dram_tensor(name + "_gath", [4 * D, H], F32, kind="Internal").ap()
    stok, gtok, cctok = Tok("snd"), Tok("gath"), Tok("cc")
    P.dma("sp", snd[0:D, :], x_next[:, H:2 * H], [], [stok], stok, **kw)
    P.dma("sp", snd[D:2 * D, :], x_next[:, T:T + H], [], [stok], stok, **kw)
    P.coll(lambda e: e.collective_compute("AllGather", ALU.bypass, replica_groups=groups,
                                          ins=[snd], outs=[gath]), [stok], [gtok], cctok)
    fill(gath[D:2 * D, :], 0, 0, gtok)
    fill(gath[2 * D:3 * D, :], H + T, 1, gtok)


def build_fused_prog(nst=8):
    stages = STAGES[:nst]
    nc = bass.Bass("TRN2", target_bir_lowering=False)
    dt = lambda name, shape, kind="ExternalInput": nc.dram_tensor(name, shape, F32, kind=kind).ap()
    aps = {}
    for (sn, kind, H) in stages:
        aps[sn] = {k: dt(f"{sn}_{k}", shp) for (k, shp) in STAGE_IN[kind]}
    hmask = dt("hmask", [128, 2])
    bufs = {}
    for si, (sn, kind, H) in enumerate(stages):
        width = RB if kind == "ret" else T + 2 * H
        bufs[sn] = dt(f"{sn}_x_in", [D, width], "ExternalInput" if si == 0 else "Internal")
    y = dt("x_out", [D, T], "ExternalOutput")
    with contextlib.ExitStack() as stack:
        P = Prog(nc, stack)
        for si, (sn, kind, H) in enumerate(stages):
            last = si == len(stages) - 1
            x_in = bufs[sn]
            if last:
                x_out = y
            else:
                nsn, nkind, nH = stages[si + 1]
                x_out = bufs[nsn][:, nH:nH + T]
            a = aps[sn]
            with contextlib.ExitStack() as st:
                P.stack = st
                P.pfx = sn + "_"
                C = make_common(P, nfr=STAGE_NFR[kind])
                if kind == "conv":
                    emit_conv(P, C, x_in, x_out, a["mask"], a["g1c"], a["w_in"], a["b_in"], a["dw_w"], a["dw_b"], a["ln_g"],
                              a["ln_b"], a["w_out"], is_out=last)
                elif kind == "ffn":
                    emit_ffn(P, C, x_in, x_out, a["g2c"], a["w_up"], a["dww"], a["dwb"], a["w_down"], is_out=last)
                elif kind == "nat":
                    emit_nat(P, C, x_in, x_out, a["g1c"], a["w_qkv"], a["qg"], a["kg"], a["bias"], a["pen"], a["ohk"], a["bd"],
                             a["w_out"], is_out=last)
                else:
                    emit_ret(P, C, nc, x_in, x_out, a["g1c"], a["w_in"], a["cosT"], a["sinT"], a["l2d"], a["gng"], a["w_out"],
                             is_out=last)
            P.barrier()
            if not last:
                with contextlib.ExitStack() as st:
                    P.stack = st
                    P.pfx = sn + "x_"
                    C = make_common(P, nfr=2)
                    hm_sb = P.sb([128, 2], F32, "hmask")
                    hmtok = load_cols(P, C, hm_sb, hmask)
                    emit_exchange(P, C, nc, sn + "x", bufs[nsn], nH, hm_sb, hmtok)
                P.barrier()
        P.stack = stack
        P.pfx = ""
        P.emit()
        print("fused prog stats", P.stats)
    return nc


def fused_inmaps(a):
    per_stage = {}
    per_stage["c0"] = conv_inmaps(a["x"], 0, a["norm1_g"][0], a["conv_w_in"], a["conv_b_in"], a["conv_dw_w"], a["conv_dw_b"],
                                  a["conv_ln_g"], a["conv_ln_b"], a["conv_w_out"])
    per_stage["c3"] = conv_inmaps(None, 1, a["norm1_g"][3], a["conv_w_in"], a["conv_b_in"], a["conv_dw_w"], a["conv_dw_b"],
                                  a["conv_ln_g"], a["conv_ln_b"], a["conv_w_out"])
    per_stage["n1"] = nat_inmaps(None, a["norm1_g"][1], a["nat_w_qkv"], a["nat_q_norm_g"], a["nat_k_norm_g"], a["nat_rpb"],
                                 a["nat_w_out"])
    per_stage["r2"] = ret_inmaps(None, a["norm1_g"][2], a["ret_w_in"], a["ret_log2_inv_decay"], a["ret_gn_g"], a["ret_w_out"])
    for i in range(4):
        per_stage[f"f{i}"] = ffn_inmaps(None, i, a["norm2_g"], a["ffn_w_up"], a["ffn_dw_w"], a["ffn_dw_b"], a["ffn_w_down"])
    maps = []
    for c in range(NCORES):
        m = {}
        for sn, lst in per_stage.items():
            for k, v in lst[c].items():
                m[f"{sn}_{k}"] = v
        hm = np.zeros((128, 2), np.float32)
        hm[:, 0] = float(c % 2)
        hm[:, 1] = float(1 - c % 2)
        m["hmask"] = hm
        maps.append(m)
    return maps


_PROGS = {}


def _prog(name, builder):
    if name not in _PROGS:
        _PROGS[name] = builder()
    return _PROGS[name]


def _launch(nc, maps):
    res = run_bass_kernel_spmd(nc, maps, core_ids=list(range(NCORES)))
    return unshard_tokens_fm([r["x_out"] for r in res.results])


def kernel_unfused(x, norm1_g, norm2_g, conv_w_in, conv_b_in, conv_dw_w, conv_dw_b, conv_ln_g, conv_ln_b, conv_w_out,
           nat_w_qkv, nat_q_norm_g, nat_k_norm_g, nat_rpb, nat_w_out, ret_w_in, ret_log2_inv_decay, ret_gn_g,
           ret_w_out, ffn_w_up, ffn_dw_w, ffn_dw_b, ffn_w_down):
    a = {k: np.asarray(v, np.float32) for k, v in locals().items()}
    xc = a["x"]
    for i in range(4):
        mixer, j = i % 3, i // 3
        if mixer == 0:
            maps = conv_inmaps(xc, j, a["norm1_g"][i], a["conv_w_in"], a["conv_b_in"], a["conv_dw_w"], a["conv_dw_b"],
                               a["conv_ln_g"], a["conv_ln_b"], a["conv_w_out"])
            xc = _launch(_prog("conv", build_conv_prog), maps)
        elif mixer == 1:
            maps = nat_inmaps(xc, a["norm1_g"][i], a["nat_w_qkv"], a["nat_q_norm_g"], a["nat_k_norm_g"], a["nat_rpb"],
                              a["nat_w_out"])
            xc = _launch(_prog("nat", build_nat_prog), maps)
        else:
            maps = ret_inmaps(xc, a["norm1_g"][i], a["ret_w_in"], a["ret_log2_inv_decay"], a["ret_gn_g"], a["ret_w_out"])
            xc = _launch(_prog("ret", build_ret_prog), maps)
        maps = ffn_inmaps(xc, i, a["norm2_g"], a["ffn_w_up"], a["ffn_dw_w"], a["ffn_dw_b"], a["ffn_w_down"])
        xc = _launch(_prog("ffn", build_ffn_prog), maps)
    return xc


def kernel(x, norm1_g, norm2_g, conv_w_in, conv_b_in, conv_dw_w, conv_dw_b, conv_ln_g, conv_ln_b, conv_w_out,
           nat_w_qkv, nat_q_norm_g, nat_k_norm_g, nat_rpb, nat_w_out, ret_w_in, ret_log2_inv_decay, ret_gn_g,
           ret_w_out, ffn_w_up, ffn_dw_w, ffn_dw_b, ffn_w_down):
    a = {k: np.asarray(v, np.float32) for k, v in locals().items()}
    maps = fused_inmaps(a)
    nc = _prog("fused", build_fused_prog)
    res = run_bass_kernel_spmd(nc, maps, core_ids=list(range(NCORES)))
    return unshard_tokens_fm([r["x_out"] for r in res.results])
```

```python
import contextlib
import numpy as np
import concourse.bass as bass
import concourse.mybir as mybir
from concourse.bass_utils import run_bass_kernel_spmd

F32 = mybir.dt.float32
BF16 = mybir.dt.bfloat16
AF = mybir.ActivationFunctionType
ALU = mybir.AluOpType
AX = mybir.AxisListType

D = 1024
SEQ = 4096
BATCH = 4
T = 2048
NCORES = 8
FFN = 2816
NH = FFN // 128
EPS = 1e-6


class Tok:
    __slots__ = ("lw", "rd", "name", "sem", "dcount", "last_dma")

    def __init__(self, name=""):
        self.lw = None
        self.rd = []
        self.name = name
        self.sem = None
        self.dcount = 0
        self.last_dma = None


class Op:
    __slots__ = ("eng", "fn", "deps", "is_dma", "dtok", "has_dep", "sem", "val",
                 "waits", "know", "is_out", "inc", "is_barrier")

    def __init__(self, eng, fn, is_dma=False, dtok=None):
        self.eng = eng
        self.fn = fn
        self.deps = set()
        self.is_dma = is_dma
        self.dtok = dtok
        self.has_dep = False
        self.sem = None
        self.val = 0
        self.waits = ()
        self.know = None
        self.is_out = False
        self.inc = 16 if is_dma else 1
        self.is_barrier = False


class Prog:
    ENGS = ("pe", "act", "dve", "pool", "sp")

    def __init__(self, nc, stack):
        self.nc = nc
        self.stack = stack
        self.ops = []
        self.nsb = 0
        self.nsem = 0
        self.out_ops = []
        self.pfx = ""
        self.bar_start = 0
        self.prev_bar = []

    def sb(self, shape, dtype, name=None):
        self.nsb += 1
        return self.stack.enter_context(
            self.nc.sbuf_tensor(self.pfx + (name or f"sb{self.nsb}"), list(shape), dtype))

    def ps(self, shape, dtype, name=None):
        self.nsb += 1
        return self.stack.enter_context(
            self.nc.psum_tensor(self.pfx + (name or f"ps{self.nsb}"), list(shape), dtype))

    def barrier(self):
        last = {}
        dmas = {}
        for op in self.ops[self.bar_start:]:
            if op.is_dma:
                dmas[id(op.dtok)] = op
            else:
                last[op.eng] = op
        deps = set(last.values()) | set(dmas.values()) | set(self.prev_bar)
        bars = []
        for eng in self.ENGS:
            op = Op(eng, lambda e: e.nop())
            op.deps = set(deps)
            op.is_barrier = (eng == self.ENGS[0])
            self.ops.append(op)
            bars.append(op)
        self.prev_bar = bars
        self.bar_start = len(self.ops)

    def new_sem(self, name=None):
        self.nsem += 1
        return self.stack.enter_context(self.nc.semaphore(name or f"sem{self.nsem}"))

    def add(self, eng, fn, reads=(), writes=(), is_dma=False, dtok=None, is_out=False):
        op = Op(eng, fn, is_dma, dtok)
        op.is_out = is_out
        for t in reads:
            if t.lw is not None:
                op.deps.add(t.lw)
        for t in writes:
            for r in t.rd:
                op.deps.add(r)
            if t.lw is not None:
                op.deps.add(t.lw)
        if is_dma:
            if dtok.last_dma is not None:
                op.deps.add(dtok.last_dma)
            dtok.last_dma = op
        for t in reads:
            t.rd.append(op)
        for t in writes:
            t.rd = []
            t.lw = op
        op.deps.discard(op)
        if eng == "pe" and not is_dma:
            op.deps = {d for d in op.deps if not (d.eng == "pe" and not d.is_dma)}
        self.ops.append(op)
        if is_out:
            self.out_ops.append(op)
        return op

    def coll(self, fn, reads, writes, dtok):
        op = self.add("pool", fn, reads, writes, is_dma=True, dtok=dtok)
        op.inc = 1
        return op

    def dma(self, queue, out, in_, reads, writes, dtok, is_out=False, **kw):
        return self.add(queue, lambda e: e.dma_start(out=out, in_=in_, **kw),
                        reads, writes, is_dma=True, dtok=dtok, is_out=is_out)

    def emit(self):
        ops = self.ops
        for op in ops:
            for d in op.deps:
                d.has_dep = True
        esem = {e: self.new_sem("eng_" + e) for e in self.ENGS}
        cnt = {e: 0 for e in self.ENGS}
        free_sems = []
        live_toks = []
        for op in ops:
            if op.is_barrier:
                for t in live_toks:
                    free_sems.append((t.sem, t.dcount))
                live_toks = []
            if op.is_dma:
                t = op.dtok
                if t.sem is None:
                    if free_sems:
                        t.sem, t.dcount = free_sems.pop()
                    else:
                        t.sem = self.new_sem()
                    live_toks.append(t)
                t.dcount += op.inc
                op.sem = t.sem
                op.val = t.dcount
            elif op.has_dep:
                cnt[op.eng] += 1
                op.sem = esem[op.eng]
                op.val = cnt[op.eng]
        seen = {e: {} for e in self.ENGS}
        nwaits = 0
        for op in ops:
            s = seen[op.eng]
            waits = {}
            for d in op.deps:
                k = id(d.sem)
                if s.get(k, (None, 0))[1] >= d.val:
                    continue
                if waits.get(k, (None, 0))[1] < d.val:
                    waits[k] = (d.sem, d.val)
            for d in op.deps:
                if d.know is not None:
                    for k, v in d.know.items():
                        if s.get(k, (None, 0))[1] < v[1]:
                            s[k] = v
            for k, v in waits.items():
                if s.get(k, (None, 0))[1] < v[1]:
                    s[k] = v
            op.waits = list(waits.values())
            nwaits += len(op.waits)
            if op.sem is not None:
                kn = dict(s)
                kn[id(op.sem)] = (op.sem, op.val)
                op.know = kn
                if not op.is_dma:
                    s[id(op.sem)] = (op.sem, op.val)
        by = {e: [o for o in ops if o.eng == e] for e in self.ENGS}
        finals = [(o.sem, o.val) for o in self.out_ops]
        self.stats = dict(nops=len(ops), nwaits=nwaits,
                          per_eng={e: len(by[e]) for e in self.ENGS}, nsem=self.nsem)

        def run(name, e):
            for op in by[name]:
                for sem, val in op.waits:
                    e.wait_ge(sem, val)
                inst = op.fn(e)
                if op.sem is not None:
                    inst.then_inc(op.sem, op.inc)
            if name == "sp":
                for sem, val in finals:
                    e.wait_ge(sem, val)

        with self.nc.Block() as block:
            @block.tensor
            def _(e):
                run("pe", e)

            @block.scalar
            def _(e):
                run("act", e)

            @block.vector
            def _(e):
                run("dve", e)

            @block.gpsimd
            def _(e):
                run("pool", e)

            @block.sync
            def _(e):
                run("sp", e)


class Ring:
    def __init__(self, P, n, shape, dtype, name, psum=False):
        self.bufs = []
        for i in range(n):
            t = (P.ps if psum else P.sb)(shape, dtype, f"{name}{i}")
            self.bufs.append((t, Tok(f"{name}{i}")))
        self.i = 0

    def next(self):
        b = self.bufs[self.i % len(self.bufs)]
        self.i += 1
        return b


def mm(P, out, lhsT, rhs, start, stop, reads, writes, skip=False):
    return P.add("pe", lambda e: e.matmul(out, lhsT, rhs, start=start, stop=stop, skip_group_check=skip),
                 reads, writes)


def emit_rmsnorm(P, C, x_dram, g_col, gtok, h_all, h_tok, ps_ring, tiles, two_pass=False, hcol=None):
    fr = C["fr"]
    sqr = C["sqr"]
    ones = C["ones_bf"]
    assert len(fr.bufs) >= (4 if two_pass else 9)
    for ti, (t0, n) in enumerate(tiles):
        d0 = t0 if hcol is None else hcol[ti]
        xs = []
        pst, pstok = ps_ring.next()
        for c in range(8):
            xt, xtok = fr.next()
            P.dma("sp", xt[:, 0:n], x_dram[c * 128:(c + 1) * 128, t0:t0 + n], [], [xtok], xtok)
            xs.append((xt, xtok))
            sq, sqtok = sqr.next()
            P.add("act", lambda e, o=sq[:, 0:n], i=xt[:, 0:n]: e.activation(out=o, in_=i, func=AF.Square),
                  [xtok], [sqtok])
            mm(P, pst[:, 0:n], ones[:], sq[:, 0:n], c == 0, c == 7, [sqtok, C["ctok"]], [pstok])
        rs, rstok = C["rsr"].next()
        P.add("act", lambda e, o=rs[:, 0:n], i=pst[:, 0:n]: e.activation(
            out=o, in_=i, func=AF.Sqrt, bias=C["eps_col"][:, 0:1], scale=1.0 / D), [pstok, C["ctok"]], [rstok])
        P.add("dve", lambda e, o=rs[:, 0:n]: e.reciprocal(out=o, in_=o), [rstok], [rstok])
        for c in range(8):
            if two_pass:
                xt, xtok = fr.next()
                P.dma("sp", xt[:, 0:n], x_dram[c * 128:(c + 1) * 128, t0:t0 + n], [], [xtok], xtok)
            else:
                xt, xtok = xs[c]
            P.add("dve", lambda e, o=h_all[:, c, d0:d0 + n], i=xt[:, 0:n], g=g_col[:, c:c + 1], r=rs[:, 0:n]:
                  e.scalar_tensor_tensor(out=o, in0=i, scalar=g, in1=r, op0=ALU.mult, op1=ALU.mult),
                  [xtok, rstok, gtok], [h_tok[ti][c]])


def make_common(P, nfr=14):
    C = {}
    C["fr"] = Ring(P, nfr, [128, 512], F32, "fr")
    C["sqr"] = Ring(P, 3, [128, 512], BF16, "sqr")
    C["rsr"] = Ring(P, 2, [128, 512], F32, "rsr")
    C["ones_bf"] = P.sb([128, 128], BF16, "ones_bf")
    C["eps_col"] = P.sb([128, 1], F32, "eps_col")
    C["ctok"] = Tok("consts")
    P.add("pool", lambda e: e.memset(C["ones_bf"][:], 1.0), [], [C["ctok"]])
    P.add("pool", lambda e: e.memset(C["eps_col"][:], EPS), [], [C["ctok"]])
    return C


def load_cols(P, C, dst, src_dram, scale=None):
    tok = Tok("par")
    P.dma("sp", dst[:], src_dram, [], [tok], tok)
    if scale is not None:
        P.add("pool", lambda e: e.tensor_scalar(out=dst[:], in0=dst[:], scalar1=float(scale), scalar2=None,
                                                 op0=ALU.mult), [tok], [tok])
    return tok


def htoks_for(h_tok, tiles, c, lo, hi):
    return [h_tok[i][c] for i, (t0, n) in enumerate(tiles) if t0 < hi and t0 + n > lo]


def emit_ffn(P, C, x_in, x_out, g2c, w_up, dww, dwb, w_down, is_out=False):
    NT = T + 2
    g_col = P.sb([128, 8], F32, "ffn_g")
    gtok = load_cols(P, C, g_col, g2c)
    dww_sb = P.sb([128, 3 * 2 * NH], F32, "ffn_dww")
    dwb_sb = P.sb([128, 2 * NH], F32, "ffn_dwb")
    dwtok = load_cols(P, C, dww_sb, dww)
    dbtok = load_cols(P, C, dwb_sb, dwb)

    h_all = P.sb([128, 8, NT], BF16, "ffn_h")
    tiles = [(0, 410), (410, 410), (820, 410), (1230, 410), (1640, NT - 1640)]
    h_tok = [[Tok(f"h{i}_{c}") for c in range(8)] for i in range(len(tiles))]
    ps_stat = Ring(P, 1, [128, 512], F32, "ps_stat", psum=True)
    emit_rmsnorm(P, C, x_in, g_col, gtok, h_all, h_tok, ps_stat, tiles)

    act_all = P.sb([128, NH, T], BF16, "ffn_act")
    act_tok = [[Tok(f"act{j}_{i}") for i in range(5)] for j in range(NH)]

    wst = Ring(P, 2, [128, NH * 128], F32, "wst")
    wbf = Ring(P, 2, [128, NH * 128], BF16, "wbf")
    ps_up = Ring(P, 7, [128, 512], F32, "ps_up", psum=True)
    fr = C["fr"]
    w_up_v = w_up.rearrange("(kc p) n -> p kc n", p=128)
    ctiles = [(0, 410), (410, 410), (820, 410), (1230, 410), (1640, 408)]
    def load_up(j):
        st, sttok = wst.next()
        stv = st[:, 0:2048].rearrange("p (k n) -> p k n", k=8)
        P.dma("sp", stv[:, :, 0:128], w_up_v[:, :, j * 128:(j + 1) * 128], [], [sttok], sttok)
        P.dma("sp", stv[:, :, 128:256], w_up_v[:, :, FFN + j * 128:FFN + (j + 1) * 128], [], [sttok], sttok)
        wb, wbtok = wbf.next()
        P.add("pool", lambda e, o=wb[:, 0:2048], i=st[:, 0:2048]: e.tensor_copy(out=o, in_=i), [sttok], [wbtok])
        return wb[:, 0:2048].rearrange("p (k n) -> p k n", k=8), wbtok

    nxt = load_up(0)
    for j in range(NH):
        wbv, wbtok = nxt
        if j + 1 < NH:
            nxt = load_up(j + 1)
        for ci, (o0, n) in enumerate(ctiles):
            ncol = n + 2
            pv, pvtok = ps_up.next()
            pg, pgtok = ps_up.next()
            for half, (pt, pttok) in enumerate(((pv, pvtok), (pg, pgtok))):
                for kc in range(8):
                    mm(P, pt[:, 0:ncol], wbv[:, kc, half * 128:(half + 1) * 128], h_all[:, kc, o0:o0 + ncol],
                       kc == 0, kc == 7, [wbtok] + htoks_for(h_tok, tiles, kc, o0, o0 + ncol), [pttok])
            av, avtok = fr.next()
            ag, agtok = fr.next()
            for half, (pt, pttok, acc, acctok) in enumerate(((pv, pvtok, av, avtok), (pg, pgtok, ag, agtok))):
                ch = half * NH + j
                w0 = dww_sb[:, 0 * 2 * NH + ch:0 * 2 * NH + ch + 1]
                w1 = dww_sb[:, 1 * 2 * NH + ch:1 * 2 * NH + ch + 1]
                w2 = dww_sb[:, 2 * 2 * NH + ch:2 * 2 * NH + ch + 1]
                bb = dwb_sb[:, ch:ch + 1]
                P.add("act", lambda e, o=acc[:, 0:n], i=pt[:, 1:n + 1], s=w1, b=bb:
                      e.activation(out=o, in_=i, func=AF.Identity, bias=b, scale=s),
                      [pttok, dwtok, dbtok], [acctok])
                P.add("dve", lambda e, o=acc[:, 0:n], i=pt[:, 0:n], s=w0:
                      e.scalar_tensor_tensor(out=o, in0=i, scalar=s, in1=o, op0=ALU.mult, op1=ALU.add),
                      [pttok, acctok, dwtok], [acctok])
                P.add("dve", lambda e, o=acc[:, 0:n], i=pt[:, 2:n + 2], s=w2:
                      e.scalar_tensor_tensor(out=o, in0=i, scalar=s, in1=o, op0=ALU.mult, op1=ALU.add),
                      [pttok, acctok, dwtok], [acctok])
            ge, getok = fr.next()
            P.add("act", lambda e, o=ge[:, 0:n], i=ag[:, 0:n]: e.activation(out=o, in_=i, func=AF.Gelu_apprx_tanh),
                  [agtok], [getok])
            P.add("dve", lambda e, o=act_all[:, j, o0:o0 + n], a=ge[:, 0:n], b=av[:, 0:n]:
                  e.tensor_tensor(out=o, in0=a, in1=b, op=ALU.mult), [getok, avtok], [act_tok[j][ci]])

    ps_dn = ps_up
    w_dn_v = w_down.rearrange("(j p) n -> p j n", p=128)
    def load_dn(o):
        st, sttok = wst.next()
        stv = st[:].rearrange("p (j n) -> p j n", j=NH)
        P.dma("sp", stv, w_dn_v[:, :, o * 128:(o + 1) * 128], [], [sttok], sttok)
        wb, wbtok = wbf.next()
        P.add("pool", lambda e, oo=wb[:], i=st[:]: e.tensor_copy(out=oo, in_=i), [sttok], [wbtok])
        return wb[:].rearrange("p (j n) -> p j n", j=NH), wbtok

    nxt = load_dn(0)
    for o in range(8):
        wbv, wbtok = nxt
        if o + 1 < 8:
            nxt = load_dn(o + 1)
        for tt in range(T // 512):
            t0 = tt * 512
            xt, xtok = fr.next()
            P.dma("sp", xt[:], x_in[o * 128:(o + 1) * 128, 1 + t0:1 + t0 + 512], [], [xtok], xtok)
            pt, pttok = ps_dn.next()
            for j in range(NH):
                rd = [wbtok] + [act_tok[j][ci] for ci, (o0, n) in enumerate(ctiles) if o0 < t0 + 512 and o0 + n > t0]
                mm(P, pt[:], wbv[:, j, :], act_all[:, j, t0:t0 + 512], j == 0, j == NH - 1, rd, [pttok])
            P.add("dve", lambda e, oo=xt[:], a=pt[:]: e.tensor_tensor(out=oo, in0=a, in1=oo, op=ALU.add),
                  [pttok, xtok], [xtok])
            P.dma("sp", x_out[o * 128:(o + 1) * 128, t0:t0 + 512], xt[:], [xtok], [], xtok, is_out=is_out)


CW = 31
HC = 15


def emit_conv(P, C, x_in, x_out, mask, g1c, w_in, b_in, dw_w, dw_b, ln_g, ln_b, w_out, is_out=False):
    NT = T + 2 * HC
    fr = C["fr"]
    g_col = P.sb([128, 8], F32, "cv_g")
    gtok = load_cols(P, C, g_col, g1c)
    bin_sb = P.sb([128, 16], F32, "cv_bin")
    bintok = load_cols(P, C, bin_sb, b_in)
    dww_sb = P.sb([128, CW * 8], F32, "cv_dww")
    dwwtok = load_cols(P, C, dww_sb, dw_w)
    dwb_sb = P.sb([128, 8], F32, "cv_dwb")
    dwbtok = load_cols(P, C, dwb_sb, dw_b)
    lng_sb = P.sb([128, 8], F32, "cv_lng")
    lngtok = load_cols(P, C, lng_sb, ln_g)
    lnb_sb = P.sb([128, 8], F32, "cv_lnb")
    lnbtok = load_cols(P, C, lnb_sb, ln_b)
    mask_sb = P.sb([128, 2 * HC], F32, "cv_mask")
    masktok = load_cols(P, C, mask_sb, mask)
    ones_f = P.sb([128, 128], F32, "ones_f")
    P.add("pool", lambda e: e.memset(ones_f[:], 1.0), [], [C["ctok"]])

    h_all = P.sb([128, 8, NT], BF16, "cv_h")
    tiles = [(0, 416), (416, 416), (832, 416), (1248, 416), (1664, NT - 1664)]
    h_tok = [[Tok(f"cvh{i}_{c}") for c in range(8)] for i in range(len(tiles))]
    ps_stat = Ring(P, 2, [128, 512], F32, "ps_stat", psum=True)
    emit_rmsnorm(P, C, x_in, g_col, gtok, h_all, h_tok, ps_stat, tiles)

    v_all = P.sb([128, 8, T], F32, "cv_v")
    v_tok = [[Tok(f"cvv{c}_{i}") for i in range(4)] for c in range(8)]
    ur = Ring(P, 2, [128, NT], F32, "cv_u")
    wst = Ring(P, 2, [128, 2048], F32, "cv_wst")
    wbf = Ring(P, 2, [128, 2048], BF16, "cv_wbf")
    ps_up = Ring(P, 4, [128, 512], F32, "ps_up", psum=True)
    w_in_v = w_in.rearrange("(kc p) n -> p kc n", p=128)
    KD = 20
    for c in range(8):
        st, sttok = wst.next()
        stv = st[:].rearrange("p (k n) -> p k n", k=8)
        P.dma("sp", stv[:, :, 0:128], w_in_v[:, :, c * 128:(c + 1) * 128], [], [sttok], sttok)
        P.dma("sp", stv[:, :, 128:256], w_in_v[:, :, D + c * 128:D + (c + 1) * 128], [], [sttok], sttok)
        wb, wbtok = wbf.next()
        P.add("pool", lambda e, o=wb[:], i=st[:]: e.tensor_copy(out=o, in_=i), [sttok], [wbtok])
        wbv = wb[:].rearrange("p (k n) -> p k n", k=8)
        u, utok = ur.next()
        for ti, (t0, n) in enumerate(tiles):
            pa, patok = ps_up.next()
            pg, pgtok = ps_up.next()
            for half, (pt, pttok) in enumerate(((pa, patok), (pg, pgtok))):
                for kc in range(8):
                    mm(P, pt[:, 0:n], wbv[:, kc, half * 128:(half + 1) * 128], h_all[:, kc, t0:t0 + n],
                       kc == 0, kc == 7, [wbtok, h_tok[ti][kc]], [pttok])
            sg, sgtok = fr.next()
            P.add("act", lambda e, o=sg[:, 0:n], i=pg[:, 0:n], b=bin_sb[:, 8 + c:9 + c]:
                  e.activation(out=o, in_=i, func=AF.Sigmoid, bias=b, scale=1.0), [pgtok, bintok], [sgtok])
            P.add("dve", lambda e, o=u[:, t0:t0 + n], i=pa[:, 0:n], b=bin_sb[:, c:c + 1], g=sg[:, 0:n]:
                  e.scalar_tensor_tensor(out=o, in0=i, scalar=b, in1=g, op0=ALU.add, op1=ALU.mult),
                  [patok, sgtok, bintok], [utok])
        P.add("pool", lambda e, o=u[:, 0:HC], m=mask_sb[:, 0:HC]: e.tensor_tensor(out=o, in0=o, in1=m, op=ALU.mult),
              [utok, masktok], [utok])
        P.add("pool", lambda e, o=u[:, T + HC:NT], m=mask_sb[:, HC:2 * HC]: e.tensor_tensor(out=o, in0=o, in1=m, op=ALU.mult),
              [utok, masktok], [utok])
        for tt in range(4):
            t0 = tt * 512
            va = v_all[:, c, t0:t0 + 512]
            vb, vbtok = fr.next()
            wk = lambda k: dww_sb[:, k * 8 + c:k * 8 + c + 1]
            P.add("act", lambda e, o=va, i=u[:, t0:t0 + 512], s=wk(0), b=dwb_sb[:, c:c + 1]:
                  e.activation(out=o, in_=i, func=AF.Identity, bias=b, scale=s), [utok, dwwtok, dwbtok], [v_tok[c][tt]])
            for k in range(1, KD + 1):
                P.add("dve", lambda e, o=va, i=u[:, t0 + k:t0 + k + 512], s=wk(k):
                      e.scalar_tensor_tensor(out=o, in0=i, scalar=s, in1=o, op0=ALU.mult, op1=ALU.add),
                      [utok, dwwtok, v_tok[c][tt]], [v_tok[c][tt]])
            P.add("act", lambda e, o=vb[:], i=u[:, t0 + KD + 1:t0 + KD + 1 + 512], s=wk(KD + 1):
                  e.activation(out=o, in_=i, func=AF.Identity, scale=s), [utok, dwwtok], [vbtok])
            for k in range(KD + 2, CW):
                tp, tptok = fr.next()
                P.add("act", lambda e, o=tp[:], i=u[:, t0 + k:t0 + k + 512], s=wk(k):
                      e.activation(out=o, in_=i, func=AF.Identity, scale=s), [utok, dwwtok], [tptok])
                P.add("pool", lambda e, o=vb[:], a=tp[:]: e.tensor_tensor(out=o, in0=o, in1=a, op=ALU.add),
                      [tptok, vbtok], [vbtok])
            P.add("dve", lambda e, o=va, b=vb[:]: e.tensor_tensor(out=o, in0=o, in1=b, op=ALU.add),
                  [vbtok, v_tok[c][tt]], [v_tok[c][tt]])

    wo_bf = P.sb([128, 8, D], BF16, "cv_wo")
    wotok = [Tok(f"wo{i}") for i in range(4)]
    w_out_v = w_out.rearrange("(kc p) n -> p kc n", p=128)
    for i in range(4):
        st, sttok = wst.next()
        stv = st[:].rearrange("p (k n) -> p k n", k=8)
        P.dma("sp", stv, w_out_v[:, :, i * 256:(i + 1) * 256], [], [sttok], sttok)
        P.add("pool", lambda e, o=wo_bf[:, :, i * 256:(i + 1) * 256], s_=stv: e.tensor_copy(out=o, in_=s_),
              [sttok], [wotok[i]])

    ps_o = Ring(P, 2, [128, 512], F32, "ps_o", psum=True)
    for tt in range(4):
        t0 = tt * 512
        p1, p1tok = ps_stat.next()
        p2, p2tok = ps_stat.next()
        for c in range(8):
            mm(P, p1[:], ones_f[:], v_all[:, c, t0:t0 + 512], c == 0, c == 7, [v_tok[c][tt], C["ctok"]], [p1tok])
        for c in range(8):
            sq, sqtok = fr.next()
            P.add("act", lambda e, o=sq[:], i=v_all[:, c, t0:t0 + 512]: e.activation(out=o, in_=i, func=AF.Square),
                  [v_tok[c][tt]], [sqtok])
            mm(P, p2[:], ones_f[:], sq[:], c == 0, c == 7, [sqtok, C["ctok"]], [p2tok])
        mu, mutok = fr.next()
        P.add("act", lambda e, o=mu[:], i=p1[:]: e.activation(out=o, in_=i, func=AF.Identity, scale=1.0 / D),
              [p1tok], [mutok])
        rs, rstok = fr.next()
        P.add("dve", lambda e, o=rs[:], a=mu[:]: e.tensor_tensor(out=o, in0=a, in1=a, op=ALU.mult), [mutok], [rstok])
        P.add("dve", lambda e, o=rs[:], i=p2[:]: e.scalar_tensor_tensor(out=o, in0=i, scalar=1.0 / D, in1=o,
                                                                        op0=ALU.mult, op1=ALU.subtract),
              [p2tok, rstok], [rstok])
        P.add("act", lambda e, o=rs[:]: e.activation(out=o, in_=o, func=AF.Sqrt, bias=C["eps_col"][:, 0:1], scale=1.0),
              [rstok, C["ctok"]], [rstok])
        P.add("dve", lambda e, o=rs[:]: e.reciprocal(out=o, in_=o), [rstok], [rstok])
        for c in range(8):
            dd, ddtok = fr.next()
            P.add("pool", lambda e, o=dd[:], a=v_all[:, c, t0:t0 + 512], m=mu[:]:
                  e.tensor_tensor(out=o, in0=a, in1=m, op=ALU.subtract), [v_tok[c][tt], mutok], [ddtok])
            P.add("dve", lambda e, o=dd[:], r=rs[:]: e.tensor_tensor(out=o, in0=o, in1=r, op=ALU.mult),
                  [ddtok, rstok], [ddtok])
            P.add("act", lambda e, o=h_all[:, c, t0:t0 + 512], i=dd[:], g=lng_sb[:, c:c + 1], b=lnb_sb[:, c:c + 1]:
                  e.activation(out=o, in_=i, func=AF.Silu, bias=b, scale=g),
                  [ddtok, lngtok, lnbtok], [h_tok[i][c] for i in range(len(tiles))])
        for o in range(8):
            pt, pttok = ps_o.next()
            for kc in range(8):
                mm(P, pt[:], wo_bf[:, kc, o * 128:(o + 1) * 128], h_all[:, kc, t0:t0 + 512], kc == 0, kc == 7,
                   [wotok[o // 2], h_tok[tt][kc]], [pttok])
            xt, xtok = fr.next()
            P.dma("sp", xt[:], x_in[o * 128:(o + 1) * 128, HC + t0:HC + t0 + 512], [], [xtok], xtok)
            P.add("dve", lambda e, oo=xt[:], a=pt[:]: e.tensor_tensor(out=oo, in0=a, in1=oo, op=ALU.add),
                  [pttok, xtok], [xtok])
            P.dma("sp", x_out[o * 128:(o + 1) * 128, t0:t0 + 512], xt[:], [xtok], [], xtok, is_out=is_out)


HN = 256
NEG = -30000.0
NE = 7


def nat_es(qp):
    if qp == 0:
        return list(range(0, 6))
    if qp == 15:
        return list(range(-1, 5))
    return list(range(0, 5))


def nat_pidx(qp):
    return {0: 0, 1: 1, 14: 3, 15: 4}.get(qp, 2)


def emit_nat(P, C, x_in, x_out, g1c, w_qkv, qg, kg, bias, pen, ohk, bd, w_out, is_out=False):
    NT = T + 2 * HN
    fr = C["fr"]
    g_col = P.sb([128, 8], F32, "nt_g")
    gtok = load_cols(P, C, g_col, g1c)
    qg_sb = P.sb([128, 2], F32, "nt_qg")
    qgtok = load_cols(P, C, qg_sb, qg, scale=0.125)
    kg_sb = P.sb([128, 1], F32, "nt_kg")
    kgtok = load_cols(P, C, kg_sb, kg)
    ctok = C["ctok"]
    bd_f = P.sb([128, 128], F32, "nt_bd")
    bdtok = load_cols(P, C, bd_f, bd)
    ident_f = P.sb([128, 128], F32, "nt_idf")
    ident = P.sb([128, 128], BF16, "nt_id")
    P.add("pool", lambda e: e.memset(ident_f[:], 1.0), [], [ctok])
    P.add("pool", lambda e: e.affine_select(out=ident_f[:], in_=ident_f[:], pattern=[[-1, 128]], compare_op=ALU.is_equal,
                                            fill=0.0, base=0, channel_multiplier=1), [ctok], [ctok])
    P.add("pool", lambda e: e.tensor_copy(out=ident[:], in_=ident_f[:]), [ctok], [ctok])
    pen_f = P.sb([2, 5 * NE * 128], F32, "nt_penf")
    pen_bf = P.sb([2, 5 * NE * 128], BF16, "nt_pen")
    pentok = load_cols(P, C, pen_f, pen)
    P.add("pool", lambda e: e.tensor_copy(out=pen_bf[:], in_=pen_f[:]), [pentok], [pentok])
    ohk_f = P.sb([2, 128], F32, "nt_ohkf")
    ohk_bf = P.sb([2, 128], BF16, "nt_ohk")
    ohktok = load_cols(P, C, ohk_f, ohk)
    P.add("pool", lambda e: e.tensor_copy(out=ohk_bf[:], in_=ohk_f[:]), [ohktok], [ohktok])

    h_all = P.sb([128, 8, NT], BF16, "nt_h")
    tiles = [(i * 512, 512) for i in range(NT // 512)]
    h_tok = [[Tok(f"nth{i}_{c}") for c in range(8)] for i in range(len(tiles))]
    ps_pr = Ring(P, 2, [128, 512], F32, "ps_pr", psum=True)
    emit_rmsnorm(P, C, x_in, g_col, gtok, h_all, h_tok, ps_pr, tiles, two_pass=True)

    attn_all = P.sb([128, 8, T], BF16, "nt_attn")
    attn_tok = [Tok(f"attn{hp}") for hp in range(8)]
    wst = Ring(P, 2, [128, 8, 128], F32, "nt_wst")
    wq_r = Ring(P, 2, [128, 8, 128], BF16, "nt_wq")
    wk_r = Ring(P, 2, [128, 8, 128], BF16, "nt_wk")
    wv_r = Ring(P, 2, [128, 8, 128], BF16, "nt_wv")
    bst = Ring(P, 1, [128, 2 * NE * 128], F32, "nt_bst")
    bbf = Ring(P, 2, [128, 2 * NE * 128], BF16, "nt_bbf")
    q_r = Ring(P, 1, [128, 2, T], BF16, "nt_q")
    k_r = Ring(P, 1, [128, NT], BF16, "nt_k")
    v_r = Ring(P, 1, [128, 2, NT // 128, 128], BF16, "nt_v")
    for (vb_, vbtok_) in v_r.bufs:
        P.add("pool", lambda e, o=vb_[:]: e.memset(o, 0.0), [], [vbtok_])
    onesz = P.sb([128, 2, 128], BF16, "nt_onesz")
    P.add("pool", lambda e: e.memset(onesz[:], 0.0), [], [ctok])
    P.add("pool", lambda e: e.memset(onesz[:, 0, 0:64], 1.0), [], [ctok])
    P.add("pool", lambda e: e.memset(onesz[:, 1, 64:128], 1.0), [], [ctok])
    p_r = Ring(P, 3, [128, 6 * 128], BF16, "nt_p")
    ps_sc = Ring(P, 2, [128, 1024], F32, "ps_sc", psum=True)
    ps_pv = Ring(P, 2, [128, 512], F32, "ps_pv", psum=True)
    w_v = w_qkv.rearrange("(kc p) n -> p kc n", p=128)
    allh = lambda ti: [h_tok[ti][c] for c in range(8)]

    for hp in range(8):
        wts = []
        for which, ring in enumerate((wq_r, wk_r, wv_r)):
            st, sttok = wst.next()
            P.dma("sp", st[:], w_v[:, :, which * D + hp * 128:which * D + (hp + 1) * 128], [], [sttok], sttok)
            wb, wbtok = ring.next()
            P.add("pool", lambda e, o=wb[:], i=st[:]: e.tensor_copy(out=o, in_=i), [sttok], [wbtok])
            wts.append((wb, wbtok))
        (wq, wqtok), (wk, wktok), (wv, wvtok) = wts
        bs, bstok = bst.next()
        P.dma("sp", bs[:], bias[hp], [], [bstok], bstok)
        bb, bbtok = bbf.next()
        P.add("pool", lambda e, o=bb[:], i=bs[:]: e.tensor_copy(out=o, in_=i), [bstok], [bbtok])

        q_sb, qtok = q_r.next()
        k_sb, ktok = k_r.next()
        v_sb, vtok = v_r.next()
        for (dst, dtok, wmat, wtok, gsb, gt, tl) in (
                (q_sb, qtok, wq, wqtok, qg_sb, qgtok, [(HN + i * 512, i * 512) for i in range(4)]),
                (k_sb, ktok, wk, wktok, kg_sb, kgtok, [(i * 512, i * 512) for i in range(5)])):
            for (hs, ds) in tl:
                ti = hs // 512
                pr, prtok = ps_pr.next()
                for kc in range(8):
                    mm(P, pr[:], wmat[:, kc, :], h_all[:, kc, hs:hs + 512], kc == 0, kc == 7,
                       [wtok] + [h_tok[i][kc] for i in range(len(tiles)) if i * 512 < hs + 512 and (i + 1) * 512 > hs], [prtok])
                sq, sqtok = fr.next()
                P.add("act", lambda e, o=sq[:], i=pr[:]: e.activation(out=o, in_=i, func=AF.Square), [prtok], [sqtok])
                pq, pqtok = ps_pr.next()
                mm(P, pq[:], bd_f[:], sq[:], True, True, [sqtok, bdtok], [pqtok])
                rs, rstok = fr.next()
                P.add("act", lambda e, o=rs[:], i=pq[:]: e.activation(out=o, in_=i, func=AF.Sqrt,
                                                                      bias=C["eps_col"][:, 0:1], scale=1.0 / 64),
                      [pqtok, ctok], [rstok])
                P.add("dve", lambda e, o=rs[:]: e.reciprocal(out=o, in_=o), [rstok], [rstok])
                if dst is q_sb:
                    for hh_ in range(2):
                        P.add("dve", lambda e, o=dst[:, hh_, ds:ds + 512], i=pr[:], g=gsb[:, hh_:hh_ + 1], r=rs[:]:
                              e.scalar_tensor_tensor(out=o, in0=i, scalar=g, in1=r, op0=ALU.mult, op1=ALU.mult),
                              [prtok, rstok, gt], [dtok])
                else:
                    P.add("dve", lambda e, o=dst[:, ds:ds + 512], i=pr[:], g=gsb[:, 0:1], r=rs[:]:
                          e.scalar_tensor_tensor(out=o, in0=i, scalar=g, in1=r, op0=ALU.mult, op1=ALU.mult),
                          [prtok, rstok, gt], [dtok])
        for blk in range(NT // 128):
            pr, prtok = ps_pr.next()
            for kc in range(8):
                mm(P, pr[:, 0:128], h_all[:, kc, blk * 128:(blk + 1) * 128], wv[:, kc, :], kc == 0, kc == 7,
                   [wvtok, h_tok[blk // 4][kc]], [prtok])
            for hh_ in range(2):
                P.add("act", lambda e, o=v_sb[:, hh_, blk, 64 * hh_:64 * hh_ + 64], i=pr[:, 64 * hh_:64 * hh_ + 64]:
                      e.activation(out=o, in_=i, func=AF.Identity), [prtok], [vtok])
        for qp in range(16):
            es = nat_es(qp)
            ne = len(es)
            pix = nat_pidx(qp)
            pts = []
            for hh in range(2):
                sc, sctok = ps_sc.next()
                for idx, e_ in enumerate(es):
                    kb = qp + e_
                    mm(P, sc[:, idx * 128:(idx + 1) * 128], k_sb[:, kb * 128:(kb + 1) * 128],
                       q_sb[:, hh, qp * 128:(qp + 1) * 128], idx % 4 == 0, False, [ktok, qtok], [sctok], skip=True)
                boff = (hh * NE + es[0] + 1) * 128
                poff = (pix * NE + es[0] + 1) * 128
                for (c0, c1) in ((0, 512), (512, ne * 128)):
                    mm(P, sc[:, c0:c1], ident[:], bb[:, boff + c0:boff + c1], False, False, [bbtok, ctok], [sctok], skip=True)
                    mm(P, sc[:, c0:c1], ohk_bf[:], pen_bf[:, poff + c0:poff + c1], False, True, [ohktok, pentok], [sctok], skip=True)
                pt, pttok = p_r.next()
                for (c0, c1) in ((0, 512), (512, ne * 128)):
                    P.add("act", lambda e, o=pt[:, c0:c1], i=sc[:, c0:c1]: e.activation(out=o, in_=i, func=AF.Exp),
                          [sctok], [pttok])
                pts.append((pt, pttok))
            pv, pvtok = ps_pv.next()
            n_mm = 2 * ne
            cnt = 0
            for hh in range(2):
                pt, pttok = pts[hh]
                for idx, e_ in enumerate(es):
                    kb = qp + e_
                    mm(P, pv[:, 0:128], v_sb[:, hh, kb, :], pt[:, idx * 128:(idx + 1) * 128], cnt == 0, cnt == n_mm - 1,
                       [vtok, pttok], [pvtok])
                    cnt += 1
            cnt = 0
            for hh in range(2):
                pt, pttok = pts[hh]
                for idx, e_ in enumerate(es):
                    mm(P, pv[:, 128:256], onesz[:, hh, :], pt[:, idx * 128:(idx + 1) * 128], cnt == 0, cnt == n_mm - 1,
                       [pttok, ctok], [pvtok])
                    cnt += 1
            if C.get("dbg") is not None and hp == 0 and qp == 2:
                dbg_dump(P, C, 0, pts[0][0][:, 0:128], [pts[0][1]])
                dbg_dump(P, C, 1, pts[0][0][:, 128:256], [pts[0][1]])
                dbg_dump(P, C, 2, q_sb[:, 0, 256:384], [qtok])
                dbg_dump(P, C, 3, q_sb[:, 1, 256:384], [qtok])
                dbg_dump(P, C, 4, k_sb[:, 256:384], [ktok])
                dbg_dump(P, C, 5, k_sb[:, 384:512], [ktok])
                dbg_dump(P, C, 6, v_sb[:, 0, 2, :], [vtok])
                dbg_dump(P, C, 7, v_sb[:, 1, 2, :], [vtok])
            rd, rdtok = fr.next()
            P.add("dve", lambda e, o=rd[:, 0:128], i=pv[:, 128:256]: e.reciprocal(out=o, in_=i), [pvtok], [rdtok])
            P.add("dve", lambda e, o=attn_all[:, hp, qp * 128:(qp + 1) * 128], a=pv[:, 0:128], b=rd[:, 0:128]:
                  e.tensor_tensor(out=o, in0=a, in1=b, op=ALU.mult), [pvtok, rdtok], [attn_tok[hp]])
            if C.get("dbg") is not None and hp == 0 and qp == 2:
                dbg_dump(P, C, 8, attn_all[:, 0, 256:384], [attn_tok[0]])
                dbg_dump(P, C, 9, rd[:, 0:128], [rdtok])

    wo_st = Ring(P, 2, [128, 8, 128], F32, "nt_wost")
    wo_r = Ring(P, 2, [128, 8, 128], BF16, "nt_wo")
    w_out_v = w_out.rearrange("(kc p) n -> p kc n", p=128)
    for o in range(8):
        st, sttok = wo_st.next()
        P.dma("sp", st[:], w_out_v[:, :, o * 128:(o + 1) * 128], [], [sttok], sttok)
        wb, wbtok = wo_r.next()
        P.add("pool", lambda e, oo=wb[:], i=st[:]: e.tensor_copy(out=oo, in_=i), [sttok], [wbtok])
        for tt in range(4):
            t0 = tt * 512
            pt, pttok = ps_pr.next()
            for kc in range(8):
                mm(P, pt[:], wb[:, kc, :], attn_all[:, kc, t0:t0 + 512], kc == 0, kc == 7, [wbtok, attn_tok[kc]], [pttok])
            xt, xtok = fr.next()
            P.dma("sp", xt[:], x_in[o * 128:(o + 1) * 128, HN + t0:HN + t0 + 512], [], [xtok], xtok)
            P.add("dve", lambda e, oo=xt[:], a=pt[:]: e.tensor_tensor(out=oo, in0=a, in1=oo, op=ALU.add),
                  [pttok, xtok], [xtok])
            P.dma("sp", x_out[o * 128:(o + 1) * 128, t0:t0 + 512], xt[:], [xtok], [], xtok, is_out=is_out)


NB = 32
RB = NB * 128
NBT = 48
RH = 4
LN16 = -2.772588722239781


def emit_ret(P, C, nc, x_in, x_out, g1c, w_in, cosT, sinT, l2d, gng, w_out, hm, is_out=False):
    fr = C["fr"]
    ctok = C["ctok"]
    g_col = P.sb([128, 8], F32, "rt_g")
    gtok = load_cols(P, C, g_col, g1c)
    gng_sb = P.sb([128, 16], F32, "rt_gng")
    gngtok = load_cols(P, C, gng_sb, gng)
    hm_sb = P.sb([128, 2], F32, "rt_hm")
    hmtok = load_cols(P, C, hm_sb, hm)
    ones_f = P.sb([128, 128], F32, "rt_ones_f")
    P.add("pool", lambda e: e.memset(ones_f[:], 1.0), [], [ctok])
    lg = P.sb([128, 8], F32, "rt_lg")
    nlg = P.sb([128, 8], F32, "rt_nlg")
    one_col = P.sb([128, 1], F32, "rt_one")
    ln16_col = P.sb([128, 1], F32, "rt_ln16")
    P.add("pool", lambda e: e.memset(one_col[:], 1.0), [], [ctok])
    P.add("pool", lambda e: e.memset(ln16_col[:], LN16), [], [ctok])
    lgtok = load_cols(P, C, lg, l2d)
    P.add("act", lambda e: e.activation(out=lg[:], in_=lg[:], func=AF.Exp, scale=-0.6931471805599453), [lgtok], [lgtok])
    P.add("act", lambda e: e.activation(out=lg[:], in_=lg[:], func=AF.Ln, bias=one_col[:, 0:1], scale=-1.0),
          [lgtok, ctok], [lgtok])
    P.add("dve", lambda e: e.tensor_scalar(out=nlg[:], in0=lg[:], scalar1=-1.0, scalar2=None, op0=ALU.mult),
          [lgtok], [lgtok])
    d1i = P.sb([128, 128], mybir.dt.int32, "rt_d1i")
    d1 = P.sb([128, 128], F32, "rt_d1")
    dbi = P.sb([128, NBT], mybir.dt.int32, "rt_dbi")
    dbf = P.sb([128, NBT], F32, "rt_dbf")
    dri = P.sb([128, NBT], mybir.dt.int32, "rt_dri")
    drf = P.sb([128, NBT], F32, "rt_drf")
    itok = Tok("iota")
    P.add("pool", lambda e: e.iota(d1i[:], pattern=[[1, 128]], base=0, channel_multiplier=-1), [], [itok])
    P.add("pool", lambda e: e.iota(dbi[:], pattern=[[128, NBT]], base=0, channel_multiplier=0), [], [itok])
    P.add("pool", lambda e: e.iota(dri[:], pattern=[[-128, NBT]], base=128 * (NBT - 1), channel_multiplier=0), [], [itok])
    P.add("dve", lambda e: e.tensor_copy(out=drf[:], in_=dri[:]), [itok], [itok])
    P.add("dve", lambda e: e.tensor_copy(out=d1[:], in_=d1i[:]), [itok], [itok])
    P.add("dve", lambda e: e.tensor_copy(out=dbf[:], in_=dbi[:]), [itok], [itok])

    h_dram = nc.dram_tensor("rt_h_dram", [D, RB], BF16, kind="Internal").ap()
    gT_dram = nc.dram_tensor("rt_gT_dram", [2 * D, T], BF16, kind="Internal").ap()
    h_dv = h_dram.rearrange("(c p) t -> p c t", p=128)
    gT_dv = gT_dram.rearrange("(c p) t -> p c t", p=128)
    bigr = Ring(P, 2, [128, 16, 512], BF16, "rt_big")
    ps_a = Ring(P, 2, [128, 512], F32, "ps_a", psum=True)
    ps_s = Ring(P, 2, [128, 512], F32, "ps_s", psum=True)
    ps_o = Ring(P, 2, [128, 512], F32, "ps_o", psum=True)
    ntile = RB // 512
    hd_tok = [Tok(f"hd{i}") for i in range(ntile)]
    for ti in range(ntile):
        hb, hbtok = bigr.next()
        emit_rmsnorm(P, C, x_in, g_col, gtok, hb, [[hbtok] * 8], ps_a, [(ti * 512, 512)], two_pass=True, hcol=[0])
        P.dma("sp", h_dv[:, :, ti * 512:(ti + 1) * 512], hb[:, 0:8, :], [hbtok], [hd_tok[ti]], hbtok)

    k_fm = P.sb([128, 2, RB], BF16, "rt_k")
    v_tok = P.sb([128, NB, 512], BF16, "rt_v")
    q_fm = P.sb([128, 2, T], BF16, "rt_q")
    o_fm = P.sb([128, 4, T], F32, "rt_o")
    ktok, vtok, qtok = Tok("k"), Tok("v"), Tok("q")
    otok = [Tok(f"o{i}") for i in range(16)]
    wq = P.sb([128, 8, 256], BF16, "rt_wq")
    wk = P.sb([128, 8, 256], BF16, "rt_wk")
    wv = P.sb([128, 8, 512], BF16, "rt_wv")
    wg = P.sb([128, 8, 512], BF16, "rt_wg")
    wtok = Tok("w")
    wqtok, wgtok = Tok("wqkv"), Tok("wg")
    wst = Ring(P, 2, [128, 8, 128], F32, "rt_wst")
    w_v = w_in.rearrange("(kc p) n -> p kc n", p=128)
    gf = P.sb([128, 128], F32, "rt_gf")
    gb = P.sb([128, 128], F32, "rt_gb")
    gd = P.sb([128, 128], F32, "rt_gd")
    gd2 = P.sb([128, 128], F32, "rt_gd2")
    sf = P.sb([128, NBT], F32, "rt_sf")
    sbk = P.sb([128, NBT], F32, "rt_sb")
    sfr = P.sb([128, NBT], F32, "rt_sfr")
    so = P.sb([128, 256], F32, "rt_so")
    go = P.sb([128, 128], F32, "rt_go")
    gtk = Tok("G")
    p_r = Ring(P, 10, [128, 128], BF16, "rt_p")
    gT_tok = [[Tok(f"gT{h}_{t}") for t in range(4)] for h in range(RH)]

    for h in range(RH):
        def load_w(hh_, which):
            segs = []
            if which == "qkv":
                segs += [(wq, i_ * 128, hh_ * 256 + i_ * 128, wqtok) for i_ in range(2)]
                segs += [(wk, i_ * 128, D + hh_ * 256 + i_ * 128, wqtok) for i_ in range(2)]
                segs += [(wv, i_ * 128, 2 * D + hh_ * 512 + i_ * 128, wqtok) for i_ in range(4)]
            else:
                segs += [(wg, i_ * 128, 4 * D + hh_ * 512 + i_ * 128, wgtok) for i_ in range(4)]
            for (dst, dcol, scol, tk) in segs:
                st, sttok = wst.next()
                P.dma("sp", st[:], w_v[:, :, scol:scol + 128], [], [sttok], sttok)
                P.add("pool", lambda e, o=dst[:, :, dcol:dcol + 128], i=st[:]: e.tensor_copy(out=o, in_=i), [sttok], [tk])

        if h == 0:
            load_w(0, "qkv")
        load_w(h, "g")
        P.add("act", lambda e, sc_=lg[:, h:h + 1]: e.activation(out=gf[:], in_=d1[:], func=AF.Exp, bias=ln16_col[:, 0:1], scale=sc_),
              [itok, lgtok, ctok], [gtk])
        P.add("act", lambda e, sc_=nlg[:, 4 + h:5 + h]: e.activation(out=gb[:], in_=d1[:], func=AF.Exp, bias=ln16_col[:, 0:1], scale=sc_),
              [itok, lgtok, ctok], [gtk])
        P.add("pool", lambda e: e.affine_select(out=gd[:], in_=gf[:], pattern=[[1, 128]], compare_op=ALU.is_ge, fill=0.0,
                                                base=0, channel_multiplier=-1), [gtk], [gtk])
        P.add("pool", lambda e: e.affine_select(out=gd2[:], in_=gb[:], pattern=[[-1, 128]], compare_op=ALU.is_gt, fill=0.0,
                                                base=0, channel_multiplier=1), [gtk], [gtk])
        P.add("pool", lambda e: e.tensor_tensor(out=gd[:], in0=gd[:], in1=gd2[:], op=ALU.add), [gtk], [gtk])
        P.add("act", lambda e, sc_=lg[:, h:h + 1]: e.activation(out=sf[:], in_=dbf[:], func=AF.Exp, scale=sc_), [itok, lgtok], [gtk])
        P.add("act", lambda e, sc_=lg[:, 4 + h:5 + h]: e.activation(out=sbk[:], in_=dbf[:], func=AF.Exp, scale=sc_), [itok, lgtok], [gtk])
        P.add("act", lambda e, sc_=lg[:, h:h + 1]: e.activation(out=sfr[:], in_=drf[:], func=AF.Exp, scale=sc_), [itok, lgtok], [gtk])
        P.add("dve", lambda e: e.tensor_scalar(out=go[:], in0=gb[:], scalar1=hm_sb[:, 1:2], scalar2=None, op0=ALU.mult), [gtk, hmtok], [gtk])
        P.add("dve", lambda e: e.scalar_tensor_tensor(out=go[:], in0=gf[:], scalar=hm_sb[:, 0:1], in1=go[:], op0=ALU.mult, op1=ALU.add),
              [gtk, hmtok], [gtk])
        for i_ in range(16):
            P.add("dve", lambda e, o=so[:, i_ * 16:(i_ + 1) * 16], a=sbk[:, 16 - i_:32 - i_]:
                  e.tensor_scalar(out=o, in0=a, scalar1=hm_sb[:, 1:2], scalar2=None, op0=ALU.mult), [gtk, hmtok], [gtk])
            P.add("dve", lambda e, o=so[:, i_ * 16:(i_ + 1) * 16], a=sfr[:, 31 - i_:47 - i_]:
                  e.scalar_tensor_tensor(out=o, in0=a, scalar=hm_sb[:, 0:1], in1=o, op0=ALU.mult, op1=ALU.add), [gtk, hmtok], [gtk])

        for ti in range(ntile):
            hb, hbtok = bigr.next()
            P.dma("sp", hb[:, 0:8, :], h_dv[:, :, ti * 512:(ti + 1) * 512], [hd_tok[ti]], [hbtok], hbtok)
            cs, cstok = fr.next()
            sn, sntok = fr.next()
            P.dma("sp", cs[:], cosT[:, ti * 512:(ti + 1) * 512], [], [cstok], cstok)
            P.dma("sp", sn[:], sinT[:, ti * 512:(ti + 1) * 512], [], [sntok], sntok)
            todo = [(wk, k_fm, ktok, ti * 512)]
            if ti < 4:
                todo.append((wq, q_fm, qtok, ti * 512))
            for (wmat, dst, dtok, dcol) in todo:
                p1, p1tok = ps_a.next()
                p2, p2tok = ps_a.next()
                for dc, (pt, pttok) in enumerate(((p1, p1tok), (p2, p2tok))):
                    for kc in range(8):
                        mm(P, pt[:], wmat[:, kc, dc * 128:(dc + 1) * 128], hb[:, kc, :], kc == 0, kc == 7, [wqtok, hbtok], [pttok])
                t1, t1tok = fr.next()
                t2, t2tok = fr.next()
                P.add("dve", lambda e, o=t1[:], a=p1[:], b=cs[:]: e.tensor_tensor(out=o, in0=a, in1=b, op=ALU.mult), [p1tok, cstok], [t1tok])
                P.add("dve", lambda e, o=t2[:], a=p2[:], b=sn[:]: e.tensor_tensor(out=o, in0=a, in1=b, op=ALU.mult), [p2tok, sntok], [t2tok])
                P.add("pool", lambda e, o=dst[:, 0, dcol:dcol + 512], a=t1[:], b=t2[:]: e.tensor_tensor(out=o, in0=a, in1=b, op=ALU.subtract),
                      [t1tok, t2tok], [dtok])
                P.add("dve", lambda e, o=t1[:], a=p1[:], b=sn[:]: e.tensor_tensor(out=o, in0=a, in1=b, op=ALU.mult), [p1tok, sntok], [t1tok])
                P.add("dve", lambda e, o=t2[:], a=p2[:], b=cs[:]: e.tensor_tensor(out=o, in0=a, in1=b, op=ALU.mult), [p2tok, cstok], [t2tok])
                P.add("pool", lambda e, o=dst[:, 1, dcol:dcol + 512], a=t1[:], b=t2[:]: e.tensor_tensor(out=o, in0=a, in1=b, op=ALU.add),
                      [t1tok, t2tok], [dtok])
            for bl in range(4):
                blk = ti * 4 + bl
                pv, pvtok = ps_a.next()
                for kc in range(8):
                    mm(P, pv[:], hb[:, kc, bl * 128:(bl + 1) * 128], wv[:, kc, :], kc == 0, kc == 7, [wqtok, hbtok], [pvtok])
                P.add("act", lambda e, o=v_tok[:, blk, :], i=pv[:]: e.activation(out=o, in_=i, func=AF.Identity), [pvtok], [vtok])

        if h + 1 < RH:
            load_w(h + 1, "qkv")
        NG = NB // 4

        def scores(i, cg):
            sc, sctok = ps_s.next()
            for sub in range(4):
                c = cg * 4 + sub
                for dc in range(2):
                    mm(P, sc[:, sub * 128:(sub + 1) * 128], k_fm[:, dc, c * 128:(c + 1) * 128], q_fm[:, dc, i * 128:(i + 1) * 128],
                       sub == 0 and dc == 0, dc == 1, [ktok, qtok], [sctok], skip=True)
            return sc, sctok

        seq = [(i, cg) for i in range(16) for cg in range(NG)]
        cur = scores(*seq[0])
        po, potok = None, None
        for si_, (i, cg) in enumerate(seq):
            sc, sctok = cur
            if cg == 0:
                po, potok = ps_o.next()
            pts = []
            for sub in range(4):
                c = cg * 4 + sub
                dl = i - c
                pt, pttok = p_r.next()
                if c >= 16:
                    P.add("dve", lambda e, o=pt[:], a=sc[:, sub * 128:(sub + 1) * 128], s_=so[:, i * 16 + c - 16:i * 16 + c - 15]:
                          e.scalar_tensor_tensor(out=o, in0=a, scalar=s_, in1=go[:], op0=ALU.mult, op1=ALU.mult), [sctok, gtk], [pttok])
                elif dl > 0:
                    P.add("dve", lambda e, o=pt[:], a=sc[:, sub * 128:(sub + 1) * 128], s_=sf[:, dl:dl + 1]:
                          e.scalar_tensor_tensor(out=o, in0=a, scalar=s_, in1=gf[:], op0=ALU.mult, op1=ALU.mult), [sctok, gtk], [pttok])
                elif dl < 0:
                    P.add("dve", lambda e, o=pt[:], a=sc[:, sub * 128:(sub + 1) * 128], s_=sbk[:, -dl:-dl + 1]:
                          e.scalar_tensor_tensor(out=o, in0=a, scalar=s_, in1=gb[:], op0=ALU.mult, op1=ALU.mult), [sctok, gtk], [pttok])
                else:
                    P.add("dve", lambda e, o=pt[:], a=sc[:, sub * 128:(sub + 1) * 128]:
                          e.tensor_tensor(out=o, in0=a, in1=gd[:], op=ALU.mult), [sctok, gtk], [pttok])
                pts.append((pt, pttok, c))
            if si_ + 1 < len(seq):
                cur = scores(*seq[si_ + 1])
            for (pt, pttok, c) in pts:
                for ec in range(4):
                    mm(P, po[:, ec * 128:(ec + 1) * 128], v_tok[:, c, ec * 128:(ec + 1) * 128], pt[:],
                       c == 0 and ec == 0, c == NB - 1, [vtok, pttok], [potok], skip=True)
            if cg == NG - 1:
                P.add("act", lambda e, o=o_fm[:, :, i * 128:(i + 1) * 128], a=po[:].rearrange("p (a b) -> p a b", a=4):
                      e.activation(out=o, in_=a, func=AF.Identity), [potok], [otok[i]])

        for tt in range(4):
            t0 = tt * 512
            ots = [otok[tt * 4 + b_] for b_ in range(4)]
            p1, p1tok = ps_a.next()
            p2, p2tok = ps_a.next()
            for ec in range(4):
                mm(P, p1[:], ones_f[:], o_fm[:, ec, t0:t0 + 512], ec == 0, ec == 3, ots + [ctok], [p1tok])
            for ec in range(4):
                sq, sqtok = fr.next()
                P.add("act", lambda e, o=sq[:], a=o_fm[:, ec, t0:t0 + 512]: e.activation(out=o, in_=a, func=AF.Square), ots, [sqtok])
                mm(P, p2[:], ones_f[:], sq[:], ec == 0, ec == 3, [sqtok, ctok], [p2tok])
            mu, mutok = C["rsr"].next()
            P.add("act", lambda e, o=mu[:], a=p1[:]: e.activation(out=o, in_=a, func=AF.Identity, scale=1.0 / 512), [p1tok], [mutok])
            rs, rstok = C["rsr"].next()
            P.add("dve", lambda e, o=rs[:], a=mu[:]: e.tensor_tensor(out=o, in0=a, in1=a, op=ALU.mult), [mutok], [rstok])
            P.add("dve", lambda e, o=rs[:], a=p2[:]: e.scalar_tensor_tensor(out=o, in0=a, scalar=1.0 / 512, in1=o, op0=ALU.mult, op1=ALU.subtract),
                  [p2tok, rstok], [rstok])
            P.add("act", lambda e, o=rs[:]: e.activation(out=o, in_=o, func=AF.Sqrt, bias=C["eps_col"][:, 0:1], scale=1.0), [rstok, ctok], [rstok])
            P.add("dve", lambda e, o=rs[:]: e.reciprocal(out=o, in_=o), [rstok], [rstok])
            hb, hbtok = bigr.next()
            P.dma("sp", hb[:, 0:8, :], h_dv[:, :, t0:t0 + 512], [hd_tok[tt]], [hbtok], hbtok)
            gt_sb, gttok = bigr.next()
            for ec in range(4):
                pg, pgtok = ps_a.next()
                for kc in range(8):
                    mm(P, pg[:], wg[:, kc, ec * 128:(ec + 1) * 128], hb[:, kc, :], kc == 0, kc == 7, [wgtok, hbtok], [pgtok])
                sg, sgtok = fr.next()
                P.add("act", lambda e, o=sg[:], a=pg[:]: e.activation(out=o, in_=a, func=AF.Silu), [pgtok], [sgtok])
                dd, ddtok = fr.next()
                P.add("pool", lambda e, o=dd[:], a=o_fm[:, ec, t0:t0 + 512], m=mu[:]: e.tensor_tensor(out=o, in0=a, in1=m, op=ALU.subtract),
                      ots + [mutok], [ddtok])
                P.add("dve", lambda e, o=dd[:], r=rs[:]: e.tensor_tensor(out=o, in0=o, in1=r, op=ALU.mult), [ddtok, rstok], [ddtok])
                P.add("dve", lambda e, o=gt_sb[:, ec, :], a=dd[:], g_=gng_sb[:, h * 4 + ec:h * 4 + ec + 1], s_=sg[:]:
                      e.scalar_tensor_tensor(out=o, in0=a, scalar=g_, in1=s_, op0=ALU.mult, op1=ALU.mult), [ddtok, sgtok, gngtok], [gttok])
            P.dma("sp", gT_dv[:, h * 4:(h + 1) * 4, t0:t0 + 512], gt_sb[:, 0:4, :], [gttok], [gT_tok[h][tt]], gttok)

    wo_bufs = [wq[:].rearrange("p a b -> p (a b)").rearrange("p (j n) -> p j n", j=16),
               wk[:].rearrange("p a b -> p (a b)").rearrange("p (j n) -> p j n", j=16)]
    w_out_v = w_out.rearrange("(j p) n -> p j n", p=128)
    for o in range(8):
        wb, wbtok = wo_bufs[o % 2], wqtok
        for half in range(2):
            st, sttok = wst.next()
            P.dma("sp", st[:], w_out_v[:, half * 8:(half + 1) * 8, o * 128:(o + 1) * 128], [], [sttok], sttok)
            P.add("pool", lambda e, oo=wb[:, half * 8:(half + 1) * 8, :], i=st[:]: e.tensor_copy(out=oo, in_=i), [sttok], [wbtok])
        for tt in range(4):
            t0 = tt * 512
            gb_, gbtok = bigr.next()
            P.dma("sp", gb_[:], gT_dv[:, :, t0:t0 + 512], [gT_tok[h_][tt] for h_ in range(RH)], [gbtok], gbtok)
            pt, pttok = ps_a.next()
            for j in range(16):
                mm(P, pt[:], wb[:, j, :], gb_[:, j, :], j == 0, j == 15, [wbtok, gbtok], [pttok])
            xt, xtok = fr.next()
            P.dma("sp", xt[:], x_in[o * 128:(o + 1) * 128, t0:t0 + 512], [], [xtok], xtok)
            P.add("dve", lambda e, oo=xt[:], a=pt[:]: e.tensor_tensor(out=oo, in0=a, in1=oo, op=ALU.add), [pttok, xtok], [xtok])
            P.dma("sp", x_out[o * 128:(o + 1) * 128, t0:t0 + 512], xt[:], [xtok], [], xtok, is_out=is_out)


def build_ffn_prog():
    nc = bass.Bass("TRN2", target_bir_lowering=False)
    x_in = nc.dram_tensor("x_in", [D, T + 2], F32, kind="ExternalInput").ap()
    g2c = nc.dram_tensor("g2c", [128, 8], F32, kind="ExternalInput").ap()
    w_up = nc.dram_tensor("w_up", [D, 2 * FFN], F32, kind="ExternalInput").ap()
    dww = nc.dram_tensor("dww", [128, 3 * 2 * NH], F32, kind="ExternalInput").ap()
    dwb = nc.dram_tensor("dwb", [128, 2 * NH], F32, kind="ExternalInput").ap()
    w_down = nc.dram_tensor("w_down", [FFN, D], F32, kind="ExternalInput").ap()
    x_out = nc.dram_tensor("x_out", [D, T], F32, kind="ExternalOutput").ap()
    with contextlib.ExitStack() as stack:
        P = Prog(nc, stack)
        C = make_common(P)
        emit_ffn(P, C, x_in, x_out, g2c, w_up, dww, dwb, w_down, is_out=True)
        P.emit()
        print("ffn prog stats", P.stats)
    return nc


def cols(v, n):
    return np.ascontiguousarray(v.reshape(n, 128).T)


def shard_tokens_fm(xfull, halo):
    out = []
    for c in range(NCORES):
        b, hf = c // 2, c % 2
        lo, hi = hf * T - halo, (hf + 1) * T + halo
        buf = np.zeros((T + 2 * halo, D), np.float32)
        slo, shi = max(lo, 0), min(hi, SEQ)
        buf[slo - lo:shi - lo] = xfull[b, slo:shi]
        out.append(np.ascontiguousarray(buf.T))
    return out


def unshard_tokens_fm(outs):
    x = np.empty((BATCH, SEQ, D), np.float32)
    for c in range(NCORES):
        b, hf = c // 2, c % 2
        x[b, hf * T:(hf + 1) * T] = outs[c].T
    return x


def ffn_inmaps(x, i, norm2_g, ffn_w_up, ffn_dw_w, ffn_dw_b, ffn_w_down):
    xs = shard_tokens_fm(x, 1) if x is not None else None
    dww = np.concatenate([cols(ffn_dw_w[i, k], 2 * NH) for k in range(3)], axis=1)
    common = {
        "g2c": cols(norm2_g[i], 8),
        "w_up": np.ascontiguousarray(ffn_w_up[i]),
        "dww": np.ascontiguousarray(dww),
        "dwb": cols(ffn_dw_b[i], 2 * NH),
        "w_down": np.ascontiguousarray(ffn_w_down[i]),
    }
    return [dict(common, x_in=xs[c]) if xs is not None else dict(common) for c in range(NCORES)]


def build_conv_prog():
    nc = bass.Bass("TRN2", target_bir_lowering=False)
    dt = lambda name, shape, kind="ExternalInput": nc.dram_tensor(name, shape, F32, kind=kind).ap()
    x_in = dt("x_in", [D, T + 2 * HC])
    mask = dt("mask", [128, 2 * HC])
    g1c = dt("g1c", [128, 8])
    w_in = dt("w_in", [D, 2 * D])
    b_in = dt("b_in", [128, 16])
    dw_w = dt("dw_w", [128, CW * 8])
    dw_b = dt("dw_b", [128, 8])
    ln_g = dt("ln_g", [128, 8])
    ln_b = dt("ln_b", [128, 8])
    w_out = dt("w_out", [D, D])
    x_out = dt("x_out", [D, T], "ExternalOutput")
    with contextlib.ExitStack() as stack:
        P = Prog(nc, stack)
        C = make_common(P, nfr=12)
        emit_conv(P, C, x_in, x_out, mask, g1c, w_in, b_in, dw_w, dw_b, ln_g, ln_b, w_out, is_out=True)
        P.emit()
        print("conv prog stats", P.stats)
    return nc


def conv_inmaps(x, j, g1, conv_w_in, conv_b_in, conv_dw_w, conv_dw_b, conv_ln_g, conv_ln_b, conv_w_out):
    xs = shard_tokens_fm(x, HC) if x is not None else None
    dww = np.concatenate([cols(conv_dw_w[j, k], 8) for k in range(CW)], axis=1)
    common = {
        "g1c": cols(g1, 8),
        "w_in": np.ascontiguousarray(conv_w_in[j]),
        "b_in": cols(conv_b_in[j], 16),
        "dw_w": np.ascontiguousarray(dww),
        "dw_b": cols(conv_dw_b[j], 8),
        "ln_g": cols(conv_ln_g[j], 8),
        "ln_b": cols(conv_ln_b[j], 8),
        "w_out": np.ascontiguousarray(conv_w_out[j]),
    }
    maps = []
    for c in range(NCORES):
        hf = c % 2
        m = np.ones((128, 2 * HC), np.float32)
        if hf == 0:
            m[:, :HC] = 0.0
        else:
            m[:, HC:] = 0.0
        maps.append(dict(common, x_in=xs[c], mask=m) if xs is not None else dict(common, mask=m))
    return maps


def dbg_dump(P, C, slot, src, toks):
    t, ttok = C["dbgr"].next()
    P.add("act", lambda e: e.activation(out=t[:], in_=src, func=AF.Identity), toks, [ttok])
    P.dma("sp", C["dbg"][:, slot * 128:(slot + 1) * 128], t[:], [ttok], [], ttok, is_out=True)


def build_nat_prog(debug=False):
    nc = bass.Bass("TRN2", target_bir_lowering=False)
    dt = lambda name, shape, kind="ExternalInput": nc.dram_tensor(name, shape, F32, kind=kind).ap()
    x_in = dt("x_in", [D, T + 2 * HN])
    g1c = dt("g1c", [128, 8])
    w_qkv = dt("w_qkv", [D, 3 * D])
    qg = dt("qg", [128, 2])
    kg = dt("kg", [128, 1])
    bias = dt("bias", [8, 128, 2 * NE * 128])
    pen = dt("pen", [2, 5 * NE * 128])
    ohk = dt("ohk", [2, 128])
    bd = dt("bd", [128, 128])
    w_out = dt("w_out", [D, D])
    x_out = dt("x_out", [D, T], "ExternalOutput")
    with contextlib.ExitStack() as stack:
        P = Prog(nc, stack)
        C = make_common(P, nfr=6)
        if debug:
            C["dbg"] = dt("dbg", [128, 16 * 128], "ExternalOutput")
            C["dbgr"] = Ring(P, 2, [128, 128], F32, "dbgr")
        emit_nat(P, C, x_in, x_out, g1c, w_qkv, qg, kg, bias, pen, ohk, bd, w_out, is_out=True)
        P.emit()
        print("nat prog stats", P.stats)
    return nc


def nat_bias_table(rpb):
    kc = np.arange(64)[:, None]
    qc = np.arange(64)[None, :]
    cs = np.clip(qc - 8, 0, 48)
    win = (kc >= cs) & (kc < cs + 16)
    dc = np.clip(kc - qc + 15, 0, 30)
    out = np.full((8, 128, 2, NE, 128), NEG, np.float32)
    for hp in range(8):
        for hh in range(2):
            h = 2 * hp + hh
            for ei in range(NE):
                e_ = ei - 1
                for kp in range(2):
                    for qp_ in range(2):
                        dr = 2 * e_ + 3 + kp - qp_
                        if dr < 0 or dr > 14:
                            continue
                        blk = np.where(win, rpb[h, dr][dc], np.float32(NEG))
                        out[hp, kp * 64:(kp + 1) * 64, hh, ei, qp_ * 64:(qp_ + 1) * 64] = blk
    return out.reshape(8, 128, 2 * NE * 128)


def nat_pen_table(hf):
    out = np.full((2, 5, NE, 128), NEG, np.float32)
    for pix, qp in enumerate((0, 1, 7, 14, 15)):
        for ei in range(NE):
            e_ = ei - 1
            for kp in range(2):
                for qp_ in range(2):
                    r = 32 * hf + 2 * qp + qp_
                    kr = 32 * hf + 2 * qp + 2 * e_ - 4 + kp
                    rs = min(max(r - 4, 0), 56)
                    if 0 <= kr < 64 and rs <= kr < rs + 8:
                        out[kp, pix, ei, qp_ * 64:(qp_ + 1) * 64] = 0.0
    return out.reshape(2, 5 * NE * 128)


def nat_qg2(g):
    out = np.zeros((128, 2), np.float32)
    out[0:64, 0] = g
    out[64:128, 1] = g
    return out


def nat_inmaps(x, g1, nat_w_qkv, nat_q_norm_g, nat_k_norm_g, nat_rpb, nat_w_out):
    xs = shard_tokens_fm(x, HN) if x is not None else None
    ohk = np.zeros((2, 128), np.float32)
    ohk[0, :64] = 1.0
    ohk[1, 64:] = 1.0
    common = {
        "g1c": cols(g1, 8),
        "w_qkv": np.ascontiguousarray(nat_w_qkv[0]),
        "qg": nat_qg2(nat_q_norm_g[0]),
        "kg": np.ascontiguousarray(np.tile(nat_k_norm_g[0], 2)[:, None]),
        "bias": nat_bias_table(nat_rpb[0]),
        "ohk": ohk,
        "bd": np.kron(np.eye(2, dtype=np.float32), np.ones((64, 64), np.float32)),
        "w_out": np.ascontiguousarray(nat_w_out[0]),
    }
    pens = [nat_pen_table(0), nat_pen_table(1)]
    return [dict(common, x_in=xs[c], pen=pens[c % 2]) if xs is not None else dict(common, pen=pens[c % 2]) for c in range(NCORES)]


def build_ret_prog():
    nc = bass.Bass("TRN2", target_bir_lowering=False)
    dt = lambda name, shape, kind="ExternalInput": nc.dram_tensor(name, shape, F32, kind=kind).ap()
    x_in = dt("x_in", [D, RB])
    g1c = dt("g1c", [128, 8])
    w_in = dt("w_in", [D, 6 * D])
    cosT = dt("cosT", [128, RB])
    sinT = dt("sinT", [128, RB])
    l2d = dt("l2d", [128, 8])
    gng = dt("gng", [128, 16])
    w_out = dt("w_out", [2 * D, D])
    hm = dt("hm", [128, 2])
    x_out = dt("x_out", [D, T], "ExternalOutput")
    with contextlib.ExitStack() as stack:
        P = Prog(nc, stack)
        C = make_common(P, nfr=8)
        emit_ret(P, C, nc, x_in, x_out, g1c, w_in, cosT, sinT, l2d, gng, w_out, hm, is_out=True)
        P.emit()
        print("ret prog stats", P.stats)
    return nc


def ret_rope_tables(hf):
    theta = (1.0 / (np.float32(10000.0) ** np.linspace(0.0, 1.0, 128, dtype=np.float32))).astype(np.float32)
    u = np.arange(RB)
    pos = np.where(u < T, T * hf + u, T * (1 - hf) + (u - T)).astype(np.float32)
    ang = (theta[:, None] * pos[None, :]).astype(np.float32)
    return np.cos(ang).astype(np.float32), np.sin(ang).astype(np.float32)


def ret_inmaps(x, g1, ret_w_in, ret_log2_inv_decay, ret_gn_g, ret_w_out):
    common = {
        "g1c": cols(g1, 8),
        "w_in": np.ascontiguousarray(ret_w_in[0]),
        "l2d": np.ascontiguousarray(np.tile(ret_log2_inv_decay[0].reshape(1, 8), (128, 1))),
        "gng": cols(ret_gn_g[0], 16),
        "w_out": np.ascontiguousarray(ret_w_out[0]),
    }
    tabs = [ret_rope_tables(0), ret_rope_tables(1)]
    maps = []
    for c in range(NCORES):
        b, hf = c // 2, c % 2
        hm = np.zeros((128, 2), np.float32)
        hm[:, 0] = float(hf)
        hm[:, 1] = float(1 - hf)
        if x is None:
            maps.append(dict(common, cosT=tabs[hf][0], sinT=tabs[hf][1], hm=hm))
            continue
        buf = np.concatenate([x[b, hf * T:(hf + 1) * T], x[b, (1 - hf) * T:(2 - hf) * T]], axis=0)
        maps.append(dict(common, x_in=np.ascontiguousarray(buf.T), cosT=tabs[hf][0], sinT=tabs[hf][1], hm=hm))
    return maps


STAGES = [("c0", "conv", HC), ("f0", "ffn", 1), ("n1", "nat", HN), ("f1", "ffn", 1),
          ("r2", "ret", 2048), ("f2", "ffn", 1), ("c3", "conv", HC), ("f3", "ffn", 1)]
STAGE_IN = {
    "conv": [("mask", [128, 2 * HC]), ("g1c", [128, 8]), ("w_in", [D, 2 * D]), ("b_in", [128, 16]), ("dw_w", [128, CW * 8]),
             ("dw_b", [128, 8]), ("ln_g", [128, 8]), ("ln_b", [128, 8]), ("w_out", [D, D])],
    "ffn": [("g2c", [128, 8]), ("w_up", [D, 2 * FFN]), ("dww", [128, 3 * 2 * NH]), ("dwb", [128, 2 * NH]), ("w_down", [FFN, D])],
    "nat": [("g1c", [128, 8]), ("w_qkv", [D, 3 * D]), ("qg", [128, 2]), ("kg", [128, 1]), ("bias", [8, 128, 2 * NE * 128]),
            ("pen", [2, 5 * NE * 128]), ("ohk", [2, 128]), ("bd", [128, 128]), ("w_out", [D, D])],
    "ret": [("g1c", [128, 8]), ("w_in", [D, 6 * D]), ("cosT", [128, RB]), ("sinT", [128, RB]), ("l2d", [128, 8]),
            ("gng", [128, 16]), ("w_out", [2 * D, D]), ("hm", [128, 2])],
}
STAGE_NFR = {"conv": 12, "ffn": 14, "nat": 6, "ret": 8}


def emit_exchange(P, C, nc, name, x_next, H, hmask_sb, hmtok):
    kw = dict(allow_slow_non_contiguous=True) if H < 8 else {}
    groups = [[0, 1], [2, 3], [4, 5], [6, 7]]
    xv = x_next.rearrange("(c p) t -> p c t", p=128)
    W = min(H, 512)
    hr = Ring(P, 2, [128, 8, W], F32, "xh")

    def fill(src2d, dcol, mi, gtok):
        srcv = src2d.rearrange("(c p) t -> p c t", p=128)
        xt, xtok = hr.next()
        P.dma("sp", xt[:], srcv, [gtok], [xtok], xtok, **kw)
        P.add("dve", lambda e, o=xt[:], m=hmask_sb[:, mi:mi + 1]:
              e.tensor_scalar(out=o, in0=o, scalar1=m, scalar2=None, op0=ALU.mult), [xtok, hmtok], [xtok])
        P.dma("sp", xv[:, :, dcol:dcol + W], xt[:], [xtok], [], xtok, **kw)

    if H == 2048:
        for q in range(4):
            snd = nc.dram_tensor(f"{name}_snd{q}", [D, 512], F32, kind="Internal").ap()
            gath = nc.dram_tensor(f"{name}_gath{q}", [2 * D, 512], F32, kind="Internal").ap()
            stok, gtok, cctok = Tok("snd"), Tok("gath"), Tok("cc")
            P.dma("sp", snd, x_next[:, q * 512:(q + 1) * 512], [], [stok], stok)
            P.coll(lambda e, s_=snd, g_=gath: e.collective_compute("AllGather", ALU.bypass, replica_groups=groups,
                                                                   ins=[s_], outs=[g_]), [stok], [gtok], cctok)
            r0, r0tok = hr.next()
            r1, r1tok = hr.next()
            P.dma("sp", r0[:], gath[0:D, :].rearrange("(c p) t -> p c t", p=128), [gtok], [r0tok], r0tok)
            P.dma("sp", r1[:], gath[D:2 * D, :].rearrange("(c p) t -> p c t", p=128), [gtok], [r1tok], r1tok)
            P.add("dve", lambda e, o=r0[:]: e.tensor_scalar(out=o, in0=o, scalar1=hmask_sb[:, 0:1], scalar2=None, op0=ALU.mult),
                  [r0tok, hmtok], [r0tok])
            P.add("dve", lambda e, o=r0[:], a=r1[:]: e.scalar_tensor_tensor(out=o, in0=a, scalar=hmask_sb[:, 1:2], in1=o,
                                                                           op0=ALU.mult, op1=ALU.add), [r0tok, r1tok, hmtok], [r0tok])
            P.dma("sp", xv[:, :, T + q * 512:T + (q + 1) * 512], r0[:], [r0tok], [], r0tok)
        return
    snd = nc.dram_tensor(name + "_snd", [2 * D, H], F32, kind="Internal").ap()
    gath = nc.dram_tensor(name + "_gath", [4 * D, H], F32, kind="Internal").ap()
    stok, gtok, cctok = Tok("snd"), Tok("gath"), Tok("cc")
    P.dma("sp", snd[0:D, :], x_next[:, H:2 * H], [], [stok], stok, **kw)
    P.dma("sp", snd[D:2 * D, :], x_next[:, T:T + H], [], [stok], stok, **kw)
    P.coll(lambda e: e.collective_compute("AllGather", ALU.bypass, replica_groups=groups,
                                          ins=[snd], outs=[gath]), [stok], [gtok], cctok)
    fill(gath[D:2 * D, :], 0, 0, gtok)
    fill(gath[2 * D:3 * D, :], H + T, 1, gtok)


def build_fused_prog(nst=8):
    stages = STAGES[:nst]
    nc = bass.Bass("TRN2", target_bir_lowering=False)
    dt = lambda name, shape, kind="ExternalInput": nc.dram_tensor(name, shape, F32, kind=kind).ap()
    aps = {}
    for (sn, kind, H) in stages:
        aps[sn] = {k: dt(f"{sn}_{k}", shp) for (k, shp) in STAGE_IN[kind]}
    hmask = dt("hmask", [128, 2])
    bufs = {}
    for si, (sn, kind, H) in enumerate(stages):
        width = RB if kind == "ret" else T + 2 * H
        bufs[sn] = dt(f"{sn}_x_in", [D, width], "ExternalInput" if si == 0 else "Internal")
    y = dt("x_out", [D, T], "ExternalOutput")
    with contextlib.ExitStack() as stack:
        P = Prog(nc, stack)
        for si, (sn, kind, H) in enumerate(stages):
            last = si == len(stages) - 1
            x_in = bufs[sn]
            if last:
                x_out = y
            else:
                nsn, nkind, nH = stages[si + 1]
                x_out = bufs[nsn][:, 0:T] if nkind == "ret" else bufs[nsn][:, nH:nH + T]
            a = aps[sn]
            with contextlib.ExitStack() as st:
                P.stack = st
                P.pfx = sn + "_"
                C = make_common(P, nfr=STAGE_NFR[kind])
                if kind == "conv":
                    emit_conv(P, C, x_in, x_out, a["mask"], a["g1c"], a["w_in"], a["b_in"], a["dw_w"], a["dw_b"], a["ln_g"],
                              a["ln_b"], a["w_out"], is_out=last)
                elif kind == "ffn":
                    emit_ffn(P, C, x_in, x_out, a["g2c"], a["w_up"], a["dww"], a["dwb"], a["w_down"], is_out=last)
                elif kind == "nat":
                    emit_nat(P, C, x_in, x_out, a["g1c"], a["w_qkv"], a["qg"], a["kg"], a["bias"], a["pen"], a["ohk"], a["bd"],
                             a["w_out"], is_out=last)
                else:
                    emit_ret(P, C, nc, x_in, x_out, a["g1c"], a["w_in"], a["cosT"], a["sinT"], a["l2d"], a["gng"], a["w_out"],
                             a["hm"], is_out=last)
            P.barrier()
            if not last:
                with contextlib.ExitStack() as st:
                    P.stack = st
                    P.pfx = sn + "x_"
                    C = make_common(P, nfr=2)
                    hm_sb = P.sb([128, 2], F32, "hmask")
                    hmtok = load_cols(P, C, hm_sb, hmask)
                    emit_exchange(P, C, nc, sn + "x", bufs[nsn], nH, hm_sb, hmtok)
                P.barrier()
        P.stack = stack
        P.pfx = ""
        P.emit()
        print("fused prog stats", P.stats)
    return nc


def fused_inmaps(a):
    per_stage = {}
    per_stage["c0"] = conv_inmaps(a["x"], 0, a["norm1_g"][0], a["conv_w_in"], a["conv_b_in"], a["conv_dw_w"], a["conv_dw_b"],
                                  a["conv_ln_g"], a["conv_ln_b"], a["conv_w_out"])
    per_stage["c3"] = conv_inmaps(None, 1, a["norm1_g"][3], a["conv_w_in"], a["conv_b_in"], a["conv_dw_w"], a["conv_dw_b"],
                                  a["conv_ln_g"], a["conv_ln_b"], a["conv_w_out"])
    per_stage["n1"] = nat_inmaps(None, a["norm1_g"][1], a["nat_w_qkv"], a["nat_q_norm_g"], a["nat_k_norm_g"], a["nat_rpb"],
                                 a["nat_w_out"])
    per_stage["r2"] = ret_inmaps(None, a["norm1_g"][2], a["ret_w_in"], a["ret_log2_inv_decay"], a["ret_gn_g"], a["ret_w_out"])
    for i in range(4):
        per_stage[f"f{i}"] = ffn_inmaps(None, i, a["norm2_g"], a["ffn_w_up"], a["ffn_dw_w"], a["ffn_dw_b"], a["ffn_w_down"])
    maps = []
    for c in range(NCORES):
        m = {}
        for sn, lst in per_stage.items():
            for k, v in lst[c].items():
                m[f"{sn}_{k}"] = v
        hm = np.zeros((128, 2), np.float32)
        hm[:, 0] = float(c % 2)
        hm[:, 1] = float(1 - c % 2)
        m["hmask"] = hm
        maps.append(m)
    return maps


_PROGS = {}


def _prog(name, builder):
    if name not in _PROGS:
        _PROGS[name] = builder()
    return _PROGS[name]


def _launch(nc, maps):
    res = run_bass_kernel_spmd(nc, maps, core_ids=list(range(NCORES)))
    return unshard_tokens_fm([r["x_out"] for r in res.results])


def kernel_unfused(x, norm1_g, norm2_g, conv_w_in, conv_b_in, conv_dw_w, conv_dw_b, conv_ln_g, conv_ln_b, conv_w_out,
           nat_w_qkv, nat_q_norm_g, nat_k_norm_g, nat_rpb, nat_w_out, ret_w_in, ret_log2_inv_decay, ret_gn_g,
           ret_w_out, ffn_w_up, ffn_dw_w, ffn_dw_b, ffn_w_down):
    a = {k: np.asarray(v, np.float32) for k, v in locals().items()}
    xc = a["x"]
    for i in range(4):
        mixer, j = i % 3, i // 3
        if mixer == 0:
            maps = conv_inmaps(xc, j, a["norm1_g"][i], a["conv_w_in"], a["conv_b_in"], a["conv_dw_w"], a["conv_dw_b"],
                               a["conv_ln_g"], a["conv_ln_b"], a["conv_w_out"])
            xc = _launch(_prog("conv", build_conv_prog), maps)
        elif mixer == 1:
            maps = nat_inmaps(xc, a["norm1_g"][i], a["nat_w_qkv"], a["nat_q_norm_g"], a["nat_k_norm_g"], a["nat_rpb"],
                              a["nat_w_out"])
            xc = _launch(_prog("nat", build_nat_prog), maps)
        else:
            maps = ret_inmaps(xc, a["norm1_g"][i], a["ret_w_in"], a["ret_log2_inv_decay"], a["ret_gn_g"], a["ret_w_out"])
            xc = _launch(_prog("ret", build_ret_prog), maps)
        maps = ffn_inmaps(xc, i, a["norm2_g"], a["ffn_w_up"], a["ffn_dw_w"], a["ffn_dw_b"], a["ffn_w_down"])
        xc = _launch(_prog("ffn", build_ffn_prog), maps)
    return xc


def kernel(x, norm1_g, norm2_g, conv_w_in, conv_b_in, conv_dw_w, conv_dw_b, conv_ln_g, conv_ln_b, conv_w_out,
           nat_w_qkv, nat_q_norm_g, nat_k_norm_g, nat_rpb, nat_w_out, ret_w_in, ret_log2_inv_decay, ret_gn_g,
           ret_w_out, ffn_w_up, ffn_dw_w, ffn_dw_b, ffn_w_down):
    a = {k: np.asarray(v, np.float32) for k, v in locals().items()}
    maps = fused_inmaps(a)
    nc = _prog("fused", build_fused_prog)
    res = run_bass_kernel_spmd(nc, maps, core_ids=list(range(NCORES)))
    return unshard_tokens_fm([r["x_out"] for r in res.results])
```

```python
import contextlib
import numpy as np
import concourse.bass as bass
import concourse.mybir as mybir
from concourse.bass_utils import run_bass_kernel_spmd

F32 = mybir.dt.float32
BF16 = mybir.dt.bfloat16
AF = mybir.ActivationFunctionType
ALU = mybir.AluOpType
AX = mybir.AxisListType

D = 1024
SEQ = 4096
BATCH = 4
T = 2048
NCORES = 8
FFN = 2816
NH = FFN // 128
EPS = 1e-6


class Tok:
    __slots__ = ("lw", "rd", "name", "sem", "dcount", "last_dma")

    def __init__(self, name=""):
        self.lw = None
        self.rd = []
        self.name = name
        self.sem = None
        self.dcount = 0
        self.last_dma = None


class Op:
    __slots__ = ("eng", "fn", "deps", "is_dma", "dtok", "has_dep", "sem", "val",
                 "waits", "know", "is_out", "inc", "is_barrier")

    def __init__(self, eng, fn, is_dma=False, dtok=None):
        self.eng = eng
        self.fn = fn
        self.deps = set()
        self.is_dma = is_dma
        self.dtok = dtok
        self.has_dep = False
        self.sem = None
        self.val = 0
        self.waits = ()
        self.know = None
        self.is_out = False
        self.inc = 16 if is_dma else 1
        self.is_barrier = False


class Prog:
    ENGS = ("pe", "act", "dve", "pool", "sp")

    def __init__(self, nc, stack):
        self.nc = nc
        self.stack = stack
        self.ops = []
        self.nsb = 0
        self.nsem = 0
        self.out_ops = []
        self.pfx = ""
        self.bar_start = 0
        self.prev_bar = []

    def sb(self, shape, dtype, name=None):
        self.nsb += 1
        return self.stack.enter_context(
            self.nc.sbuf_tensor(self.pfx + (name or f"sb{self.nsb}"), list(shape), dtype))

    def ps(self, shape, dtype, name=None):
        self.nsb += 1
        return self.stack.enter_context(
            self.nc.psum_tensor(self.pfx + (name or f"ps{self.nsb}"), list(shape), dtype))

    def barrier(self):
        last = {}
        dmas = {}
        for op in self.ops[self.bar_start:]:
            if op.is_dma:
                dmas[id(op.dtok)] = op
            else:
                last[op.eng] = op
        deps = set(last.values()) | set(dmas.values()) | set(self.prev_bar)
        bars = []
        for eng in self.ENGS:
            op = Op(eng, lambda e: e.nop())
            op.deps = set(deps)
            op.is_barrier = (eng == self.ENGS[0])
            self.ops.append(op)
            bars.append(op)
        self.prev_bar = bars
        self.bar_start = len(self.ops)

    def new_sem(self, name=None):
        self.nsem += 1
        return self.stack.enter_context(self.nc.semaphore(name or f"sem{self.nsem}"))

    def add(self, eng, fn, reads=(), writes=(), is_dma=False, dtok=None, is_out=False):
        op = Op(eng, fn, is_dma, dtok)
        op.is_out = is_out
        for t in reads:
            if t.lw is not None:
                op.deps.add(t.lw)
        for t in writes:
            for r in t.rd:
                op.deps.add(r)
            if t.lw is not None:
                op.deps.add(t.lw)
        if is_dma:
            if dtok.last_dma is not None:
                op.deps.add(dtok.last_dma)
            dtok.last_dma = op
        for t in reads:
            t.rd.append(op)
        for t in writes:
            t.rd = []
            t.lw = op
        op.deps.discard(op)
        if eng == "pe" and not is_dma:
            op.deps = {d for d in op.deps if not (d.eng == "pe" and not d.is_dma)}
        self.ops.append(op)
        if is_out:
            self.out_ops.append(op)
        return op

    def coll(self, fn, reads, writes, dtok):
        op = self.add("pool", fn, reads, writes, is_dma=True, dtok=dtok)
        op.inc = 1
        return op

    def dma(self, queue, out, in_, reads, writes, dtok, is_out=False, **kw):
        return self.add(queue, lambda e: e.dma_start(out=out, in_=in_, **kw),
                        reads, writes, is_dma=True, dtok=dtok, is_out=is_out)

    def emit(self):
        ops = self.ops
        for op in ops:
            for d in op.deps:
                d.has_dep = True
        esem = {e: self.new_sem("eng_" + e) for e in self.ENGS}
        cnt = {e: 0 for e in self.ENGS}
        free_sems = []
        live_toks = []
        for op in ops:
            if op.is_barrier:
                for t in live_toks:
                    free_sems.append((t.sem, t.dcount))
                live_toks = []
            if op.is_dma:
                t = op.dtok
                if t.sem is None:
                    if free_sems:
                        t.sem, t.dcount = free_sems.pop()
                    else:
                        t.sem = self.new_sem()
                    live_toks.append(t)
                t.dcount += op.inc
                op.sem = t.sem
                op.val = t.dcount
            elif op.has_dep:
                cnt[op.eng] += 1
                op.sem = esem[op.eng]
                op.val = cnt[op.eng]
        seen = {e: {} for e in self.ENGS}
        nwaits = 0
        for op in ops:
            s = seen[op.eng]
            waits = {}
            for d in op.deps:
                k = id(d.sem)
                if s.get(k, (None, 0))[1] >= d.val:
                    continue
                if waits.get(k, (None, 0))[1] < d.val:
                    waits[k] = (d.sem, d.val)
            for d in op.deps:
                if d.know is not None:
                    for k, v in d.know.items():
                        if s.get(k, (None, 0))[1] < v[1]:
                            s[k] = v
            for k, v in waits.items():
                if s.get(k, (None, 0))[1] < v[1]:
                    s[k] = v
            op.waits = list(waits.values())
            nwaits += len(op.waits)
            if op.sem is not None:
                kn = dict(s)
                kn[id(op.sem)] = (op.sem, op.val)
                op.know = kn
                if not op.is_dma:
                    s[id(op.sem)] = (op.sem, op.val)
        by = {e: [o for o in ops if o.eng == e] for e in self.ENGS}
        finals = [(o.sem, o.val) for o in self.out_ops]
        self.stats = dict(nops=len(ops), nwaits=nwaits,
                          per_eng={e: len(by[e]) for e in self.ENGS}, nsem=self.nsem)

        def run(name, e):
            for op in by[name]:
                for sem, val in op.waits:
                    e.wait_ge(sem, val)
                inst = op.fn(e)
                if op.sem is not None:
                    inst.then_inc(op.sem, op.inc)
            if name == "sp":
                for sem, val in finals:
                    e.wait_ge(sem, val)

        with self.nc.Block() as block:
            @block.tensor
            def _(e):
                run("pe", e)

            @block.scalar
            def _(e):
                run("act", e)

            @block.vector
            def _(e):
                run("dve", e)

            @block.gpsimd
            def _(e):
                run("pool", e)

            @block.sync
            def _(e):
                run("sp", e)


class Ring:
    def __init__(self, P, n, shape, dtype, name, psum=False):
        self.bufs = []
        for i in range(n):
            t = (P.ps if psum else P.sb)(shape, dtype, f"{name}{i}")
            self.bufs.append((t, Tok(f"{name}{i}")))
        self.i = 0

    def next(self):
        b = self.bufs[self.i % len(self.bufs)]
        self.i += 1
        return b


def mm(P, out, lhsT, rhs, start, stop, reads, writes, skip=False):
    return P.add("pe", lambda e: e.matmul(out, lhsT, rhs, start=start, stop=stop, skip_group_check=skip),
                 reads, writes)


def emit_rmsnorm(P, C, x_dram, g_col, gtok, h_all, h_tok, ps_ring, tiles, two_pass=False, hcol=None):
    fr = C["fr"]
    sqr = C["sqr"]
    ones = C["ones_bf"]
    assert len(fr.bufs) >= (4 if two_pass else 9)
    for ti, (t0, n) in enumerate(tiles):
        d0 = t0 if hcol is None else hcol[ti]
        xs = []
        pst, pstok = ps_ring.next()
        for c in range(8):
            xt, xtok = fr.next()
            P.dma("sp", xt[:, 0:n], x_dram[c * 128:(c + 1) * 128, t0:t0 + n], [], [xtok], xtok)
            xs.append((xt, xtok))
            sq, sqtok = sqr.next()
            P.add("act", lambda e, o=sq[:, 0:n], i=xt[:, 0:n]: e.activation(out=o, in_=i, func=AF.Square),
                  [xtok], [sqtok])
            mm(P, pst[:, 0:n], ones[:], sq[:, 0:n], c == 0, c == 7, [sqtok, C["ctok"]], [pstok])
        rs, rstok = C["rsr"].next()
        P.add("act", lambda e, o=rs[:, 0:n], i=pst[:, 0:n]: e.activation(
            out=o, in_=i, func=AF.Sqrt, bias=C["eps_col"][:, 0:1], scale=1.0 / D), [pstok, C["ctok"]], [rstok])
        P.add("dve", lambda e, o=rs[:, 0:n]: e.reciprocal(out=o, in_=o), [rstok], [rstok])
        for c in range(8):
            if two_pass:
                xt, xtok = fr.next()
                P.dma("sp", xt[:, 0:n], x_dram[c * 128:(c + 1) * 128, t0:t0 + n], [], [xtok], xtok)
            else:
                xt, xtok = xs[c]
            P.add("dve", lambda e, o=h_all[:, c, d0:d0 + n], i=xt[:, 0:n], g=g_col[:, c:c + 1], r=rs[:, 0:n]:
                  e.scalar_tensor_tensor(out=o, in0=i, scalar=g, in1=r, op0=ALU.mult, op1=ALU.mult),
                  [xtok, rstok, gtok], [h_tok[ti][c]])


def make_common(P, nfr=14):
    C = {}
    C["fr"] = Ring(P, nfr, [128, 512], F32, "fr")
    C["sqr"] = Ring(P, 3, [128, 512], BF16, "sqr")
    C["rsr"] = Ring(P, 2, [128, 512], F32, "rsr")
    C["ones_bf"] = P.sb([128, 128], BF16, "ones_bf")
    C["eps_col"] = P.sb([128, 1], F32, "eps_col")
    C["ctok"] = Tok("consts")
    P.add("pool", lambda e: e.memset(C["ones_bf"][:], 1.0), [], [C["ctok"]])
    P.add("pool", lambda e: e.memset(C["eps_col"][:], EPS), [], [C["ctok"]])
    return C


def load_cols(P, C, dst, src_dram, scale=None):
    tok = Tok("par")
    P.dma("sp", dst[:], src_dram, [], [tok], tok)
    if scale is not None:
        P.add("pool", lambda e: e.tensor_scalar(out=dst[:], in0=dst[:], scalar1=float(scale), scalar2=None,
                                                 op0=ALU.mult), [tok], [tok])
    return tok


def htoks_for(h_tok, tiles, c, lo, hi):
    return [h_tok[i][c] for i, (t0, n) in enumerate(tiles) if t0 < hi and t0 + n > lo]


def emit_ffn(P, C, x_in, x_out, g2c, w_up, dww, dwb, w_down, is_out=False):
    NT = T + 2
    g_col = P.sb([128, 8], F32, "ffn_g")
    gtok = load_cols(P, C, g_col, g2c)
    dww_sb = P.sb([128, 3 * 2 * NH], F32, "ffn_dww")
    dwb_sb = P.sb([128, 2 * NH], F32, "ffn_dwb")
    dwtok = load_cols(P, C, dww_sb, dww)
    dbtok = load_cols(P, C, dwb_sb, dwb)

    h_all = P.sb([128, 8, NT], BF16, "ffn_h")
    tiles = [(0, 410), (410, 410), (820, 410), (1230, 410), (1640, NT - 1640)]
    h_tok = [[Tok(f"h{i}_{c}") for c in range(8)] for i in range(len(tiles))]
    ps_stat = Ring(P, 1, [128, 512], F32, "ps_stat", psum=True)
    emit_rmsnorm(P, C, x_in, g_col, gtok, h_all, h_tok, ps_stat, tiles)

    act_all = P.sb([128, NH, T], BF16, "ffn_act")
    act_tok = [[Tok(f"act{j}_{i}") for i in range(5)] for j in range(NH)]

    wst = Ring(P, 2, [128, NH * 128], F32, "wst")
    wbf = Ring(P, 2, [128, NH * 128], BF16, "wbf")
    ps_up = Ring(P, 7, [128, 512], F32, "ps_up", psum=True)
    fr = C["fr"]
    w_up_v = w_up.rearrange("(kc p) n -> p kc n", p=128)
    ctiles = [(0, 410), (410, 410), (820, 410), (1230, 410), (1640, 408)]
    def load_up(j):
        st, sttok = wst.next()
        stv = st[:, 0:2048].rearrange("p (k n) -> p k n", k=8)
        P.dma("sp", stv[:, :, 0:128], w_up_v[:, :, j * 128:(j + 1) * 128], [], [sttok], sttok)
        P.dma("sp", stv[:, :, 128:256], w_up_v[:, :, FFN + j * 128:FFN + (j + 1) * 128], [], [sttok], sttok)
        wb, wbtok = wbf.next()
        P.add("pool", lambda e, o=wb[:, 0:2048], i=st[:, 0:2048]: e.tensor_copy(out=o, in_=i), [sttok], [wbtok])
        return wb[:, 0:2048].rearrange("p (k n) -> p k n", k=8), wbtok

    nxt = load_up(0)
    for j in range(NH):
        wbv, wbtok = nxt
        if j + 1 < NH:
            nxt = load_up(j + 1)
        for ci, (o0, n) in enumerate(ctiles):
            ncol = n + 2
            pv, pvtok = ps_up.next()
            pg, pgtok = ps_up.next()
            for half, (pt, pttok) in enumerate(((pv, pvtok), (pg, pgtok))):
                for kc in range(8):
                    mm(P, pt[:, 0:ncol], wbv[:, kc, half * 128:(half + 1) * 128], h_all[:, kc, o0:o0 + ncol],
                       kc == 0, kc == 7, [wbtok] + htoks_for(h_tok, tiles, kc, o0, o0 + ncol), [pttok])
            av, avtok = fr.next()
            ag, agtok = fr.next()
            for half, (pt, pttok, acc, acctok) in enumerate(((pv, pvtok, av, avtok), (pg, pgtok, ag, agtok))):
                ch = half * NH + j
                w0 = dww_sb[:, 0 * 2 * NH + ch:0 * 2 * NH + ch + 1]
                w1 = dww_sb[:, 1 * 2 * NH + ch:1 * 2 * NH + ch + 1]
                w2 = dww_sb[:, 2 * 2 * NH + ch:2 * 2 * NH + ch + 1]
                bb = dwb_sb[:, ch:ch + 1]
                P.add("act", lambda e, o=acc[:, 0:n], i=pt[:, 1:n + 1], s=w1, b=bb:
                      e.activation(out=o, in_=i, func=AF.Identity, bias=b, scale=s),
                      [pttok, dwtok, dbtok], [acctok])
                P.add("dve", lambda e, o=acc[:, 0:n], i=pt[:, 0:n], s=w0:
                      e.scalar_tensor_tensor(out=o, in0=i, scalar=s, in1=o, op0=ALU.mult, op1=ALU.add),
                      [pttok, acctok, dwtok], [acctok])
                P.add("dve", lambda e, o=acc[:, 0:n], i=pt[:, 2:n + 2], s=w2:
                      e.scalar_tensor_tensor(out=o, in0=i, scalar=s, in1=o, op0=ALU.mult, op1=ALU.add),
                      [pttok, acctok, dwtok], [acctok])
            ge, getok = fr.next()
            P.add("act", lambda e, o=ge[:, 0:n], i=ag[:, 0:n]: e.activation(out=o, in_=i, func=AF.Gelu_apprx_tanh),
                  [agtok], [getok])
            P.add("dve", lambda e, o=act_all[:, j, o0:o0 + n], a=ge[:, 0:n], b=av[:, 0:n]:
                  e.tensor_tensor(out=o, in0=a, in1=b, op=ALU.mult), [getok, avtok], [act_tok[j][ci]])

    ps_dn = ps_up
    w_dn_v = w_down.rearrange("(j p) n -> p j n", p=128)
    def load_dn(o):
        st, sttok = wst.next()
        stv = st[:].rearrange("p (j n) -> p j n", j=NH)
        P.dma("sp", stv, w_dn_v[:, :, o * 128:(o + 1) * 128], [], [sttok], sttok)
        wb, wbtok = wbf.next()
        P.add("pool", lambda e, oo=wb[:], i=st[:]: e.tensor_copy(out=oo, in_=i), [sttok], [wbtok])
        return wb[:].rearrange("p (j n) -> p j n", j=NH), wbtok

    nxt = load_dn(0)
    for o in range(8):
        wbv, wbtok = nxt
        if o + 1 < 8:
            nxt = load_dn(o + 1)
        for tt in range(T // 512):
            t0 = tt * 512
            xt, xtok = fr.next()
            P.dma("sp", xt[:], x_in[o * 128:(o + 1) * 128, 1 + t0:1 + t0 + 512], [], [xtok], xtok)
            pt, pttok = ps_dn.next()
            for j in range(NH):
                rd = [wbtok] + [act_tok[j][ci] for ci, (o0, n) in enumerate(ctiles) if o0 < t0 + 512 and o0 + n > t0]
                mm(P, pt[:], wbv[:, j, :], act_all[:, j, t0:t0 + 512], j == 0, j == NH - 1, rd, [pttok])
            P.add("dve", lambda e, oo=xt[:], a=pt[:]: e.tensor_tensor(out=oo, in0=a, in1=oo, op=ALU.add),
                  [pttok, xtok], [xtok])
            P.dma("sp", x_out[o * 128:(o + 1) * 128, t0:t0 + 512], xt[:], [xtok], [], xtok, is_out=is_out)


CW = 31
HC = 15


def emit_conv(P, C, x_in, x_out, mask, g1c, w_in, b_in, dw_w, dw_b, ln_g, ln_b, w_out, is_out=False):
    NT = T + 2 * HC
    fr = C["fr"]
    g_col = P.sb([128, 8], F32, "cv_g")
    gtok = load_cols(P, C, g_col, g1c)
    bin_sb = P.sb([128, 16], F32, "cv_bin")
    bintok = load_cols(P, C, bin_sb, b_in)
    dww_sb = P.sb([128, CW * 8], F32, "cv_dww")
    dwwtok = load_cols(P, C, dww_sb, dw_w)
    dwb_sb = P.sb([128, 8], F32, "cv_dwb")
    dwbtok = load_cols(P, C, dwb_sb, dw_b)
    lng_sb = P.sb([128, 8], F32, "cv_lng")
    lngtok = load_cols(P, C, lng_sb, ln_g)
    lnb_sb = P.sb([128, 8], F32, "cv_lnb")
    lnbtok = load_cols(P, C, lnb_sb, ln_b)
    mask_sb = P.sb([128, 2 * HC], F32, "cv_mask")
    masktok = load_cols(P, C, mask_sb, mask)
    ones_f = P.sb([128, 128], F32, "ones_f")
    P.add("pool", lambda e: e.memset(ones_f[:], 1.0), [], [C["ctok"]])

    KP = 20
    ident_f = P.sb([128, 128], F32, "cv_idf")
    P.add("pool", lambda e: e.memset(ident_f[:], 1.0), [], [C["ctok"]])
    P.add("pool", lambda e: e.affine_select(out=ident_f[:], in_=ident_f[:], pattern=[[-1, 128]], compare_op=ALU.is_equal,
                                            fill=0.0, base=0, channel_multiplier=1), [C["ctok"]], [C["ctok"]])
    dgr = Ring(P, 2, [128, KP, 128], BF16, "cv_diag")
    ubr = Ring(P, 2, [128, NT], BF16, "cv_ubf")
    h_all = P.sb([128, 8, NT], BF16, "cv_h")
    tiles = [(0, 416), (416, 416), (832, 416), (1248, 416), (1664, NT - 1664)]
    h_tok = [[Tok(f"cvh{i}_{c}") for c in range(8)] for i in range(len(tiles))]
    ps_stat = Ring(P, 2, [128, 512], F32, "ps_stat", psum=True)
    emit_rmsnorm(P, C, x_in, g_col, gtok, h_all, h_tok, ps_stat, tiles)

    v_all = P.sb([128, 8, T], F32, "cv_v")
    v_tok = [[Tok(f"cvv{c}_{i}") for i in range(4)] for c in range(8)]
    ur = Ring(P, 2, [128, NT], F32, "cv_u")
    wst = Ring(P, 2, [128, 2048], F32, "cv_wst")
    wbf = Ring(P, 2, [128, 2048], BF16, "cv_wbf")
    ps_up = Ring(P, 4, [128, 512], F32, "ps_up", psum=True)
    w_in_v = w_in.rearrange("(kc p) n -> p kc n", p=128)
    KD = 30
    def conv_glu(c):
        st, sttok = wst.next()
        stv = st[:].rearrange("p (k n) -> p k n", k=8)
        P.dma("sp", stv[:, :, 0:128], w_in_v[:, :, c * 128:(c + 1) * 128], [], [sttok], sttok)
        P.dma("sp", stv[:, :, 128:256], w_in_v[:, :, D + c * 128:D + (c + 1) * 128], [], [sttok], sttok)
        wb, wbtok = wbf.next()
        P.add("pool", lambda e, o=wb[:], i=st[:]: e.tensor_copy(out=o, in_=i), [sttok], [wbtok])
        wbv = wb[:].rearrange("p (k n) -> p k n", k=8)
        u, utok = ur.next()
        for ti, (t0, n) in enumerate(tiles):
            pa, patok = ps_up.next()
            pg, pgtok = ps_up.next()
            for half, (pt, pttok) in enumerate(((pa, patok), (pg, pgtok))):
                for kc in range(8):
                    mm(P, pt[:, 0:n], wbv[:, kc, half * 128:(half + 1) * 128], h_all[:, kc, t0:t0 + n],
                       kc == 0, kc == 7, [wbtok, h_tok[ti][kc]], [pttok])
            sg, sgtok = fr.next()
            P.add("act", lambda e, o=sg[:, 0:n], i=pg[:, 0:n], b=bin_sb[:, 8 + c:9 + c]:
                  e.activation(out=o, in_=i, func=AF.Sigmoid, bias=b, scale=1.0), [pgtok, bintok], [sgtok])
            P.add("dve", lambda e, o=u[:, t0:t0 + n], i=pa[:, 0:n], b=bin_sb[:, c:c + 1], g=sg[:, 0:n]:
                  e.scalar_tensor_tensor(out=o, in0=i, scalar=b, in1=g, op0=ALU.add, op1=ALU.mult),
                  [patok, sgtok, bintok], [utok])
        P.add("pool", lambda e, o=u[:, 0:HC], m=mask_sb[:, 0:HC]: e.tensor_tensor(out=o, in0=o, in1=m, op=ALU.mult),
              [utok, masktok], [utok])
        P.add("pool", lambda e, o=u[:, T + HC:NT], m=mask_sb[:, HC:2 * HC]: e.tensor_tensor(out=o, in0=o, in1=m, op=ALU.mult),
              [utok, masktok], [utok])
        return u, utok

    def conv_taps(c, u, utok):
        ub, ubtok = ubr.next()
        P.add("act", lambda e, o=ub[:], i=u[:]: e.activation(out=o, in_=i, func=AF.Identity), [utok], [ubtok])
        dg, dgtok = dgr.next()
        for k in range(KP):
            P.add("act", lambda e, o=dg[:, k, :], s_=dww_sb[:, k * 8 + c:k * 8 + c + 1]:
                  e.activation(out=o, in_=ident_f[:], func=AF.Identity, scale=s_), [C["ctok"], dwwtok], [dgtok])
        for tt in range(4):
            t0 = tt * 512
            va = v_all[:, c, t0:t0 + 512]
            wk = lambda k: dww_sb[:, k * 8 + c:k * 8 + c + 1]
            pc, pctok = ps_up.next()
            for k in range(KP):
                mm(P, pc[:], dg[:, k, :], ub[:, t0 + k:t0 + k + 512], k == 0, k == KP - 1, [dgtok, ubtok], [pctok])
            P.add("act", lambda e, o=va, i=pc[:], b=dwb_sb[:, c:c + 1]:
                  e.activation(out=o, in_=i, func=AF.Identity, bias=b, scale=1.0), [pctok, dwbtok], [v_tok[c][tt]])
            for k in range(KP, KD + 1):
                P.add("dve", lambda e, o=va, i=u[:, t0 + k:t0 + k + 512], s=wk(k):
                      e.scalar_tensor_tensor(out=o, in0=i, scalar=s, in1=o, op0=ALU.mult, op1=ALU.add),
                      [utok, dwwtok, v_tok[c][tt]], [v_tok[c][tt]])

    nxt_u = conv_glu(0)
    for c in range(8):
        cur_u = nxt_u
        if c + 1 < 8:
            nxt_u = conv_glu(c + 1)
        conv_taps(c, *cur_u)

    wo_bf = P.sb([128, 8, D], BF16, "cv_wo")
    wotok = [Tok(f"wo{i}") for i in range(4)]
    w_out_v = w_out.rearrange("(kc p) n -> p kc n", p=128)
    for i in range(4):
        st, sttok = wst.next()
        stv = st[:].rearrange("p (k n) -> p k n", k=8)
        P.dma("sp", stv, w_out_v[:, :, i * 256:(i + 1) * 256], [], [sttok], sttok)
        P.add("pool", lambda e, o=wo_bf[:, :, i * 256:(i + 1) * 256], s_=stv: e.tensor_copy(out=o, in_=s_),
              [sttok], [wotok[i]])

    ps_o = Ring(P, 2, [128, 512], F32, "ps_o", psum=True)
    for tt in range(4):
        t0 = tt * 512
        p1, p1tok = ps_stat.next()
        p2, p2tok = ps_stat.next()
        for c in range(8):
            mm(P, p1[:], ones_f[:], v_all[:, c, t0:t0 + 512], c == 0, c == 7, [v_tok[c][tt], C["ctok"]], [p1tok])
        for c in range(8):
            sq, sqtok = fr.next()
            P.add("act", lambda e, o=sq[:], i=v_all[:, c, t0:t0 + 512]: e.activation(out=o, in_=i, func=AF.Square),
                  [v_tok[c][tt]], [sqtok])
            mm(P, p2[:], ones_f[:], sq[:], c == 0, c == 7, [sqtok, C["ctok"]], [p2tok])
        mu, mutok = fr.next()
        P.add("act", lambda e, o=mu[:], i=p1[:]: e.activation(out=o, in_=i, func=AF.Identity, scale=1.0 / D),
              [p1tok], [mutok])
        rs, rstok = fr.next()
        P.add("dve", lambda e, o=rs[:], a=mu[:]: e.tensor_tensor(out=o, in0=a, in1=a, op=ALU.mult), [mutok], [rstok])
        P.add("dve", lambda e, o=rs[:], i=p2[:]: e.scalar_tensor_tensor(out=o, in0=i, scalar=1.0 / D, in1=o,
                                                                        op0=ALU.mult, op1=ALU.subtract),
              [p2tok, rstok], [rstok])
        P.add("act", lambda e, o=rs[:]: e.activation(out=o, in_=o, func=AF.Sqrt, bias=C["eps_col"][:, 0:1], scale=1.0),
              [rstok, C["ctok"]], [rstok])
        P.add("dve", lambda e, o=rs[:]: e.reciprocal(out=o, in_=o), [rstok], [rstok])
        for c in range(8):
            dd, ddtok = fr.next()
            P.add("pool", lambda e, o=dd[:], a=v_all[:, c, t0:t0 + 512], m=mu[:]:
                  e.tensor_tensor(out=o, in0=a, in1=m, op=ALU.subtract), [v_tok[c][tt], mutok], [ddtok])
            P.add("dve", lambda e, o=dd[:], r=rs[:]: e.tensor_tensor(out=o, in0=o, in1=r, op=ALU.mult),
                  [ddtok, rstok], [ddtok])
            P.add("act", lambda e, o=h_all[:, c, t0:t0 + 512], i=dd[:], g=lng_sb[:, c:c + 1], b=lnb_sb[:, c:c + 1]:
                  e.activation(out=o, in_=i, func=AF.Silu, bias=b, scale=g),
                  [ddtok, lngtok, lnbtok], [h_tok[i][c] for i in range(len(tiles))])
        for o in range(8):
            pt, pttok = ps_o.next()
            for kc in range(8):
                mm(P, pt[:], wo_bf[:, kc, o * 128:(o + 1) * 128], h_all[:, kc, t0:t0 + 512], kc == 0, kc == 7,
                   [wotok[o // 2], h_tok[tt][kc]], [pttok])
            xt, xtok = fr.next()
            P.dma("sp", xt[:], x_in[o * 128:(o + 1) * 128, HC + t0:HC + t0 + 512], [], [xtok], xtok)
            P.add("dve", lambda e, oo=xt[:], a=pt[:]: e.tensor_tensor(out=oo, in0=a, in1=oo, op=ALU.add),
                  [pttok, xtok], [xtok])
            P.dma("sp", x_out[o * 128:(o + 1) * 128, t0:t0 + 512], xt[:], [xtok], [], xtok, is_out=is_out)


HN = 256
NEG = -30000.0
NE = 7


def nat_es(qp):
    if qp == 0:
        return list(range(0, 6))
    if qp == 15:
        return list(range(-1, 5))
    return list(range(0, 5))


def nat_pidx(qp):
    return {0: 0, 1: 1, 14: 3, 15: 4}.get(qp, 2)


def emit_nat(P, C, x_in, x_out, g1c, w_qkv, qg, kg, bias, pen, ohk, bd, w_out, is_out=False):
    NT = T + 2 * HN
    fr = C["fr"]
    g_col = P.sb([128, 8], F32, "nt_g")
    gtok = load_cols(P, C, g_col, g1c)
    qg_sb = P.sb([128, 2], F32, "nt_qg")
    qgtok = load_cols(P, C, qg_sb, qg, scale=0.125)
    kg_sb = P.sb([128, 1], F32, "nt_kg")
    kgtok = load_cols(P, C, kg_sb, kg)
    ctok = C["ctok"]
    bd_f = P.sb([128, 128], F32, "nt_bd")
    bdtok = load_cols(P, C, bd_f, bd)
    ident_f = P.sb([128, 128], F32, "nt_idf")
    ident = P.sb([128, 128], BF16, "nt_id")
    P.add("pool", lambda e: e.memset(ident_f[:], 1.0), [], [ctok])
    P.add("pool", lambda e: e.affine_select(out=ident_f[:], in_=ident_f[:], pattern=[[-1, 128]], compare_op=ALU.is_equal,
                                            fill=0.0, base=0, channel_multiplier=1), [ctok], [ctok])
    P.add("pool", lambda e: e.tensor_copy(out=ident[:], in_=ident_f[:]), [ctok], [ctok])
    pen_f = P.sb([2, 5 * NE * 128], F32, "nt_penf")
    pen_bf = P.sb([2, 5 * NE * 128], BF16, "nt_pen")
    pentok = load_cols(P, C, pen_f, pen)
    P.add("pool", lambda e: e.tensor_copy(out=pen_bf[:], in_=pen_f[:]), [pentok], [pentok])
    ohk_f = P.sb([2, 128], F32, "nt_ohkf")
    ohk_bf = P.sb([2, 128], BF16, "nt_ohk")
    ohktok = load_cols(P, C, ohk_f, ohk)
    P.add("pool", lambda e: e.tensor_copy(out=ohk_bf[:], in_=ohk_f[:]), [ohktok], [ohktok])

    h_all = P.sb([128, 8, NT], BF16, "nt_h")
    tiles = [(i * 512, 512) for i in range(NT // 512)]
    h_tok = [[Tok(f"nth{i}_{c}") for c in range(8)] for i in range(len(tiles))]
    ps_pr = Ring(P, 2, [128, 512], F32, "ps_pr", psum=True)
    emit_rmsnorm(P, C, x_in, g_col, gtok, h_all, h_tok, ps_pr, tiles, two_pass=True)

    attn_all = P.sb([128, 8, T], BF16, "nt_attn")
    attn_tok = [Tok(f"attn{hp}") for hp in range(8)]
    wst = Ring(P, 2, [128, 8, 128], F32, "nt_wst")
    wq_r = Ring(P, 2, [128, 8, 128], BF16, "nt_wq")
    wk_r = Ring(P, 2, [128, 8, 128], BF16, "nt_wk")
    wv_r = Ring(P, 2, [128, 8, 128], BF16, "nt_wv")
    bst = Ring(P, 1, [128, 2 * NE * 128], F32, "nt_bst")
    bbf = Ring(P, 2, [128, 2 * NE * 128], BF16, "nt_bbf")
    q_r = Ring(P, 1, [128, 2, T], BF16, "nt_q")
    k_r = Ring(P, 1, [128, NT], BF16, "nt_k")
    v_r = Ring(P, 1, [128, 2, NT // 128, 128], BF16, "nt_v")
    for (vb_, vbtok_) in v_r.bufs:
        P.add("pool", lambda e, o=vb_[:]: e.memset(o, 0.0), [], [vbtok_])
    onesz = P.sb([128, 2, 128], BF16, "nt_onesz")
    P.add("pool", lambda e: e.memset(onesz[:], 0.0), [], [ctok])
    P.add("pool", lambda e: e.memset(onesz[:, 0, 0:64], 1.0), [], [ctok])
    P.add("pool", lambda e: e.memset(onesz[:, 1, 64:128], 1.0), [], [ctok])
    p_r = Ring(P, 3, [128, 6 * 128], BF16, "nt_p")
    ps_sc = Ring(P, 2, [128, 1024], F32, "ps_sc", psum=True)
    ps_pv = Ring(P, 2, [128, 512], F32, "ps_pv", psum=True)
    w_v = w_qkv.rearrange("(kc p) n -> p kc n", p=128)
    allh = lambda ti: [h_tok[ti][c] for c in range(8)]

    for hp in range(8):
        wts = []
        for which, ring in enumerate((wq_r, wk_r, wv_r)):
            st, sttok = wst.next()
            P.dma("sp", st[:], w_v[:, :, which * D + hp * 128:which * D + (hp + 1) * 128], [], [sttok], sttok)
            wb, wbtok = ring.next()
            P.add("pool", lambda e, o=wb[:], i=st[:]: e.tensor_copy(out=o, in_=i), [sttok], [wbtok])
            wts.append((wb, wbtok))
        (wq, wqtok), (wk, wktok), (wv, wvtok) = wts
        bs, bstok = bst.next()
        P.dma("sp", bs[:], bias[hp], [], [bstok], bstok)
        bb, bbtok = bbf.next()
        P.add("pool", lambda e, o=bb[:], i=bs[:]: e.tensor_copy(out=o, in_=i), [bstok], [bbtok])

        q_sb, qtok = q_r.next()
        k_sb, ktok = k_r.next()
        v_sb, vtok = v_r.next()
        for (dst, dtok, wmat, wtok, gsb, gt, tl) in (
                (q_sb, qtok, wq, wqtok, qg_sb, qgtok, [(HN + i * 512, i * 512) for i in range(4)]),
                (k_sb, ktok, wk, wktok, kg_sb, kgtok, [(i * 512, i * 512) for i in range(5)])):
            for (hs, ds) in tl:
                ti = hs // 512
                pr, prtok = ps_pr.next()
                for kc in range(8):
                    mm(P, pr[:], wmat[:, kc, :], h_all[:, kc, hs:hs + 512], kc == 0, kc == 7,
                       [wtok] + [h_tok[i][kc] for i in range(len(tiles)) if i * 512 < hs + 512 and (i + 1) * 512 > hs], [prtok])
                sq, sqtok = fr.next()
                P.add("act", lambda e, o=sq[:], i=pr[:]: e.activation(out=o, in_=i, func=AF.Square), [prtok], [sqtok])
                pq, pqtok = ps_pr.next()
                mm(P, pq[:], bd_f[:], sq[:], True, True, [sqtok, bdtok], [pqtok])
                rs, rstok = fr.next()
                P.add("act", lambda e, o=rs[:], i=pq[:]: e.activation(out=o, in_=i, func=AF.Sqrt,
                                                                      bias=C["eps_col"][:, 0:1], scale=1.0 / 64),
                      [pqtok, ctok], [rstok])
                P.add("dve", lambda e, o=rs[:]: e.reciprocal(out=o, in_=o), [rstok], [rstok])
                if dst is q_sb:
                    for hh_ in range(2):
                        P.add("dve", lambda e, o=dst[:, hh_, ds:ds + 512], i=pr[:], g=gsb[:, hh_:hh_ + 1], r=rs[:]:
                              e.scalar_tensor_tensor(out=o, in0=i, scalar=g, in1=r, op0=ALU.mult, op1=ALU.mult),
                              [prtok, rstok, gt], [dtok])
                else:
                    P.add("dve", lambda e, o=dst[:, ds:ds + 512], i=pr[:], g=gsb[:, 0:1], r=rs[:]:
                          e.scalar_tensor_tensor(out=o, in0=i, scalar=g, in1=r, op0=ALU.mult, op1=ALU.mult),
                          [prtok, rstok, gt], [dtok])
        for blk in range(NT // 128):
            pr, prtok = ps_pr.next()
            for kc in range(8):
                mm(P, pr[:, 0:128], h_all[:, kc, blk * 128:(blk + 1) * 128], wv[:, kc, :], kc == 0, kc == 7,
                   [wvtok, h_tok[blk // 4][kc]], [prtok])
            for hh_ in range(2):
                P.add("act", lambda e, o=v_sb[:, hh_, blk, 64 * hh_:64 * hh_ + 64], i=pr[:, 64 * hh_:64 * hh_ + 64]:
                      e.activation(out=o, in_=i, func=AF.Identity), [prtok], [vtok])
        for qp in range(16):
            es = nat_es(qp)
            ne = len(es)
            pix = nat_pidx(qp)
            pts = []
            for hh in range(2):
                sc, sctok = ps_sc.next()
                for idx, e_ in enumerate(es):
                    kb = qp + e_
                    mm(P, sc[:, idx * 128:(idx + 1) * 128], k_sb[:, kb * 128:(kb + 1) * 128],
                       q_sb[:, hh, qp * 128:(qp + 1) * 128], idx % 4 == 0, False, [ktok, qtok], [sctok], skip=True)
                boff = (hh * NE + es[0] + 1) * 128
                poff = (pix * NE + es[0] + 1) * 128
                for (c0, c1) in ((0, 512), (512, ne * 128)):
                    mm(P, sc[:, c0:c1], ident[:], bb[:, boff + c0:boff + c1], False, False, [bbtok, ctok], [sctok], skip=True)
                    mm(P, sc[:, c0:c1], ohk_bf[:], pen_bf[:, poff + c0:poff + c1], False, True, [ohktok, pentok], [sctok], skip=True)
                pt, pttok = p_r.next()
                for (c0, c1) in ((0, 512), (512, ne * 128)):
                    P.add("act", lambda e, o=pt[:, c0:c1], i=sc[:, c0:c1]: e.activation(out=o, in_=i, func=AF.Exp),
                          [sctok], [pttok])
                pts.append((pt, pttok))
            pv, pvtok = ps_pv.next()
            n_mm = 2 * ne
            cnt = 0
            for hh in range(2):
                pt, pttok = pts[hh]
                for idx, e_ in enumerate(es):
                    kb = qp + e_
                    mm(P, pv[:, 0:128], v_sb[:, hh, kb, :], pt[:, idx * 128:(idx + 1) * 128], cnt == 0, cnt == n_mm - 1,
                       [vtok, pttok], [pvtok])
                    cnt += 1
            cnt = 0
            for hh in range(2):
                pt, pttok = pts[hh]
                for idx, e_ in enumerate(es):
                    mm(P, pv[:, 128:256], onesz[:, hh, :], pt[:, idx * 128:(idx + 1) * 128], cnt == 0, cnt == n_mm - 1,
                       [pttok, ctok], [pvtok])
                    cnt += 1
            if C.get("dbg") is not None and hp == 0 and qp == 2:
                dbg_dump(P, C, 0, pts[0][0][:, 0:128], [pts[0][1]])
                dbg_dump(P, C, 1, pts[0][0][:, 128:256], [pts[0][1]])
                dbg_dump(P, C, 2, q_sb[:, 0, 256:384], [qtok])
                dbg_dump(P, C, 3, q_sb[:, 1, 256:384], [qtok])
                dbg_dump(P, C, 4, k_sb[:, 256:384], [ktok])
                dbg_dump(P, C, 5, k_sb[:, 384:512], [ktok])
                dbg_dump(P, C, 6, v_sb[:, 0, 2, :], [vtok])
                dbg_dump(P, C, 7, v_sb[:, 1, 2, :], [vtok])
            rd, rdtok = fr.next()
            P.add("dve", lambda e, o=rd[:, 0:128], i=pv[:, 128:256]: e.reciprocal(out=o, in_=i), [pvtok], [rdtok])
            P.add("dve", lambda e, o=attn_all[:, hp, qp * 128:(qp + 1) * 128], a=pv[:, 0:128], b=rd[:, 0:128]:
                  e.tensor_tensor(out=o, in0=a, in1=b, op=ALU.mult), [pvtok, rdtok], [attn_tok[hp]])
            if C.get("dbg") is not None and hp == 0 and qp == 2:
                dbg_dump(P, C, 8, attn_all[:, 0, 256:384], [attn_tok[0]])
                dbg_dump(P, C, 9, rd[:, 0:128], [rdtok])

    wo_st = Ring(P, 2, [128, 8, 128], F32, "nt_wost")
    wo_r = Ring(P, 2, [128, 8, 128], BF16, "nt_wo")
    w_out_v = w_out.rearrange("(kc p) n -> p kc n", p=128)
    for o in range(8):
        st, sttok = wo_st.next()
        P.dma("sp", st[:], w_out_v[:, :, o * 128:(o + 1) * 128], [], [sttok], sttok)
        wb, wbtok = wo_r.next()
        P.add("pool", lambda e, oo=wb[:], i=st[:]: e.tensor_copy(out=oo, in_=i), [sttok], [wbtok])
        for tt in range(4):
            t0 = tt * 512
            pt, pttok = ps_pr.next()
            for kc in range(8):
                mm(P, pt[:], wb[:, kc, :], attn_all[:, kc, t0:t0 + 512], kc == 0, kc == 7, [wbtok, attn_tok[kc]], [pttok])
            xt, xtok = fr.next()
            P.dma("sp", xt[:], x_in[o * 128:(o + 1) * 128, HN + t0:HN + t0 + 512], [], [xtok], xtok)
            P.add("dve", lambda e, oo=xt[:], a=pt[:]: e.tensor_tensor(out=oo, in0=a, in1=oo, op=ALU.add),
                  [pttok, xtok], [xtok])
            P.dma("sp", x_out[o * 128:(o + 1) * 128, t0:t0 + 512], xt[:], [xtok], [], xtok, is_out=is_out)


NB = 32
RB = NB * 128
NBT = 48
RH = 4
LN16 = -2.772588722239781


def emit_ret(P, C, nc, x_in, x_out, g1c, w_in, cosT, sinT, l2d, gng, w_out, hm, is_out=False):
    fr = C["fr"]
    ctok = C["ctok"]
    g_col = P.sb([128, 8], F32, "rt_g")
    gtok = load_cols(P, C, g_col, g1c)
    gng_sb = P.sb([128, 16], F32, "rt_gng")
    gngtok = load_cols(P, C, gng_sb, gng)
    hm_sb = P.sb([128, 2], F32, "rt_hm")
    hmtok = load_cols(P, C, hm_sb, hm)
    ones_f = P.sb([128, 128], F32, "rt_ones_f")
    P.add("pool", lambda e: e.memset(ones_f[:], 1.0), [], [ctok])
    lg = P.sb([128, 8], F32, "rt_lg")
    nlg = P.sb([128, 8], F32, "rt_nlg")
    one_col = P.sb([128, 1], F32, "rt_one")
    ln16_col = P.sb([128, 1], F32, "rt_ln16")
    P.add("pool", lambda e: e.memset(one_col[:], 1.0), [], [ctok])
    P.add("pool", lambda e: e.memset(ln16_col[:], LN16), [], [ctok])
    lgtok = load_cols(P, C, lg, l2d)
    P.add("act", lambda e: e.activation(out=lg[:], in_=lg[:], func=AF.Exp, scale=-0.6931471805599453), [lgtok], [lgtok])
    P.add("act", lambda e: e.activation(out=lg[:], in_=lg[:], func=AF.Ln, bias=one_col[:, 0:1], scale=-1.0),
          [lgtok, ctok], [lgtok])
    P.add("dve", lambda e: e.tensor_scalar(out=nlg[:], in0=lg[:], scalar1=-1.0, scalar2=None, op0=ALU.mult),
          [lgtok], [lgtok])
    d1i = P.sb([128, 128], mybir.dt.int32, "rt_d1i")
    d1 = P.sb([128, 128], F32, "rt_d1")
    dbi = P.sb([128, NBT], mybir.dt.int32, "rt_dbi")
    dbf = P.sb([128, NBT], F32, "rt_dbf")
    dri = P.sb([128, NBT], mybir.dt.int32, "rt_dri")
    drf = P.sb([128, NBT], F32, "rt_drf")
    itok = Tok("iota")
    P.add("pool", lambda e: e.iota(d1i[:], pattern=[[1, 128]], base=0, channel_multiplier=-1), [], [itok])
    P.add("pool", lambda e: e.iota(dbi[:], pattern=[[128, NBT]], base=0, channel_multiplier=0), [], [itok])
    P.add("pool", lambda e: e.iota(dri[:], pattern=[[-128, NBT]], base=128 * (NBT - 1), channel_multiplier=0), [], [itok])
    P.add("dve", lambda e: e.tensor_copy(out=drf[:], in_=dri[:]), [itok], [itok])
    P.add("dve", lambda e: e.tensor_copy(out=d1[:], in_=d1i[:]), [itok], [itok])
    P.add("dve", lambda e: e.tensor_copy(out=dbf[:], in_=dbi[:]), [itok], [itok])

    h_dram = nc.dram_tensor("rt_h_dram", [D, RB], BF16, kind="Internal").ap()
    gT_dram = nc.dram_tensor("rt_gT_dram", [2 * D, T], BF16, kind="Internal").ap()
    h_dv = h_dram.rearrange("(c p) t -> p c t", p=128)
    gT_dv = gT_dram.rearrange("(c p) t -> p c t", p=128)
    bigr = Ring(P, 2, [128, 16, 512], BF16, "rt_big")
    ps_a = Ring(P, 2, [128, 512], F32, "ps_a", psum=True)
    ps_s = Ring(P, 2, [128, 512], F32, "ps_s", psum=True)
    ps_o = Ring(P, 2, [128, 512], F32, "ps_o", psum=True)
    ntile = RB // 512
    hd_tok = [Tok(f"hd{i}") for i in range(ntile)]
    for ti in range(ntile):
        hb, hbtok = bigr.next()
        emit_rmsnorm(P, C, x_in, g_col, gtok, hb, [[hbtok] * 8], ps_a, [(ti * 512, 512)], two_pass=True, hcol=[0])
        P.dma("sp", h_dv[:, :, ti * 512:(ti + 1) * 512], hb[:, 0:8, :], [hbtok], [hd_tok[ti]], hbtok)

    k_fm = P.sb([128, 2, RB], BF16, "rt_k")
    v_tok = P.sb([128, NB, 512], BF16, "rt_v")
    q_fm = P.sb([128, 2, T], BF16, "rt_q")
    o_fm = P.sb([128, 4, T], F32, "rt_o")
    ktok, vtok, qtok = Tok("k"), Tok("v"), Tok("q")
    otok = [Tok(f"o{i}") for i in range(16)]
    wq = P.sb([128, 8, 256], BF16, "rt_wq")
    wk = P.sb([128, 8, 256], BF16, "rt_wk")
    wv = P.sb([128, 8, 512], BF16, "rt_wv")
    wg = P.sb([128, 8, 512], BF16, "rt_wg")
    wtok = Tok("w")
    wqtok, wgtok = Tok("wqkv"), Tok("wg")
    wst = Ring(P, 2, [128, 8, 128], F32, "rt_wst")
    w_v = w_in.rearrange("(kc p) n -> p kc n", p=128)
    gf = P.sb([128, 128], F32, "rt_gf")
    gb = P.sb([128, 128], F32, "rt_gb")
    gd = P.sb([128, 128], F32, "rt_gd")
    gd2 = P.sb([128, 128], F32, "rt_gd2")
    sf = P.sb([128, NBT], F32, "rt_sf")
    sbk = P.sb([128, NBT], F32, "rt_sb")
    sfr = P.sb([128, NBT], F32, "rt_sfr")
    so = P.sb([128, 256], F32, "rt_so")
    go = P.sb([128, 128], F32, "rt_go")
    gtk = Tok("G")
    p_r = Ring(P, 10, [128, 128], BF16, "rt_p")
    gT_tok = [[Tok(f"gT{h}_{t}") for t in range(4)] for h in range(RH)]

    for h in range(RH):
        def load_w(hh_, which):
            segs = []
            if which == "qkv":
                segs += [(wq, i_ * 128, hh_ * 256 + i_ * 128, wqtok) for i_ in range(2)]
                segs += [(wk, i_ * 128, D + hh_ * 256 + i_ * 128, wqtok) for i_ in range(2)]
                segs += [(wv, i_ * 128, 2 * D + hh_ * 512 + i_ * 128, wqtok) for i_ in range(4)]
            else:
                segs += [(wg, i_ * 128, 4 * D + hh_ * 512 + i_ * 128, wgtok) for i_ in range(4)]
            for (dst, dcol, scol, tk) in segs:
                st, sttok = wst.next()
                P.dma("sp", st[:], w_v[:, :, scol:scol + 128], [], [sttok], sttok)
                P.add("pool", lambda e, o=dst[:, :, dcol:dcol + 128], i=st[:]: e.tensor_copy(out=o, in_=i), [sttok], [tk])

        if h == 0:
            load_w(0, "qkv")
        load_w(h, "g")
        P.add("act", lambda e, sc_=lg[:, h:h + 1]: e.activation(out=gf[:], in_=d1[:], func=AF.Exp, bias=ln16_col[:, 0:1], scale=sc_),
              [itok, lgtok, ctok], [gtk])
        P.add("act", lambda e, sc_=nlg[:, 4 + h:5 + h]: e.activation(out=gb[:], in_=d1[:], func=AF.Exp, bias=ln16_col[:, 0:1], scale=sc_),
              [itok, lgtok, ctok], [gtk])
        P.add("pool", lambda e: e.affine_select(out=gd[:], in_=gf[:], pattern=[[1, 128]], compare_op=ALU.is_ge, fill=0.0,
                                                base=0, channel_multiplier=-1), [gtk], [gtk])
        P.add("pool", lambda e: e.affine_select(out=gd2[:], in_=gb[:], pattern=[[-1, 128]], compare_op=ALU.is_gt, fill=0.0,
                                                base=0, channel_multiplier=1), [gtk], [gtk])
        P.add("pool", lambda e: e.tensor_tensor(out=gd[:], in0=gd[:], in1=gd2[:], op=ALU.add), [gtk], [gtk])
        P.add("act", lambda e, sc_=lg[:, h:h + 1]: e.activation(out=sf[:], in_=dbf[:], func=AF.Exp, scale=sc_), [itok, lgtok], [gtk])
        P.add("act", lambda e, sc_=lg[:, 4 + h:5 + h]: e.activation(out=sbk[:], in_=dbf[:], func=AF.Exp, scale=sc_), [itok, lgtok], [gtk])
        P.add("act", lambda e, sc_=lg[:, h:h + 1]: e.activation(out=sfr[:], in_=drf[:], func=AF.Exp, scale=sc_), [itok, lgtok], [gtk])
        P.add("dve", lambda e: e.tensor_scalar(out=go[:], in0=gb[:], scalar1=hm_sb[:, 1:2], scalar2=None, op0=ALU.mult), [gtk, hmtok], [gtk])
        P.add("dve", lambda e: e.scalar_tensor_tensor(out=go[:], in0=gf[:], scalar=hm_sb[:, 0:1], in1=go[:], op0=ALU.mult, op1=ALU.add),
              [gtk, hmtok], [gtk])
        for i_ in range(16):
            P.add("dve", lambda e, o=so[:, i_ * 16:(i_ + 1) * 16], a=sbk[:, 16 - i_:32 - i_]:
                  e.tensor_scalar(out=o, in0=a, scalar1=hm_sb[:, 1:2], scalar2=None, op0=ALU.mult), [gtk, hmtok], [gtk])
            P.add("dve", lambda e, o=so[:, i_ * 16:(i_ + 1) * 16], a=sfr[:, 31 - i_:47 - i_]:
                  e.scalar_tensor_tensor(out=o, in0=a, scalar=hm_sb[:, 0:1], in1=o, op0=ALU.mult, op1=ALU.add), [gtk, hmtok], [gtk])

        for ti in range(ntile):
            hb, hbtok = bigr.next()
            P.dma("sp", hb[:, 0:8, :], h_dv[:, :, ti * 512:(ti + 1) * 512], [hd_tok[ti]], [hbtok], hbtok)
            cs, cstok = fr.next()
            sn, sntok = fr.next()
            P.dma("sp", cs[:], cosT[:, ti * 512:(ti + 1) * 512], [], [cstok], cstok)
            P.dma("sp", sn[:], sinT[:, ti * 512:(ti + 1) * 512], [], [sntok], sntok)
            todo = [(wk, k_fm, ktok, ti * 512)]
            if ti < 4:
                todo.append((wq, q_fm, qtok, ti * 512))
            for (wmat, dst, dtok, dcol) in todo:
                p1, p1tok = ps_a.next()
                p2, p2tok = ps_a.next()
                for dc, (pt, pttok) in enumerate(((p1, p1tok), (p2, p2tok))):
                    for kc in range(8):
                        mm(P, pt[:], wmat[:, kc, dc * 128:(dc + 1) * 128], hb[:, kc, :], kc == 0, kc == 7, [wqtok, hbtok], [pttok])
                t1, t1tok = fr.next()
                t2, t2tok = fr.next()
                P.add("dve", lambda e, o=t1[:], a=p1[:], b=cs[:]: e.tensor_tensor(out=o, in0=a, in1=b, op=ALU.mult), [p1tok, cstok], [t1tok])
                P.add("dve", lambda e, o=t2[:], a=p2[:], b=sn[:]: e.tensor_tensor(out=o, in0=a, in1=b, op=ALU.mult), [p2tok, sntok], [t2tok])
                P.add("pool", lambda e, o=dst[:, 0, dcol:dcol + 512], a=t1[:], b=t2[:]: e.tensor_tensor(out=o, in0=a, in1=b, op=ALU.subtract),
                      [t1tok, t2tok], [dtok])
                P.add("dve", lambda e, o=t1[:], a=p1[:], b=sn[:]: e.tensor_tensor(out=o, in0=a, in1=b, op=ALU.mult), [p1tok, sntok], [t1tok])
                P.add("dve", lambda e, o=t2[:], a=p2[:], b=cs[:]: e.tensor_tensor(out=o, in0=a, in1=b, op=ALU.mult), [p2tok, cstok], [t2tok])
                P.add("pool", lambda e, o=dst[:, 1, dcol:dcol + 512], a=t1[:], b=t2[:]: e.tensor_tensor(out=o, in0=a, in1=b, op=ALU.add),
                      [t1tok, t2tok], [dtok])
            for bl in range(4):
                blk = ti * 4 + bl
                pv, pvtok = ps_a.next()
                for kc in range(8):
                    mm(P, pv[:], hb[:, kc, bl * 128:(bl + 1) * 128], wv[:, kc, :], kc == 0, kc == 7, [wqtok, hbtok], [pvtok])
                P.add("act", lambda e, o=v_tok[:, blk, :], i=pv[:]: e.activation(out=o, in_=i, func=AF.Identity), [pvtok], [vtok])

        if h + 1 < RH:
            load_w(h + 1, "qkv")
        NG = NB // 4

        def scores(i, cg):
            sc, sctok = ps_s.next()
            for sub in range(4):
                c = cg * 4 + sub
                for dc in range(2):
                    mm(P, sc[:, sub * 128:(sub + 1) * 128], k_fm[:, dc, c * 128:(c + 1) * 128], q_fm[:, dc, i * 128:(i + 1) * 128],
                       sub == 0 and dc == 0, dc == 1, [ktok, qtok], [sctok], skip=True)
            return sc, sctok

        seq = [(i, cg) for i in range(16) for cg in range(NG)]
        cur = scores(*seq[0])
        po, potok = None, None
        for si_, (i, cg) in enumerate(seq):
            sc, sctok = cur
            if cg == 0:
                po, potok = ps_o.next()
            pts = []
            for sub in range(4):
                c = cg * 4 + sub
                dl = i - c
                pt, pttok = p_r.next()
                if c >= 16:
                    P.add("dve", lambda e, o=pt[:], a=sc[:, sub * 128:(sub + 1) * 128], s_=so[:, i * 16 + c - 16:i * 16 + c - 15]:
                          e.scalar_tensor_tensor(out=o, in0=a, scalar=s_, in1=go[:], op0=ALU.mult, op1=ALU.mult), [sctok, gtk], [pttok])
                elif dl > 0:
                    P.add("dve", lambda e, o=pt[:], a=sc[:, sub * 128:(sub + 1) * 128], s_=sf[:, dl:dl + 1]:
                          e.scalar_tensor_tensor(out=o, in0=a, scalar=s_, in1=gf[:], op0=ALU.mult, op1=ALU.mult), [sctok, gtk], [pttok])
                elif dl < 0:
                    P.add("dve", lambda e, o=pt[:], a=sc[:, sub * 128:(sub + 1) * 128], s_=sbk[:, -dl:-dl + 1]:
                          e.scalar_tensor_tensor(out=o, in0=a, scalar=s_, in1=gb[:], op0=ALU.mult, op1=ALU.mult), [sctok, gtk], [pttok])
                else:
                    P.add("dve", lambda e, o=pt[:], a=sc[:, sub * 128:(sub + 1) * 128]:
                          e.tensor_tensor(out=o, in0=a, in1=gd[:], op=ALU.mult), [sctok, gtk], [pttok])
                pts.append((pt, pttok, c))
            if si_ + 1 < len(seq):
                cur = scores(*seq[si_ + 1])
            for (pt, pttok, c) in pts:
                for ec in range(4):
                    mm(P, po[:, ec * 128:(ec + 1) * 128], v_tok[:, c, ec * 128:(ec + 1) * 128], pt[:],
                       c == 0 and ec == 0, c == NB - 1, [vtok, pttok], [potok], skip=True)
            if cg == NG - 1:
                P.add("act", lambda e, o=o_fm[:, :, i * 128:(i + 1) * 128], a=po[:].rearrange("p (a b) -> p a b", a=4):
                      e.activation(out=o, in_=a, func=AF.Identity), [potok], [otok[i]])

        for tt in range(4):
            t0 = tt * 512
            ots = [otok[tt * 4 + b_] for b_ in range(4)]
            p1, p1tok = ps_a.next()
            p2, p2tok = ps_a.next()
            for ec in range(4):
                mm(P, p1[:], ones_f[:], o_fm[:, ec, t0:t0 + 512], ec == 0, ec == 3, ots + [ctok], [p1tok])
            for ec in range(4):
                sq, sqtok = fr.next()
                P.add("act", lambda e, o=sq[:], a=o_fm[:, ec, t0:t0 + 512]: e.activation(out=o, in_=a, func=AF.Square), ots, [sqtok])
                mm(P, p2[:], ones_f[:], sq[:], ec == 0, ec == 3, [sqtok, ctok], [p2tok])
            mu, mutok = C["rsr"].next()
            P.add("act", lambda e, o=mu[:], a=p1[:]: e.activation(out=o, in_=a, func=AF.Identity, scale=1.0 / 512), [p1tok], [mutok])
            rs, rstok = C["rsr"].next()
            P.add("dve", lambda e, o=rs[:], a=mu[:]: e.tensor_tensor(out=o, in0=a, in1=a, op=ALU.mult), [mutok], [rstok])
            P.add("dve", lambda e, o=rs[:], a=p2[:]: e.scalar_tensor_tensor(out=o, in0=a, scalar=1.0 / 512, in1=o, op0=ALU.mult, op1=ALU.subtract),
                  [p2tok, rstok], [rstok])
            P.add("act", lambda e, o=rs[:]: e.activation(out=o, in_=o, func=AF.Sqrt, bias=C["eps_col"][:, 0:1], scale=1.0), [rstok, ctok], [rstok])
            P.add("dve", lambda e, o=rs[:]: e.reciprocal(out=o, in_=o), [rstok], [rstok])
            hb, hbtok = bigr.next()
            P.dma("sp", hb[:, 0:8, :], h_dv[:, :, t0:t0 + 512], [hd_tok[tt]], [hbtok], hbtok)
            gt_sb, gttok = bigr.next()
            for ec in range(4):
                pg, pgtok = ps_a.next()
                for kc in range(8):
                    mm(P, pg[:], wg[:, kc, ec * 128:(ec + 1) * 128], hb[:, kc, :], kc == 0, kc == 7, [wgtok, hbtok], [pgtok])
                sg, sgtok = fr.next()
                P.add("act", lambda e, o=sg[:], a=pg[:]: e.activation(out=o, in_=a, func=AF.Silu), [pgtok], [sgtok])
                dd, ddtok = fr.next()
                P.add("pool", lambda e, o=dd[:], a=o_fm[:, ec, t0:t0 + 512], m=mu[:]: e.tensor_tensor(out=o, in0=a, in1=m, op=ALU.subtract),
                      ots + [mutok], [ddtok])
                P.add("dve", lambda e, o=dd[:], r=rs[:]: e.tensor_tensor(out=o, in0=o, in1=r, op=ALU.mult), [ddtok, rstok], [ddtok])
                P.add("dve", lambda e, o=gt_sb[:, ec, :], a=dd[:], g_=gng_sb[:, h * 4 + ec:h * 4 + ec + 1], s_=sg[:]:
                      e.scalar_tensor_tensor(out=o, in0=a, scalar=g_, in1=s_, op0=ALU.mult, op1=ALU.mult), [ddtok, sgtok, gngtok], [gttok])
            P.dma("sp", gT_dv[:, h * 4:(h + 1) * 4, t0:t0 + 512], gt_sb[:, 0:4, :], [gttok], [gT_tok[h][tt]], gttok)

    wo_bufs = [wq[:].rearrange("p a b -> p (a b)").rearrange("p (j n) -> p j n", j=16),
               wk[:].rearrange("p a b -> p (a b)").rearrange("p (j n) -> p j n", j=16)]
    w_out_v = w_out.rearrange("(j p) n -> p j n", p=128)
    for o in range(8):
        wb, wbtok = wo_bufs[o % 2], wqtok
        for half in range(2):
            st, sttok = wst.next()
            P.dma("sp", st[:], w_out_v[:, half * 8:(half + 1) * 8, o * 128:(o + 1) * 128], [], [sttok], sttok)
            P.add("pool", lambda e, oo=wb[:, half * 8:(half + 1) * 8, :], i=st[:]: e.tensor_copy(out=oo, in_=i), [sttok], [wbtok])
        for tt in range(4):
            t0 = tt * 512
            gb_, gbtok = bigr.next()
            P.dma("sp", gb_[:], gT_dv[:, :, t0:t0 + 512], [gT_tok[h_][tt] for h_ in range(RH)], [gbtok], gbtok)
            pt, pttok = ps_a.next()
            for j in range(16):
                mm(P, pt[:], wb[:, j, :], gb_[:, j, :], j == 0, j == 15, [wbtok, gbtok], [pttok])
            xt, xtok = fr.next()
            P.dma("sp", xt[:], x_in[o * 128:(o + 1) * 128, t0:t0 + 512], [], [xtok], xtok)
            P.add("dve", lambda e, oo=xt[:], a=pt[:]: e.tensor_tensor(out=oo, in0=a, in1=oo, op=ALU.add), [pttok, xtok], [xtok])
            P.dma("sp", x_out[o * 128:(o + 1) * 128, t0:t0 + 512], xt[:], [xtok], [], xtok, is_out=is_out)


def build_ffn_prog():
    nc = bass.Bass("TRN2", target_bir_lowering=False)
    x_in = nc.dram_tensor("x_in", [D, T + 2], F32, kind="ExternalInput").ap()
    g2c = nc.dram_tensor("g2c", [128, 8], F32, kind="ExternalInput").ap()
    w_up = nc.dram_tensor("w_up", [D, 2 * FFN], F32, kind="ExternalInput").ap()
    dww = nc.dram_tensor("dww", [128, 3 * 2 * NH], F32, kind="ExternalInput").ap()
    dwb = nc.dram_tensor("dwb", [128, 2 * NH], F32, kind="ExternalInput").ap()
    w_down = nc.dram_tensor("w_down", [FFN, D], F32, kind="ExternalInput").ap()
    x_out = nc.dram_tensor("x_out", [D, T], F32, kind="ExternalOutput").ap()
    with contextlib.ExitStack() as stack:
        P = Prog(nc, stack)
        C = make_common(P)
        emit_ffn(P, C, x_in, x_out, g2c, w_up, dww, dwb, w_down, is_out=True)
        P.emit()
        print("ffn prog stats", P.stats)
    return nc


def cols(v, n):
    return np.ascontiguousarray(v.reshape(n, 128).T)


def shard_tokens_fm(xfull, halo):
    out = []
    for c in range(NCORES):
        b, hf = c // 2, c % 2
        lo, hi = hf * T - halo, (hf + 1) * T + halo
        buf = np.zeros((T + 2 * halo, D), np.float32)
        slo, shi = max(lo, 0), min(hi, SEQ)
        buf[slo - lo:shi - lo] = xfull[b, slo:shi]
        out.append(np.ascontiguousarray(buf.T))
    return out


def unshard_tokens_fm(outs):
    x = np.empty((BATCH, SEQ, D), np.float32)
    for c in range(NCORES):
        b, hf = c // 2, c % 2
        x[b, hf * T:(hf + 1) * T] = outs[c].T
    return x


def ffn_inmaps(x, i, norm2_g, ffn_w_up, ffn_dw_w, ffn_dw_b, ffn_w_down):
    xs = shard_tokens_fm(x, 1) if x is not None else None
    dww = np.concatenate([cols(ffn_dw_w[i, k], 2 * NH) for k in range(3)], axis=1)
    common = {
        "g2c": cols(norm2_g[i], 8),
        "w_up": np.ascontiguousarray(ffn_w_up[i]),
        "dww": np.ascontiguousarray(dww),
        "dwb": cols(ffn_dw_b[i], 2 * NH),
        "w_down": np.ascontiguousarray(ffn_w_down[i]),
    }
    return [dict(common, x_in=xs[c]) if xs is not None else dict(common) for c in range(NCORES)]


def build_conv_prog():
    nc = bass.Bass("TRN2", target_bir_lowering=False)
    dt = lambda name, shape, kind="ExternalInput": nc.dram_tensor(name, shape, F32, kind=kind).ap()
    x_in = dt("x_in", [D, T + 2 * HC])
    mask = dt("mask", [128, 2 * HC])
    g1c = dt("g1c", [128, 8])
    w_in = dt("w_in", [D, 2 * D])
    b_in = dt("b_in", [128, 16])
    dw_w = dt("dw_w", [128, CW * 8])
    dw_b = dt("dw_b", [128, 8])
    ln_g = dt("ln_g", [128, 8])
    ln_b = dt("ln_b", [128, 8])
    w_out = dt("w_out", [D, D])
    x_out = dt("x_out", [D, T], "ExternalOutput")
    with contextlib.ExitStack() as stack:
        P = Prog(nc, stack)
        C = make_common(P, nfr=12)
        emit_conv(P, C, x_in, x_out, mask, g1c, w_in, b_in, dw_w, dw_b, ln_g, ln_b, w_out, is_out=True)
        P.emit()
        print("conv prog stats", P.stats)
    return nc


def conv_inmaps(x, j, g1, conv_w_in, conv_b_in, conv_dw_w, conv_dw_b, conv_ln_g, conv_ln_b, conv_w_out):
    xs = shard_tokens_fm(x, HC) if x is not None else None
    dww = np.concatenate([cols(conv_dw_w[j, k], 8) for k in range(CW)], axis=1)
    common = {
        "g1c": cols(g1, 8),
        "w_in": np.ascontiguousarray(conv_w_in[j]),
        "b_in": cols(conv_b_in[j], 16),
        "dw_w": np.ascontiguousarray(dww),
        "dw_b": cols(conv_dw_b[j], 8),
        "ln_g": cols(conv_ln_g[j], 8),
        "ln_b": cols(conv_ln_b[j], 8),
        "w_out": np.ascontiguousarray(conv_w_out[j]),
    }
    maps = []
    for c in range(NCORES):
        hf = c % 2
        m = np.ones((128, 2 * HC), np.float32)
        if hf == 0:
            m[:, :HC] = 0.0
        else:
            m[:, HC:] = 0.0
        maps.append(dict(common, x_in=xs[c], mask=m) if xs is not None else dict(common, mask=m))
    return maps


def dbg_dump(P, C, slot, src, toks):
    t, ttok = C["dbgr"].next()
    P.add("act", lambda e: e.activation(out=t[:], in_=src, func=AF.Identity), toks, [ttok])
    P.dma("sp", C["dbg"][:, slot * 128:(slot + 1) * 128], t[:], [ttok], [], ttok, is_out=True)


def build_nat_prog(debug=False):
    nc = bass.Bass("TRN2", target_bir_lowering=False)
    dt = lambda name, shape, kind="ExternalInput": nc.dram_tensor(name, shape, F32, kind=kind).ap()
    x_in = dt("x_in", [D, T + 2 * HN])
    g1c = dt("g1c", [128, 8])
    w_qkv = dt("w_qkv", [D, 3 * D])
    qg = dt("qg", [128, 2])
    kg = dt("kg", [128, 1])
    bias = dt("bias", [8, 128, 2 * NE * 128])
    pen = dt("pen", [2, 5 * NE * 128])
    ohk = dt("ohk", [2, 128])
    bd = dt("bd", [128, 128])
    w_out = dt("w_out", [D, D])
    x_out = dt("x_out", [D, T], "ExternalOutput")
    with contextlib.ExitStack() as stack:
        P = Prog(nc, stack)
        C = make_common(P, nfr=6)
        if debug:
            C["dbg"] = dt("dbg", [128, 16 * 128], "ExternalOutput")
            C["dbgr"] = Ring(P, 2, [128, 128], F32, "dbgr")
        emit_nat(P, C, x_in, x_out, g1c, w_qkv, qg, kg, bias, pen, ohk, bd, w_out, is_out=True)
        P.emit()
        print("nat prog stats", P.stats)
    return nc


def nat_bias_table(rpb):
    kc = np.arange(64)[:, None]
    qc = np.arange(64)[None, :]
    cs = np.clip(qc - 8, 0, 48)
    win = (kc >= cs) & (kc < cs + 16)
    dc = np.clip(kc - qc + 15, 0, 30)
    out = np.full((8, 128, 2, NE, 128), NEG, np.float32)
    for hp in range(8):
        for hh in range(2):
            h = 2 * hp + hh
            for ei in range(NE):
                e_ = ei - 1
                for kp in range(2):
                    for qp_ in range(2):
                        dr = 2 * e_ + 3 + kp - qp_
                        if dr < 0 or dr > 14:
                            continue
                        blk = np.where(win, rpb[h, dr][dc], np.float32(NEG))
                        out[hp, kp * 64:(kp + 1) * 64, hh, ei, qp_ * 64:(qp_ + 1) * 64] = blk
    return out.reshape(8, 128, 2 * NE * 128)


def nat_pen_table(hf):
    out = np.full((2, 5, NE, 128), NEG, np.float32)
    for pix, qp in enumerate((0, 1, 7, 14, 15)):
        for ei in range(NE):
            e_ = ei - 1
            for kp in range(2):
                for qp_ in range(2):
                    r = 32 * hf + 2 * qp + qp_
                    kr = 32 * hf + 2 * qp + 2 * e_ - 4 + kp
                    rs = min(max(r - 4, 0), 56)
                    if 0 <= kr < 64 and rs <= kr < rs + 8:
                        out[kp, pix, ei, qp_ * 64:(qp_ + 1) * 64] = 0.0
    return out.reshape(2, 5 * NE * 128)


def nat_qg2(g):
    out = np.zeros((128, 2), np.float32)
    out[0:64, 0] = g
    out[64:128, 1] = g
    return out


def nat_inmaps(x, g1, nat_w_qkv, nat_q_norm_g, nat_k_norm_g, nat_rpb, nat_w_out):
    xs = shard_tokens_fm(x, HN) if x is not None else None
    ohk = np.zeros((2, 128), np.float32)
    ohk[0, :64] = 1.0
    ohk[1, 64:] = 1.0
    common = {
        "g1c": cols(g1, 8),
        "w_qkv": np.ascontiguousarray(nat_w_qkv[0]),
        "qg": nat_qg2(nat_q_norm_g[0]),
        "kg": np.ascontiguousarray(np.tile(nat_k_norm_g[0], 2)[:, None]),
        "bias": nat_bias_table(nat_rpb[0]),
        "ohk": ohk,
        "bd": np.kron(np.eye(2, dtype=np.float32), np.ones((64, 64), np.float32)),
        "w_out": np.ascontiguousarray(nat_w_out[0]),
    }
    pens = [nat_pen_table(0), nat_pen_table(1)]
    return [dict(common, x_in=xs[c], pen=pens[c % 2]) if xs is not None else dict(common, pen=pens[c % 2]) for c in range(NCORES)]


def build_ret_prog():
    nc = bass.Bass("TRN2", target_bir_lowering=False)
    dt = lambda name, shape, kind="ExternalInput": nc.dram_tensor(name, shape, F32, kind=kind).ap()
    x_in = dt("x_in", [D, RB])
    g1c = dt("g1c", [128, 8])
    w_in = dt("w_in", [D, 6 * D])
    cosT = dt("cosT", [128, RB])
    sinT = dt("sinT", [128, RB])
    l2d = dt("l2d", [128, 8])
    gng = dt("gng", [128, 16])
    w_out = dt("w_out", [2 * D, D])
    hm = dt("hm", [128, 2])
    x_out = dt("x_out", [D, T], "ExternalOutput")
    with contextlib.ExitStack() as stack:
        P = Prog(nc, stack)
        C = make_common(P, nfr=8)
        emit_ret(P, C, nc, x_in, x_out, g1c, w_in, cosT, sinT, l2d, gng, w_out, hm, is_out=True)
        P.emit()
        print("ret prog stats", P.stats)
    return nc


def ret_rope_tables(hf):
    theta = (1.0 / (np.float32(10000.0) ** np.linspace(0.0, 1.0, 128, dtype=np.float32))).astype(np.float32)
    u = np.arange(RB)
    pos = np.where(u < T, T * hf + u, T * (1 - hf) + (u - T)).astype(np.float32)
    ang = (theta[:, None] * pos[None, :]).astype(np.float32)
    return np.cos(ang).astype(np.float32), np.sin(ang).astype(np.float32)


def ret_inmaps(x, g1, ret_w_in, ret_log2_inv_decay, ret_gn_g, ret_w_out):
    common = {
        "g1c": cols(g1, 8),
        "w_in": np.ascontiguousarray(ret_w_in[0]),
        "l2d": np.ascontiguousarray(np.tile(ret_log2_inv_decay[0].reshape(1, 8), (128, 1))),
        "gng": cols(ret_gn_g[0], 16),
        "w_out": np.ascontiguousarray(ret_w_out[0]),
    }
    tabs = [ret_rope_tables(0), ret_rope_tables(1)]
    maps = []
    for c in range(NCORES):
        b, hf = c // 2, c % 2
        hm = np.zeros((128, 2), np.float32)
        hm[:, 0] = float(hf)
        hm[:, 1] = float(1 - hf)
        if x is None:
            maps.append(dict(common, cosT=tabs[hf][0], sinT=tabs[hf][1], hm=hm))
            continue
        buf = np.concatenate([x[b, hf * T:(hf + 1) * T], x[b, (1 - hf) * T:(2 - hf) * T]], axis=0)
        maps.append(dict(common, x_in=np.ascontiguousarray(buf.T), cosT=tabs[hf][0], sinT=tabs[hf][1], hm=hm))
    return maps


STAGES = [("c0", "conv", HC), ("f0", "ffn", 1), ("n1", "nat", HN), ("f1", "ffn", 1),
          ("r2", "ret", 2048), ("f2", "ffn", 1), ("c3", "conv", HC), ("f3", "ffn", 1)]
STAGE_IN = {
    "conv": [("mask", [128, 2 * HC]), ("g1c", [128, 8]), ("w_in", [D, 2 * D]), ("b_in", [128, 16]), ("dw_w", [128, CW * 8]),
             ("dw_b", [128, 8]), ("ln_g", [128, 8]), ("ln_b", [128, 8]), ("w_out", [D, D])],
    "ffn": [("g2c", [128, 8]), ("w_up", [D, 2 * FFN]), ("dww", [128, 3 * 2 * NH]), ("dwb", [128, 2 * NH]), ("w_down", [FFN, D])],
    "nat": [("g1c", [128, 8]), ("w_qkv", [D, 3 * D]), ("qg", [128, 2]), ("kg", [128, 1]), ("bias", [8, 128, 2 * NE * 128]),
            ("pen", [2, 5 * NE * 128]), ("ohk", [2, 128]), ("bd", [128, 128]), ("w_out", [D, D])],
    "ret": [("g1c", [128, 8]), ("w_in", [D, 6 * D]), ("cosT", [128, RB]), ("sinT", [128, RB]), ("l2d", [128, 8]),
            ("gng", [128, 16]), ("w_out", [2 * D, D]), ("hm", [128, 2])],
}
STAGE_NFR = {"conv": 12, "ffn": 14, "nat": 6, "ret": 8}


def emit_exchange(P, C, nc, name, x_next, H, hmask_sb, hmtok):
    kw = dict(allow_slow_non_contiguous=True) if H < 8 else {}
    groups = [[0, 1], [2, 3], [4, 5], [6, 7]]
    xv = x_next.rearrange("(c p) t -> p c t", p=128)
    W = min(H, 512)
    hr = Ring(P, 2, [128, 8, W], F32, "xh")

    def fill(src2d, dcol, mi, gtok):
        srcv = src2d.rearrange("(c p) t -> p c t", p=128)
        xt, xtok = hr.next()
        P.dma("sp", xt[:], srcv, [gtok], [xtok], xtok, **kw)
        P.add("dve", lambda e, o=xt[:], m=hmask_sb[:, mi:mi + 1]:
              e.tensor_scalar(out=o, in0=o, scalar1=m, scalar2=None, op0=ALU.mult), [xtok, hmtok], [xtok])
        P.dma("sp", xv[:, :, dcol:dcol + W], xt[:], [xtok], [], xtok, **kw)

    if H == 2048:
        for q in range(4):
            snd = nc.dram_tensor(f"{name}_snd{q}", [D, 512], F32, kind="Internal").ap()
            gath = nc.dram_tensor(f"{name}_gath{q}", [2 * D, 512], F32, kind="Internal").ap()
            stok, gtok, cctok = Tok("snd"), Tok("gath"), Tok("cc")
            P.dma("sp", snd, x_next[:, q * 512:(q + 1) * 512], [], [stok], stok)
            P.coll(lambda e, s_=snd, g_=gath: e.collective_compute("AllGather", ALU.bypass, replica_groups=groups,
                                                                   ins=[s_], outs=[g_]), [stok], [gtok], cctok)
            r0, r0tok = hr.next()
            r1, r1tok = hr.next()
            P.dma("sp", r0[:], gath[0:D, :].rearrange("(c p) t -> p c t", p=128), [gtok], [r0tok], r0tok)
            P.dma("sp", r1[:], gath[D:2 * D, :].rearrange("(c p) t -> p c t", p=128), [gtok], [r1tok], r1tok)
            P.add("dve", lambda e, o=r0[:]: e.tensor_scalar(out=o, in0=o, scalar1=hmask_sb[:, 0:1], scalar2=None, op0=ALU.mult),
                  [r0tok, hmtok], [r0tok])
            P.add("dve", lambda e, o=r0[:], a=r1[:]: e.scalar_tensor_tensor(out=o, in0=a, scalar=hmask_sb[:, 1:2], in1=o,
                                                                           op0=ALU.mult, op1=ALU.add), [r0tok, r1tok, hmtok], [r0tok])
            P.dma("sp", xv[:, :, T + q * 512:T + (q + 1) * 512], r0[:], [r0tok], [], r0tok)
        return
    snd = nc.dram_tensor(name + "_snd", [2 * D, H], F32, kind="Internal").ap()
    gath = nc.dram_tensor(name + "_gath", [4 * D, H], F32, kind="Internal").ap()
    stok, gtok, cctok = Tok("snd"), Tok("gath"), Tok("cc")
    P.dma("sp", snd[0:D, :], x_next[:, H:2 * H], [], [stok], stok, **kw)
    P.dma("sp", snd[D:2 * D, :], x_next[:, T:T + H], [], [stok], stok, **kw)
    P.coll(lambda e: e.collective_compute("AllGather", ALU.bypass, replica_groups=groups,
                                          ins=[snd], outs=[gath]), [stok], [gtok], cctok)
    fill(gath[D:2 * D, :], 0, 0, gtok)
    fill(gath[2 * D:3 * D, :], H + T, 1, gtok)


def build_fused_prog(nst=8):
    stages = STAGES[:nst]
    nc = bass.Bass("TRN2", target_bir_lowering=False)
    dt = lambda name, shape, kind="ExternalInput": nc.dram_tensor(name, shape, F32, kind=kind).ap()
    aps = {}
    for (sn, kind, H) in stages:
        aps[sn] = {k: dt(f"{sn}_{k}", shp) for (k, shp) in STAGE_IN[kind]}
    hmask = dt("hmask", [128, 2])
    bufs = {}
    for si, (sn, kind, H) in enumerate(stages):
        width = RB if kind == "ret" else T + 2 * H
        bufs[sn] = dt(f"{sn}_x_in", [D, width], "ExternalInput" if si == 0 else "Internal")
    y = dt("x_out", [D, T], "ExternalOutput")
    with contextlib.ExitStack() as stack:
        P = Prog(nc, stack)
        for si, (sn, kind, H) in enumerate(stages):
            last = si == len(stages) - 1
            x_in = bufs[sn]
            if last:
                x_out = y
            else:
                nsn, nkind, nH = stages[si + 1]
                x_out = bufs[nsn][:, 0:T] if nkind == "ret" else bufs[nsn][:, nH:nH + T]
            a = aps[sn]
            with contextlib.ExitStack() as st:
                P.stack = st
                P.pfx = sn + "_"
                C = make_common(P, nfr=STAGE_NFR[kind])
                if kind == "conv":
                    emit_conv(P, C, x_in, x_out, a["mask"], a["g1c"], a["w_in"], a["b_in"], a["dw_w"], a["dw_b"], a["ln_g"],
                              a["ln_b"], a["w_out"], is_out=last)
                elif kind == "ffn":
                    emit_ffn(P, C, x_in, x_out, a["g2c"], a["w_up"], a["dww"], a["dwb"], a["w_down"], is_out=last)
                elif kind == "nat":
                    emit_nat(P, C, x_in, x_out, a["g1c"], a["w_qkv"], a["qg"], a["kg"], a["bias"], a["pen"], a["ohk"], a["bd"],
                             a["w_out"], is_out=last)
                else:
                    emit_ret(P, C, nc, x_in, x_out, a["g1c"], a["w_in"], a["cosT"], a["sinT"], a["l2d"], a["gng"], a["w_out"],
                             a["hm"], is_out=last)
            P.barrier()
            if not last:
                with contextlib.ExitStack() as st:
                    P.stack = st
                    P.pfx = sn + "x_"
                    C = make_common(P, nfr=2)
                    hm_sb = P.sb([128, 2], F32, "hmask")
                    hmtok = load_cols(P, C, hm_sb, hmask)
                    emit_exchange(P, C, nc, sn + "x", bufs[nsn], nH, hm_sb, hmtok)
                P.barrier()
        P.stack = stack
        P.pfx = ""
        P.emit()
        print("fused prog stats", P.stats)
    return nc


def fused_inmaps(a):
    per_stage = {}
    per_stage["c0"] = conv_inmaps(a["x"], 0, a["norm1_g"][0], a["conv_w_in"], a["conv_b_in"], a["conv_dw_w"], a["conv_dw_b"],
                                  a["conv_ln_g"], a["conv_ln_b"], a["conv_w_out"])
    per_stage["c3"] = conv_inmaps(None, 1, a["norm1_g"][3], a["conv_w_in"], a["conv_b_in"], a["conv_dw_w"], a["conv_dw_b"],
                                  a["conv_ln_g"], a["conv_ln_b"], a["conv_w_out"])
    per_stage["n1"] = nat_inmaps(None, a["norm1_g"][1], a["nat_w_qkv"], a["nat_q_norm_g"], a["nat_k_norm_g"], a["nat_rpb"],
                                 a["nat_w_out"])
    per_stage["r2"] = ret_inmaps(None, a["norm1_g"][2], a["ret_w_in"], a["ret_log2_inv_decay"], a["ret_gn_g"], a["ret_w_out"])
    for i in range(4):
        per_stage[f"f{i}"] = ffn_inmaps(None, i, a["norm2_g"], a["ffn_w_up"], a["ffn_dw_w"], a["ffn_dw_b"], a["ffn_w_down"])
    maps = []
    for c in range(NCORES):
        m = {}
        for sn, lst in per_stage.items():
            for k, v in lst[c].items():
                m[f"{sn}_{k}"] = v
        hm = np.zeros((128, 2), np.float32)
        hm[:, 0] = float(c % 2)
        hm[:, 1] = float(1 - c % 2)
        m["hmask"] = hm
        maps.append(m)
    return maps


_PROGS = {}


def _prog(name, builder):
    if name not in _PROGS:
        _PROGS[name] = builder()
    return _PROGS[name]


def _launch(nc, maps):
    res = run_bass_kernel_spmd(nc, maps, core_ids=list(range(NCORES)))
    return unshard_tokens_fm([r["x_out"] for r in res.results])


def kernel_unfused(x, norm1_g, norm2_g, conv_w_in, conv_b_in, conv_dw_w, conv_dw_b, conv_ln_g, conv_ln_b, conv_w_out,
           nat_w_qkv, nat_q_norm_g, nat_k_norm_g, nat_rpb, nat_w_out, ret_w_in, ret_log2_inv_decay, ret_gn_g,
           ret_w_out, ffn_w_up, ffn_dw_w, ffn_dw_b, ffn_w_down):
    a = {k: np.asarray(v, np.float32) for k, v in locals().items()}
    xc = a["x"]
    for i in range(4):
        mixer, j = i % 3, i // 3
        if mixer == 0:
            maps = conv_inmaps(xc, j, a["norm1_g"][i], a["conv_w_in"], a["conv_b_in"], a["conv_dw_w"], a["conv_dw_b"],
                               a["conv_ln_g"], a["conv_ln_b"], a["conv_w_out"])
            xc = _launch(_prog("conv", build_conv_prog), maps)
        elif mixer == 1:
            maps = nat_inmaps(xc, a["norm1_g"][i], a["nat_w_qkv"], a["nat_q_norm_g"], a["nat_k_norm_g"], a["nat_rpb"],
                              a["nat_w_out"])
            xc = _launch(_prog("nat", build_nat_prog), maps)
        else:
            maps = ret_inmaps(xc, a["norm1_g"][i], a["ret_w_in"], a["ret_log2_inv_decay"], a["ret_gn_g"], a["ret_w_out"])
            xc = _launch(_prog("ret", build_ret_prog), maps)
        maps = ffn_inmaps(xc, i, a["norm2_g"], a["ffn_w_up"], a["ffn_dw_w"], a["ffn_dw_b"], a["ffn_w_down"])
        xc = _launch(_prog("ffn", build_ffn_prog), maps)
    return xc


def kernel(x, norm1_g, norm2_g, conv_w_in, conv_b_in, conv_dw_w, conv_dw_b, conv_ln_g, conv_ln_b, conv_w_out,
           nat_w_qkv, nat_q_norm_g, nat_k_norm_g, nat_rpb, nat_w_out, ret_w_in, ret_log2_inv_decay, ret_gn_g,
           ret_w_out, ffn_w_up, ffn_dw_w, ffn_dw_b, ffn_w_down):
    a = {k: np.asarray(v, np.float32) for k, v in locals().items()}
    maps = fused_inmaps(a)
    nc = _prog("fused", build_fused_prog)
    res = run_bass_kernel_spmd(nc, maps, core_ids=list(range(NCORES)))
    return unshard_tokens_fm([r["x_out"] for r in res.results])
```

```python
import contextlib
import numpy as np
import concourse.bass as bass
import concourse.mybir as mybir
from concourse.bass_utils import run_bass_kernel_spmd

F32 = mybir.dt.float32
BF16 = mybir.dt.bfloat16
AF = mybir.ActivationFunctionType
ALU = mybir.AluOpType
AX = mybir.AxisListType

D = 1024
SEQ = 4096
BATCH = 4
T = 2048
NCORES = 8
FFN = 2816
NH = FFN // 128
EPS = 1e-6


class Tok:
    __slots__ = ("lw", "rd", "name", "sem", "dcount", "last_dma")

    def __init__(self, name=""):
        self.lw = None
        self.rd = []
        self.name = name
        self.sem = None
        self.dcount = 0
        self.last_dma = None


class Op:
    __slots__ = ("eng", "fn", "deps", "is_dma", "dtok", "has_dep", "sem", "val",
                 "waits", "know", "is_out", "inc", "is_barrier")

    def __init__(self, eng, fn, is_dma=False, dtok=None):
        self.eng = eng
        self.fn = fn
        self.deps = set()
        self.is_dma = is_dma
        self.dtok = dtok
        self.has_dep = False
        self.sem = None
        self.val = 0
        self.waits = ()
        self.know = None
        self.is_out = False
        self.inc = 16 if is_dma else 1
        self.is_barrier = False


class Prog:
    ENGS = ("pe", "act", "dve", "pool", "sp")

    def __init__(self, nc, stack):
        self.nc = nc
        self.stack = stack
        self.ops = []
        self.nsb = 0
        self.nsem = 0
        self.out_ops = []
        self.pfx = ""
        self.bar_start = 0
        self.prev_bar = []

    def sb(self, shape, dtype, name=None):
        self.nsb += 1
        return self.stack.enter_context(
            self.nc.sbuf_tensor(self.pfx + (name or f"sb{self.nsb}"), list(shape), dtype))

    def ps(self, shape, dtype, name=None):
        self.nsb += 1
        return self.stack.enter_context(
            self.nc.psum_tensor(self.pfx + (name or f"ps{self.nsb}"), list(shape), dtype))

    def barrier(self):
        last = {}
        dmas = {}
        for op in self.ops[self.bar_start:]:
            if op.is_dma:
                dmas[id(op.dtok)] = op
            else:
                last[op.eng] = op
        deps = set(last.values()) | set(dmas.values()) | set(self.prev_bar)
        bars = []
        for eng in self.ENGS:
            op = Op(eng, lambda e: e.nop())
            op.deps = set(deps)
            op.is_barrier = (eng == self.ENGS[0])
            self.ops.append(op)
            bars.append(op)
        self.prev_bar = bars
        self.bar_start = len(self.ops)

    def new_sem(self, name=None):
        self.nsem += 1
        return self.stack.enter_context(self.nc.semaphore(name or f"sem{self.nsem}"))

    def add(self, eng, fn, reads=(), writes=(), is_dma=False, dtok=None, is_out=False):
        op = Op(eng, fn, is_dma, dtok)
        op.is_out = is_out
        for t in reads:
            if t.lw is not None:
                op.deps.add(t.lw)
        for t in writes:
            for r in t.rd:
                op.deps.add(r)
            if t.lw is not None:
                op.deps.add(t.lw)
        if is_dma:
            if dtok.last_dma is not None:
                op.deps.add(dtok.last_dma)
            dtok.last_dma = op
        for t in reads:
            t.rd.append(op)
        for t in writes:
            t.rd = []
            t.lw = op
        op.deps.discard(op)
        if eng == "pe" and not is_dma:
            op.deps = {d for d in op.deps if not (d.eng == "pe" and not d.is_dma)}
        self.ops.append(op)
        if is_out:
            self.out_ops.append(op)
        return op

    def coll(self, fn, reads, writes, dtok):
        op = self.add("pool", fn, reads, writes, is_dma=True, dtok=dtok)
        op.inc = 1
        return op

    def dma(self, queue, out, in_, reads, writes, dtok, is_out=False, **kw):
        return self.add(queue, lambda e: e.dma_start(out=out, in_=in_, **kw),
                        reads, writes, is_dma=True, dtok=dtok, is_out=is_out)

    def emit(self):
        ops = self.ops
        for op in ops:
            for d in op.deps:
                d.has_dep = True
        esem = {e: self.new_sem("eng_" + e) for e in self.ENGS}
        cnt = {e: 0 for e in self.ENGS}
        free_sems = []
        live_toks = []
        for op in ops:
            if op.is_barrier:
                for t in live_toks:
                    free_sems.append((t.sem, t.dcount))
                live_toks = []
            if op.is_dma:
                t = op.dtok
                if t.sem is None:
                    if free_sems:
                        t.sem, t.dcount = free_sems.pop()
                    else:
                        t.sem = self.new_sem()
                    live_toks.append(t)
                t.dcount += op.inc
                op.sem = t.sem
                op.val = t.dcount
            elif op.has_dep:
                cnt[op.eng] += 1
                op.sem = esem[op.eng]
                op.val = cnt[op.eng]
        seen = {e: {} for e in self.ENGS}
        nwaits = 0
        for op in ops:
            s = seen[op.eng]
            waits = {}
            for d in op.deps:
                k = id(d.sem)
                if s.get(k, (None, 0))[1] >= d.val:
                    continue
                if waits.get(k, (None, 0))[1] < d.val:
                    waits[k] = (d.sem, d.val)
            for d in op.deps:
                if d.know is not None:
                    for k, v in d.know.items():
                        if s.get(k, (None, 0))[1] < v[1]:
                            s[k] = v
            for k, v in waits.items():
                if s.get(k, (None, 0))[1] < v[1]:
                    s[k] = v
            op.waits = list(waits.values())
            nwaits += len(op.waits)
            if op.sem is not None:
                kn = dict(s)
                kn[id(op.sem)] = (op.sem, op.val)
                op.know = kn
                if not op.is_dma:
                    s[id(op.sem)] = (op.sem, op.val)
        by = {e: [o for o in ops if o.eng == e] for e in self.ENGS}
        finals = [(o.sem, o.val) for o in self.out_ops]
        self.stats = dict(nops=len(ops), nwaits=nwaits,
                          per_eng={e: len(by[e]) for e in self.ENGS}, nsem=self.nsem)

        def run(name, e):
            for op in by[name]:
                for sem, val in op.waits:
                    e.wait_ge(sem, val)
                inst = op.fn(e)
                if op.sem is not None:
                    inst.then_inc(op.sem, op.inc)
            if name == "sp":
                for sem, val in finals:
                    e.wait_ge(sem, val)

        with self.nc.Block() as block:
            @block.tensor
            def _(e):
                run("pe", e)

            @block.scalar
            def _(e):
                run("act", e)

            @block.vector
            def _(e):
                run("dve", e)

            @block.gpsimd
            def _(e):
                run("pool", e)

            @block.sync
            def _(e):
                run("sp", e)


class Ring:
    def __init__(self, P, n, shape, dtype, name, psum=False):
        self.bufs = []
        for i in range(n):
            t = (P.ps if psum else P.sb)(shape, dtype, f"{name}{i}")
            self.bufs.append((t, Tok(f"{name}{i}")))
        self.i = 0

    def next(self):
        b = self.bufs[self.i % len(self.bufs)]
        self.i += 1
        return b


def mm(P, out, lhsT, rhs, start, stop, reads, writes, skip=False):
    return P.add("pe", lambda e: e.matmul(out, lhsT, rhs, start=start, stop=stop, skip_group_check=skip),
                 reads, writes)


def emit_rmsnorm(P, C, x_dram, g_col, gtok, h_all, h_tok, ps_ring, tiles, two_pass=False, hcol=None):
    fr = C["fr"]
    sqr = C["sqr"]
    ones = C["ones_bf"]
    assert len(fr.bufs) >= (4 if two_pass else 9)
    for ti, (t0, n) in enumerate(tiles):
        d0 = t0 if hcol is None else hcol[ti]
        xs = []
        pst, pstok = ps_ring.next()
        for c in range(8):
            xt, xtok = fr.next()
            P.dma("sp", xt[:, 0:n], x_dram[c * 128:(c + 1) * 128, t0:t0 + n], [], [xtok], xtok)
            xs.append((xt, xtok))
            sq, sqtok = sqr.next()
            P.add("act", lambda e, o=sq[:, 0:n], i=xt[:, 0:n]: e.activation(out=o, in_=i, func=AF.Square),
                  [xtok], [sqtok])
            mm(P, pst[:, 0:n], ones[:], sq[:, 0:n], c == 0, c == 7, [sqtok, C["ctok"]], [pstok])
        rs, rstok = C["rsr"].next()
        P.add("act", lambda e, o=rs[:, 0:n], i=pst[:, 0:n]: e.activation(
            out=o, in_=i, func=AF.Sqrt, bias=C["eps_col"][:, 0:1], scale=1.0 / D), [pstok, C["ctok"]], [rstok])
        P.add("dve", lambda e, o=rs[:, 0:n]: e.reciprocal(out=o, in_=o), [rstok], [rstok])
        for c in range(8):
            if two_pass:
                xt, xtok = fr.next()
                P.dma("sp", xt[:, 0:n], x_dram[c * 128:(c + 1) * 128, t0:t0 + n], [], [xtok], xtok)
            else:
                xt, xtok = xs[c]
            P.add("dve", lambda e, o=h_all[:, c, d0:d0 + n], i=xt[:, 0:n], g=g_col[:, c:c + 1], r=rs[:, 0:n]:
                  e.scalar_tensor_tensor(out=o, in0=i, scalar=g, in1=r, op0=ALU.mult, op1=ALU.mult),
                  [xtok, rstok, gtok], [h_tok[ti][c]])


def make_common(P, nfr=14):
    C = {}
    C["fr"] = Ring(P, nfr, [128, 512], F32, "fr")
    C["sqr"] = Ring(P, 3, [128, 512], BF16, "sqr")
    C["rsr"] = Ring(P, 2, [128, 512], F32, "rsr")
    C["ones_bf"] = P.sb([128, 128], BF16, "ones_bf")
    C["eps_col"] = P.sb([128, 1], F32, "eps_col")
    C["ctok"] = Tok("consts")
    P.add("pool", lambda e: e.memset(C["ones_bf"][:], 1.0), [], [C["ctok"]])
    P.add("pool", lambda e: e.memset(C["eps_col"][:], EPS), [], [C["ctok"]])
    return C


def load_cols(P, C, dst, src_dram, scale=None):
    tok = Tok("par")
    P.dma("sp", dst[:], src_dram, [], [tok], tok)
    if scale is not None:
        P.add("pool", lambda e: e.tensor_scalar(out=dst[:], in0=dst[:], scalar1=float(scale), scalar2=None,
                                                 op0=ALU.mult), [tok], [tok])
    return tok


def htoks_for(h_tok, tiles, c, lo, hi):
    return [h_tok[i][c] for i, (t0, n) in enumerate(tiles) if t0 < hi and t0 + n > lo]


def emit_ffn(P, C, x_in, x_out, g2c, w_up, dww, dwb, w_down, is_out=False):
    NT = T + 2
    g_col = P.sb([128, 8], F32, "ffn_g")
    gtok = load_cols(P, C, g_col, g2c)
    dww_sb = P.sb([128, 3 * 2 * NH], F32, "ffn_dww")
    dwb_sb = P.sb([128, 2 * NH], F32, "ffn_dwb")
    dwtok = load_cols(P, C, dww_sb, dww)
    dbtok = load_cols(P, C, dwb_sb, dwb)

    h_all = P.sb([128, 8, NT], BF16, "ffn_h")
    tiles = [(0, 410), (410, 410), (820, 410), (1230, 410), (1640, NT - 1640)]
    h_tok = [[Tok(f"h{i}_{c}") for c in range(8)] for i in range(len(tiles))]
    ps_stat = Ring(P, 1, [128, 512], F32, "ps_stat", psum=True)
    emit_rmsnorm(P, C, x_in, g_col, gtok, h_all, h_tok, ps_stat, tiles)

    act_all = P.sb([128, NH, T], BF16, "ffn_act")
    act_tok = [[Tok(f"act{j}_{i}") for i in range(5)] for j in range(NH)]

    wst = Ring(P, 2, [128, NH * 128], F32, "wst")
    wbf = Ring(P, 2, [128, NH * 128], BF16, "wbf")
    ps_up = Ring(P, 7, [128, 512], F32, "ps_up", psum=True)
    fr = C["fr"]
    w_up_v = w_up.rearrange("(kc p) n -> p kc n", p=128)
    ctiles = [(0, 410), (410, 410), (820, 410), (1230, 410), (1640, 408)]
    def load_up(j):
        st, sttok = wst.next()
        stv = st[:, 0:2048].rearrange("p (k n) -> p k n", k=8)
        P.dma("sp", stv[:, :, 0:128], w_up_v[:, :, j * 128:(j + 1) * 128], [], [sttok], sttok)
        P.dma("sp", stv[:, :, 128:256], w_up_v[:, :, FFN + j * 128:FFN + (j + 1) * 128], [], [sttok], sttok)
        wb, wbtok = wbf.next()
        P.add("pool", lambda e, o=wb[:, 0:2048], i=st[:, 0:2048]: e.tensor_copy(out=o, in_=i), [sttok], [wbtok])
        return wb[:, 0:2048].rearrange("p (k n) -> p k n", k=8), wbtok

    nxt = load_up(0)
    for j in range(NH):
        wbv, wbtok = nxt
        if j + 1 < NH:
            nxt = load_up(j + 1)
        for ci, (o0, n) in enumerate(ctiles):
            ncol = n + 2
            pv, pvtok = ps_up.next()
            pg, pgtok = ps_up.next()
            for half, (pt, pttok) in enumerate(((pv, pvtok), (pg, pgtok))):
                for kc in range(8):
                    mm(P, pt[:, 0:ncol], wbv[:, kc, half * 128:(half + 1) * 128], h_all[:, kc, o0:o0 + ncol],
                       kc == 0, kc == 7, [wbtok] + htoks_for(h_tok, tiles, kc, o0, o0 + ncol), [pttok])
            av, avtok = fr.next()
            ag, agtok = fr.next()
            for half, (pt, pttok, acc, acctok) in enumerate(((pv, pvtok, av, avtok), (pg, pgtok, ag, agtok))):
                ch = half * NH + j
                w0 = dww_sb[:, 0 * 2 * NH + ch:0 * 2 * NH + ch + 1]
                w1 = dww_sb[:, 1 * 2 * NH + ch:1 * 2 * NH + ch + 1]
                w2 = dww_sb[:, 2 * 2 * NH + ch:2 * 2 * NH + ch + 1]
                bb = dwb_sb[:, ch:ch + 1]
                P.add("act", lambda e, o=acc[:, 0:n], i=pt[:, 1:n + 1], s=w1, b=bb:
                      e.activation(out=o, in_=i, func=AF.Identity, bias=b, scale=s),
                      [pttok, dwtok, dbtok], [acctok])
                P.add("dve", lambda e, o=acc[:, 0:n], i=pt[:, 0:n], s=w0:
                      e.scalar_tensor_tensor(out=o, in0=i, scalar=s, in1=o, op0=ALU.mult, op1=ALU.add),
                      [pttok, acctok, dwtok], [acctok])
                P.add("dve", lambda e, o=acc[:, 0:n], i=pt[:, 2:n + 2], s=w2:
                      e.scalar_tensor_tensor(out=o, in0=i, scalar=s, in1=o, op0=ALU.mult, op1=ALU.add),
                      [pttok, acctok, dwtok], [acctok])
            ge, getok = fr.next()
            P.add("act", lambda e, o=ge[:, 0:n], i=ag[:, 0:n]: e.activation(out=o, in_=i, func=AF.Gelu_apprx_tanh),
                  [agtok], [getok])
            P.add("dve", lambda e, o=act_all[:, j, o0:o0 + n], a=ge[:, 0:n], b=av[:, 0:n]:
                  e.tensor_tensor(out=o, in0=a, in1=b, op=ALU.mult), [getok, avtok], [act_tok[j][ci]])

    ps_dn = ps_up
    w_dn_v = w_down.rearrange("(j p) n -> p j n", p=128)
    def load_dn(o):
        st, sttok = wst.next()
        stv = st[:].rearrange("p (j n) -> p j n", j=NH)
        P.dma("sp", stv, w_dn_v[:, :, o * 128:(o + 1) * 128], [], [sttok], sttok)
        wb, wbtok = wbf.next()
        P.add("pool", lambda e, oo=wb[:], i=st[:]: e.tensor_copy(out=oo, in_=i), [sttok], [wbtok])
        return wb[:].rearrange("p (j n) -> p j n", j=NH), wbtok

    nxt = load_dn(0)
    for o in range(8):
        wbv, wbtok = nxt
        if o + 1 < 8:
            nxt = load_dn(o + 1)
        for tt in range(T // 512):
            t0 = tt * 512
            xt, xtok = fr.next()
            P.dma("sp", xt[:], x_in[o * 128:(o + 1) * 128, 1 + t0:1 + t0 + 512], [], [xtok], xtok)
            pt, pttok = ps_dn.next()
            for j in range(NH):
                rd = [wbtok] + [act_tok[j][ci] for ci, (o0, n) in enumerate(ctiles) if o0 < t0 + 512 and o0 + n > t0]
                mm(P, pt[:], wbv[:, j, :], act_all[:, j, t0:t0 + 512], j == 0, j == NH - 1, rd, [pttok])
            P.add("dve", lambda e, oo=xt[:], a=pt[:]: e.tensor_tensor(out=oo, in0=a, in1=oo, op=ALU.add),
                  [pttok, xtok], [xtok])
            P.dma("sp", x_out[o * 128:(o + 1) * 128, t0:t0 + 512], xt[:], [xtok], [], xtok, is_out=is_out)


CW = 31
HC = 15


def emit_conv(P, C, x_in, x_out, mask, g1c, w_in, b_in, dw_w, dw_b, ln_g, ln_b, w_out, is_out=False):
    NT = T + 2 * HC
    fr = C["fr"]
    g_col = P.sb([128, 8], F32, "cv_g")
    gtok = load_cols(P, C, g_col, g1c)
    bin_sb = P.sb([128, 16], F32, "cv_bin")
    bintok = load_cols(P, C, bin_sb, b_in)
    dww_sb = P.sb([128, CW * 8], F32, "cv_dww")
    dwwtok = load_cols(P, C, dww_sb, dw_w)
    dwb_sb = P.sb([128, 8], F32, "cv_dwb")
    dwbtok = load_cols(P, C, dwb_sb, dw_b)
    lng_sb = P.sb([128, 8], F32, "cv_lng")
    lngtok = load_cols(P, C, lng_sb, ln_g)
    lnb_sb = P.sb([128, 8], F32, "cv_lnb")
    lnbtok = load_cols(P, C, lnb_sb, ln_b)
    mask_sb = P.sb([128, 2 * HC], F32, "cv_mask")
    masktok = load_cols(P, C, mask_sb, mask)
    ones_f = P.sb([128, 128], F32, "ones_f")
    P.add("pool", lambda e: e.memset(ones_f[:], 1.0), [], [C["ctok"]])

    KP = 20
    ident_f = P.sb([128, 128], F32, "cv_idf")
    P.add("pool", lambda e: e.memset(ident_f[:], 1.0), [], [C["ctok"]])
    P.add("pool", lambda e: e.affine_select(out=ident_f[:], in_=ident_f[:], pattern=[[-1, 128]], compare_op=ALU.is_equal,
                                            fill=0.0, base=0, channel_multiplier=1), [C["ctok"]], [C["ctok"]])
    dgr = Ring(P, 2, [128, KP, 128], BF16, "cv_diag")
    ubr = Ring(P, 2, [128, NT], BF16, "cv_ubf")
    h_all = P.sb([128, 8, NT], BF16, "cv_h")
    tiles = [(0, 416), (416, 416), (832, 416), (1248, 416), (1664, NT - 1664)]
    h_tok = [[Tok(f"cvh{i}_{c}") for c in range(8)] for i in range(len(tiles))]
    ps_stat = Ring(P, 2, [128, 512], F32, "ps_stat", psum=True)
    emit_rmsnorm(P, C, x_in, g_col, gtok, h_all, h_tok, ps_stat, tiles)

    v_all = P.sb([128, 8, T], F32, "cv_v")
    v_tok = [[Tok(f"cvv{c}_{i}") for i in range(4)] for c in range(8)]
    ur = Ring(P, 2, [128, NT], F32, "cv_u")
    wst = Ring(P, 2, [128, 2048], F32, "cv_wst")
    wbf = Ring(P, 2, [128, 2048], BF16, "cv_wbf")
    ps_up = Ring(P, 4, [128, 512], F32, "ps_up", psum=True)
    w_in_v = w_in.rearrange("(kc p) n -> p kc n", p=128)
    KD = 30
    def conv_glu(c):
        st, sttok = wst.next()
        stv = st[:].rearrange("p (k n) -> p k n", k=8)
        P.dma("sp", stv[:, :, 0:128], w_in_v[:, :, c * 128:(c + 1) * 128], [], [sttok], sttok)
        P.dma("sp", stv[:, :, 128:256], w_in_v[:, :, D + c * 128:D + (c + 1) * 128], [], [sttok], sttok)
        wb, wbtok = wbf.next()
        P.add("pool", lambda e, o=wb[:], i=st[:]: e.tensor_copy(out=o, in_=i), [sttok], [wbtok])
        wbv = wb[:].rearrange("p (k n) -> p k n", k=8)
        u, utok = ur.next()
        for ti, (t0, n) in enumerate(tiles):
            pa, patok = ps_up.next()
            pg, pgtok = ps_up.next()
            for half, (pt, pttok) in enumerate(((pa, patok), (pg, pgtok))):
                for kc in range(8):
                    mm(P, pt[:, 0:n], wbv[:, kc, half * 128:(half + 1) * 128], h_all[:, kc, t0:t0 + n],
                       kc == 0, kc == 7, [wbtok, h_tok[ti][kc]], [pttok])
            sg, sgtok = fr.next()
            P.add("act", lambda e, o=sg[:, 0:n], i=pg[:, 0:n], b=bin_sb[:, 8 + c:9 + c]:
                  e.activation(out=o, in_=i, func=AF.Sigmoid, bias=b, scale=1.0), [pgtok, bintok], [sgtok])
            P.add("dve", lambda e, o=u[:, t0:t0 + n], i=pa[:, 0:n], b=bin_sb[:, c:c + 1], g=sg[:, 0:n]:
                  e.scalar_tensor_tensor(out=o, in0=i, scalar=b, in1=g, op0=ALU.add, op1=ALU.mult),
                  [patok, sgtok, bintok], [utok])
        P.add("pool", lambda e, o=u[:, 0:HC], m=mask_sb[:, 0:HC]: e.tensor_tensor(out=o, in0=o, in1=m, op=ALU.mult),
              [utok, masktok], [utok])
        P.add("pool", lambda e, o=u[:, T + HC:NT], m=mask_sb[:, HC:2 * HC]: e.tensor_tensor(out=o, in0=o, in1=m, op=ALU.mult),
              [utok, masktok], [utok])
        return u, utok

    def conv_taps(c, u, utok):
        ub, ubtok = ubr.next()
        P.add("act", lambda e, o=ub[:], i=u[:]: e.activation(out=o, in_=i, func=AF.Identity), [utok], [ubtok])
        dg, dgtok = dgr.next()
        for k in range(KP):
            P.add("act", lambda e, o=dg[:, k, :], s_=dww_sb[:, k * 8 + c:k * 8 + c + 1]:
                  e.activation(out=o, in_=ident_f[:], func=AF.Identity, scale=s_), [C["ctok"], dwwtok], [dgtok])
        for tt in range(4):
            t0 = tt * 512
            va = v_all[:, c, t0:t0 + 512]
            wk = lambda k: dww_sb[:, k * 8 + c:k * 8 + c + 1]
            pc, pctok = ps_up.next()
            for k in range(KP):
                mm(P, pc[:], dg[:, k, :], ub[:, t0 + k:t0 + k + 512], k == 0, k == KP - 1, [dgtok, ubtok], [pctok])
            P.add("act", lambda e, o=va, i=pc[:], b=dwb_sb[:, c:c + 1]:
                  e.activation(out=o, in_=i, func=AF.Identity, bias=b, scale=1.0), [pctok, dwbtok], [v_tok[c][tt]])
            for k in range(KP, KD + 1):
                P.add("dve", lambda e, o=va, i=u[:, t0 + k:t0 + k + 512], s=wk(k):
                      e.scalar_tensor_tensor(out=o, in0=i, scalar=s, in1=o, op0=ALU.mult, op1=ALU.add),
                      [utok, dwwtok, v_tok[c][tt]], [v_tok[c][tt]])

    nxt_u = conv_glu(0)
    for c in range(8):
        cur_u = nxt_u
        if c + 1 < 8:
            nxt_u = conv_glu(c + 1)
        conv_taps(c, *cur_u)

    wo_bf = P.sb([128, 8, D], BF16, "cv_wo")
    wotok = [Tok(f"wo{i}") for i in range(4)]
    w_out_v = w_out.rearrange("(kc p) n -> p kc n", p=128)
    for i in range(4):
        st, sttok = wst.next()
        stv = st[:].rearrange("p (k n) -> p k n", k=8)
        P.dma("sp", stv, w_out_v[:, :, i * 256:(i + 1) * 256], [], [sttok], sttok)
        P.add("pool", lambda e, o=wo_bf[:, :, i * 256:(i + 1) * 256], s_=stv: e.tensor_copy(out=o, in_=s_),
              [sttok], [wotok[i]])

    ps_o = Ring(P, 2, [128, 512], F32, "ps_o", psum=True)
    for tt in range(4):
        t0 = tt * 512
        p1, p1tok = ps_stat.next()
        p2, p2tok = ps_stat.next()
        for c in range(8):
            mm(P, p1[:], ones_f[:], v_all[:, c, t0:t0 + 512], c == 0, c == 7, [v_tok[c][tt], C["ctok"]], [p1tok])
        for c in range(8):
            sq, sqtok = fr.next()
            P.add("act", lambda e, o=sq[:], i=v_all[:, c, t0:t0 + 512]: e.activation(out=o, in_=i, func=AF.Square),
                  [v_tok[c][tt]], [sqtok])
            mm(P, p2[:], ones_f[:], sq[:], c == 0, c == 7, [sqtok, C["ctok"]], [p2tok])
        mu, mutok = fr.next()
        P.add("act", lambda e, o=mu[:], i=p1[:]: e.activation(out=o, in_=i, func=AF.Identity, scale=1.0 / D),
              [p1tok], [mutok])
        rs, rstok = fr.next()
        P.add("dve", lambda e, o=rs[:], a=mu[:]: e.tensor_tensor(out=o, in0=a, in1=a, op=ALU.mult), [mutok], [rstok])
        P.add("dve", lambda e, o=rs[:], i=p2[:]: e.scalar_tensor_tensor(out=o, in0=i, scalar=1.0 / D, in1=o,
                                                                        op0=ALU.mult, op1=ALU.subtract),
              [p2tok, rstok], [rstok])
        P.add("act", lambda e, o=rs[:]: e.activation(out=o, in_=o, func=AF.Sqrt, bias=C["eps_col"][:, 0:1], scale=1.0),
              [rstok, C["ctok"]], [rstok])
        P.add("dve", lambda e, o=rs[:]: e.reciprocal(out=o, in_=o), [rstok], [rstok])
        for c in range(8):
            dd, ddtok = fr.next()
            P.add("pool", lambda e, o=dd[:], a=v_all[:, c, t0:t0 + 512], m=mu[:]:
                  e.tensor_tensor(out=o, in0=a, in1=m, op=ALU.subtract), [v_tok[c][tt], mutok], [ddtok])
            P.add("dve", lambda e, o=dd[:], r=rs[:]: e.tensor_tensor(out=o, in0=o, in1=r, op=ALU.mult),
                  [ddtok, rstok], [ddtok])
            P.add("act", lambda e, o=h_all[:, c, t0:t0 + 512], i=dd[:], g=lng_sb[:, c:c + 1], b=lnb_sb[:, c:c + 1]:
                  e.activation(out=o, in_=i, func=AF.Silu, bias=b, scale=g),
                  [ddtok, lngtok, lnbtok], [h_tok[i][c] for i in range(len(tiles))])
        for o in range(8):
            pt, pttok = ps_o.next()
            for kc in range(8):
                mm(P, pt[:], wo_bf[:, kc, o * 128:(o + 1) * 128], h_all[:, kc, t0:t0 + 512], kc == 0, kc == 7,
                   [wotok[o // 2], h_tok[tt][kc]], [pttok])
            xt, xtok = fr.next()
            P.dma("sp", xt[:], x_in[o * 128:(o + 1) * 128, HC + t0:HC + t0 + 512], [], [xtok], xtok)
            P.add("dve", lambda e, oo=xt[:], a=pt[:]: e.tensor_tensor(out=oo, in0=a, in1=oo, op=ALU.add),
                  [pttok, xtok], [xtok])
            P.dma("sp", x_out[o * 128:(o + 1) * 128, t0:t0 + 512], xt[:], [xtok], [], xtok, is_out=is_out)


HN = 256
NEG = -30000.0
NE = 7


def nat_es(qp):
    if qp == 0:
        return list(range(0, 6))
    if qp == 15:
        return list(range(-1, 5))
    return list(range(0, 5))


def nat_pidx(qp):
    return {0: 0, 1: 1, 14: 3, 15: 4}.get(qp, 2)


def emit_nat(P, C, x_in, x_out, g1c, w_qkv, qg, kg, bias, pen, ohk, bd, w_out, is_out=False):
    NT = T + 2 * HN
    fr = C["fr"]
    g_col = P.sb([128, 8], F32, "nt_g")
    gtok = load_cols(P, C, g_col, g1c)
    qg_sb = P.sb([128, 2], F32, "nt_qg")
    qgtok = load_cols(P, C, qg_sb, qg, scale=0.125)
    kg_sb = P.sb([128, 1], F32, "nt_kg")
    kgtok = load_cols(P, C, kg_sb, kg)
    ctok = C["ctok"]
    bd_f = P.sb([128, 128], F32, "nt_bd")
    bdtok = load_cols(P, C, bd_f, bd)
    ident_f = P.sb([128, 128], F32, "nt_idf")
    ident = P.sb([128, 128], BF16, "nt_id")
    P.add("pool", lambda e: e.memset(ident_f[:], 1.0), [], [ctok])
    P.add("pool", lambda e: e.affine_select(out=ident_f[:], in_=ident_f[:], pattern=[[-1, 128]], compare_op=ALU.is_equal,
                                            fill=0.0, base=0, channel_multiplier=1), [ctok], [ctok])
    P.add("pool", lambda e: e.tensor_copy(out=ident[:], in_=ident_f[:]), [ctok], [ctok])
    pen_f = P.sb([2, 5 * NE * 128], F32, "nt_penf")
    pen_bf = P.sb([2, 5 * NE * 128], BF16, "nt_pen")
    pentok = load_cols(P, C, pen_f, pen)
    P.add("pool", lambda e: e.tensor_copy(out=pen_bf[:], in_=pen_f[:]), [pentok], [pentok])
    ohk_f = P.sb([2, 128], F32, "nt_ohkf")
    ohk_bf = P.sb([2, 128], BF16, "nt_ohk")
    ohktok = load_cols(P, C, ohk_f, ohk)
    P.add("pool", lambda e: e.tensor_copy(out=ohk_bf[:], in_=ohk_f[:]), [ohktok], [ohktok])

    h_all = P.sb([128, 8, NT], BF16, "nt_h")
    tiles = [(i * 512, 512) for i in range(NT // 512)]
    h_tok = [[Tok(f"nth{i}_{c}") for c in range(8)] for i in range(len(tiles))]
    ps_pr = Ring(P, 2, [128, 512], F32, "ps_pr", psum=True)
    emit_rmsnorm(P, C, x_in, g_col, gtok, h_all, h_tok, ps_pr, tiles, two_pass=True)

    attn_all = P.sb([128, 8, T], BF16, "nt_attn")
    attn_tok = [Tok(f"attn{hp}") for hp in range(8)]
    wst = Ring(P, 2, [128, 8, 128], F32, "nt_wst")
    wq_r = Ring(P, 2, [128, 8, 128], BF16, "nt_wq")
    wk_r = Ring(P, 2, [128, 8, 128], BF16, "nt_wk")
    wv_r = Ring(P, 2, [128, 8, 128], BF16, "nt_wv")
    bst = Ring(P, 1, [128, 2 * NE * 128], F32, "nt_bst")
    bbf = Ring(P, 2, [128, 2 * NE * 128], BF16, "nt_bbf")
    q_r = Ring(P, 1, [128, 2, T], BF16, "nt_q")
    k_r = Ring(P, 1, [128, NT], BF16, "nt_k")
    v_r = Ring(P, 1, [128, 2, NT // 128, 128], BF16, "nt_v")
    for (vb_, vbtok_) in v_r.bufs:
        P.add("pool", lambda e, o=vb_[:]: e.memset(o, 0.0), [], [vbtok_])
    onesz = P.sb([128, 2, 128], BF16, "nt_onesz")
    P.add("pool", lambda e: e.memset(onesz[:], 0.0), [], [ctok])
    P.add("pool", lambda e: e.memset(onesz[:, 0, 0:64], 1.0), [], [ctok])
    P.add("pool", lambda e: e.memset(onesz[:, 1, 64:128], 1.0), [], [ctok])
    p_r = Ring(P, 3, [128, 6 * 128], BF16, "nt_p")
    ps_sc = Ring(P, 2, [128, 1024], F32, "ps_sc", psum=True)
    ps_pv = Ring(P, 2, [128, 512], F32, "ps_pv", psum=True)
    w_v = w_qkv.rearrange("(kc p) n -> p kc n", p=128)
    allh = lambda ti: [h_tok[ti][c] for c in range(8)]

    for hp in range(8):
        wts = []
        for which, ring in enumerate((wq_r, wk_r, wv_r)):
            st, sttok = wst.next()
            P.dma("sp", st[:], w_v[:, :, which * D + hp * 128:which * D + (hp + 1) * 128], [], [sttok], sttok)
            wb, wbtok = ring.next()
            P.add("pool", lambda e, o=wb[:], i=st[:]: e.tensor_copy(out=o, in_=i), [sttok], [wbtok])
            wts.append((wb, wbtok))
        (wq, wqtok), (wk, wktok), (wv, wvtok) = wts
        bs, bstok = bst.next()
        P.dma("sp", bs[:], bias[hp], [], [bstok], bstok)
        bb, bbtok = bbf.next()
        P.add("pool", lambda e, o=bb[:], i=bs[:]: e.tensor_copy(out=o, in_=i), [bstok], [bbtok])

        q_sb, qtok = q_r.next()
        k_sb, ktok = k_r.next()
        v_sb, vtok = v_r.next()
        for (dst, dtok, wmat, wtok, gsb, gt, tl) in (
                (q_sb, qtok, wq, wqtok, qg_sb, qgtok, [(HN + i * 512, i * 512) for i in range(4)]),
                (k_sb, ktok, wk, wktok, kg_sb, kgtok, [(i * 512, i * 512) for i in range(5)])):
            for (hs, ds) in tl:
                ti = hs // 512
                pr, prtok = ps_pr.next()
                for kc in range(8):
                    mm(P, pr[:], wmat[:, kc, :], h_all[:, kc, hs:hs + 512], kc == 0, kc == 7,
                       [wtok] + [h_tok[i][kc] for i in range(len(tiles)) if i * 512 < hs + 512 and (i + 1) * 512 > hs], [prtok])
                sq, sqtok = fr.next()
                P.add("act", lambda e, o=sq[:], i=pr[:]: e.activation(out=o, in_=i, func=AF.Square), [prtok], [sqtok])
                pq, pqtok = ps_pr.next()
                mm(P, pq[:], bd_f[:], sq[:], True, True, [sqtok, bdtok], [pqtok])
                rs, rstok = fr.next()
                P.add("act", lambda e, o=rs[:], i=pq[:]: e.activation(out=o, in_=i, func=AF.Sqrt,
                                                                      bias=C["eps_col"][:, 0:1], scale=1.0 / 64),
                      [pqtok, ctok], [rstok])
                P.add("dve", lambda e, o=rs[:]: e.reciprocal(out=o, in_=o), [rstok], [rstok])
                if dst is q_sb:
                    for hh_ in range(2):
                        P.add("dve", lambda e, o=dst[:, hh_, ds:ds + 512], i=pr[:], g=gsb[:, hh_:hh_ + 1], r=rs[:]:
                              e.scalar_tensor_tensor(out=o, in0=i, scalar=g, in1=r, op0=ALU.mult, op1=ALU.mult),
                              [prtok, rstok, gt], [dtok])
                else:
                    P.add("dve", lambda e, o=dst[:, ds:ds + 512], i=pr[:], g=gsb[:, 0:1], r=rs[:]:
                          e.scalar_tensor_tensor(out=o, in0=i, scalar=g, in1=r, op0=ALU.mult, op1=ALU.mult),
                          [prtok, rstok, gt], [dtok])
        for blk in range(NT // 128):
            pr, prtok = ps_pr.next()
            for kc in range(8):
                mm(P, pr[:, 0:128], h_all[:, kc, blk * 128:(blk + 1) * 128], wv[:, kc, :], kc == 0, kc == 7,
                   [wvtok, h_tok[blk // 4][kc]], [prtok])
            for hh_ in range(2):
                P.add("act", lambda e, o=v_sb[:, hh_, blk, 64 * hh_:64 * hh_ + 64], i=pr[:, 64 * hh_:64 * hh_ + 64]:
                      e.activation(out=o, in_=i, func=AF.Identity), [prtok], [vtok])
        for qp in range(16):
            es = nat_es(qp)
            ne = len(es)
            pix = nat_pidx(qp)
            pts = []
            for hh in range(2):
                sc, sctok = ps_sc.next()
                for idx, e_ in enumerate(es):
                    kb = qp + e_
                    mm(P, sc[:, idx * 128:(idx + 1) * 128], k_sb[:, kb * 128:(kb + 1) * 128],
                       q_sb[:, hh, qp * 128:(qp + 1) * 128], idx % 4 == 0, False, [ktok, qtok], [sctok], skip=True)
                boff = (hh * NE + es[0] + 1) * 128
                poff = (pix * NE + es[0] + 1) * 128
                for (c0, c1) in ((0, 512), (512, ne * 128)):
                    mm(P, sc[:, c0:c1], ident[:], bb[:, boff + c0:boff + c1], False, False, [bbtok, ctok], [sctok], skip=True)
                    mm(P, sc[:, c0:c1], ohk_bf[:], pen_bf[:, poff + c0:poff + c1], False, True, [ohktok, pentok], [sctok], skip=True)
                pt, pttok = p_r.next()
                for (c0, c1) in ((0, 512), (512, ne * 128)):
                    P.add("act", lambda e, o=pt[:, c0:c1], i=sc[:, c0:c1]: e.activation(out=o, in_=i, func=AF.Exp),
                          [sctok], [pttok])
                pts.append((pt, pttok))
            pv, pvtok = ps_pv.next()
            n_mm = 2 * ne
            cnt = 0
            for hh in range(2):
                pt, pttok = pts[hh]
                for idx, e_ in enumerate(es):
                    kb = qp + e_
                    mm(P, pv[:, 0:128], v_sb[:, hh, kb, :], pt[:, idx * 128:(idx + 1) * 128], cnt == 0, cnt == n_mm - 1,
                       [vtok, pttok], [pvtok])
                    cnt += 1
            cnt = 0
            for hh in range(2):
                pt, pttok = pts[hh]
                for idx, e_ in enumerate(es):
                    mm(P, pv[:, 128:256], onesz[:, hh, :], pt[:, idx * 128:(idx + 1) * 128], cnt == 0, cnt == n_mm - 1,
                       [pttok, ctok], [pvtok])
                    cnt += 1
            if C.get("dbg") is not None and hp == 0 and qp == 2:
                dbg_dump(P, C, 0, pts[0][0][:, 0:128], [pts[0][1]])
                dbg_dump(P, C, 1, pts[0][0][:, 128:256], [pts[0][1]])
                dbg_dump(P, C, 2, q_sb[:, 0, 256:384], [qtok])
                dbg_dump(P, C, 3, q_sb[:, 1, 256:384], [qtok])
                dbg_dump(P, C, 4, k_sb[:, 256:384], [ktok])
                dbg_dump(P, C, 5, k_sb[:, 384:512], [ktok])
                dbg_dump(P, C, 6, v_sb[:, 0, 2, :], [vtok])
                dbg_dump(P, C, 7, v_sb[:, 1, 2, :], [vtok])
            rd, rdtok = fr.next()
            P.add("dve", lambda e, o=rd[:, 0:128], i=pv[:, 128:256]: e.reciprocal(out=o, in_=i), [pvtok], [rdtok])
            P.add("dve", lambda e, o=attn_all[:, hp, qp * 128:(qp + 1) * 128], a=pv[:, 0:128], b=rd[:, 0:128]:
                  e.tensor_tensor(out=o, in0=a, in1=b, op=ALU.mult), [pvtok, rdtok], [attn_tok[hp]])
            if C.get("dbg") is not None and hp == 0 and qp == 2:
                dbg_dump(P, C, 8, attn_all[:, 0, 256:384], [attn_tok[0]])
                dbg_dump(P, C, 9, rd[:, 0:128], [rdtok])

    wo_st = Ring(P, 2, [128, 8, 128], F32, "nt_wost")
    wo_r = Ring(P, 2, [128, 8, 128], BF16, "nt_wo")
    w_out_v = w_out.rearrange("(kc p) n -> p kc n", p=128)
    for o in range(8):
        st, sttok = wo_st.next()
        P.dma("sp", st[:], w_out_v[:, :, o * 128:(o + 1) * 128], [], [sttok], sttok)
        wb, wbtok = wo_r.next()
        P.add("pool", lambda e, oo=wb[:], i=st[:]: e.tensor_copy(out=oo, in_=i), [sttok], [wbtok])
        for tt in range(4):
            t0 = tt * 512
            pt, pttok = ps_pr.next()
            for kc in range(8):
                mm(P, pt[:], wb[:, kc, :], attn_all[:, kc, t0:t0 + 512], kc == 0, kc == 7, [wbtok, attn_tok[kc]], [pttok])
            xt, xtok = fr.next()
            P.dma("sp", xt[:], x_in[o * 128:(o + 1) * 128, HN + t0:HN + t0 + 512], [], [xtok], xtok)
            P.add("dve", lambda e, oo=xt[:], a=pt[:]: e.tensor_tensor(out=oo, in0=a, in1=oo, op=ALU.add),
                  [pttok, xtok], [xtok])
            P.dma("sp", x_out[o * 128:(o + 1) * 128, t0:t0 + 512], xt[:], [xtok], [], xtok, is_out=is_out)


NB = 32
RB = NB * 128
NBT = 48
RH = 4
LN16 = -2.772588722239781


def emit_ret(P, C, nc, x_in, x_out, g1c, w_in, cosT, sinT, l2d, gng, w_out, hm, is_out=False):
    fr = C["fr"]
    ctok = C["ctok"]
    g_col = P.sb([128, 8], F32, "rt_g")
    gtok = load_cols(P, C, g_col, g1c)
    gng_sb = P.sb([128, 16], F32, "rt_gng")
    gngtok = load_cols(P, C, gng_sb, gng)
    hm_sb = P.sb([128, 2], F32, "rt_hm")
    hmtok = load_cols(P, C, hm_sb, hm)
    ones_f = P.sb([128, 128], F32, "rt_ones_f")
    P.add("pool", lambda e: e.memset(ones_f[:], 1.0), [], [ctok])
    lg = P.sb([128, 8], F32, "rt_lg")
    nlg = P.sb([128, 8], F32, "rt_nlg")
    one_col = P.sb([128, 1], F32, "rt_one")
    ln16_col = P.sb([128, 1], F32, "rt_ln16")
    P.add("pool", lambda e: e.memset(one_col[:], 1.0), [], [ctok])
    P.add("pool", lambda e: e.memset(ln16_col[:], LN16), [], [ctok])
    lgtok = load_cols(P, C, lg, l2d)
    P.add("act", lambda e: e.activation(out=lg[:], in_=lg[:], func=AF.Exp, scale=-0.6931471805599453), [lgtok], [lgtok])
    P.add("act", lambda e: e.activation(out=lg[:], in_=lg[:], func=AF.Ln, bias=one_col[:, 0:1], scale=-1.0),
          [lgtok, ctok], [lgtok])
    P.add("dve", lambda e: e.tensor_scalar(out=nlg[:], in0=lg[:], scalar1=-1.0, scalar2=None, op0=ALU.mult),
          [lgtok], [lgtok])
    d1i = P.sb([128, 128], mybir.dt.int32, "rt_d1i")
    d1 = P.sb([128, 128], F32, "rt_d1")
    dbi = P.sb([128, NBT], mybir.dt.int32, "rt_dbi")
    dbf = P.sb([128, NBT], F32, "rt_dbf")
    dri = P.sb([128, NBT], mybir.dt.int32, "rt_dri")
    drf = P.sb([128, NBT], F32, "rt_drf")
    itok = Tok("iota")
    P.add("pool", lambda e: e.iota(d1i[:], pattern=[[1, 128]], base=0, channel_multiplier=-1), [], [itok])
    P.add("pool", lambda e: e.iota(dbi[:], pattern=[[128, NBT]], base=0, channel_multiplier=0), [], [itok])
    P.add("pool", lambda e: e.iota(dri[:], pattern=[[-128, NBT]], base=128 * (NBT - 1), channel_multiplier=0), [], [itok])
    P.add("dve", lambda e: e.tensor_copy(out=drf[:], in_=dri[:]), [itok], [itok])
    P.add("dve", lambda e: e.tensor_copy(out=d1[:], in_=d1i[:]), [itok], [itok])
    P.add("dve", lambda e: e.tensor_copy(out=dbf[:], in_=dbi[:]), [itok], [itok])

    h_dram = nc.dram_tensor("rt_h_dram", [D, RB], BF16, kind="Internal").ap()
    gT_dram = nc.dram_tensor("rt_gT_dram", [2 * D, T], BF16, kind="Internal").ap()
    h_dv = h_dram.rearrange("(c p) t -> p c t", p=128)
    gT_dv = gT_dram.rearrange("(c p) t -> p c t", p=128)
    bigr = Ring(P, 2, [128, 16, 512], BF16, "rt_big")
    ps_a = Ring(P, 2, [128, 512], F32, "ps_a", psum=True)
    ps_s = Ring(P, 2, [128, 512], F32, "ps_s", psum=True)
    ps_o = Ring(P, 2, [128, 512], F32, "ps_o", psum=True)
    ntile = RB // 512
    hd_tok = [Tok(f"hd{i}") for i in range(ntile)]
    for ti in range(ntile):
        hb, hbtok = bigr.next()
        emit_rmsnorm(P, C, x_in, g_col, gtok, hb, [[hbtok] * 8], ps_a, [(ti * 512, 512)], two_pass=True, hcol=[0])
        P.dma("sp", h_dv[:, :, ti * 512:(ti + 1) * 512], hb[:, 0:8, :], [hbtok], [hd_tok[ti]], hbtok)

    k_fm = P.sb([128, 2, RB], BF16, "rt_k")
    v_tok = P.sb([128, NB, 512], BF16, "rt_v")
    q_fm = P.sb([128, 2, T], BF16, "rt_q")
    o_fm = P.sb([128, 4, T], F32, "rt_o")
    ktok, vtok, qtok = Tok("k"), Tok("v"), Tok("q")
    otok = [Tok(f"o{i}") for i in range(16)]
    wq = P.sb([128, 8, 256], BF16, "rt_wq")
    wk = P.sb([128, 8, 256], BF16, "rt_wk")
    wv = P.sb([128, 8, 512], BF16, "rt_wv")
    wg = P.sb([128, 8, 512], BF16, "rt_wg")
    wtok = Tok("w")
    wqtok, wgtok = Tok("wqkv"), Tok("wg")
    wst = Ring(P, 2, [128, 8, 128], F32, "rt_wst")
    w_v = w_in.rearrange("(kc p) n -> p kc n", p=128)
    gf = P.sb([128, 128], F32, "rt_gf")
    gb = P.sb([128, 128], F32, "rt_gb")
    gd = P.sb([128, 128], F32, "rt_gd")
    gd2 = P.sb([128, 128], F32, "rt_gd2")
    sf = P.sb([128, NBT], F32, "rt_sf")
    sbk = P.sb([128, NBT], F32, "rt_sb")
    sfr = P.sb([128, NBT], F32, "rt_sfr")
    so = P.sb([128, 256], F32, "rt_so")
    go = P.sb([128, 128], F32, "rt_go")
    gtk = Tok("G")
    p_r = Ring(P, 10, [128, 128], BF16, "rt_p")
    gT_tok = [[Tok(f"gT{h}_{t}") for t in range(4)] for h in range(RH)]

    for h in range(RH):
        def load_w(hh_, which):
            segs = []
            if which == "qkv":
                segs += [(wq, i_ * 128, hh_ * 256 + i_ * 128, wqtok) for i_ in range(2)]
                segs += [(wk, i_ * 128, D + hh_ * 256 + i_ * 128, wqtok) for i_ in range(2)]
                segs += [(wv, i_ * 128, 2 * D + hh_ * 512 + i_ * 128, wqtok) for i_ in range(4)]
            else:
                segs += [(wg, i_ * 128, 4 * D + hh_ * 512 + i_ * 128, wgtok) for i_ in range(4)]
            for (dst, dcol, scol, tk) in segs:
                st, sttok = wst.next()
                P.dma("sp", st[:], w_v[:, :, scol:scol + 128], [], [sttok], sttok)
                P.add("pool", lambda e, o=dst[:, :, dcol:dcol + 128], i=st[:]: e.tensor_copy(out=o, in_=i), [sttok], [tk])

        if h == 0:
            load_w(0, "qkv")
        load_w(h, "g")
        P.add("act", lambda e, sc_=lg[:, h:h + 1]: e.activation(out=gf[:], in_=d1[:], func=AF.Exp, bias=ln16_col[:, 0:1], scale=sc_),
              [itok, lgtok, ctok], [gtk])
        P.add("act", lambda e, sc_=nlg[:, 4 + h:5 + h]: e.activation(out=gb[:], in_=d1[:], func=AF.Exp, bias=ln16_col[:, 0:1], scale=sc_),
              [itok, lgtok, ctok], [gtk])
        P.add("pool", lambda e: e.affine_select(out=gd[:], in_=gf[:], pattern=[[1, 128]], compare_op=ALU.is_ge, fill=0.0,
                                                base=0, channel_multiplier=-1), [gtk], [gtk])
        P.add("pool", lambda e: e.affine_select(out=gd2[:], in_=gb[:], pattern=[[-1, 128]], compare_op=ALU.is_gt, fill=0.0,
                                                base=0, channel_multiplier=1), [gtk], [gtk])
        P.add("pool", lambda e: e.tensor_tensor(out=gd[:], in0=gd[:], in1=gd2[:], op=ALU.add), [gtk], [gtk])
        P.add("act", lambda e, sc_=lg[:, h:h + 1]: e.activation(out=sf[:], in_=dbf[:], func=AF.Exp, scale=sc_), [itok, lgtok], [gtk])
        P.add("act", lambda e, sc_=lg[:, 4 + h:5 + h]: e.activation(out=sbk[:], in_=dbf[:], func=AF.Exp, scale=sc_), [itok, lgtok], [gtk])
        P.add("act", lambda e, sc_=lg[:, h:h + 1]: e.activation(out=sfr[:], in_=drf[:], func=AF.Exp, scale=sc_), [itok, lgtok], [gtk])
        P.add("dve", lambda e: e.tensor_scalar(out=go[:], in0=gb[:], scalar1=hm_sb[:, 1:2], scalar2=None, op0=ALU.mult), [gtk, hmtok], [gtk])
        P.add("dve", lambda e: e.scalar_tensor_tensor(out=go[:], in0=gf[:], scalar=hm_sb[:, 0:1], in1=go[:], op0=ALU.mult, op1=ALU.add),
              [gtk, hmtok], [gtk])
        for i_ in range(16):
            P.add("dve", lambda e, o=so[:, i_ * 16:(i_ + 1) * 16], a=sbk[:, 16 - i_:32 - i_]:
                  e.tensor_scalar(out=o, in0=a, scalar1=hm_sb[:, 1:2], scalar2=None, op0=ALU.mult), [gtk, hmtok], [gtk])
            P.add("dve", lambda e, o=so[:, i_ * 16:(i_ + 1) * 16], a=sfr[:, 31 - i_:47 - i_]:
                  e.scalar_tensor_tensor(out=o, in0=a, scalar=hm_sb[:, 0:1], in1=o, op0=ALU.mult, op1=ALU.add), [gtk, hmtok], [gtk])

        for ti in range(ntile):
            hb, hbtok = bigr.next()
            P.dma("sp", hb[:, 0:8, :], h_dv[:, :, ti * 512:(ti + 1) * 512], [hd_tok[ti]], [hbtok], hbtok)
            cs, cstok = fr.next()
            sn, sntok = fr.next()
            P.dma("sp", cs[:], cosT[:, ti * 512:(ti + 1) * 512], [], [cstok], cstok)
            P.dma("sp", sn[:], sinT[:, ti * 512:(ti + 1) * 512], [], [sntok], sntok)
            todo = [(wk, k_fm, ktok, ti * 512)]
            if ti < 4:
                todo.append((wq, q_fm, qtok, ti * 512))
            for (wmat, dst, dtok, dcol) in todo:
                p1, p1tok = ps_a.next()
                p2, p2tok = ps_a.next()
                for dc, (pt, pttok) in enumerate(((p1, p1tok), (p2, p2tok))):
                    for kc in range(8):
                        mm(P, pt[:], wmat[:, kc, dc * 128:(dc + 1) * 128], hb[:, kc, :], kc == 0, kc == 7, [wqtok, hbtok], [pttok])
                t1, t1tok = fr.next()
                t2, t2tok = fr.next()
                P.add("dve", lambda e, o=t1[:], a=p1[:], b=cs[:]: e.tensor_tensor(out=o, in0=a, in1=b, op=ALU.mult), [p1tok, cstok], [t1tok])
                P.add("dve", lambda e, o=t2[:], a=p2[:], b=sn[:]: e.tensor_tensor(out=o, in0=a, in1=b, op=ALU.mult), [p2tok, sntok], [t2tok])
                P.add("pool", lambda e, o=dst[:, 0, dcol:dcol + 512], a=t1[:], b=t2[:]: e.tensor_tensor(out=o, in0=a, in1=b, op=ALU.subtract),
                      [t1tok, t2tok], [dtok])
                P.add("dve", lambda e, o=t1[:], a=p1[:], b=sn[:]: e.tensor_tensor(out=o, in0=a, in1=b, op=ALU.mult), [p1tok, sntok], [t1tok])
                P.add("dve", lambda e, o=t2[:], a=p2[:], b=cs[:]: e.tensor_tensor(out=o, in0=a, in1=b, op=ALU.mult), [p2tok, cstok], [t2tok])
                P.add("pool", lambda e, o=dst[:, 1, dcol:dcol + 512], a=t1[:], b=t2[:]: e.tensor_tensor(out=o, in0=a, in1=b, op=ALU.add),
                      [t1tok, t2tok], [dtok])
            for bl in range(4):
                blk = ti * 4 + bl
                pv, pvtok = ps_a.next()
                for kc in range(8):
                    mm(P, pv[:], hb[:, kc, bl * 128:(bl + 1) * 128], wv[:, kc, :], kc == 0, kc == 7, [wqtok, hbtok], [pvtok])
                P.add("act", lambda e, o=v_tok[:, blk, :], i=pv[:]: e.activation(out=o, in_=i, func=AF.Identity), [pvtok], [vtok])

        if h + 1 < RH:
            load_w(h + 1, "qkv")
        NG = NB // 4

        def scores(i, cg):
            sc, sctok = ps_s.next()
            for sub in range(4):
                c = cg * 4 + sub
                for dc in range(2):
                    mm(P, sc[:, sub * 128:(sub + 1) * 128], k_fm[:, dc, c * 128:(c + 1) * 128], q_fm[:, dc, i * 128:(i + 1) * 128],
                       sub == 0 and dc == 0, dc == 1, [ktok, qtok], [sctok], skip=True)
            return sc, sctok

        seq = [(i, cg) for i in range(16) for cg in range(NG)]
        cur = scores(*seq[0])
        po, potok = None, None
        for si_, (i, cg) in enumerate(seq):
            sc, sctok = cur
            if cg == 0:
                po, potok = ps_o.next()
            pts = []
            for sub in range(4):
                c = cg * 4 + sub
                dl = i - c
                pt, pttok = p_r.next()
                if c >= 16:
                    P.add("dve", lambda e, o=pt[:], a=sc[:, sub * 128:(sub + 1) * 128], s_=so[:, i * 16 + c - 16:i * 16 + c - 15]:
                          e.scalar_tensor_tensor(out=o, in0=a, scalar=s_, in1=go[:], op0=ALU.mult, op1=ALU.mult), [sctok, gtk], [pttok])
                elif dl > 0:
                    P.add("dve", lambda e, o=pt[:], a=sc[:, sub * 128:(sub + 1) * 128], s_=sf[:, dl:dl + 1]:
                          e.scalar_tensor_tensor(out=o, in0=a, scalar=s_, in1=gf[:], op0=ALU.mult, op1=ALU.mult), [sctok, gtk], [pttok])
                elif dl < 0:
                    P.add("dve", lambda e, o=pt[:], a=sc[:, sub * 128:(sub + 1) * 128], s_=sbk[:, -dl:-dl + 1]:
                          e.scalar_tensor_tensor(out=o, in0=a, scalar=s_, in1=gb[:], op0=ALU.mult, op1=ALU.mult), [sctok, gtk], [pttok])
                else:
                    P.add("dve", lambda e, o=pt[:], a=sc[:, sub * 128:(sub + 1) * 128]:
                          e.tensor_tensor(out=o, in0=a, in1=gd[:], op=ALU.mult), [sctok, gtk], [pttok])
                pts.append((pt, pttok, c))
            if si_ + 1 < len(seq):
                cur = scores(*seq[si_ + 1])
            for (pt, pttok, c) in pts:
                for ec in range(4):
                    mm(P, po[:, ec * 128:(ec + 1) * 128], v_tok[:, c, ec * 128:(ec + 1) * 128], pt[:],
                       c == 0 and ec == 0, c == NB - 1, [vtok, pttok], [potok], skip=True)
            if cg == NG - 1:
                P.add("act", lambda e, o=o_fm[:, :, i * 128:(i + 1) * 128], a=po[:].rearrange("p (a b) -> p a b", a=4):
                      e.activation(out=o, in_=a, func=AF.Identity), [potok], [otok[i]])

        for tt in range(4):
            t0 = tt * 512
            ots = [otok[tt * 4 + b_] for b_ in range(4)]
            p1, p1tok = ps_a.next()
            p2, p2tok = ps_a.next()
            for ec in range(4):
                mm(P, p1[:], ones_f[:], o_fm[:, ec, t0:t0 + 512], ec == 0, ec == 3, ots + [ctok], [p1tok])
            for ec in range(4):
                sq, sqtok = fr.next()
                P.add("act", lambda e, o=sq[:], a=o_fm[:, ec, t0:t0 + 512]: e.activation(out=o, in_=a, func=AF.Square), ots, [sqtok])
                mm(P, p2[:], ones_f[:], sq[:], ec == 0, ec == 3, [sqtok, ctok], [p2tok])
            mu, mutok = C["rsr"].next()
            P.add("act", lambda e, o=mu[:], a=p1[:]: e.activation(out=o, in_=a, func=AF.Identity, scale=1.0 / 512), [p1tok], [mutok])
            rs, rstok = C["rsr"].next()
            P.add("dve", lambda e, o=rs[:], a=mu[:]: e.tensor_tensor(out=o, in0=a, in1=a, op=ALU.mult), [mutok], [rstok])
            P.add("dve", lambda e, o=rs[:], a=p2[:]: e.scalar_tensor_tensor(out=o, in0=a, scalar=1.0 / 512, in1=o, op0=ALU.mult, op1=ALU.subtract),
                  [p2tok, rstok], [rstok])
            P.add("act", lambda e, o=rs[:]: e.activation(out=o, in_=o, func=AF.Sqrt, bias=C["eps_col"][:, 0:1], scale=1.0), [rstok, ctok], [rstok])
            P.add("dve", lambda e, o=rs[:]: e.reciprocal(out=o, in_=o), [rstok], [rstok])
            hb, hbtok = bigr.next()
            P.dma("sp", hb[:, 0:8, :], h_dv[:, :, t0:t0 + 512], [hd_tok[tt]], [hbtok], hbtok)
            gt_sb, gttok = bigr.next()
            for ec in range(4):
                pg, pgtok = ps_a.next()
                for kc in range(8):
                    mm(P, pg[:], wg[:, kc, ec * 128:(ec + 1) * 128], hb[:, kc, :], kc == 0, kc == 7, [wgtok, hbtok], [pgtok])
                sg, sgtok = fr.next()
                P.add("act", lambda e, o=sg[:], a=pg[:]: e.activation(out=o, in_=a, func=AF.Silu), [pgtok], [sgtok])
                dd, ddtok = fr.next()
                P.add("pool", lambda e, o=dd[:], a=o_fm[:, ec, t0:t0 + 512], m=mu[:]: e.tensor_tensor(out=o, in0=a, in1=m, op=ALU.subtract),
                      ots + [mutok], [ddtok])
                P.add("dve", lambda e, o=dd[:], r=rs[:]: e.tensor_tensor(out=o, in0=o, in1=r, op=ALU.mult), [ddtok, rstok], [ddtok])
                P.add("dve", lambda e, o=gt_sb[:, ec, :], a=dd[:], g_=gng_sb[:, h * 4 + ec:h * 4 + ec + 1], s_=sg[:]:
                      e.scalar_tensor_tensor(out=o, in0=a, scalar=g_, in1=s_, op0=ALU.mult, op1=ALU.mult), [ddtok, sgtok, gngtok], [gttok])
            P.dma("sp", gT_dv[:, h * 4:(h + 1) * 4, t0:t0 + 512], gt_sb[:, 0:4, :], [gttok], [gT_tok[h][tt]], gttok)

    def as_chunks(t):
        flat = t[:].rearrange("p a b -> p (a b)")
        return [flat[:, i_ * 2048:(i_ + 1) * 2048].rearrange("p (j n) -> p j n", j=16) for i_ in range(flat.shape[1] // 2048)]

    wo_extra = P.sb([128, 2, 2048], BF16, "rt_wo_extra")
    wxtok = Tok("wo_extra")
    wo_chunks = ([(v_, wqtok) for v_ in as_chunks(wq) + as_chunks(wk) + as_chunks(wv)] +
                 [(v_, wgtok) for v_ in as_chunks(wg)] +
                 [(wo_extra[:, i_, :].rearrange("p (j n) -> p j n", j=16), wxtok) for i_ in range(2)])
    assert len(wo_chunks) >= 8
    w_out_v = w_out.rearrange("(j p) n -> p j n", p=128)
    for o in range(8):
        wb, wbtok = wo_chunks[o]
        for half in range(2):
            st, sttok = wst.next()
            P.dma("sp", st[:], w_out_v[:, half * 8:(half + 1) * 8, o * 128:(o + 1) * 128], [], [sttok], sttok)
            P.add("pool", lambda e, oo=wb[:, half * 8:(half + 1) * 8, :], i=st[:]: e.tensor_copy(out=oo, in_=i), [sttok], [wbtok])
    for tt in range(4):
        t0 = tt * 512
        gb_, gbtok = bigr.next()
        P.dma("sp", gb_[:], gT_dv[:, :, t0:t0 + 512], [gT_tok[h_][tt] for h_ in range(RH)], [gbtok], gbtok)
        for o in range(8):
            wb, wbtok = wo_chunks[o]
            pt, pttok = ps_a.next()
            for j in range(16):
                mm(P, pt[:], wb[:, j, :], gb_[:, j, :], j == 0, j == 15, [wbtok, gbtok], [pttok])
            xt, xtok = fr.next()
            P.dma("sp", xt[:], x_in[o * 128:(o + 1) * 128, t0:t0 + 512], [], [xtok], xtok)
            P.add("dve", lambda e, oo=xt[:], a=pt[:]: e.tensor_tensor(out=oo, in0=a, in1=oo, op=ALU.add), [pttok, xtok], [xtok])
            P.dma("sp", x_out[o * 128:(o + 1) * 128, t0:t0 + 512], xt[:], [xtok], [], xtok, is_out=is_out)


def build_ffn_prog():
    nc = bass.Bass("TRN2", target_bir_lowering=False)
    x_in = nc.dram_tensor("x_in", [D, T + 2], F32, kind="ExternalInput").ap()
    g2c = nc.dram_tensor("g2c", [128, 8], F32, kind="ExternalInput").ap()
    w_up = nc.dram_tensor("w_up", [D, 2 * FFN], F32, kind="ExternalInput").ap()
    dww = nc.dram_tensor("dww", [128, 3 * 2 * NH], F32, kind="ExternalInput").ap()
    dwb = nc.dram_tensor("dwb", [128, 2 * NH], F32, kind="ExternalInput").ap()
    w_down = nc.dram_tensor("w_down", [FFN, D], F32, kind="ExternalInput").ap()
    x_out = nc.dram_tensor("x_out", [D, T], F32, kind="ExternalOutput").ap()
    with contextlib.ExitStack() as stack:
        P = Prog(nc, stack)
        C = make_common(P)
        emit_ffn(P, C, x_in, x_out, g2c, w_up, dww, dwb, w_down, is_out=True)
        P.emit()
        print("ffn prog stats", P.stats)
    return nc


def cols(v, n):
    return np.ascontiguousarray(v.reshape(n, 128).T)


def shard_tokens_fm(xfull, halo):
    out = []
    for c in range(NCORES):
        b, hf = c // 2, c % 2
        lo, hi = hf * T - halo, (hf + 1) * T + halo
        buf = np.zeros((T + 2 * halo, D), np.float32)
        slo, shi = max(lo, 0), min(hi, SEQ)
        buf[slo - lo:shi - lo] = xfull[b, slo:shi]
        out.append(np.ascontiguousarray(buf.T))
    return out


def unshard_tokens_fm(outs):
    x = np.empty((BATCH, SEQ, D), np.float32)
    for c in range(NCORES):
        b, hf = c // 2, c % 2
        x[b, hf * T:(hf + 1) * T] = outs[c].T
    return x


def ffn_inmaps(x, i, norm2_g, ffn_w_up, ffn_dw_w, ffn_dw_b, ffn_w_down):
    xs = shard_tokens_fm(x, 1) if x is not None else None
    dww = np.concatenate([cols(ffn_dw_w[i, k], 2 * NH) for k in range(3)], axis=1)
    common = {
        "g2c": cols(norm2_g[i], 8),
        "w_up": np.ascontiguousarray(ffn_w_up[i]),
        "dww": np.ascontiguousarray(dww),
        "dwb": cols(ffn_dw_b[i], 2 * NH),
        "w_down": np.ascontiguousarray(ffn_w_down[i]),
    }
    return [dict(common, x_in=xs[c]) if xs is not None else dict(common) for c in range(NCORES)]


def build_conv_prog():
    nc = bass.Bass("TRN2", target_bir_lowering=False)
    dt = lambda name, shape, kind="ExternalInput": nc.dram_tensor(name, shape, F32, kind=kind).ap()
    x_in = dt("x_in", [D, T + 2 * HC])
    mask = dt("mask", [128, 2 * HC])
    g1c = dt("g1c", [128, 8])
    w_in = dt("w_in", [D, 2 * D])
    b_in = dt("b_in", [128, 16])
    dw_w = dt("dw_w", [128, CW * 8])
    dw_b = dt("dw_b", [128, 8])
    ln_g = dt("ln_g", [128, 8])
    ln_b = dt("ln_b", [128, 8])
    w_out = dt("w_out", [D, D])
    x_out = dt("x_out", [D, T], "ExternalOutput")
    with contextlib.ExitStack() as stack:
        P = Prog(nc, stack)
        C = make_common(P, nfr=12)
        emit_conv(P, C, x_in, x_out, mask, g1c, w_in, b_in, dw_w, dw_b, ln_g, ln_b, w_out, is_out=True)
        P.emit()
        print("conv prog stats", P.stats)
    return nc


def conv_inmaps(x, j, g1, conv_w_in, conv_b_in, conv_dw_w, conv_dw_b, conv_ln_g, conv_ln_b, conv_w_out):
    xs = shard_tokens_fm(x, HC) if x is not None else None
    dww = np.concatenate([cols(conv_dw_w[j, k], 8) for k in range(CW)], axis=1)
    common = {
        "g1c": cols(g1, 8),
        "w_in": np.ascontiguousarray(conv_w_in[j]),
        "b_in": cols(conv_b_in[j], 16),
        "dw_w": np.ascontiguousarray(dww),
        "dw_b": cols(conv_dw_b[j], 8),
        "ln_g": cols(conv_ln_g[j], 8),
        "ln_b": cols(conv_ln_b[j], 8),
        "w_out": np.ascontiguousarray(conv_w_out[j]),
    }
    maps = []
    for c in range(NCORES):
        hf = c % 2
        m = np.ones((128, 2 * HC), np.float32)
        if hf == 0:
            m[:, :HC] = 0.0
        else:
            m[:, HC:] = 0.0
        maps.append(dict(common, x_in=xs[c], mask=m) if xs is not None else dict(common, mask=m))
    return maps


def dbg_dump(P, C, slot, src, toks):
    t, ttok = C["dbgr"].next()
    P.add("act", lambda e: e.activation(out=t[:], in_=src, func=AF.Identity), toks, [ttok])
    P.dma("sp", C["dbg"][:, slot * 128:(slot + 1) * 128], t[:], [ttok], [], ttok, is_out=True)


def build_nat_prog(debug=False):
    nc = bass.Bass("TRN2", target_bir_lowering=False)
    dt = lambda name, shape, kind="ExternalInput": nc.dram_tensor(name, shape, F32, kind=kind).ap()
    x_in = dt("x_in", [D, T + 2 * HN])
    g1c = dt("g1c", [128, 8])
    w_qkv = dt("w_qkv", [D, 3 * D])
    qg = dt("qg", [128, 2])
    kg = dt("kg", [128, 1])
    bias = dt("bias", [8, 128, 2 * NE * 128])
    pen = dt("pen", [2, 5 * NE * 128])
    ohk = dt("ohk", [2, 128])
    bd = dt("bd", [128, 128])
    w_out = dt("w_out", [D, D])
    x_out = dt("x_out", [D, T], "ExternalOutput")
    with contextlib.ExitStack() as stack:
        P = Prog(nc, stack)
        C = make_common(P, nfr=6)
        if debug:
            C["dbg"] = dt("dbg", [128, 16 * 128], "ExternalOutput")
            C["dbgr"] = Ring(P, 2, [128, 128], F32, "dbgr")
        emit_nat(P, C, x_in, x_out, g1c, w_qkv, qg, kg, bias, pen, ohk, bd, w_out, is_out=True)
        P.emit()
        print("nat prog stats", P.stats)
    return nc


def nat_bias_table(rpb):
    kc = np.arange(64)[:, None]
    qc = np.arange(64)[None, :]
    cs = np.clip(qc - 8, 0, 48)
    win = (kc >= cs) & (kc < cs + 16)
    dc = np.clip(kc - qc + 15, 0, 30)
    out = np.full((8, 128, 2, NE, 128), NEG, np.float32)
    for hp in range(8):
        for hh in range(2):
            h = 2 * hp + hh
            for ei in range(NE):
                e_ = ei - 1
                for kp in range(2):
                    for qp_ in range(2):
                        dr = 2 * e_ + 3 + kp - qp_
                        if dr < 0 or dr > 14:
                            continue
                        blk = np.where(win, rpb[h, dr][dc], np.float32(NEG))
                        out[hp, kp * 64:(kp + 1) * 64, hh, ei, qp_ * 64:(qp_ + 1) * 64] = blk
    return out.reshape(8, 128, 2 * NE * 128)


def nat_pen_table(hf):
    out = np.full((2, 5, NE, 128), NEG, np.float32)
    for pix, qp in enumerate((0, 1, 7, 14, 15)):
        for ei in range(NE):
            e_ = ei - 1
            for kp in range(2):
                for qp_ in range(2):
                    r = 32 * hf + 2 * qp + qp_
                    kr = 32 * hf + 2 * qp + 2 * e_ - 4 + kp
                    rs = min(max(r - 4, 0), 56)
                    if 0 <= kr < 64 and rs <= kr < rs + 8:
                        out[kp, pix, ei, qp_ * 64:(qp_ + 1) * 64] = 0.0
    return out.reshape(2, 5 * NE * 128)


def nat_qg2(g):
    out = np.zeros((128, 2), np.float32)
    out[0:64, 0] = g
    out[64:128, 1] = g
    return out


def nat_inmaps(x, g1, nat_w_qkv, nat_q_norm_g, nat_k_norm_g, nat_rpb, nat_w_out):
    xs = shard_tokens_fm(x, HN) if x is not None else None
    ohk = np.zeros((2, 128), np.float32)
    ohk[0, :64] = 1.0
    ohk[1, 64:] = 1.0
    common = {
        "g1c": cols(g1, 8),
        "w_qkv": np.ascontiguousarray(nat_w_qkv[0]),
        "qg": nat_qg2(nat_q_norm_g[0]),
        "kg": np.ascontiguousarray(np.tile(nat_k_norm_g[0], 2)[:, None]),
        "bias": nat_bias_table(nat_rpb[0]),
        "ohk": ohk,
        "bd": np.kron(np.eye(2, dtype=np.float32), np.ones((64, 64), np.float32)),
        "w_out": np.ascontiguousarray(nat_w_out[0]),
    }
    pens = [nat_pen_table(0), nat_pen_table(1)]
    return [dict(common, x_in=xs[c], pen=pens[c % 2]) if xs is not None else dict(common, pen=pens[c % 2]) for c in range(NCORES)]


def build_ret_prog():
    nc = bass.Bass("TRN2", target_bir_lowering=False)
    dt = lambda name, shape, kind="ExternalInput": nc.dram_tensor(name, shape, F32, kind=kind).ap()
    x_in = dt("x_in", [D, RB])
    g1c = dt("g1c", [128, 8])
    w_in = dt("w_in", [D, 6 * D])
    cosT = dt("cosT", [128, RB])
    sinT = dt("sinT", [128, RB])
    l2d = dt("l2d", [128, 8])
    gng = dt("gng", [128, 16])
    w_out = dt("w_out", [2 * D, D])
    hm = dt("hm", [128, 2])
    x_out = dt("x_out", [D, T], "ExternalOutput")
    with contextlib.ExitStack() as stack:
        P = Prog(nc, stack)
        C = make_common(P, nfr=8)
        emit_ret(P, C, nc, x_in, x_out, g1c, w_in, cosT, sinT, l2d, gng, w_out, hm, is_out=True)
        P.emit()
        print("ret prog stats", P.stats)
    return nc


def ret_rope_tables(hf):
    theta = (1.0 / (np.float32(10000.0) ** np.linspace(0.0, 1.0, 128, dtype=np.float32))).astype(np.float32)
    u = np.arange(RB)
    pos = np.where(u < T, T * hf + u, T * (1 - hf) + (u - T)).astype(np.float32)
    ang = (theta[:, None] * pos[None, :]).astype(np.float32)
    return np.cos(ang).astype(np.float32), np.sin(ang).astype(np.float32)


def ret_inmaps(x, g1, ret_w_in, ret_log2_inv_decay, ret_gn_g, ret_w_out):
    common = {
        "g1c": cols(g1, 8),
        "w_in": np.ascontiguousarray(ret_w_in[0]),
        "l2d": np.ascontiguousarray(np.tile(ret_log2_inv_decay[0].reshape(1, 8), (128, 1))),
        "gng": cols(ret_gn_g[0], 16),
        "w_out": np.ascontiguousarray(ret_w_out[0]),
    }
    tabs = [ret_rope_tables(0), ret_rope_tables(1)]
    maps = []
    for c in range(NCORES):
        b, hf = c // 2, c % 2
        hm = np.zeros((128, 2), np.float32)
        hm[:, 0] = float(hf)
        hm[:, 1] = float(1 - hf)
        if x is None:
            maps.append(dict(common, cosT=tabs[hf][0], sinT=tabs[hf][1], hm=hm))
            continue
        buf = np.concatenate([x[b, hf * T:(hf + 1) * T], x[b, (1 - hf) * T:(2 - hf) * T]], axis=0)
        maps.append(dict(common, x_in=np.ascontiguousarray(buf.T), cosT=tabs[hf][0], sinT=tabs[hf][1], hm=hm))
    return maps


STAGES = [("c0", "conv", HC), ("f0", "ffn", 1), ("n1", "nat", HN), ("f1", "ffn", 1),
          ("r2", "ret", 2048), ("f2", "ffn", 1), ("c3", "conv", HC), ("f3", "ffn", 1)]
STAGE_IN = {
    "conv": [("mask", [128, 2 * HC]), ("g1c", [128, 8]), ("w_in", [D, 2 * D]), ("b_in", [128, 16]), ("dw_w", [128, CW * 8]),
             ("dw_b", [128, 8]), ("ln_g", [128, 8]), ("ln_b", [128, 8]), ("w_out", [D, D])],
    "ffn": [("g2c", [128, 8]), ("w_up", [D, 2 * FFN]), ("dww", [128, 3 * 2 * NH]), ("dwb", [128, 2 * NH]), ("w_down", [FFN, D])],
    "nat": [("g1c", [128, 8]), ("w_qkv", [D, 3 * D]), ("qg", [128, 2]), ("kg", [128, 1]), ("bias", [8, 128, 2 * NE * 128]),
            ("pen", [2, 5 * NE * 128]), ("ohk", [2, 128]), ("bd", [128, 128]), ("w_out", [D, D])],
    "ret": [("g1c", [128, 8]), ("w_in", [D, 6 * D]), ("cosT", [128, RB]), ("sinT", [128, RB]), ("l2d", [128, 8]),
            ("gng", [128, 16]), ("w_out", [2 * D, D]), ("hm", [128, 2])],
}
STAGE_NFR = {"conv": 12, "ffn": 14, "nat": 6, "ret": 8}


def emit_exchange(P, C, nc, name, x_next, H, hmask_sb, hmtok):
    kw = dict(allow_slow_non_contiguous=True) if H < 8 else {}
    groups = [[0, 1], [2, 3], [4, 5], [6, 7]]
    xv = x_next.rearrange("(c p) t -> p c t", p=128)
    W = min(H, 512)
    hr = Ring(P, 2, [128, 8, W], F32, "xh")

    def fill(src2d, dcol, mi, gtok):
        srcv = src2d.rearrange("(c p) t -> p c t", p=128)
        xt, xtok = hr.next()
        P.dma("sp", xt[:], srcv, [gtok], [xtok], xtok, **kw)
        P.add("dve", lambda e, o=xt[:], m=hmask_sb[:, mi:mi + 1]:
              e.tensor_scalar(out=o, in0=o, scalar1=m, scalar2=None, op0=ALU.mult), [xtok, hmtok], [xtok])
        P.dma("sp", xv[:, :, dcol:dcol + W], xt[:], [xtok], [], xtok, **kw)

    if H == 2048:
        for q in range(4):
            snd = nc.dram_tensor(f"{name}_snd{q}", [D, 512], F32, kind="Internal").ap()
            gath = nc.dram_tensor(f"{name}_gath{q}", [2 * D, 512], F32, kind="Internal").ap()
            stok, gtok, cctok = Tok("snd"), Tok("gath"), Tok("cc")
            P.dma("sp", snd, x_next[:, q * 512:(q + 1) * 512], [], [stok], stok)
            P.coll(lambda e, s_=snd, g_=gath: e.collective_compute("AllGather", ALU.bypass, replica_groups=groups,
                                                                   ins=[s_], outs=[g_]), [stok], [gtok], cctok)
            r0, r0tok = hr.next()
            r1, r1tok = hr.next()
            P.dma("sp", r0[:], gath[0:D, :].rearrange("(c p) t -> p c t", p=128), [gtok], [r0tok], r0tok)
            P.dma("sp", r1[:], gath[D:2 * D, :].rearrange("(c p) t -> p c t", p=128), [gtok], [r1tok], r1tok)
            P.add("dve", lambda e, o=r0[:]: e.tensor_scalar(out=o, in0=o, scalar1=hmask_sb[:, 0:1], scalar2=None, op0=ALU.mult),
                  [r0tok, hmtok], [r0tok])
            P.add("dve", lambda e, o=r0[:], a=r1[:]: e.scalar_tensor_tensor(out=o, in0=a, scalar=hmask_sb[:, 1:2], in1=o,
                                                                           op0=ALU.mult, op1=ALU.add), [r0tok, r1tok, hmtok], [r0tok])
            P.dma("sp", xv[:, :, T + q * 512:T + (q + 1) * 512], r0[:], [r0tok], [], r0tok)
        return
    snd = nc.dram_tensor(name + "_snd", [2 * D, H], F32, kind="Internal").ap()
    gath = nc.dram_tensor(name + "_gath", [4 * D, H], F32, kind="Internal").ap()
    stok, gtok, cctok = Tok("snd"), Tok("gath"), Tok("cc")
    P.dma("sp", snd[0:D, :], x_next[:, H:2 * H], [], [stok], stok, **kw)
    P.dma("sp", snd[D:2 * D, :], x_next[:, T:T + H], [], [stok], stok, **kw)
    P.coll(lambda e: e.collective_compute("AllGather", ALU.bypass, replica_groups=groups,
                                          ins=[snd], outs=[gath]), [stok], [gtok], cctok)
    fill(gath[D:2 * D, :], 0, 0, gtok)
    fill(gath[2 * D:3 * D, :], H + T, 1, gtok)


def build_fused_prog(nst=8):
    stages = STAGES[:nst]
    nc = bass.Bass("TRN2", target_bir_lowering=False)
    dt = lambda name, shape, kind="ExternalInput": nc.dram_tensor(name, shape, F32, kind=kind).ap()
    aps = {}
    for (sn, kind, H) in stages:
        aps[sn] = {k: dt(f"{sn}_{k}", shp) for (k, shp) in STAGE_IN[kind]}
    hmask = dt("hmask", [128, 2])
    bufs = {}
    for si, (sn, kind, H) in enumerate(stages):
        width = RB if kind == "ret" else T + 2 * H
        bufs[sn] = dt(f"{sn}_x_in", [D, width], "ExternalInput" if si == 0 else "Internal")
    y = dt("x_out", [D, T], "ExternalOutput")
    with contextlib.ExitStack() as stack:
        P = Prog(nc, stack)
        for si, (sn, kind, H) in enumerate(stages):
            last = si == len(stages) - 1
            x_in = bufs[sn]
            if last:
                x_out = y
            else:
                nsn, nkind, nH = stages[si + 1]
                x_out = bufs[nsn][:, 0:T] if nkind == "ret" else bufs[nsn][:, nH:nH + T]
            a = aps[sn]
            with contextlib.ExitStack() as st:
                P.stack = st
                P.pfx = sn + "_"
                C = make_common(P, nfr=STAGE_NFR[kind])
                if kind == "conv":
                    emit_conv(P, C, x_in, x_out, a["mask"], a["g1c"], a["w_in"], a["b_in"], a["dw_w"], a["dw_b"], a["ln_g"],
                              a["ln_b"], a["w_out"], is_out=last)
                elif kind == "ffn":
                    emit_ffn(P, C, x_in, x_out, a["g2c"], a["w_up"], a["dww"], a["dwb"], a["w_down"], is_out=last)
                elif kind == "nat":
                    emit_nat(P, C, x_in, x_out, a["g1c"], a["w_qkv"], a["qg"], a["kg"], a["bias"], a["pen"], a["ohk"], a["bd"],
                             a["w_out"], is_out=last)
                else:
                    emit_ret(P, C, nc, x_in, x_out, a["g1c"], a["w_in"], a["cosT"], a["sinT"], a["l2d"], a["gng"], a["w_out"],
                             a["hm"], is_out=last)
            P.barrier()
            if not last:
                with contextlib.ExitStack() as st:
                    P.stack = st
                    P.pfx = sn + "x_"
                    C = make_common(P, nfr=2)
                    hm_sb = P.sb([128, 2], F32, "hmask")
                    hmtok = load_cols(P, C, hm_sb, hmask)
                    emit_exchange(P, C, nc, sn + "x", bufs[nsn], nH, hm_sb, hmtok)
                P.barrier()
        P.stack = stack
        P.pfx = ""
        P.emit()
        print("fused prog stats", P.stats)
    return nc


def fused_inmaps(a):
    per_stage = {}
    per_stage["c0"] = conv_inmaps(a["x"], 0, a["norm1_g"][0], a["conv_w_in"], a["conv_b_in"], a["conv_dw_w"], a["conv_dw_b"],
                                  a["conv_ln_g"], a["conv_ln_b"], a["conv_w_out"])
    per_stage["c3"] = conv_inmaps(None, 1, a["norm1_g"][3], a["conv_w_in"], a["conv_b_in"], a["conv_dw_w"], a["conv_dw_b"],
                                  a["conv_ln_g"], a["conv_ln_b"], a["conv_w_out"])
    per_stage["n1"] = nat_inmaps(None, a["norm1_g"][1], a["nat_w_qkv"], a["nat_q_norm_g"], a["nat_k_norm_g"], a["nat_rpb"],
                                 a["nat_w_out"])
    per_stage["r2"] = ret_inmaps(None, a["norm1_g"][2], a["ret_w_in"], a["ret_log2_inv_decay"], a["ret_gn_g"], a["ret_w_out"])
    for i in range(4):
        per_stage[f"f{i}"] = ffn_inmaps(None, i, a["norm2_g"], a["ffn_w_up"], a["ffn_dw_w"], a["ffn_dw_b"], a["ffn_w_down"])
    maps = []
    for c in range(NCORES):
        m = {}
        for sn, lst in per_stage.items():
            for k, v in lst[c].items():
                m[f"{sn}_{k}"] = v
        hm = np.zeros((128, 2), np.float32)
        hm[:, 0] = float(c % 2)
        hm[:, 1] = float(1 - c % 2)
        m["hmask"] = hm
        maps.append(m)
    return maps


_PROGS = {}


def _prog(name, builder):
    if name not in _PROGS:
        _PROGS[name] = builder()
    return _PROGS[name]


def _launch(nc, maps):
    res = run_bass_kernel_spmd(nc, maps, core_ids=list(range(NCORES)))
    return unshard_tokens_fm([r["x_out"] for r in res.results])


def kernel_unfused(x, norm1_g, norm2_g, conv_w_in, conv_b_in, conv_dw_w, conv_dw_b, conv_ln_g, conv_ln_b, conv_w_out,
           nat_w_qkv, nat_q_norm_g, nat_k_norm_g, nat_rpb, nat_w_out, ret_w_in, ret_log2_inv_decay, ret_gn_g,
           ret_w_out, ffn_w_up, ffn_dw_w, ffn_dw_b, ffn_w_down):
    a = {k: np.asarray(v, np.float32) for k, v in locals().items()}
    xc = a["x"]
    for i in range(4):
        mixer, j = i % 3, i // 3
        if mixer == 0:
            maps = conv_inmaps(xc, j, a["norm1_g"][i], a["conv_w_in"], a["conv_b_in"], a["conv_dw_w"], a["conv_dw_b"],
                               a["conv_ln_g"], a["conv_ln_b"], a["conv_w_out"])
            xc = _launch(_prog("conv", build_conv_prog), maps)
        elif mixer == 1:
            maps = nat_inmaps(xc, a["norm1_g"][i], a["nat_w_qkv"], a["nat_q_norm_g"], a["nat_k_norm_g"], a["nat_rpb"],
                              a["nat_w_out"])
            xc = _launch(_prog("nat", build_nat_prog), maps)
        else:
            maps = ret_inmaps(xc, a["norm1_g"][i], a["ret_w_in"], a["ret_log2_inv_decay"], a["ret_gn_g"], a["ret_w_out"])
            xc = _launch(_prog("ret", build_ret_prog), maps)
        maps = ffn_inmaps(xc, i, a["norm2_g"], a["ffn_w_up"], a["ffn_dw_w"], a["ffn_dw_b"], a["ffn_w_down"])
        xc = _launch(_prog("ffn", build_ffn_prog), maps)
    return xc


def kernel(x, norm1_g, norm2_g, conv_w_in, conv_b_in, conv_dw_w, conv_dw_b, conv_ln_g, conv_ln_b, conv_w_out,
           nat_w_qkv, nat_q_norm_g, nat_k_norm_g, nat_rpb, nat_w_out, ret_w_in, ret_log2_inv_decay, ret_gn_g,
           ret_w_out, ffn_w_up, ffn_dw_w, ffn_dw_b, ffn_w_down):
    a = {k: np.asarray(v, np.float32) for k, v in locals().items()}
    maps = fused_inmaps(a)
    nc = _prog("fused", build_fused_prog)
    res = run_bass_kernel_spmd(nc, maps, core_ids=list(range(NCORES)))
    return unshard_tokens_fm([r["x_out"] for r in res.results])
```
